# Optimizing a Trainium2 kernel written in Bass

```python
import jax, jax.numpy as jnp
from jax import lax
import numpy as np

D_MODEL = 4096
BATCH = 2
SEQ = 4096
DEPTH = 1
DEC_BATCH = 32
DEC_SEQ = 4
PAST_LEN = 8192
PAGE_SIZE = 128

HEAD_DIM = 128
POOL_WIDTH = D_MODEL // 4
POOL_WINDOWS = (2, 4, 8, 16)
POOL_GROUP = POOL_WIDTH // len(POOL_WINDOWS)
POOL_MAX = max(POOL_WINDOWS)
NSA_WIDTH = D_MODEL - POOL_WIDTH
N_HEADS = NSA_WIDTH // HEAD_DIM
KV_HEADS = 4
GROUP = N_HEADS // KV_HEADS
KV_WIDTH = KV_HEADS * HEAD_DIM
N_BRANCH = 3
D_IN_PROJ = POOL_WIDTH + NSA_WIDTH + 6 * KV_WIDTH + N_BRANCH * N_HEADS
CMP_LEN = 32
CMP_STRIDE = 16
CMP_RATIO = CMP_LEN // CMP_STRIDE
SLC_LEN = 64
N_SEL = 16
WINDOW = 512
Q_BLOCK = 128
ROPE_THETA = 10000.0
D_FF = -(-8 * D_MODEL // (3 * 256)) * 256
LN_EPS = 1e-5
ALPHA = (2 * DEPTH) ** 0.25
BETA = (8 * DEPTH) ** -0.25

kernel_name = "hymba_pool_nsa_deepnorm_step"


def _rope(x, pos):
    half = HEAD_DIM // 2
    inv = ROPE_THETA ** (-jnp.arange(half, dtype=jnp.float32) / half)
    ang = pos.astype(jnp.float32)[:, None] * inv[None, :]
    cos = jnp.cos(ang)[:, None, :]
    sin = jnp.sin(ang)[:, None, :]
    xf = x.astype(jnp.float32)
    x1, x2 = xf[..., :half], xf[..., half:]
    return jnp.concatenate([x1 * cos - x2 * sin, x2 * cos + x1 * sin], -1).astype(x.dtype)


def _layernorm(x, g, b):
    xf = x.astype(jnp.float32)
    mu = jnp.mean(xf, -1, keepdims=True)
    var = jnp.mean(jnp.square(xf - mu), -1, keepdims=True)
    return ((xf - mu) * lax.rsqrt(var + LN_EPS) * g + b).astype(x.dtype)


def _masked_softmax(s, mask, axis):
    s = jnp.where(mask, s, -jnp.inf)
    m = jnp.max(s, axis=axis, keepdims=True)
    m = jnp.where(jnp.isfinite(m), m, 0.0)
    e = jnp.where(mask, jnp.exp(s - m), 0.0)
    return e / jnp.maximum(jnp.sum(e, axis=axis, keepdims=True), 1e-30)


def _project(x, w_in_l, pos):
    B, T, _ = x.shape
    z = jnp.einsum('btd,de->bte', x, w_in_l)
    u = z[..., :POOL_WIDTH]
    o = POOL_WIDTH
    q = _rope(z[..., o:o + NSA_WIDTH].reshape(B, T, N_HEADS, HEAD_DIM), pos)
    q = q.reshape(B, T, KV_HEADS, GROUP, HEAD_DIM)
    o += NSA_WIDTH
    kv = z[..., o:o + 6 * KV_WIDTH].reshape(B, T, 6, KV_HEADS, HEAD_DIM)
    o += 6 * KV_WIDTH
    gates = jax.nn.sigmoid(z[..., o:o + N_BRANCH * N_HEADS].astype(jnp.float32))
    gates = gates.reshape(B, T, KV_HEADS, GROUP, N_BRANCH)
    kc, vc = kv[:, :, 0], kv[:, :, 1]
    ks, vs = _rope(kv[:, :, 2], pos), kv[:, :, 3]
    kw, vw = _rope(kv[:, :, 4], pos), kv[:, :, 5]
    return u, q, kc, vc, ks, vs, kw, vw, gates


def _pool_mix(u_all, pos_out, w_pool_l, scale_l):
    B, Ta, _ = u_all.shape
    n = pos_out.shape[0]
    uf = u_all.astype(jnp.float32)
    cs = jnp.concatenate([jnp.zeros((B, 1, POOL_WIDTH), jnp.float32), jnp.cumsum(uf, axis=1)], axis=1)
    hi = cs[:, Ta - n + 1:]
    cur = uf[:, Ta - n:]
    rows = jnp.arange(Ta - n, Ta, dtype=jnp.int32)
    outs = []
    for g, w in enumerate(POOL_WINDOWS):
        c0, c1 = g * POOL_GROUP, (g + 1) * POOL_GROUP
        lo = cs[:, jnp.maximum(rows + 1 - w, 0), c0:c1]
        cnt = jnp.minimum(pos_out + 1, w).astype(jnp.float32)[None, :, None]
        d = (hi[..., c0:c1] - lo) / cnt - cur[..., c0:c1]
        outs.append(jnp.einsum('btc,ce->bte', d.astype(u_all.dtype), w_pool_l[g]))
    return jnp.concatenate(outs, -1) * scale_l


def _compress(rows, w1, pe, w2):
    B, T = rows.shape[:2]
    n_chunk = T // CMP_STRIDE
    chunks = rows[:, :n_chunk * CMP_STRIDE].reshape(B, n_chunk, CMP_STRIDE, KV_HEADS, HEAD_DIM)
    part = jnp.einsum('bnpgd,apde->abnge', chunks,
                      w1.reshape(CMP_RATIO, CMP_STRIDE, HEAD_DIM, HEAD_DIM))
    n_cmp = n_chunk - CMP_RATIO + 1
    hid = jnp.einsum('pd,pde->e', pe, w1)
    for a in range(CMP_RATIO):
        hid = hid + part[a, :, a:a + n_cmp]
    return jnp.einsum('bnge,ef->bngf', jax.nn.gelu(hid), w2)


def _to_blocks(rows):
    B, T = rows.shape[:2]
    n_slc = -(-T // SLC_LEN)
    rows = jnp.pad(rows, ((0, 0), (0, n_slc * SLC_LEN - T), (0, 0), (0, 0)))
    return rows.reshape(B, n_slc, SLC_LEN, KV_HEADS, HEAD_DIM).transpose(0, 3, 1, 2, 4)


def _nsa_keys(kc_rows, vc_rows, ks_rows, vs_rows, w1k, pek, w2k, w1v, pev, w2v):
    kcb = _compress(kc_rows, w1k, pek, w2k)
    n_cmp = kcb.shape[1]
    cmp_end = jnp.arange(n_cmp, dtype=jnp.int32) * CMP_STRIDE + CMP_LEN - 1
    kcb = _rope(kcb, cmp_end)
    vcb = _compress(vc_rows, w1v, pev, w2v)
    return kcb, vcb, cmp_end, _to_blocks(ks_rows), _to_blocks(vs_rows)


def _nsa_attend(q, q_pos, kcb, vcb, cmp_end, k_sb, v_sb, kw, vw, win_pos, gates):
    B, Tq = q.shape[:2]
    f32 = jnp.float32
    scale = HEAD_DIM ** -0.5
    s_c = jnp.einsum('btgrd,bngd->bgrtn', q, kcb).astype(f32) * scale
    p_c = _masked_softmax(s_c, cmp_end[None, :] <= q_pos[:, None], -1)
    o_c = jnp.einsum('bgrtn,bngd->btgrd', p_c.astype(vcb.dtype), vcb)
    n_slc = k_sb.shape[2]
    blk = jnp.arange(n_slc, dtype=jnp.int32)
    blk_start = blk * SLC_LEN
    cmp_start = cmp_end - (CMP_LEN - 1)
    overlap = ((cmp_start[:, None] < blk_start[None, :] + SLC_LEN)
               & (cmp_end[:, None] >= blk_start[None, :])).astype(f32)
    imp = jnp.einsum('bgrtn,ns->bgts', p_c, overlap)
    cur = q_pos // SLC_LEN
    forced = (blk[None, :] == 0) | (blk[None, :] == cur[:, None]) | (blk[None, :] == cur[:, None] - 1)
    imp = jnp.where(forced, jnp.inf, jnp.where(blk[None, :] <= cur[:, None], imp, -jnp.inf))
    top_v, top_i = lax.top_k(imp, min(N_SEL, n_slc))
    gather = jax.vmap(jax.vmap(lambda kb, ii: kb[ii]))
    k_sel = gather(k_sb, top_i)
    v_sel = gather(v_sb, top_i)
    s_s = jnp.einsum('btgrd,bgtksd->bgrtks', q, k_sel).astype(f32) * scale
    key_pos = top_i[..., None] * SLC_LEN + jnp.arange(SLC_LEN, dtype=jnp.int32)
    mask_s = (top_v > -jnp.inf)[..., None] & (key_pos <= q_pos[:, None, None])
    sh = s_s.shape
    p_s = _masked_softmax(s_s.reshape(sh[:4] + (-1,)),
                          mask_s[:, :, None].reshape(B, KV_HEADS, 1, Tq, -1), -1).reshape(sh)
    o_s = jnp.einsum('bgrtks,bgtksd->btgrd', p_s.astype(v_sel.dtype), v_sel)
    s_w = jnp.einsum('btgrd,bsgd->bgrts', q, kw).astype(f32) * scale
    rel = q_pos[:, None] - win_pos[None, :]
    mask_w = (rel >= 0) & (rel < WINDOW) & (win_pos[None, :] >= 0)
    p_w = _masked_softmax(s_w, mask_w, -1)
    o_w = jnp.einsum('bgrts,bsgd->btgrd', p_w.astype(vw.dtype), vw)
    o = (gates[..., 0:1] * o_c.astype(f32) + gates[..., 1:2] * o_s.astype(f32)
         + gates[..., 2:3] * o_w.astype(f32))
    return o.astype(q.dtype)


def _nsa_prompt(q, gates, kcb, vcb, cmp_end, k_sb, v_sb, kw, vw):
    B, T = q.shape[:2]
    kw_pad = jnp.pad(kw, ((0, 0), (WINDOW, 0), (0, 0), (0, 0)))
    vw_pad = jnp.pad(vw, ((0, 0), (WINDOW, 0), (0, 0), (0, 0)))

    def one_block(c):
        start = c * Q_BLOCK
        qb = lax.dynamic_slice_in_dim(q, start, Q_BLOCK, axis=1)
        gb = lax.dynamic_slice_in_dim(gates, start, Q_BLOCK, axis=1)
        kwb = lax.dynamic_slice_in_dim(kw_pad, start, WINDOW + Q_BLOCK, axis=1)
        vwb = lax.dynamic_slice_in_dim(vw_pad, start, WINDOW + Q_BLOCK, axis=1)
        q_pos = start + jnp.arange(Q_BLOCK, dtype=jnp.int32)
        win_pos = start - WINDOW + jnp.arange(WINDOW + Q_BLOCK, dtype=jnp.int32)
        return _nsa_attend(qb, q_pos, kcb, vcb, cmp_end, k_sb, v_sb, kwb, vwb, win_pos, gb)

    o = lax.map(one_block, jnp.arange(T // Q_BLOCK, dtype=jnp.int32))
    return jnp.moveaxis(o, 0, 1).reshape(B, T, NSA_WIDTH)


def _finish(x, pool_out, nsa_out, w_o_l, ln1_g_l, ln1_b_l, w_gate_l, w_up_l, w_down_l, ln2_g_l, ln2_b_l):
    mixed = jnp.concatenate([pool_out.astype(x.dtype), nsa_out.astype(x.dtype)], -1)
    h = _layernorm(ALPHA * x + jnp.einsum('btd,de->bte', mixed, w_o_l), ln1_g_l, ln1_b_l)
    ff = jax.nn.silu(jnp.einsum('btd,df->btf', h, w_gate_l)) * jnp.einsum('btd,df->btf', h, w_up_l)
    f = jnp.einsum('btf,fd->btd', ff, w_down_l)
    return _layernorm(ALPHA * h + f, ln2_g_l, ln2_b_l)


def setup_inputs(seed: int = 0) -> dict:
    key = jax.random.key(seed)
    k = jax.random.split(key, 32)
    f32 = jnp.float32
    n_pages = PAST_LEN // PAGE_SIZE
    n_phys = (5 * DEC_BATCH * n_pages + 3) // 4
    win_buf = min(WINDOW, PAST_LEN)

    def nrm(kk, shape, scale=1.0):
        return scale * jax.random.normal(kk, shape, f32)

    page_table = jax.random.permutation(k[0], n_phys)[:DEC_BATCH * n_pages]
    page_table = page_table.reshape(DEC_BATCH, n_pages).astype(jnp.int32)
    paged = (DEPTH, n_phys, PAGE_SIZE, KV_HEADS, HEAD_DIM)
    win = (DEPTH, DEC_BATCH, win_buf, KV_HEADS, HEAD_DIM)
    cmp1 = (CMP_LEN * HEAD_DIM) ** -0.5
    return {
        'x_prompt': nrm(k[1], (BATCH, SEQ, D_MODEL)),
        'x_sample': nrm(k[2], (DEC_BATCH, DEC_SEQ, D_MODEL)),
        'cache_k_cmp': nrm(k[3], paged),
        'cache_v_cmp': nrm(k[4], paged),
        'cache_k_slc': nrm(k[5], paged),
        'cache_v_slc': nrm(k[6], paged),
        'state_k_win': nrm(k[7], win),
        'state_v_win': nrm(k[8], win),
        'state_pool': nrm(k[9], (DEPTH, DEC_BATCH, POOL_MAX - 1, POOL_WIDTH)),
        'page_table': page_table,
        'w_in': nrm(k[10], (DEPTH, D_MODEL, D_IN_PROJ), D_MODEL ** -0.5),
        'w_cmp1_k': nrm(k[11], (DEPTH, CMP_LEN, HEAD_DIM, HEAD_DIM), cmp1),
        'pe_cmp_k': nrm(k[12], (DEPTH, CMP_LEN, HEAD_DIM), 0.5),
        'w_cmp2_k': nrm(k[13], (DEPTH, HEAD_DIM, HEAD_DIM), HEAD_DIM ** -0.5),
        'w_cmp1_v': nrm(k[14], (DEPTH, CMP_LEN, HEAD_DIM, HEAD_DIM), cmp1),
        'pe_cmp_v': nrm(k[15], (DEPTH, CMP_LEN, HEAD_DIM), 0.5),
        'w_cmp2_v': nrm(k[16], (DEPTH, HEAD_DIM, HEAD_DIM), HEAD_DIM ** -0.5),
        'w_pool': nrm(k[17], (DEPTH, len(POOL_WINDOWS), POOL_GROUP, POOL_GROUP), POOL_GROUP ** -0.5),
        'pool_scale': 1.0 + nrm(k[18], (DEPTH, POOL_WIDTH), 0.02),
        'w_o': nrm(k[19], (DEPTH, D_MODEL, D_MODEL), BETA * D_MODEL ** -0.5),
        'ln1_g': 1.0 + nrm(k[20], (DEPTH, D_MODEL), 0.02),
        'ln1_b': nrm(k[21], (DEPTH, D_MODEL), 0.02),
        'w_gate': nrm(k[22], (DEPTH, D_MODEL, D_FF), D_MODEL ** -0.5),
        'w_up': nrm(k[23], (DEPTH, D_MODEL, D_FF), D_MODEL ** -0.5),
        'w_down': nrm(k[24], (DEPTH, D_FF, D_MODEL), BETA * D_FF ** -0.5),
        'ln2_g': 1.0 + nrm(k[25], (DEPTH, D_MODEL), 0.02),
        'ln2_b': nrm(k[26], (DEPTH, D_MODEL), 0.02),
    }


def reference(x_prompt, x_sample, cache_k_cmp, cache_v_cmp, cache_k_slc, cache_v_slc,
              state_k_win, state_v_win, state_pool, page_table, w_in,
              w_cmp1_k, pe_cmp_k, w_cmp2_k, w_cmp1_v, pe_cmp_v, w_cmp2_v,
              w_pool, pool_scale, w_o, ln1_g, ln1_b, w_gate, w_up, w_down, ln2_g, ln2_b):
    B, T, _ = x_prompt.shape
    DB, S, _ = x_sample.shape
    past = page_table.shape[1] * PAGE_SIZE
    wb = state_k_win.shape[2]
    wbp = min(WINDOW, T)
    pos_p = jnp.arange(T, dtype=jnp.int32)
    pos_s = past + jnp.arange(S, dtype=jnp.int32)
    win_pos_s = past - wb + jnp.arange(wb + S, dtype=jnp.int32)
    hp, hs = x_prompt, x_sample
    prompt_states, sample_states = [], []
    for l in range(DEPTH):
        cmp_w = (w_cmp1_k[l], pe_cmp_k[l], w_cmp2_k[l], w_cmp1_v[l], pe_cmp_v[l], w_cmp2_v[l])
        tail = (w_o[l], ln1_g[l], ln1_b[l], w_gate[l], w_up[l], w_down[l], ln2_g[l], ln2_b[l])
        u, q, kc, vc, ks, vs, kw, vw, g = _project(hp, w_in[l], pos_p)
        pool_out = _pool_mix(u, pos_p, w_pool[l], pool_scale[l])
        kcb, vcb, cmp_end, k_sb, v_sb = _nsa_keys(kc, vc, ks, vs, *cmp_w)
        nsa_out = _nsa_prompt(q, g, kcb, vcb, cmp_end, k_sb, v_sb, kw, vw)
        prompt_states.append((kc, vc, ks, vs, kw[:, T - wbp:], vw[:, T - wbp:], u[:, T - (POOL_MAX - 1):]))
        hp = _finish(hp, pool_out, nsa_out, *tail)
        u, q, kc, vc, ks, vs, kw, vw, g = _project(hs, w_in[l], pos_s)

        def paged_rows(cache):
            return cache[l, page_table].reshape(DB, past, KV_HEADS, HEAD_DIM)

        kc_all = jnp.concatenate([paged_rows(cache_k_cmp), kc], 1)
        vc_all = jnp.concatenate([paged_rows(cache_v_cmp), vc], 1)
        ks_all = jnp.concatenate([paged_rows(cache_k_slc), ks], 1)
        vs_all = jnp.concatenate([paged_rows(cache_v_slc), vs], 1)
        kw_all = jnp.concatenate([state_k_win[l], kw], 1)
        vw_all = jnp.concatenate([state_v_win[l], vw], 1)
        u_all = jnp.concatenate([state_pool[l], u], 1)
        pool_out = _pool_mix(u_all, pos_s, w_pool[l], pool_scale[l])
        kcb, vcb, cmp_end, k_sb, v_sb = _nsa_keys(kc_all, vc_all, ks_all, vs_all, *cmp_w)
        nsa_out = _nsa_attend(q, pos_s, kcb, vcb, cmp_end, k_sb, v_sb, kw_all, vw_all,
                              win_pos_s, g).reshape(DB, S, NSA_WIDTH)
        sample_states.append((kc, vc, ks, vs, kw_all[:, S:], vw_all[:, S:], u_all[:, S:]))
        hs = _finish(hs, pool_out, nsa_out, *tail)
    (kc_p, vc_p, ks_p, vs_p, kw_p, vw_p, pool_p) = [jnp.stack(a, 0) for a in zip(*prompt_states)]
    (kc_s, vc_s, ks_s, vs_s, kw_s, vw_s, pool_s) = [jnp.stack(a, 0) for a in zip(*sample_states)]
    return (hp, hs, kc_p, vc_p, ks_p, vs_p, kw_p, vw_p, pool_p,
            kc_s, vc_s, ks_s, vs_s, kw_s, vw_s, pool_s)
```

```python
import contextlib
import numpy as np
import concourse.bass as bass
import concourse.mybir as mybir
from concourse.bass_utils import run_bass_kernel_spmd

F32 = mybir.dt.float32
BF16 = mybir.dt.bfloat16
I32 = mybir.dt.int32
AF = mybir.ActivationFunctionType
ALU = mybir.AluOpType
AX = mybir.AxisListType

D = 4096
DFF = 11008
NFC = 86
TOK = 1040
ALPHA = 2.0 ** 0.25
EPS = 1e-5
E_PAD = 7680


class Buf:
    __slots__ = ("name", "w", "r")

    def __init__(self, name=""):
        self.name = name
        self.w = None
        self.r = {}


class Op:
    __slots__ = ("eng", "fn", "deps", "dma", "needed", "sem", "semval", "done", "slot")

    def __init__(self, eng, fn, dma):
        self.eng = eng
        self.fn = fn
        self.deps = []
        self.dma = dma
        self.needed = False
        self.sem = None
        self.semval = None
        self.done = False
        self.slot = None


class Prog:
    ENGS = ("pe", "act", "dve", "pool", "sp")
    NDS = 16

    def __init__(self, nc, st):
        self.nc = nc
        self.ops = {k: [] for k in self.ENGS}
        self.engsem = {k: st.enter_context(nc.semaphore("es_" + k)) for k in self.ENGS}
        self.nds = {"sp": 16, "pool": 6, "act": 4}
        self.dsem = {q: [st.enter_context(nc.semaphore("ds_%s_%d" % (q, i))) for i in range(self.nds[q])]
                     for q in ("sp", "pool", "act")}
        self.cnt = {k: 0 for k in self.ENGS}
        self.dma_n = {q: 0 for q in self.dsem}
        self.dma_uses = {q: [0] * self.nds[q] for q in self.dsem}
        self.dma_last = {q: [None] * self.nds[q] for q in self.dsem}
        self.phase_dma = []

    def clear_sems(self):
        nc = self.nc
        with nc.Block() as block:
            def body(e):
                for s in self.engsem.values():
                    e.sem_clear(s)
                for lst in self.dsem.values():
                    for s in lst:
                        e.sem_clear(s)
            block.sync(body)

    def op(self, eng, fn, reads=(), writes=(), dma=False):
        o = Op(eng, fn, dma)
        deps = []
        for b in reads:
            if b.w is not None:
                deps.append(b.w)
        for b in writes:
            if b.w is not None:
                deps.append(b.w)
            deps.extend(b.r.values())
        if dma:
            s = self.dma_n[eng] % self.nds[eng]
            self.dma_n[eng] += 1
            o.slot = s
            if self.dma_last[eng][s] is not None:
                deps.append(self.dma_last[eng][s])
            self.dma_uses[eng][s] += 1
            o.sem = self.dsem[eng][s]
            o.semval = 16 * self.dma_uses[eng][s]
            self.dma_last[eng][s] = o
            self.phase_dma.append(o)
        seen = set()
        for d in deps:
            if id(d) in seen or d.done:
                continue
            seen.add(id(d))
            if (not d.dma) and d.eng == eng and eng == "pe":
                continue
            d.needed = True
            o.deps.append(d)
        self.ops[eng].append(o)
        for b in writes:
            b.w = o
            b.r = {}
        for b in reads:
            if b.w is o:
                continue
            key = ("dma", eng, o.slot) if dma else eng
            b.r[key] = o
        return o

    def emit_phase(self):
        nc = self.nc
        for k in self.ENGS:
            for o in self.ops[k]:
                if not o.dma and o.needed:
                    self.cnt[k] += 1
                    o.sem = self.engsem[k]
                    o.semval = self.cnt[k]
        finals = {}
        for o in self.phase_dma:
            finals[(o.eng, o.slot)] = o
        finals = list(finals.values())

        def make_body(k):
            def body(e):
                waited = {}

                def wait(d):
                    sid = id(d.sem)
                    if waited.get(sid, 0) >= d.semval:
                        return
                    waited[sid] = d.semval
                    e.wait_ge(d.sem, d.semval)

                for o in self.ops[k]:
                    for d in o.deps:
                        wait(d)
                    ins = o.fn(e)
                    if o.dma:
                        ins.then_inc(o.sem, 16)
                    elif o.needed:
                        ins.then_inc(o.sem, 1)
                if k == "sp":
                    for d in finals:
                        wait(d)
            return body

        with nc.Block() as block:
            block.tensor(make_body("pe"))
            block.scalar(make_body("act"))
            block.vector(make_body("dve"))
            block.gpsimd(make_body("pool"))
            block.sync(make_body("sp"))
        for k in self.ENGS:
            for o in self.ops[k]:
                o.done = True
            self.ops[k] = []
        self.phase_dma = []


def build_program():
    nc = bass.Bass("TRN2", target_bir_lowering=False)

    def din(name, shape, dt=F32):
        return nc.dram_tensor(name, list(shape), dt, kind="ExternalInput").ap()

    def dout(name, shape, dt=F32):
        return nc.dram_tensor(name, list(shape), dt, kind="ExternalOutput").ap()

    _uid = [0]

    def uname(name):
        _uid[0] += 1
        return "%s_%d" % (name, _uid[0])

    def dscr(name, shape, dt=F32):
        return nc.dram_tensor(name, list(shape), dt, kind="Internal").ap()

    xTs = din("xTs", [4, 128, 32, 1024])
    xsT = din("xsT", [128, 32, 16])
    x_own = din("x_own", [TOK, D])
    w_in = din("w_in", [15, 128, 32, 512])
    w_o = din("w_o", [8, 128, 32, 512])
    w_g = din("w_g", [NFC, 128, 32, 128])
    w_u = din("w_u", [NFC, 128, 32, 128])
    w_d = din("w_d", [8, 128, NFC, 512])
    cosT = din("cosT", [4112, 64])
    sinT = din("sinT", [4112, 64])
    lnp = din("lnp", [4, 128, D])
    ident = din("ident", [128, 128])
    st_kw = din("st_kw", [4, 512, 512])
    st_vw = din("st_vw", [4, 512, 512])
    st_pool = din("st_pool", [4, 15, 1024])
    rc_d = din("rc_d", [128, 4, TOK])
    wp_d = din("wp_d", [128, 4, 2, 256])
    psc_d = din("psc_d", [128, 8])
    stp_d = din("stp_d", [128, 8, 4, 15])
    cbc_d = din("cbc_d", [128, 128])
    cba_d = din("cba_d", [128, 128])
    padb_d = din("padb_d", [128, 32])
    w1k_d = din("w1k_d", [128, 32, 128])
    w1v_d = din("w1v_d", [128, 32, 128])
    w2_d = din("w2_d", [128, 3, 128])
    peT_d = din("peT_d", [128, 2, 32])
    ccmp_d = din("ccmp_d", [128, 256])
    scmp_d = din("scmp_d", [128, 256])
    cval_d = din("cval_d", [128, 8, 256])
    addm_d = din("addm_d", [128, 8, 64])
    vblk_d = din("vblk_d", [128, 8, 64])
    ckc_d = din("ckc_d", [2560 * 32, 2048])
    cvc_d = din("cvc_d", [2560 * 32, 2048])
    cks_d = din("cks_d", [2560 * 32, 2048])
    cvs_d = din("cvs_d", [2560 * 32, 2048])
    ptab_d = din("ptab_d", [4, 64], I32)
    oh_d = din("oh_d", [128, 5])
    ccs_d = din("ccs_d", [128, 512])
    scs_d = din("scs_d", [128, 512])
    addms_d = din("addms_d", [4, 129])

    kvout = dout("kvout", [6, 4112, 512])
    kws_o = dout("kws_o", [4, 512, 512])
    vws_o = dout("vws_o", [4, 512, 512])
    poolp_o = dout("poolp_o", [15, 1024])
    pools_o = dout("pools_o", [4, 15, 1024])
    y_o = dout("y_o", [TOK, D])

    mixT_s = dscr("mixT_s", [128, 32, TOK], BF16)
    r_s = dscr("r_s", [TOK, D])
    h_s = dscr("h_s", [TOK, D])
    hT_s = dscr("hT_s", [128, 32, TOK], BF16)
    y_s = dscr("y_s", [TOK, D])
    q_s = dscr("q_s", [TOK, 3072])
    gate_s = dscr("gate_s", [TOK, 72])

    TT9 = [(i * 128, 128) for i in range(8)] + [(1024, 16)]

    with contextlib.ExitStack() as gst:
        P = Prog(nc, gst)
        P.clear_sems()
        psum = [gst.enter_context(nc.psum_tensor("ps%d" % i, [128, 512], F32)) for i in range(7)]
        psb = gst.enter_context(nc.psum_tensor("psb", [128, 1024], BF16))

        with contextlib.ExitStack() as st:
            sb = lambda name, shape, dt: st.enter_context(nc.sbuf_tensor(uname(name), list(shape), dt))
            xb_t = sb("xb_t", [128, 32, 1040], BF16)
            wt = [sb("wt%d" % i, [128, 32, 512], BF16) for i in range(2)]
            cos_t = sb("cos_t", [128, 9, 64], F32)
            sin_t = sb("sin_t", [128, 9, 64], F32)
            o32 = [sb("o32_%d" % i, [128, 512], F32) for i in range(4)]
            ta = sb("ta", [128, 512], F32)
            tb = sb("tb", [128, 512], F32)
            zt = sb("zt", [128, TOK], BF16)
            u32 = sb("u32", [128, 144], F32)
            sAB = [sb("sA", [128, 144], F32), sb("sB", [128, 144], F32)]
            ua = sb("ua", [128, 19], F32)
            dT = sb("dT", [128, 2, 128], BF16)
            mo_t = [sb("mo0", [128, 128], BF16), sb("mo1", [128, 128], BF16)]
            rc_t = sb("rc_t", [128, 4, TOK], F32)
            wp_t = sb("wp_t", [128, 4, 2, 256], BF16)
            psc_t = sb("psc_t", [128, 8], F32)
            stp_t = sb("stp_t", [128, 8, 4, 15], F32)
            Bu32, BsAB, Bua, BdT, Bmo, Brc = Buf(), [Buf(), Buf()], Buf(), Buf(), [Buf(), Buf()], Buf()
            P.op("sp", lambda e: e.dma_start(out=rc_t[:], in_=rc_d[:, :, :]), writes=[Brc], dma=True)
            P.op("pool", lambda e: e.dma_start(out=wp_t[:], in_=wp_d[:, :, :, :]), writes=[Brc], dma=True)
            P.op("sp", lambda e: e.dma_start(out=psc_t[:], in_=psc_d[:, :]), writes=[Brc], dma=True)
            P.op("sp", lambda e: e.dma_start(out=stp_t[:], in_=stp_d[:, :, :, :]), writes=[Brc], dma=True)
            Bps = [Buf() for _ in range(7)]
            Bxb, Bcs = [Buf() for _ in range(5)], Buf()
            Bwt = [[Buf(), Buf()], [Buf(), Buf()]]
            Bo32 = [Buf() for _ in range(4)]
            Bta, Btb, Bzt = Buf(), Buf(), Buf()
            P.op("dve", lambda e: e.memset(zt[:], 0.0), writes=[Bzt])
            for k in range(8, 32):
                P.op("sp", lambda e, k=k: e.dma_start(out=mixT_s[:, k, :], in_=zt[:]), reads=[Bzt], dma=True)
            for sbi in range(4):
                P.op("sp", lambda e, i=sbi: e.dma_start(out=kws_o[i, 0:508, :], in_=st_kw[i, 4:512, :]), dma=True)
                P.op("sp", lambda e, i=sbi: e.dma_start(out=vws_o[i, 0:508, :], in_=st_vw[i, 4:512, :]), dma=True)
                P.op("sp", lambda e, i=sbi: e.dma_start(out=pools_o[i, 0:11, :], in_=st_pool[i, 4:15, :]), dma=True)
            psi = 0
            oi = 0
            wi = 0
            for xb in range(4):
                for kq in range(4):
                    P.op("pool", lambda e, xb=xb, kq=kq: e.dma_start(
                        out=xb_t[:, kq * 8:(kq + 1) * 8, 0:1024], in_=xTs[xb, :, kq * 8:(kq + 1) * 8, :]),
                        writes=[Bxb[kq]], dma=True)
                if xb == 3:
                    P.op("pool", lambda e: e.dma_start(out=xb_t[:, :, 1024:1040], in_=xsT[:, :, :]),
                         writes=[Bxb[4]], dma=True)
                ntile = 9 if xb == 3 else 8
                for (tab_t, tab_d) in ((cos_t, cosT), (sin_t, sinT)):
                    P.op("sp", lambda e, tab_t=tab_t, tab_d=tab_d, xb=xb: e.dma_start(
                        out=tab_t[:, 0:8, :], in_=tab_d[xb * 1024:(xb + 1) * 1024, :].rearrange("(i p) d -> p i d", p=128)),
                        writes=[Bcs], dma=True)
                    if xb == 3:
                        P.op("sp", lambda e, tab_t=tab_t, tab_d=tab_d: e.dma_start(
                            out=tab_t[0:16, 8, :], in_=tab_d[4096:4112, :]), writes=[Bcs], dma=True)
                for eb in list(range(8, 14)) + [0, 1] + list(range(2, 8)) + [14]:
                    w = wi % 2
                    wi += 1
                    for kh in range(2):
                        P.op("pool", lambda e, eb=eb, kh=kh, w=w: e.dma_start(
                            out=wt[w][:, kh * 16:(kh + 1) * 16, :].rearrange("p k f -> p (k f)"),
                            in_=w_in[eb, :, kh * 16:(kh + 1) * 16, :].rearrange("p k f -> p (k f)")),
                            writes=[Bwt[w][kh]], dma=True)
                    for i in range(ntile):
                        m = 128 if i < 8 else 16
                        c0 = i * 128
                        S = xb * 8 + i if i < 8 else 32
                        r0 = S * 128
                        own = (i == 8) or (i % 4 == 3)
                        tok0 = (S // 4) * 128 if i < 8 else 1024
                        if 8 <= eb < 14:
                            do = True
                        elif eb < 2:
                            do = S >= 31
                        else:
                            do = own
                        if not do:
                            continue
                        p = psi % 7
                        psi += 1
                        for k in range(32):
                            P.op("pe", lambda e, p=p, k=k, c0=c0, m=m, w=w: e.matmul(
                                psum[p][0:m, :], lhsT=xb_t[:, k, c0:c0 + m], rhs=wt[w][:, k, :],
                                start=(k == 0), stop=(k == 31)), reads=Bxb + Bwt[w], writes=[Bps[p]])
                        o = oi % 4
                        oi += 1
                        kv = eb - 8
                        if kv in (2, 4) or 2 <= eb < 8:
                            z4 = psum[p][0:m, :].rearrange("p (h two d) -> p h two d", two=2, d=64)
                            a4 = ta[0:m, :].rearrange("p (h two d) -> p h two d", two=2, d=64)
                            b4 = tb[0:m, :].rearrange("p (h two d) -> p h two d", two=2, d=64)
                            cb = cos_t[0:m, i, :].unsqueeze(1).unsqueeze(1).to_broadcast([m, 4, 2, 64])
                            sbh = sin_t[0:m, i, :].unsqueeze(1).to_broadcast([m, 4, 64])
                            P.op("dve", lambda e, a4=a4, z4=z4, cb=cb: e.tensor_tensor(out=a4, in0=z4, in1=cb, op=ALU.mult),
                                 reads=[Bps[p], Bcs], writes=[Bta])
                            P.op("dve", lambda e, b4=b4, z4=z4, sbh=sbh: e.tensor_tensor(
                                out=b4[:, :, 0, :], in0=z4[:, :, 1, :], in1=sbh, op=ALU.mult),
                                reads=[Bps[p], Bcs], writes=[Btb])
                            P.op("dve", lambda e, b4=b4, z4=z4, sbh=sbh: e.tensor_tensor(
                                out=b4[:, :, 1, :], in0=z4[:, :, 0, :], in1=sbh, op=ALU.mult),
                                reads=[Bps[p], Bcs, Btb], writes=[Btb])
                            o4 = o32[o][0:m, :].rearrange("p (h two d) -> p h two d", two=2, d=64)
                            P.op("dve", lambda e, o4=o4, a4=a4, b4=b4: e.tensor_tensor(
                                out=o4[:, :, 0, :], in0=a4[:, :, 0, :], in1=b4[:, :, 0, :], op=ALU.subtract),
                                reads=[Bta, Btb], writes=[Bo32[o]])
                            P.op("dve", lambda e, o4=o4, a4=a4, b4=b4: e.tensor_tensor(
                                out=o4[:, :, 1, :], in0=a4[:, :, 1, :], in1=b4[:, :, 1, :], op=ALU.add),
                                reads=[Bta, Btb, Bo32[o]], writes=[Bo32[o]])
                        elif eb == 14:
                            P.op("act", lambda e, o=o, p=p, m=m: e.activation(out=o32[o][0:m, 0:72], in_=psum[p][0:m, 0:72], func=AF.Sigmoid),
                                 reads=[Bps[p]], writes=[Bo32[o]])
                        else:
                            P.op("act", lambda e, o=o, p=p, m=m: e.activation(out=o32[o][0:m, :], in_=psum[p][0:m, :], func=AF.Copy),
                                 reads=[Bps[p]], writes=[Bo32[o]])
                        if 8 <= eb < 14:
                            P.op("sp", lambda e, kv=kv, r0=r0, m=m, o=o: e.dma_start(
                                out=kvout[kv, r0:r0 + m, :], in_=o32[o][0:m, :]), reads=[Bo32[o]], dma=True)
                            if S == 32 and kv in (4, 5):
                                dst = kws_o if kv == 4 else vws_o
                                for sbi in range(4):
                                    P.op("sp", lambda e, dst=dst, sbi=sbi, o=o: e.dma_start(
                                        out=dst[sbi, 508:512, :], in_=o32[o][sbi * 4:(sbi + 1) * 4, :]),
                                        reads=[Bo32[o]], dma=True)
                        elif eb < 2:
                            if S == 31:
                                P.op("sp", lambda e, eb=eb, o=o: e.dma_start(
                                    out=poolp_o[:, eb * 512:(eb + 1) * 512], in_=o32[o][113:128, :]),
                                    reads=[Bo32[o]], dma=True)
                            else:
                                for sbi in range(4):
                                    P.op("sp", lambda e, eb=eb, sbi=sbi, o=o: e.dma_start(
                                        out=pools_o[sbi, 11:15, eb * 512:(eb + 1) * 512],
                                        in_=o32[o][sbi * 4:(sbi + 1) * 4, :]), reads=[Bo32[o]], dma=True)
                        elif eb < 8:
                            P.op("sp", lambda e, eb=eb, tok0=tok0, m=m, o=o: e.dma_start(
                                out=q_s[tok0:tok0 + m, (eb - 2) * 512:(eb - 1) * 512], in_=o32[o][0:m, :]),
                                reads=[Bo32[o]], dma=True)
                        else:
                            P.op("sp", lambda e, tok0=tok0, m=m, o=o: e.dma_start(
                                out=gate_s[tok0:tok0 + m, :], in_=o32[o][0:m, 0:72]), reads=[Bo32[o]], dma=True)
                    if eb >= 2:
                        continue
                    units = [(3, False), (7, False)] + ([(8, True)] if xb == 3 else [])
                    for (i, is_s) in units:
                        S = xb * 8 + i if not is_s else 32
                        tok0 = (S // 4) * 128 if not is_s else 1024
                        m = 16 if is_s else 128
                        for cc4 in range(4):
                            cg = eb * 4 + cc4
                            gi, cc = cg // 2, cg % 2
                            wsz = 2 << gi
                            p = psi % 7
                            psi += 1
                            if not is_s:
                                n = 144
                                cs0 = i * 128 - 16
                            else:
                                n = 16
                                cs0 = 1024
                            for k in range(32):
                                P.op("pe", lambda e, p=p, k=k, cs0=cs0, n=n, w=w, cc4=cc4: e.matmul(
                                    psum[p][:, 0:n], lhsT=wt[w][:, k, cc4 * 128:(cc4 + 1) * 128], rhs=xb_t[:, k, cs0:cs0 + n],
                                    start=(k == 0), stop=(k == 31)), reads=Bxb + Bwt[w], writes=[Bps[p]])
                            P.op("act", lambda e, p=p, n=n: e.activation(out=u32[:, 0:n], in_=psum[p][:, 0:n], func=AF.Copy),
                                 reads=[Bps[p]], writes=[Bu32])
                            if not is_s:
                                cur, Bcur = u32, Bu32
                                for lv in range(gi + 1):
                                    sh = 1 << lv
                                    nxt, Bnxt = sAB[lv % 2], BsAB[lv % 2]
                                    P.op("dve", lambda e, nxt=nxt, cur=cur, sh=sh: e.tensor_tensor(
                                        out=nxt[:, sh:144], in0=cur[:, sh:144], in1=cur[:, 0:144 - sh], op=ALU.add),
                                        reads=[Bcur], writes=[Bnxt])
                                    cur, Bcur = nxt, Bnxt
                                oth, Both = sAB[(gi + 1) % 2], BsAB[(gi + 1) % 2]
                                P.op("dve", lambda e, oth=oth, cur=cur, gi=gi, tok0=tok0: e.tensor_tensor(
                                    out=oth[:, 0:128], in0=cur[:, 16:144], in1=rc_t[:, gi, tok0:tok0 + 128], op=ALU.mult),
                                    reads=[Bcur, Brc], writes=[Both])
                                P.op("dve", lambda e, oth=oth, cc=cc: e.tensor_tensor(
                                    out=dT[:, cc, 0:128], in0=oth[:, 0:128], in1=u32[:, 16:144], op=ALU.subtract),
                                    reads=[Both, Bu32], writes=[BdT])
                            else:
                                for sbi in range(4):
                                    P.op("dve", lambda e, cg=cg, sbi=sbi: e.tensor_copy(out=ua[:, 0:15], in_=stp_t[:, cg, sbi, :]),
                                         reads=[Brc], writes=[Bua])
                                    P.op("dve", lambda e, sbi=sbi: e.tensor_copy(out=ua[:, 15:19], in_=u32[:, sbi * 4:(sbi + 1) * 4]),
                                         reads=[Bu32, Bua], writes=[Bua])
                                    cur, Bcur = ua, Bua
                                    for lv in range(gi + 1):
                                        sh = 1 << lv
                                        nxt, Bnxt = sAB[lv % 2], BsAB[lv % 2]
                                        P.op("dve", lambda e, nxt=nxt, cur=cur, sh=sh: e.tensor_tensor(
                                            out=nxt[:, sh:19], in0=cur[:, sh:19], in1=cur[:, 0:19 - sh], op=ALU.add),
                                            reads=[Bcur], writes=[Bnxt])
                                        cur, Bcur = nxt, Bnxt
                                    P.op("dve", lambda e, cur=cur, wsz=wsz, cc=cc, sbi=sbi: e.scalar_tensor_tensor(
                                        out=dT[:, cc, sbi * 4:(sbi + 1) * 4], in0=cur[:, 15:19], scalar=1.0 / wsz, in1=ua[:, 15:19],
                                        op0=ALU.mult, op1=ALU.subtract), reads=[Bcur, Bua, BdT], writes=[BdT])
                            if cc == 1:
                                for ec in range(2):
                                    p2 = psi % 7
                                    psi += 1
                                    for c2 in range(2):
                                        P.op("pe", lambda e, p2=p2, gi=gi, c2=c2, ec=ec, m=m: e.matmul(
                                            psum[p2][:, 0:m], lhsT=wp_t[:, gi, c2, ec * 128:(ec + 1) * 128], rhs=dT[:, c2, 0:m],
                                            start=(c2 == 0), stop=(c2 == 1)), reads=[BdT, Brc], writes=[Bps[p2]])
                                    mo = mo_t[ec]
                                    P.op("dve", lambda e, mo=mo, p2=p2, m=m, gi=gi, ec=ec: e.tensor_scalar(
                                        out=mo[:, 0:m], in0=psum[p2][:, 0:m], scalar1=psc_t[:, 2 * gi + ec:2 * gi + ec + 1], scalar2=None,
                                        op0=ALU.mult), reads=[Bps[p2], Brc], writes=[Bmo[ec]])
                                    P.op("sp", lambda e, mo=mo, gi=gi, ec=ec, tok0=tok0, m=m: e.dma_start(
                                        out=mixT_s[:, 2 * gi + ec, tok0:tok0 + m], in_=mo[:, 0:m]), reads=[Bmo[ec]], dma=True)
            P.emit_phase()

        SCALE = 128.0 ** -0.5
        NEG = -30000.0

        def gelu_cols(Pq, ps_ap, hid0_ap, n, x32, x2, sg_t, g_out, Bx, Bpsx, Bg, rd=()):
            Pq.op("act", lambda e: e.activation(out=x32[:, 0:n], in_=ps_ap, func=AF.Identity, bias=hid0_ap, scale=1.0),
                  reads=[Bpsx] + list(rd), writes=[Bx])
            Pq.op("dve", lambda e: e.tensor_tensor(out=x2[:, 0:n], in0=x32[:, 0:n], in1=x32[:, 0:n], op=ALU.mult),
                  reads=[Bx], writes=[Bx])
            Pq.op("dve", lambda e: e.tensor_scalar(out=x2[:, 0:n], in0=x2[:, 0:n], scalar1=0.044715, scalar2=1.0,
                                                   op0=ALU.mult, op1=ALU.add), reads=[Bx], writes=[Bx])
            Pq.op("dve", lambda e: e.tensor_tensor(out=x2[:, 0:n], in0=x2[:, 0:n], in1=x32[:, 0:n], op=ALU.mult),
                  reads=[Bx], writes=[Bx])
            Pq.op("act", lambda e: e.activation(out=sg_t[:, 0:n], in_=x2[:, 0:n], func=AF.Sigmoid, scale=1.5957691216057308),
                  reads=[Bx], writes=[Bx])
            Pq.op("dve", lambda e: e.tensor_tensor(out=g_out, in0=sg_t[:, 0:n], in1=x32[:, 0:n], op=ALU.mult),
                  reads=[Bx], writes=[Bg])

        with contextlib.ExitStack() as st:
            sb = lambda name, shape, dt: st.enter_context(nc.sbuf_tensor(uname(name), list(shape), dt))
            identb = sb("a_identb", [128, 384], BF16)
            cb_c = sb("a_cbc", [128, 128], BF16)
            cb_a = sb("a_cba", [128, 128], BF16)
            padb = sb("a_padb", [128, 32], F32)
            w1k = sb("a_w1k", [128, 32, 128], BF16)
            w1v = sb("a_w1v", [128, 32, 128], BF16)
            w2 = sb("a_w2", [128, 3, 128], BF16)
            peT = sb("a_peT", [128, 2, 32], BF16)
            hid0 = sb("a_hid0", [128, 2], F32)
            ccmp = sb("a_ccmp", [128, 256], F32)
            scmp = sb("a_scmp", [128, 256], F32)
            cval = sb("a_cval", [128, 8, 256], F32)
            addm = sb("a_addm", [128, 8, 64], F32)
            vblk = sb("a_vblk", [128, 8, 64], F32)
            tm = [sb("a_tm%d" % i, [128, 32, 128], BF16) for i in range(2)]
            XT = {t: sb("a_XT%d" % t, [128, 4096], BF16) for t in (0, 1, 2, 4)}
            V1 = {t: sb("a_V1%d" % t, [128, 32, 130], BF16) for t in (3, 5)}
            kcbT = sb("a_kcbT", [128, 256], BF16)
            vcb = sb("a_vcb", [128, 2, 128], BF16)
            x32 = sb("a_x32", [128, 256], F32)
            x2 = sb("a_x2", [128, 256], F32)
            sgt = sb("a_sgt", [128, 256], F32)
            Gk = sb("a_Gk", [128, 256], BF16)
            Gv = sb("a_Gv", [128, 256], BF16)
            qtm = sb("a_qtm", [128, 768], BF16)
            qT = sb("a_qT", [128, 768], BF16)
            gt = sb("a_gt", [128, 18], F32)
            e32 = sb("a_e32", [128, 256], F32)
            ev = sb("a_ev", [128, 256], F32)
            Pp = sb("a_Pp", [128, 260], F32)
            Pb = sb("a_Pb", [128, 256], BF16)
            PT = sb("a_PT", [128, 2, 128], BF16)
            sm = sb("a_sm", [128, 8], F32)
            imp = sb("a_imp", [128, 64], F32)
            imw = sb("a_imw", [128, 64], F32)
            m8 = sb("a_m8", [128, 16], F32)
            negb = sb("a_negb", [128, 64], BF16)
            negx = sb("a_negx", [128, 4096], BF16)
            PTs = [sb("a_PTs%d" % i, [128, 384], BF16) for i in range(4)]
            acc = sb("a_acc", [128, 768], F32)
            accb = sb("a_accb", [128, 768], BF16)
            mixo = sb("a_mixo", [128, 6, 128], BF16)
            sc6 = sb("a_sc6", [128, 8], F32)

            Bc = Buf()
            Btm = [Buf(), Buf()]
            BXT = {t: Buf() for t in (0, 1, 2, 4)}
            BV1 = {t: Buf() for t in (3, 5)}
            Bkcb, Bvcb, Bx, BGk, BGv = Buf(), Buf(), Buf(), Buf(), Buf()
            Bq, BqT, Bgt, Be, BPp, BPb, BPT, Bsm, Bimp, Bneg, Bnegx = (Buf() for _ in range(11))
            BPTs = [Buf() for _ in range(4)]
            Bacc, Baccb, Bmixo, Bsc6 = Buf(), Buf(), Buf(), Buf()
            Bps = [Buf() for _ in range(7)]
            Bpsb = Buf()

            for j3 in range(3):
                P.op("pool", lambda e, j3=j3: e.dma_start(out=identb[:, j3 * 128:(j3 + 1) * 128], in_=ident[:, :]), writes=[Bc], dma=True)
            P.op("pool", lambda e: e.dma_start(out=cb_c[:], in_=cbc_d[:, :]), writes=[Bc], dma=True)
            P.op("pool", lambda e: e.dma_start(out=cb_a[:], in_=cba_d[:, :]), writes=[Bc], dma=True)
            P.op("sp", lambda e: e.dma_start(out=padb[:], in_=padb_d[:, :]), writes=[Bc], dma=True)
            P.op("pool", lambda e: e.dma_start(out=w1k[:], in_=w1k_d[:, :, :]), writes=[Bc], dma=True)
            P.op("pool", lambda e: e.dma_start(out=w1v[:], in_=w1v_d[:, :, :]), writes=[Bc], dma=True)
            P.op("pool", lambda e: e.dma_start(out=w2[:], in_=w2_d[:, :, :]), writes=[Bc], dma=True)
            P.op("pool", lambda e: e.dma_start(out=peT[:], in_=peT_d[:, :, :]), writes=[Bc], dma=True)
            P.op("sp", lambda e: e.dma_start(out=ccmp[:], in_=ccmp_d[:, :]), writes=[Bc], dma=True)
            P.op("sp", lambda e: e.dma_start(out=scmp[:], in_=scmp_d[:, :]), writes=[Bc], dma=True)
            P.op("sp", lambda e: e.dma_start(out=cval[:], in_=cval_d[:, :, :]), writes=[Bc], dma=True)
            P.op("sp", lambda e: e.dma_start(out=addm[:], in_=addm_d[:, :, :]), writes=[Bc], dma=True)
            P.op("sp", lambda e: e.dma_start(out=vblk[:], in_=vblk_d[:, :, :]), writes=[Bc], dma=True)
            for t in (3, 5):
                P.op("dve", lambda e, t=t: e.memset(V1[t][:, :, 128:130], 1.0), writes=[BV1[t]])
            P.op("dve", lambda e: e.memset(Pp[:], 0.0), writes=[BPp])
            P.op("dve", lambda e: e.memset(vcb[:], 0.0), writes=[Bvcb])
            for vi, w1t in enumerate((w1k, w1v)):
                for pp in range(32):
                    P.op("pe", lambda e, vi=vi, w1t=w1t, pp=pp: e.matmul(
                        psum[0][:, vi:vi + 1], lhsT=w1t[:, pp, :], rhs=peT[:, vi, pp:pp + 1],
                        start=(pp == 0 and vi == 0), stop=(pp == 31), skip_group_check=True), reads=[Bc], writes=[Bps[0]])
            P.op("act", lambda e: e.activation(out=hid0[:, 0:2], in_=psum[0][:, 0:2], func=AF.Copy), reads=[Bps[0]], writes=[Bc])

            tmi = 0
            for g in range(4):
                for t in (0, 1, 2, 4):
                    tb_ = tmi % 2
                    tmi += 1
                    for qd in range(4):
                        P.op("pool", lambda e, t=t, g=g, qd=qd, tb_=tb_: e.dma_start(
                            out=tm[tb_][:, qd * 8:(qd + 1) * 8, :],
                            in_=kvout[t, qd * 1024:(qd + 1) * 1024, g * 128:(g + 1) * 128].rearrange("(i p) d -> p i d", p=128)),
                            writes=[Btm[tb_]], dma=True)
                    for b8 in range(4):
                        for kk in range(8):
                            it = b8 * 8 + kk
                            P.op("pe", lambda e, tb_=tb_, it=it, kk=kk: e.transpose(
                                out=psb[:, kk * 128:(kk + 1) * 128], in_=tm[tb_][:, it, :], identity=identb[:, 0:128]),
                                reads=[Btm[tb_], Bc], writes=[Bpsb])
                        eng = "act" if b8 % 2 == 0 else "dve"
                        if eng == "act":
                            P.op("act", lambda e, t=t, b8=b8: e.activation(out=XT[t][:, b8 * 1024:(b8 + 1) * 1024], in_=psb[:, :], func=AF.Copy),
                                 reads=[Bpsb], writes=[BXT[t]])
                        else:
                            P.op("dve", lambda e, t=t, b8=b8: e.tensor_copy(out=XT[t][:, b8 * 1024:(b8 + 1) * 1024], in_=psb[:, :]),
                                 reads=[Bpsb], writes=[BXT[t]])
                for t in (3, 5):
                    for qd in range(4):
                        P.op("pool", lambda e, t=t, g=g, qd=qd: e.dma_start(
                            out=V1[t][:, qd * 8:(qd + 1) * 8, 0:128],
                            in_=kvout[t, qd * 1024:(qd + 1) * 1024, g * 128:(g + 1) * 128].rearrange("(i p) d -> p i d", p=128)),
                            writes=[BV1[t]], dma=True)
                for vi, (w1t, srcT, Gt, BG) in enumerate(((w1k, XT[0], Gk, BGk), (w1v, XT[1], Gv, BGv))):
                    for ap_ in range(32):
                        a_, p16 = ap_ // 16, ap_ % 16
                        c_0 = 16 * a_ + p16
                        P.op("pe", lambda e, w1t=w1t, srcT=srcT, ap_=ap_, c_0=c_0: e.matmul(
                            psum[1][:, 0:255], lhsT=w1t[:, ap_, :], rhs=srcT[:, c_0:c_0 + 16 * 254 + 1:16],
                            start=(ap_ == 0), stop=(ap_ == 31)), reads=[Bc, BXT[vi]], writes=[Bps[1]])
                    gelu_cols(P, psum[1][:, 0:255], hid0[:, vi:vi + 1], 255, x32, x2, sgt, Gt[:, 0:255], Bx, Bps[1], BG, rd=[Bc])
                P.op("pe", lambda e: e.matmul(psum[2][:, 0:255], lhsT=w2[:, 0, :], rhs=Gk[:, 0:255], start=True, stop=True),
                     reads=[Bc, BGk], writes=[Bps[2]])
                P.op("pe", lambda e: e.matmul(psum[3][:, 0:255], lhsT=w2[:, 1, :], rhs=Gk[:, 0:255], start=True, stop=True),
                     reads=[Bc, BGk], writes=[Bps[3]])
                P.op("dve", lambda e: e.tensor_tensor(out=x32[:, 0:255], in0=psum[2][:, 0:255], in1=ccmp[:, 0:255], op=ALU.mult),
                     reads=[Bps[2], Bc, Bx], writes=[Bx])
                P.op("dve", lambda e: e.tensor_tensor(out=x2[:, 0:255], in0=psum[3][:, 0:255], in1=scmp[:, 0:255], op=ALU.mult),
                     reads=[Bps[3], Bc, Bx], writes=[Bx])
                P.op("dve", lambda e: e.tensor_tensor(out=kcbT[:, 0:255], in0=x32[:, 0:255], in1=x2[:, 0:255], op=ALU.add),
                     reads=[Bx], writes=[Bkcb])
                for nt_, nn in ((0, 128), (1, 127)):
                    P.op("pe", lambda e, nt_=nt_, nn=nn: e.matmul(
                        psum[4][0:nn, nt_ * 128:(nt_ + 1) * 128], lhsT=Gv[:, nt_ * 128:nt_ * 128 + nn], rhs=w2[:, 2, :],
                        start=(nt_ == 0), stop=True, skip_group_check=True), reads=[Bc, BGv], writes=[Bps[4]])
                P.op("act", lambda e: e.activation(out=vcb[:, 0, :], in_=psum[4][:, 0:128], func=AF.Copy), reads=[Bps[4]], writes=[Bvcb])
                P.op("act", lambda e: e.activation(out=vcb[0:127, 1, :], in_=psum[4][0:127, 128:256], func=AF.Copy), reads=[Bps[4]], writes=[Bvcb])

                for i in range(8):
                    S = 4 * i + 3
                    tok0 = i * 128
                    P.op("pool", lambda e, tok0=tok0, g=g: e.dma_start(out=qtm[:], in_=q_s[tok0:tok0 + 128, g * 768:(g + 1) * 768]),
                         writes=[Bq], dma=True)
                    P.op("sp", lambda e, tok0=tok0, g=g: e.dma_start(out=gt[:], in_=gate_s[tok0:tok0 + 128, g * 18:(g + 1) * 18]),
                         writes=[Bgt], dma=True)
                    for h in range(6):
                        P.op("pe", lambda e, h=h: e.transpose(out=psb[:, h * 128:(h + 1) * 128], in_=qtm[:, h * 128:(h + 1) * 128],
                                                              identity=identb[:, 0:128]), reads=[Bq, Bc], writes=[Bpsb])
                    P.op("act", lambda e: e.activation(out=qT[:], in_=psb[:, 0:768], func=AF.Copy), reads=[Bpsb], writes=[BqT])
                    for h in range(6):
                        P.op("pe", lambda e, h=h: e.matmul(psum[0][:, 0:255], lhsT=qT[:, h * 128:(h + 1) * 128], rhs=kcbT[:, 0:255],
                                                           start=True, stop=True), reads=[BqT, Bkcb], writes=[Bps[0]])
                        P.op("dve", lambda e: e.reduce_max(out=sm[:, 0:1], in_=psum[0][:, 0:255], axis=AX.X), reads=[Bps[0]], writes=[Bsm])
                        P.op("dve", lambda e: e.tensor_scalar(out=sm[:, 1:2], in0=sm[:, 0:1], scalar1=-SCALE, scalar2=None, op0=ALU.mult),
                             reads=[Bsm], writes=[Bsm])
                        P.op("act", lambda e: e.activation(out=e32[:, 0:255], in_=psum[0][:, 0:255], func=AF.Exp, bias=sm[:, 1:2], scale=SCALE),
                             reads=[Bps[0], Bsm], writes=[Be])
                        P.op("dve", lambda e, i=i: e.tensor_tensor(out=ev[:, 0:255], in0=e32[:, 0:255], in1=cval[:, i, 0:255], op=ALU.mult),
                             reads=[Be, Bc], writes=[Be])
                        P.op("dve", lambda e: e.reduce_sum(out=sm[:, 2:3], in_=ev[:, 0:255], axis=AX.X), reads=[Be], writes=[Bsm])
                        P.op("dve", lambda e: e.tensor_scalar(out=sm[:, 2:3], in0=sm[:, 2:3], scalar1=1e-30, scalar2=None, op0=ALU.max),
                             reads=[Bsm], writes=[Bsm])
                        P.op("dve", lambda e: e.reciprocal(out=sm[:, 3:4], in_=sm[:, 2:3]), reads=[Bsm], writes=[Bsm])
                        P.op("dve", lambda e: e.tensor_scalar(out=ev[:, 0:255], in0=ev[:, 0:255], scalar1=sm[:, 3:4], scalar2=None, op0=ALU.mult),
                             reads=[Be, Bsm], writes=[Be])
                        if h == 0:
                            P.op("dve", lambda e: e.tensor_copy(out=Pp[:, 1:256], in_=ev[:, 0:255]), reads=[Be], writes=[BPp])
                        else:
                            P.op("dve", lambda e: e.tensor_tensor(out=Pp[:, 1:256], in0=Pp[:, 1:256], in1=ev[:, 0:255], op=ALU.add),
                                 reads=[Be, BPp], writes=[BPp])
                        P.op("act", lambda e: e.activation(out=Pb[:, 0:255], in_=ev[:, 0:255], func=AF.Copy), reads=[Be], writes=[BPb])
                        for nt_, nn in ((0, 128), (1, 127)):
                            P.op("pe", lambda e, nt_=nt_, nn=nn: e.transpose(out=psb[0:nn, nt_ * 128:(nt_ + 1) * 128],
                                                                             in_=Pb[:, nt_ * 128:nt_ * 128 + nn], identity=identb[:, 0:128]),
                                 reads=[BPb, Bc], writes=[Bpsb])
                        P.op("act", lambda e: e.activation(out=PT[:, 0, :], in_=psb[:, 0:128], func=AF.Copy), reads=[Bpsb], writes=[BPT])
                        P.op("act", lambda e: e.activation(out=PT[0:127, 1, :], in_=psb[0:127, 128:256], func=AF.Copy), reads=[Bpsb, BPT], writes=[BPT])
                        for nt_, nn in ((0, 128), (1, 127)):
                            P.op("pe", lambda e, nt_=nt_, nn=nn: e.matmul(psum[1][:, 0:128], lhsT=PT[0:nn, nt_, :], rhs=vcb[0:nn, nt_, :],
                                                                          start=(nt_ == 0), stop=(nt_ == 1)), reads=[BPT, Bvcb], writes=[Bps[1]])
                        P.op("dve", lambda e, h=h: e.tensor_scalar(out=acc[:, h * 128:(h + 1) * 128], in0=psum[1][:, 0:128],
                                                                   scalar1=gt[:, 3 * h:3 * h + 1], scalar2=None, op0=ALU.mult),
                             reads=[Bps[1], Bgt], writes=[Bacc])
                    P.op("dve", lambda e: e.tensor_reduce(out=imp[:, :], in_=Pp[:, 0:256].rearrange("p (s j) -> p s j", j=4), axis=AX.X, op=ALU.add),
                         reads=[BPp], writes=[Bimp])
                    P.op("dve", lambda e: e.tensor_tensor(out=imp[:, :], in0=imp[:, :], in1=Pp[:, 4:260:4], op=ALU.add), reads=[BPp, Bimp], writes=[Bimp])
                    P.op("dve", lambda e, i=i: e.tensor_tensor(out=imp[:, :], in0=imp[:, :], in1=addm[:, i, :], op=ALU.add), reads=[Bimp, Bc], writes=[Bimp])
                    P.op("dve", lambda e: e.max(out=m8[:, 0:8], in_=imp[:, :]), reads=[Bimp], writes=[Bsm])
                    P.op("dve", lambda e: e.match_replace(out=imw[:, :], in_to_replace=m8[:, 0:8], in_values=imp[:, :], imm_value=-3.0e38),
                         reads=[Bimp, Bsm], writes=[Bimp])
                    P.op("dve", lambda e: e.max(out=m8[:, 8:16], in_=imw[:, :]), reads=[Bimp], writes=[Bsm])
                    P.op("dve", lambda e: e.tensor_scalar(out=imw[:, :], in0=imp[:, :], scalar1=m8[:, 15:16], scalar2=None, op0=ALU.is_ge),
                         reads=[Bimp, Bsm], writes=[Bimp])
                    P.op("dve", lambda e, i=i: e.tensor_tensor(out=imw[:, :], in0=imw[:, :], in1=vblk[:, i, :], op=ALU.mult), reads=[Bimp, Bc], writes=[Bimp])
                    P.op("dve", lambda e: e.tensor_scalar(out=negb[:, :], in0=imw[:, :], scalar1=-1.0, scalar2=-NEG, op0=ALU.add, op1=ALU.mult),
                         reads=[Bimp], writes=[Bneg])
                    P.op("dve", lambda e: e.tensor_copy(out=negx[:, :].rearrange("p (s j) -> p s j", j=64),
                                                        in_=negb[:, :].unsqueeze(2).to_broadcast([128, 64, 64])), reads=[Bneg], writes=[Bnegx])
                    pti = 0
                    for br in (1, 2):
                        if br == 1:
                            kts = list(range(0, S + 1))
                            KT, VV, BK, BV = XT[2], V1[3], BXT[2], BV1[3]
                        else:
                            kts = list(range(max(S - 4, 0), S + 1))
                            KT, VV, BK, BV = XT[4], V1[5], BXT[4], BV1[5]
                        for ki, kt in enumerate(kts):
                            biases = []
                            if br == 1:
                                biases.append(negx[:, kt * 128:(kt + 1) * 128])
                            if kt == S:
                                biases.append(cb_c[:, :])
                            if br == 2 and kt == S - 4:
                                biases.append(cb_a[:, :])
                            for hh in range(2):
                                ps_s = 2 + ((ki * 2 + hh) % 3)
                                P.op("pe", lambda e, ps_s=ps_s, kt=kt, hh=hh, KT=KT, nb=len(biases): e.matmul(
                                    psum[ps_s][:, 0:384], lhsT=KT[:, kt * 128:(kt + 1) * 128], rhs=qT[:, hh * 384:(hh + 1) * 384],
                                    start=True, stop=(nb == 0)), reads=[BK, BqT], writes=[Bps[ps_s]])
                                for bi, bias_ap in enumerate(biases):
                                    P.op("pe", lambda e, ps_s=ps_s, bias_ap=bias_ap, last=(bi == len(biases) - 1): e.matmul(
                                        psum[ps_s][:, 0:384], lhsT=bias_ap, rhs=identb[:, 0:384], start=False, stop=last),
                                        reads=[Bnegx, Bc], writes=[Bps[ps_s]])
                                pt_ = pti % 4
                                pti += 1
                                P.op("act", lambda e, ps_s=ps_s, pt_=pt_, kt=kt: e.activation(
                                    out=PTs[pt_][:, :], in_=psum[ps_s][:, 0:384], func=AF.Exp, bias=padb[:, kt:kt + 1], scale=SCALE),
                                    reads=[Bps[ps_s], Bc], writes=[BPTs[pt_]])
                                for hl in range(3):
                                    P.op("pe", lambda e, hh=hh, hl=hl, pt_=pt_, kt=kt, VV=VV, first=(ki == 0 and hl == 0), last=(ki == len(kts) - 1): e.matmul(
                                        psum[5 + hh][:, hl * 130:(hl + 1) * 130], lhsT=PTs[pt_][:, hl * 128:(hl + 1) * 128], rhs=VV[:, kt, :],
                                        start=first, stop=last, skip_group_check=True), reads=[BPTs[pt_], BV], writes=[Bps[5 + hh]])
                        for hh in range(2):
                            for hl in range(3):
                                h = hh * 3 + hl
                                P.op("dve", lambda e, hh=hh, hl=hl, h=h: e.reciprocal(out=sc6[:, h:h + 1], in_=psum[5 + hh][:, hl * 130 + 128:hl * 130 + 129]),
                                     reads=[Bps[5 + hh]], writes=[Bsc6])
                                P.op("dve", lambda e, h=h, br=br: e.tensor_tensor(out=sc6[:, h:h + 1], in0=sc6[:, h:h + 1], in1=gt[:, 3 * h + br:3 * h + br + 1], op=ALU.mult),
                                     reads=[Bsc6, Bgt], writes=[Bsc6])
                                P.op("dve", lambda e, hh=hh, hl=hl, h=h: e.scalar_tensor_tensor(
                                    out=acc[:, h * 128:(h + 1) * 128], in0=psum[5 + hh][:, hl * 130:hl * 130 + 128], scalar=sc6[:, h:h + 1],
                                    in1=acc[:, h * 128:(h + 1) * 128], op0=ALU.mult, op1=ALU.add), reads=[Bps[5 + hh], Bsc6, Bacc], writes=[Bacc])
                    P.op("act", lambda e: e.activation(out=accb[:], in_=acc[:], func=AF.Copy), reads=[Bacc], writes=[Baccb])
                    for h in range(6):
                        P.op("pe", lambda e, h=h: e.transpose(out=psb[:, h * 128:(h + 1) * 128], in_=accb[:, h * 128:(h + 1) * 128],
                                                              identity=identb[:, 0:128]), reads=[Baccb, Bc], writes=[Bpsb])
                    P.op("act", lambda e: e.activation(out=mixo[:, :, :].rearrange("p a b -> p (a b)"), in_=psb[:, 0:768], func=AF.Copy),
                         reads=[Bpsb], writes=[Bmixo])
                    P.op("sp", lambda e, g=g, tok0=tok0: e.dma_start(out=mixT_s[:, 8 + 6 * g:14 + 6 * g, tok0:tok0 + 128], in_=mixo[:, :, :]),
                         reads=[Bmixo], dma=True)
            P.emit_phase()

        with contextlib.ExitStack() as st:
            sb = lambda name, shape, dt: st.enter_context(nc.sbuf_tensor(uname(name), list(shape), dt))
            identb = sb("s_identb", [128, 128], BF16)
            identf = sb("s_identf", [128, 128], F32)
            id4x3 = sb("s_id4x3", [4, 12], BF16)
            cb_c = sb("s_cbc", [128, 128], BF16)
            cb_a = sb("s_cba", [128, 128], BF16)
            w1k = sb("s_w1k", [128, 32, 128], BF16)
            w1v = sb("s_w1v", [128, 32, 128], BF16)
            w2 = sb("s_w2", [128, 3, 128], BF16)
            peT = sb("s_peT", [128, 2, 32], BF16)
            hid0 = sb("s_hid0", [128, 2], F32)
            ccs = sb("s_ccs", [128, 512], F32)
            scs = sb("s_scs", [128, 512], F32)
            addms = sb("s_addms", [4, 129], F32)
            oh = sb("s_oh", [128, 5], F32)
            ptb_i = sb("s_ptb_i", [128, 64], I32)
            ptb_f = sb("s_ptb_f", [128, 64], F32)
            ptmp = sb("s_ptmp", [128, 64], F32)
            psel = sb("s_psel", [128, 16], F32)
            idx = sb("s_idx", [128, 16], I32)
            G32 = [sb("s_G32_%d" % i, [128, 2048], F32) for i in range(2)]
            XT = sb("s_XT", [128, 2, 4, 2048], BF16)
            V1 = sb("s_V1", [128, 2, 65, 130], BF16)
            XTn = sb("s_XTn", [128, 2, 4], BF16)
            kwT = sb("s_kwT", [128, 2, 516], BF16)
            V1w = sb("s_V1w", [128, 2, 5, 130], BF16)
            wtm = sb("s_wtm", [128, 4, 256], BF16)
            ntm = sb("s_ntm", [4, 256], BF16)
            kcbT = sb("s_kcbT", [128, 2, 512], BF16)
            vcb = sb("s_vcb", [128, 2, 4, 128], BF16)
            x32 = sb("s_x32", [128, 512], F32)
            x2 = sb("s_x2", [128, 512], F32)
            sgt = sb("s_sgt", [128, 512], F32)
            Gk = sb("s_Gk", [128, 512], BF16)
            Gv = sb("s_Gv", [128, 512], BF16)
            qtm = sb("s_qtm", [4, 768], BF16)
            qT = sb("s_qT", [128, 24], BF16)
            gt = sb("s_gt", [4, 18], F32)
            e32 = sb("s_e32", [4, 512], F32)
            ev = sb("s_ev", [4, 512], F32)
            Pp = sb("s_Pp", [4, 520], F32)
            Pb = sb("s_Pb", [4, 512], BF16)
            PT = sb("s_PT", [128, 4, 4], BF16)
            sm = sb("s_sm", [4, 8], F32)
            imp = sb("s_imp", [4, 129], F32)
            imw = sb("s_imw", [4, 129], F32)
            m8 = sb("s_m8", [4, 16], F32)
            negb = sb("s_negb", [4, 129], BF16)
            negx = sb("s_negx", [4, 16, 128], BF16)
            PTs = [sb("s_PTs%d" % i, [128, 12], BF16) for i in range(4)]
            acc = sb("s_acc", [4, 768], F32)
            accb = sb("s_accb", [4, 768], BF16)
            mixo = sb("s_mixo", [128, 6, 4], BF16)
            sc6 = sb("s_sc6", [4, 8], F32)

            Bc, Bidx = Buf(), Buf()
            BG32 = [Buf(), Buf()]
            BXT, BV1, BXTn, BkwT, BV1w, Bwtm, Bntm = (Buf() for _ in range(7))
            Bkcb, Bvcb, Bx, BGk, BGv = Buf(), Buf(), Buf(), Buf(), Buf()
            Bq, BqT, Bgt, Be, BPp, BPb, BPT, Bsm, Bimp, Bneg, Bnegx = (Buf() for _ in range(11))
            BPTs = [Buf() for _ in range(4)]
            Bacc, Baccb, Bmixo, Bsc6 = Buf(), Buf(), Buf(), Buf()
            Bps = [Buf() for _ in range(7)]
            Bpsb = Buf()

            P.op("pool", lambda e: e.dma_start(out=identb[:], in_=ident[:, :]), writes=[Bc], dma=True)
            P.op("sp", lambda e: e.dma_start(out=identf[:], in_=ident[:, :]), writes=[Bc], dma=True)
            for j3 in range(3):
                P.op("pool", lambda e, j3=j3: e.dma_start(out=id4x3[:, j3 * 4:(j3 + 1) * 4], in_=ident[0:4, 0:4]), writes=[Bc], dma=True)
            P.op("pool", lambda e: e.dma_start(out=cb_c[:], in_=cbc_d[:, :]), writes=[Bc], dma=True)
            P.op("pool", lambda e: e.dma_start(out=cb_a[:], in_=cba_d[:, :]), writes=[Bc], dma=True)
            P.op("pool", lambda e: e.dma_start(out=w1k[:], in_=w1k_d[:, :, :]), writes=[Bc], dma=True)
            P.op("pool", lambda e: e.dma_start(out=w1v[:], in_=w1v_d[:, :, :]), writes=[Bc], dma=True)
            P.op("pool", lambda e: e.dma_start(out=w2[:], in_=w2_d[:, :, :]), writes=[Bc], dma=True)
            P.op("pool", lambda e: e.dma_start(out=peT[:], in_=peT_d[:, :, :]), writes=[Bc], dma=True)
            P.op("sp", lambda e: e.dma_start(out=ccs[:], in_=ccs_d[:, :]), writes=[Bc], dma=True)
            P.op("sp", lambda e: e.dma_start(out=scs[:], in_=scs_d[:, :]), writes=[Bc], dma=True)
            P.op("sp", lambda e: e.dma_start(out=addms[:], in_=addms_d[:, :]), writes=[Bc], dma=True)
            P.op("sp", lambda e: e.dma_start(out=oh[:], in_=oh_d[:, :]), writes=[Bc], dma=True)
            P.op("dve", lambda e: e.memset(V1[:, :, :, 128:130], 1.0), writes=[BV1])
            P.op("dve", lambda e: e.memset(V1w[:, :, :, 128:130], 1.0), writes=[BV1w])
            P.op("dve", lambda e: e.memset(Pp[:], 0.0), writes=[BPp])
            P.op("dve", lambda e: e.memset(vcb[:], 0.0), writes=[Bvcb])
            for vi, w1t in enumerate((w1k, w1v)):
                for pp in range(32):
                    P.op("pe", lambda e, vi=vi, w1t=w1t, pp=pp: e.matmul(
                        psum[0][:, vi:vi + 1], lhsT=w1t[:, pp, :], rhs=peT[:, vi, pp:pp + 1],
                        start=(pp == 0 and vi == 0), stop=(pp == 31), skip_group_check=True), reads=[Bc], writes=[Bps[0]])
            P.op("act", lambda e: e.activation(out=hid0[:, 0:2], in_=psum[0][:, 0:2], func=AF.Copy), reads=[Bps[0]], writes=[Bc])

            caches = (ckc_d, cvc_d, cks_d, cvs_d)
            bc_reg = {}
            gi_ = 0
            tpi = 0
            for sbi in range(4):
                tokS = 1024 + 4 * sbi
                rowS = 4096 + 4 * sbi
                P.op("sp", lambda e, sbi=sbi: e.dma_start(out=ptb_i[:], in_=ptab_d[sbi, :].partition_broadcast(128)), writes=[Bidx], dma=True)
                P.op("dve", lambda e: e.tensor_copy(out=ptb_f[:], in_=ptb_i[:]), reads=[Bidx], writes=[Bidx])
                P.op("dve", lambda e: e.tensor_tensor(out=ptmp[:, :].rearrange("p (c a) -> p c a", a=4),
                                                      in0=ptb_f[:, :].rearrange("p (c a) -> p c a", a=4),
                                                      in1=oh[:, 0:4].unsqueeze(1).to_broadcast([128, 16, 4]), op=ALU.mult),
                     reads=[Bidx, Bc], writes=[Bidx])
                P.op("dve", lambda e: e.tensor_reduce(out=psel[:, :], in_=ptmp[:, :].rearrange("p (c a) -> p c a", a=4), axis=AX.X, op=ALU.add),
                     reads=[Bidx], writes=[Bidx])
                P.op("dve", lambda e: e.tensor_scalar(out=psel[:, :], in0=psel[:, :], scalar1=32.0, scalar2=oh[:, 4:5], op0=ALU.mult, op1=ALU.add),
                     reads=[Bidx, Bc], writes=[Bidx])
                P.op("dve", lambda e: e.tensor_copy(out=idx[:, :], in_=psel[:, :]), reads=[Bidx], writes=[Bidx])
                for gp in range(2):
                    for ci, cache_d in enumerate(caches):
                        for c in range(16):
                            gb = gi_ % 2
                            gi_ += 1
                            def gather_fn(e, gb=gb, c=c, cache_d=cache_d):
                                if "r" not in bc_reg:
                                    bc_reg["r"] = e.to_reg(2560 * 32 - 1)
                                return e.indirect_dma_start(
                                    out=G32[gb][:, :], out_offset=None, in_=cache_d[:, :],
                                    in_offset=bass.IndirectOffsetOnAxis(ap=idx[:, c:c + 1], axis=0),
                                    bounds_check=bc_reg["r"], oob_is_err=False)
                            P.op("pool", gather_fn, reads=[Bidx], writes=[BG32[gb]], dma=True)
                            if ci < 3:
                                for gl in range(2):
                                    g = 2 * gp + gl
                                    pb_ = 2 + (tpi % 2)
                                    tpi += 1
                                    for r4 in range(4):
                                        P.op("pe", lambda e, pb_=pb_, r4=r4, g=g, gb=gb: e.transpose(
                                            out=psum[pb_][:, r4 * 128:(r4 + 1) * 128], in_=G32[gb][:, r4 * 512 + g * 128:r4 * 512 + (g + 1) * 128],
                                            identity=identf[:, :]), reads=[BG32[gb], Bc], writes=[Bps[pb_]])
                                    eng = "act" if tpi % 2 == 0 else "dve"
                                    src = psum[pb_][:, :].rearrange("p (r q) -> p r q", q=128)
                                    if eng == "act":
                                        P.op("act", lambda e, gl=gl, c=c, src=src: e.activation(out=XT[:, gl, :, 128 * c:128 * (c + 1)], in_=src, func=AF.Copy),
                                             reads=[Bps[pb_]], writes=[BXT])
                                    else:
                                        P.op("dve", lambda e, gl=gl, c=c, src=src: e.tensor_copy(out=XT[:, gl, :, 128 * c:128 * (c + 1)], in_=src),
                                             reads=[Bps[pb_]], writes=[BXT])
                            else:
                                for gl in range(2):
                                    g = 2 * gp + gl
                                    src = G32[gb][:, :].rearrange("p (r q) -> p r q", q=512)[:, :, g * 128:(g + 1) * 128]
                                    eng = "act" if gl == 0 else "dve"
                                    if eng == "act":
                                        P.op("act", lambda e, gl=gl, c=c, src=src: e.activation(out=V1[:, gl, 4 * c:4 * c + 4, 0:128], in_=src, func=AF.Copy),
                                             reads=[BG32[gb]], writes=[BV1])
                                    else:
                                        P.op("dve", lambda e, gl=gl, c=c, src=src: e.tensor_copy(out=V1[:, gl, 4 * c:4 * c + 4, 0:128], in_=src),
                                             reads=[BG32[gb]], writes=[BV1])
                        if ci < 2:
                            w1t = w1k if ci == 0 else w1v
                            Gt, BG = (Gk, BGk) if ci == 0 else (Gv, BGv)
                            for gl in range(2):
                                mi = 0
                                for a_ in range(2):
                                    for j4 in range(4):
                                        for r4 in range(4):
                                            c_0 = 4 * a_ + j4
                                            P.op("pe", lambda e, w1t=w1t, a_=a_, j4=j4, r4=r4, gl=gl, c_0=c_0, mi=mi: e.matmul(
                                                psum[1][:, 0:511], lhsT=w1t[:, a_ * 16 + 4 * j4 + r4, :],
                                                rhs=XT[:, gl, r4, c_0:c_0 + 4 * 510 + 1:4], start=(mi == 0), stop=(mi == 31)),
                                                reads=[Bc, BXT], writes=[Bps[1]])
                                            mi += 1
                                gelu_cols(P, psum[1][:, 0:511], hid0[:, ci:ci + 1], 511, x32, x2, sgt, Gt[:, 0:511], Bx, Bps[1], BG, rd=[Bc])
                                if ci == 0:
                                    P.op("pe", lambda e: e.matmul(psum[4][:, 0:511], lhsT=w2[:, 0, :], rhs=Gk[:, 0:511], start=True, stop=True),
                                         reads=[Bc, BGk], writes=[Bps[4]])
                                    P.op("pe", lambda e: e.matmul(psum[5][:, 0:511], lhsT=w2[:, 1, :], rhs=Gk[:, 0:511], start=True, stop=True),
                                         reads=[Bc, BGk], writes=[Bps[5]])
                                    P.op("dve", lambda e: e.tensor_tensor(out=x32[:, 0:511], in0=psum[4][:, 0:511], in1=ccs[:, 0:511], op=ALU.mult),
                                         reads=[Bps[4], Bc, Bx], writes=[Bx])
                                    P.op("dve", lambda e: e.tensor_tensor(out=x2[:, 0:511], in0=psum[5][:, 0:511], in1=scs[:, 0:511], op=ALU.mult),
                                         reads=[Bps[5], Bc, Bx], writes=[Bx])
                                    P.op("dve", lambda e, gl=gl: e.tensor_tensor(out=kcbT[:, gl, 0:511], in0=x32[:, 0:511], in1=x2[:, 0:511], op=ALU.add),
                                         reads=[Bx], writes=[Bkcb])
                                else:
                                    for nt_, nn in ((0, 128), (1, 128), (2, 128), (3, 127)):
                                        P.op("pe", lambda e, nt_=nt_, nn=nn: e.matmul(
                                            psum[4][0:nn, nt_ * 128:(nt_ + 1) * 128], lhsT=Gv[:, nt_ * 128:nt_ * 128 + nn], rhs=w2[:, 2, :],
                                            start=(nt_ == 0), stop=True, skip_group_check=True), reads=[Bc, BGv], writes=[Bps[4]])
                                    for nt_, nn in ((0, 128), (1, 128), (2, 128), (3, 127)):
                                        P.op("act", lambda e, nt_=nt_, nn=nn, gl=gl: e.activation(
                                            out=vcb[0:nn, gl, nt_, :], in_=psum[4][0:nn, nt_ * 128:(nt_ + 1) * 128], func=AF.Copy),
                                            reads=[Bps[4], Bvcb], writes=[Bvcb])
                    P.op("pool", lambda e, rowS=rowS, gp=gp: e.dma_start(out=ntm[:, :], in_=kvout[2, rowS:rowS + 4, gp * 256:(gp + 1) * 256]),
                         writes=[Bntm], dma=True)
                    for gl in range(2):
                        P.op("pe", lambda e, gl=gl: e.transpose(out=psb[:, gl * 4:gl * 4 + 4], in_=ntm[0:4, gl * 128:(gl + 1) * 128], identity=identb[0:4, 0:4]),
                             reads=[Bntm, Bc], writes=[Bpsb])
                    P.op("act", lambda e: e.activation(out=XTn[:, :, :].rearrange("p a b -> p (a b)"), in_=psb[:, 0:8], func=AF.Copy),
                         reads=[Bpsb], writes=[BXTn])
                    for gl in range(2):
                        g = 2 * gp + gl
                        P.op("pool", lambda e, rowS=rowS, g=g, gl=gl: e.dma_start(out=V1[0:4, gl, 64, 0:128], in_=kvout[3, rowS:rowS + 4, g * 128:(g + 1) * 128]),
                             writes=[BV1], dma=True)
                        P.op("pool", lambda e, rowS=rowS, g=g, gl=gl: e.dma_start(out=V1w[0:4, gl, 4, 0:128], in_=kvout[5, rowS:rowS + 4, g * 128:(g + 1) * 128]),
                             writes=[BV1w], dma=True)
                        P.op("pool", lambda e, sbi=sbi, g=g, gl=gl: e.dma_start(
                            out=V1w[:, gl, 0:4, 0:128], in_=st_vw[sbi, :, g * 128:(g + 1) * 128].rearrange("(i p) d -> p i d", p=128)),
                            writes=[BV1w], dma=True)
                    P.op("pool", lambda e, sbi=sbi, gp=gp: e.dma_start(
                        out=wtm[:, :, :], in_=st_kw[sbi, :, gp * 256:(gp + 1) * 256].rearrange("(i p) d -> p i d", p=128)),
                        writes=[Bwtm], dma=True)
                    for gl in range(2):
                        for w_ in range(4):
                            P.op("pe", lambda e, gl=gl, w_=w_: e.transpose(out=psb[:, (gl * 4 + w_) * 128:(gl * 4 + w_ + 1) * 128],
                                                                           in_=wtm[:, w_, gl * 128:(gl + 1) * 128], identity=identb[:, :]),
                                 reads=[Bwtm, Bc], writes=[Bpsb])
                    P.op("act", lambda e: e.activation(out=kwT[:, :, 0:512], in_=psb[:, :].rearrange("p (a b) -> p a b", a=2), func=AF.Copy),
                         reads=[Bpsb], writes=[BkwT])
                    P.op("pool", lambda e, rowS=rowS, gp=gp: e.dma_start(out=ntm[:, :], in_=kvout[4, rowS:rowS + 4, gp * 256:(gp + 1) * 256]),
                         writes=[Bntm], dma=True)
                    for gl in range(2):
                        P.op("pe", lambda e, gl=gl: e.transpose(out=psb[:, gl * 4:gl * 4 + 4], in_=ntm[0:4, gl * 128:(gl + 1) * 128], identity=identb[0:4, 0:4]),
                             reads=[Bntm, Bc], writes=[Bpsb])
                    P.op("act", lambda e: e.activation(out=kwT[:, :, 512:516], in_=psb[:, 0:8].rearrange("p (a b) -> p a b", a=2), func=AF.Copy),
                         reads=[Bpsb, BkwT], writes=[BkwT])

                    for gl in range(2):
                        g = 2 * gp + gl
                        P.op("pool", lambda e, tokS=tokS, g=g: e.dma_start(out=qtm[:], in_=q_s[tokS:tokS + 4, g * 768:(g + 1) * 768]),
                             writes=[Bq], dma=True)
                        P.op("sp", lambda e, tokS=tokS, g=g: e.dma_start(out=gt[:], in_=gate_s[tokS:tokS + 4, g * 18:(g + 1) * 18]),
                             writes=[Bgt], dma=True)
                        for h in range(6):
                            P.op("pe", lambda e, h=h: e.transpose(out=psb[:, h * 4:(h + 1) * 4], in_=qtm[0:4, h * 128:(h + 1) * 128],
                                                                  identity=identb[0:4, 0:4]), reads=[Bq, Bc], writes=[Bpsb])
                        P.op("act", lambda e: e.activation(out=qT[:], in_=psb[:, 0:24], func=AF.Copy), reads=[Bpsb], writes=[BqT])
                        for h in range(6):
                            P.op("pe", lambda e, h=h, gl=gl: e.matmul(psum[0][0:4, 0:511], lhsT=qT[:, h * 4:(h + 1) * 4], rhs=kcbT[:, gl, 0:511],
                                                                      start=True, stop=True), reads=[BqT, Bkcb], writes=[Bps[0]])
                            P.op("dve", lambda e: e.reduce_max(out=sm[:, 0:1], in_=psum[0][0:4, 0:511], axis=AX.X), reads=[Bps[0]], writes=[Bsm])
                            P.op("dve", lambda e: e.tensor_scalar(out=sm[:, 1:2], in0=sm[:, 0:1], scalar1=-SCALE, scalar2=None, op0=ALU.mult),
                                 reads=[Bsm], writes=[Bsm])
                            P.op("act", lambda e: e.activation(out=ev[:, 0:511], in_=psum[0][0:4, 0:511], func=AF.Exp, bias=sm[:, 1:2], scale=SCALE),
                                 reads=[Bps[0], Bsm], writes=[Be])
                            P.op("dve", lambda e: e.reduce_sum(out=sm[:, 2:3], in_=ev[:, 0:511], axis=AX.X), reads=[Be], writes=[Bsm])
                            P.op("dve", lambda e: e.tensor_scalar(out=sm[:, 2:3], in0=sm[:, 2:3], scalar1=1e-30, scalar2=None, op0=ALU.max),
                                 reads=[Bsm], writes=[Bsm])
                            P.op("dve", lambda e: e.reciprocal(out=sm[:, 3:4], in_=sm[:, 2:3]), reads=[Bsm], writes=[Bsm])
                            P.op("dve", lambda e: e.tensor_scalar(out=ev[:, 0:511], in0=ev[:, 0:511], scalar1=sm[:, 3:4], scalar2=None, op0=ALU.mult),
                                 reads=[Be, Bsm], writes=[Be])
                            if h == 0:
                                P.op("dve", lambda e: e.tensor_copy(out=Pp[:, 1:512], in_=ev[:, 0:511]), reads=[Be], writes=[BPp])
                            else:
                                P.op("dve", lambda e: e.tensor_tensor(out=Pp[:, 1:512], in0=Pp[:, 1:512], in1=ev[:, 0:511], op=ALU.add),
                                     reads=[Be, BPp], writes=[BPp])
                            P.op("act", lambda e: e.activation(out=Pb[:, 0:511], in_=ev[:, 0:511], func=AF.Copy), reads=[Be], writes=[BPb])
                            for nt_, nn in ((0, 128), (1, 128), (2, 128), (3, 127)):
                                P.op("pe", lambda e, nt_=nt_, nn=nn: e.transpose(out=psb[0:nn, nt_ * 4:nt_ * 4 + 4],
                                                                                 in_=Pb[0:4, nt_ * 128:nt_ * 128 + nn], identity=identb[0:4, 0:4]),
                                     reads=[BPb, Bc], writes=[Bpsb])
                            for nt_, nn in ((0, 128), (1, 128), (2, 128), (3, 127)):
                                P.op("act", lambda e, nt_=nt_, nn=nn: e.activation(out=PT[0:nn, nt_, :], in_=psb[0:nn, nt_ * 4:nt_ * 4 + 4], func=AF.Copy),
                                     reads=[Bpsb, BPT], writes=[BPT])
                            for nt_, nn in ((0, 128), (1, 128), (2, 128), (3, 127)):
                                P.op("pe", lambda e, nt_=nt_, nn=nn, gl=gl: e.matmul(psum[1][0:4, 0:128], lhsT=PT[0:nn, nt_, :], rhs=vcb[0:nn, gl, nt_, :],
                                                                                     start=(nt_ == 0), stop=(nt_ == 3)), reads=[BPT, Bvcb], writes=[Bps[1]])
                            P.op("dve", lambda e, h=h: e.tensor_scalar(out=acc[:, h * 128:(h + 1) * 128], in0=psum[1][0:4, 0:128],
                                                                       scalar1=gt[:, 3 * h:3 * h + 1], scalar2=None, op0=ALU.mult),
                                 reads=[Bps[1], Bgt], writes=[Bacc])
                        P.op("dve", lambda e: e.tensor_reduce(out=imp[:, :], in_=Pp[:, 0:516].rearrange("p (s j) -> p s j", j=4), axis=AX.X, op=ALU.add),
                             reads=[BPp], writes=[Bimp])
                        P.op("dve", lambda e: e.tensor_tensor(out=imp[:, :], in0=imp[:, :], in1=Pp[:, 4:517:4], op=ALU.add), reads=[BPp, Bimp], writes=[Bimp])
                        P.op("dve", lambda e: e.tensor_tensor(out=imp[:, :], in0=imp[:, :], in1=addms[:, :], op=ALU.add), reads=[Bimp, Bc], writes=[Bimp])
                        P.op("dve", lambda e: e.max(out=m8[:, 0:8], in_=imp[:, :]), reads=[Bimp], writes=[Bsm])
                        P.op("dve", lambda e: e.match_replace(out=imw[:, :], in_to_replace=m8[:, 0:8], in_values=imp[:, :], imm_value=-3.0e38),
                             reads=[Bimp, Bsm], writes=[Bimp])
                        P.op("dve", lambda e: e.max(out=m8[:, 8:16], in_=imw[:, :]), reads=[Bimp], writes=[Bsm])
                        P.op("dve", lambda e: e.tensor_scalar(out=imw[:, :], in0=imp[:, :], scalar1=m8[:, 15:16], scalar2=None, op0=ALU.is_ge),
                             reads=[Bimp, Bsm], writes=[Bimp])
                        P.op("dve", lambda e: e.tensor_scalar(out=negb[:, :], in0=imw[:, :], scalar1=-1.0, scalar2=-NEG, op0=ALU.add, op1=ALU.mult),
                             reads=[Bimp], writes=[Bneg])
                        P.op("dve", lambda e: e.tensor_copy(out=negx[:, :, :].rearrange("p c (b k) -> p c b k", k=16),
                                                            in_=negb[:, 0:128].rearrange("p (c b) -> p c b", b=8).unsqueeze(3).to_broadcast([4, 16, 8, 16])),
                             reads=[Bneg], writes=[Bnegx])
                        pti = 0
                        for br in (1, 2):
                            tiles = []
                            if br == 1:
                                for c in range(16):
                                    for r4 in range(4):
                                        tiles.append((XT[:, gl, r4, 128 * c:128 * (c + 1)], V1[:, gl, 4 * c + r4, :], 128, [negx[0:4, c, :]]))
                                tiles.append((XTn[:, gl, :], V1[0:4, gl, 64, :], 4, [cb_c[0:4, 0:4]]))
                                BK, BV = [BXT, BXTn], BV1
                            else:
                                for w_ in range(4):
                                    tiles.append((kwT[:, gl, 128 * w_:128 * (w_ + 1)], V1w[:, gl, w_, :], 128, [cb_a[0:4, 0:128]] if w_ == 0 else []))
                                tiles.append((kwT[:, gl, 512:516], V1w[0:4, gl, 4, :], 4, [cb_c[0:4, 0:4]]))
                                BK, BV = [BkwT], BV1w
                            for ki, (kap, vap, nk, biases) in enumerate(tiles):
                                for hh in range(2):
                                    ps_s = 2 + ((ki * 2 + hh) % 3)
                                    P.op("pe", lambda e, ps_s=ps_s, kap=kap, nk=nk, hh=hh, nb=len(biases): e.matmul(
                                        psum[ps_s][0:nk, 0:12], lhsT=kap, rhs=qT[:, hh * 12:(hh + 1) * 12],
                                        start=True, stop=(nb == 0)), reads=BK + [BqT], writes=[Bps[ps_s]])
                                    for bi, bias_ap in enumerate(biases):
                                        P.op("pe", lambda e, ps_s=ps_s, nk=nk, bias_ap=bias_ap, last=(bi == len(biases) - 1): e.matmul(
                                            psum[ps_s][0:nk, 0:12], lhsT=bias_ap, rhs=id4x3[:, :], start=False, stop=last),
                                            reads=[Bnegx, Bc], writes=[Bps[ps_s]])
                                    pt_ = pti % 4
                                    pti += 1
                                    P.op("act", lambda e, ps_s=ps_s, pt_=pt_, nk=nk: e.activation(
                                        out=PTs[pt_][0:nk, :], in_=psum[ps_s][0:nk, 0:12], func=AF.Exp, scale=SCALE),
                                        reads=[Bps[ps_s]], writes=[BPTs[pt_]])
                                    for hl in range(3):
                                        P.op("pe", lambda e, hh=hh, hl=hl, pt_=pt_, nk=nk, vap=vap, first=(ki == 0 and hl == 0), last=(ki == len(tiles) - 1): e.matmul(
                                            psum[5 + hh][0:4, hl * 130:(hl + 1) * 130], lhsT=PTs[pt_][0:nk, hl * 4:(hl + 1) * 4], rhs=vap,
                                            start=first, stop=last, skip_group_check=True), reads=[BPTs[pt_], BV], writes=[Bps[5 + hh]])
                            for hh in range(2):
                                for hl in range(3):
                                    h = hh * 3 + hl
                                    P.op("dve", lambda e, hh=hh, hl=hl, h=h: e.reciprocal(out=sc6[:, h:h + 1], in_=psum[5 + hh][0:4, hl * 130 + 128:hl * 130 + 129]),
                                         reads=[Bps[5 + hh]], writes=[Bsc6])
                                    P.op("dve", lambda e, h=h, br=br: e.tensor_tensor(out=sc6[:, h:h + 1], in0=sc6[:, h:h + 1], in1=gt[:, 3 * h + br:3 * h + br + 1], op=ALU.mult),
                                         reads=[Bsc6, Bgt], writes=[Bsc6])
                                    P.op("dve", lambda e, hh=hh, hl=hl, h=h: e.scalar_tensor_tensor(
                                        out=acc[:, h * 128:(h + 1) * 128], in0=psum[5 + hh][0:4, hl * 130:hl * 130 + 128], scalar=sc6[:, h:h + 1],
                                        in1=acc[:, h * 128:(h + 1) * 128], op0=ALU.mult, op1=ALU.add), reads=[Bps[5 + hh], Bsc6, Bacc], writes=[Bacc])
                        P.op("act", lambda e: e.activation(out=accb[:], in_=acc[:], func=AF.Copy), reads=[Bacc], writes=[Baccb])
                        for h in range(6):
                            P.op("pe", lambda e, h=h: e.transpose(out=psb[:, h * 4:(h + 1) * 4], in_=accb[0:4, h * 128:(h + 1) * 128],
                                                                  identity=identb[0:4, 0:4]), reads=[Baccb, Bc], writes=[Bpsb])
                        P.op("act", lambda e: e.activation(out=mixo[:, :, :].rearrange("p a b -> p (a b)"), in_=psb[:, 0:24], func=AF.Copy),
                             reads=[Bpsb], writes=[Bmixo])
                        P.op("sp", lambda e, g=g, tokS=tokS: e.dma_start(out=mixT_s[:, 8 + 6 * g:14 + 6 * g, tokS:tokS + 4], in_=mixo[:, :, :]),
                             reads=[Bmixo], dma=True)
            P.emit_phase()

        with contextlib.ExitStack() as st:
            sb = lambda name, shape, dt: st.enter_context(nc.sbuf_tensor(uname(name), list(shape), dt))
            mixT = sb("mixT", [128, 32, TOK], BF16)
            wt = [sb("wo%d" % i, [128, 32, 512], BF16) for i in range(2)]
            xr = [sb("xr%d" % i, [128, 512], F32) for i in range(3)]
            Bmix = Buf()
            Bwt = [Buf(), Buf()]
            Bxr = [Buf() for _ in range(3)]
            Bps = [Buf() for _ in range(7)]
            for kq in range(4):
                P.op("sp", lambda e, kq=kq: e.dma_start(out=mixT[:, kq * 8:(kq + 1) * 8, :], in_=mixT_s[:, kq * 8:(kq + 1) * 8, :]),
                     writes=[Bmix], dma=True)
            psi = 0
            xi = 0
            for db in range(8):
                w = db % 2
                for kh in range(2):
                    P.op("pool", lambda e, db=db, kh=kh, w=w: e.dma_start(
                        out=wt[w][:, kh * 16:(kh + 1) * 16, :].rearrange("p k f -> p (k f)"),
                        in_=w_o[db, :, kh * 16:(kh + 1) * 16, :].rearrange("p k f -> p (k f)")),
                        writes=[Bwt[w]], dma=True)
                for (r0, m) in TT9:
                    p = psi % 7
                    psi += 1
                    x = xi % 3
                    xi += 1
                    P.op("sp", lambda e, x=x, r0=r0, m=m, db=db: e.dma_start(
                        out=xr[x][0:m, :], in_=x_own[r0:r0 + m, db * 512:(db + 1) * 512]), writes=[Bxr[x]], dma=True)
                    for k in range(32):
                        P.op("pe", lambda e, p=p, k=k, r0=r0, m=m, w=w: e.matmul(
                            psum[p][0:m, :], lhsT=mixT[:, k, r0:r0 + m], rhs=wt[w][:, k, :],
                            start=(k == 0), stop=(k == 31)), reads=[Bmix, Bwt[w]], writes=[Bps[p]])
                    P.op("dve", lambda e, x=x, p=p, m=m: e.scalar_tensor_tensor(
                        out=xr[x][0:m, :], in0=xr[x][0:m, :], scalar=ALPHA, in1=psum[p][0:m, :],
                        op0=ALU.mult, op1=ALU.add), reads=[Bps[p], Bxr[x]], writes=[Bxr[x]])
                    P.op("sp", lambda e, x=x, r0=r0, m=m, db=db: e.dma_start(
                        out=r_s[r0:r0 + m, db * 512:(db + 1) * 512], in_=xr[x][0:m, :]), reads=[Bxr[x]], dma=True)
            P.emit_phase()

        def ln_phase(src, g_idx, dst_f32, dst_T):
            with contextlib.ExitStack() as st:
                sb = lambda name, shape, dt: st.enter_context(nc.sbuf_tensor(uname(name), list(shape), dt))
                gt = sb("ln_g", [128, D], F32)
                bt = sb("ln_b", [128, D], F32)
                idb = sb("ln_idb", [128, 128], BF16)
                idf = sb("ln_idf", [128, 128], F32)
                rt = [sb("ln_r%d" % i, [128, D], F32) for i in range(2)]
                hb = sb("ln_hb", [128, D], BF16)
                hT = sb("ln_hT", [128, 32, 128], BF16)
                stats = sb("ln_stats", [128, 8, 6], F32)
                mv = sb("ln_mv", [128, 2], F32)
                rstd = sb("ln_rstd", [128, 1], F32)
                Bg, Bid = Buf(), Buf()
                Brt = [Buf(), Buf()]
                Bhb, BhT, Bst, Bmv, Brs = Buf(), Buf(), Buf(), Buf(), Buf()
                Bpsb = Buf()
                P.op("sp", lambda e: e.dma_start(out=gt[:], in_=lnp[g_idx]), writes=[Bg], dma=True)
                P.op("sp", lambda e: e.dma_start(out=bt[:], in_=lnp[g_idx + 1]), writes=[Bg], dma=True)
                P.op("sp", lambda e: e.dma_start(out=idf[:], in_=ident[:, :]), writes=[Bid], dma=True)
                P.op("dve", lambda e: e.tensor_copy(out=idb[:], in_=idf[:]), reads=[Bid], writes=[Bid])
                for ti, (r0, m) in enumerate(TT9):
                    r = ti % 2
                    P.op("sp", lambda e, r=r, r0=r0, m=m: e.dma_start(out=rt[r][0:m, :], in_=src[r0:r0 + m, :]),
                         writes=[Brt[r]], dma=True)
                    for c in range(8):
                        P.op("dve", lambda e, r=r, m=m, c=c: e.bn_stats(out=stats[0:m, c, :], in_=rt[r][0:m, c * 512:(c + 1) * 512]),
                             reads=[Brt[r]], writes=[Bst])
                    P.op("dve", lambda e, m=m: e.bn_aggr(out=mv[0:m, :], in_=stats[0:m, :, :].rearrange("p a b -> p (a b)")),
                         reads=[Bst], writes=[Bmv])
                    P.op("act", lambda e, m=m: e.activation(out=rstd[0:m, :], in_=mv[0:m, 1:2], func=AF.Sqrt, bias=EPS, scale=1.0),
                         reads=[Bmv], writes=[Brs])
                    P.op("dve", lambda e, m=m: e.reciprocal(out=rstd[0:m, :], in_=rstd[0:m, :]), reads=[Brs], writes=[Brs])
                    P.op("dve", lambda e, r=r, m=m: e.tensor_scalar(
                        out=rt[r][0:m, :], in0=rt[r][0:m, :], scalar1=mv[0:m, 0:1], scalar2=rstd[0:m, 0:1],
                        op0=ALU.subtract, op1=ALU.mult), reads=[Brt[r], Bmv, Brs], writes=[Brt[r]])
                    P.op("dve", lambda e, r=r, m=m: e.tensor_tensor(out=rt[r][0:m, :], in0=rt[r][0:m, :], in1=gt[0:m, :], op=ALU.mult),
                         reads=[Brt[r], Bg], writes=[Brt[r]])
                    P.op("dve", lambda e, r=r, m=m: e.tensor_tensor(out=rt[r][0:m, :], in0=rt[r][0:m, :], in1=bt[0:m, :], op=ALU.add),
                         reads=[Brt[r], Bg], writes=[Brt[r]])
                    P.op("sp", lambda e, r=r, r0=r0, m=m: e.dma_start(out=dst_f32[r0:r0 + m, :], in_=rt[r][0:m, :]),
                         reads=[Brt[r]], dma=True)
                    if dst_T is not None:
                        P.op("act", lambda e, r=r, m=m: e.activation(out=hb[0:m, :], in_=rt[r][0:m, :], func=AF.Copy),
                             reads=[Brt[r]], writes=[Bhb])
                        for kq in range(4):
                            for kk in range(8):
                                k = kq * 8 + kk
                                P.op("pe", lambda e, k=k, kk=kk, m=m: e.transpose(
                                    out=psb[:, kk * 128:kk * 128 + m], in_=hb[0:m, k * 128:(k + 1) * 128], identity=idb[0:m, 0:m]),
                                    reads=[Bhb, Bid], writes=[Bpsb])
                            pv = psb[:, :].rearrange("p (a b) -> p a b", b=128)
                            P.op("act", lambda e, kq=kq, m=m, pv=pv: e.activation(
                                out=hT[:, kq * 8:(kq + 1) * 8, 0:m], in_=pv[:, :, 0:m], func=AF.Copy),
                                reads=[Bpsb], writes=[BhT])
                        P.op("sp", lambda e, r0=r0, m=m: e.dma_start(out=dst_T[:, :, r0:r0 + m], in_=hT[:, :, 0:m]),
                             reads=[BhT], dma=True)
                P.emit_phase()

        ln_phase(r_s, 0, h_s, hT_s)

        for (t0, nt) in ((0, 512), (512, 528)):
            with contextlib.ExitStack() as st:
                sb = lambda name, shape, dt: st.enter_context(nc.sbuf_tensor(uname(name), list(shape), dt))
                hT = sb("f_hT", [128, 32, 528], BF16)
                ffT = sb("f_ffT", [128, NFC, 528], BF16)
                wg = [sb("f_wg%d" % i, [128, 32, 128], BF16) for i in range(2)]
                wu = [sb("f_wu%d" % i, [128, 32, 128], BF16) for i in range(2)]
                sg = [sb("f_sg%d" % i, [128, 528], F32) for i in range(2)]
                BhT, Bff = Buf(), [Buf() for _ in range(NFC)]
                Bwg, Bwu, Bsg = [Buf(), Buf()], [Buf(), Buf()], [Buf(), Buf()]
                Bps = [Buf() for _ in range(7)]
                Bpsb = Buf()
                for kq in range(4):
                    P.op("sp", lambda e, kq=kq: e.dma_start(out=hT[:, kq * 8:(kq + 1) * 8, 0:nt],
                                                            in_=hT_s[:, kq * 8:(kq + 1) * 8, t0:t0 + nt]),
                         writes=[BhT], dma=True)
                segs = [(0, 512)] + ([(512, 16)] if nt > 512 else [])
                for f in range(NFC):
                    w = f % 2
                    P.op("pool", lambda e, f=f, w=w: e.dma_start(out=wg[w][:, :, :].rearrange("p k f -> p (k f)"),
                                                               in_=w_g[f].rearrange("p k f -> p (k f)")), writes=[Bwg[w]], dma=True)
                    P.op("pool", lambda e, f=f, w=w: e.dma_start(out=wu[w][:, :, :].rearrange("p k f -> p (k f)"),
                                                               in_=w_u[f].rearrange("p k f -> p (k f)")), writes=[Bwu[w]], dma=True)
                    pb = 3 * (f % 2)
                    for si, (c0, n) in enumerate(segs):
                        if si == 0:
                            pg, pu, g0, u0 = pb, pb + 1, 0, 0
                        else:
                            pg, pu, g0, u0 = pb + 2, pb + 2, 0, 16
                        for k in range(32):
                            P.op("pe", lambda e, pg=pg, g0=g0, k=k, c0=c0, n=n, w=w: e.matmul(
                                psum[pg][:, g0:g0 + n], lhsT=wg[w][:, k, :], rhs=hT[:, k, c0:c0 + n],
                                start=(k == 0), stop=(k == 31)), reads=[BhT, Bwg[w]], writes=[Bps[pg]])
                        P.op("act", lambda e, pg=pg, g0=g0, c0=c0, n=n, w=w: e.activation(
                            out=sg[w][:, c0:c0 + n], in_=psum[pg][:, g0:g0 + n], func=AF.Silu), reads=[Bps[pg]], writes=[Bsg[w]])
                        for k in range(32):
                            P.op("pe", lambda e, pu=pu, u0=u0, k=k, c0=c0, n=n, w=w: e.matmul(
                                psum[pu][:, u0:u0 + n], lhsT=wu[w][:, k, :], rhs=hT[:, k, c0:c0 + n],
                                start=(k == 0), stop=(k == 31)), reads=[BhT, Bwu[w]], writes=[Bps[pu]])
                        P.op("dve", lambda e, pu=pu, u0=u0, c0=c0, n=n, w=w, f=f: e.tensor_tensor(
                            out=ffT[:, f, c0:c0 + n], in0=sg[w][:, c0:c0 + n], in1=psum[pu][:, u0:u0 + n], op=ALU.mult),
                            reads=[Bps[pu], Bsg[w]], writes=[Bff[f]])
                wd = [sb("f_wd%d" % i, [128, 6, 512], BF16) for i in range(2)]
                hr = [sb("f_hr%d" % i, [128, 512], F32) for i in range(3)]
                Bwd = [Buf(), Buf()]
                Bhr = [Buf() for _ in range(3)]
                tts = [(i * 128, 128) for i in range(4)] + ([(512, 16)] if nt > 512 else [])
                FG = [(f0, min(6, NFC - f0)) for f0 in range(0, NFC, 6)]
                wi = 0
                hi = 0
                for db in range(8):
                    for (f0, nf) in FG:
                        w = wi % 2
                        wi += 1
                        P.op("pool", lambda e, db=db, f0=f0, nf=nf, w=w: e.dma_start(
                            out=wd[w][:, 0:nf, :].rearrange("p k f -> p (k f)"),
                            in_=w_d[db, :, f0:f0 + nf, :].rearrange("p k f -> p (k f)")), writes=[Bwd[w]], dma=True)
                        for fi in range(nf):
                            f = f0 + fi
                            for ti, (c0, m) in enumerate(tts):
                                P.op("pe", lambda e, ti=ti, c0=c0, m=m, f=f, fi=fi, w=w: e.matmul(
                                    psum[ti][0:m, :], lhsT=ffT[:, f, c0:c0 + m], rhs=wd[w][:, fi, :],
                                    start=(f == 0), stop=(f == NFC - 1)), reads=[Bff[f], Bwd[w]], writes=[Bps[ti]])
                    for ti, (c0, m) in enumerate(tts):
                        x = hi % 3
                        hi += 1
                        r0 = t0 + c0
                        P.op("sp", lambda e, x=x, r0=r0, m=m, db=db: e.dma_start(
                            out=hr[x][0:m, :], in_=h_s[r0:r0 + m, db * 512:(db + 1) * 512]), writes=[Bhr[x]], dma=True)
                        P.op("dve", lambda e, x=x, ti=ti, m=m: e.scalar_tensor_tensor(
                            out=hr[x][0:m, :], in0=hr[x][0:m, :], scalar=ALPHA, in1=psum[ti][0:m, :],
                            op0=ALU.mult, op1=ALU.add), reads=[Bps[ti], Bhr[x]], writes=[Bhr[x]])
                        P.op("sp", lambda e, x=x, r0=r0, m=m, db=db: e.dma_start(
                            out=y_s[r0:r0 + m, db * 512:(db + 1) * 512], in_=hr[x][0:m, :]), reads=[Bhr[x]], dma=True)
                P.emit_phase()

        ln_phase(y_s, 2, y_o, None)
    return nc


_NC_CACHE = {}


def _tile_w(w, nblk, blk):
    return np.ascontiguousarray(w.reshape(32, 128, nblk, blk).transpose(2, 1, 0, 3))


def kernel(x_prompt, x_sample, cache_k_cmp, cache_v_cmp, cache_k_slc, cache_v_slc,
           state_k_win, state_v_win, state_pool, page_table, w_in,
           w_cmp1_k, pe_cmp_k, w_cmp2_k, w_cmp1_v, pe_cmp_v, w_cmp2_v,
           w_pool, pool_scale, w_o, ln1_g, ln1_b, w_gate, w_up, w_down, ln2_g, ln2_b):
    f32 = np.float32
    x_prompt = np.asarray(x_prompt, f32)
    x_sample = np.asarray(x_sample, f32)
    if "nc" not in _NC_CACHE:
        _NC_CACHE["nc"] = build_program()
    nc = _NC_CACHE["nc"]

    w_in_p = np.zeros((D, E_PAD), f32)
    w_in_p[:, :7240] = np.asarray(w_in, f32)[0]
    w_in_t = _tile_w(w_in_p, 15, 512)
    w_o_t = _tile_w(np.asarray(w_o, f32)[0], 8, 512)
    w_g_t = _tile_w(np.asarray(w_gate, f32)[0], NFC, 128)
    w_u_t = _tile_w(np.asarray(w_up, f32)[0], NFC, 128)
    w_d_t = np.ascontiguousarray(np.asarray(w_down, f32)[0].reshape(NFC, 128, 8, 512).transpose(2, 1, 0, 3))
    lnp = np.ascontiguousarray(np.broadcast_to(
        np.stack([np.asarray(a, f32)[0] for a in (ln1_g, ln1_b, ln2_g, ln2_b)])[:, None, :], (4, 128, D)))
    ident = np.eye(128, dtype=f32)
    wp_h = np.ascontiguousarray(np.asarray(w_pool, f32)[0].reshape(4, 2, 128, 256).transpose(2, 0, 1, 3))
    psc_h = np.ascontiguousarray(np.asarray(pool_scale, f32)[0].reshape(8, 128).T)
    half = 64
    inv = (10000.0 ** (-np.arange(half, dtype=f32) / half)).astype(f32)
    NEGV = -30000.0
    tt_, kk_ = np.meshgrid(np.arange(128), np.arange(128), indexing="ij")
    cbc_h = np.where(kk_ <= tt_, 0.0, NEGV).astype(f32)
    cba_h = np.where(kk_ > tt_, 0.0, NEGV).astype(f32)
    w1k_h = np.ascontiguousarray(np.asarray(w_cmp1_k, f32)[0].transpose(1, 0, 2))
    w1v_h = np.ascontiguousarray(np.asarray(w_cmp1_v, f32)[0].transpose(1, 0, 2))
    w2k_ = np.asarray(w_cmp2_k, f32)[0]
    w2_h = np.ascontiguousarray(np.stack([w2k_, np.roll(w2k_, 64, axis=1), np.asarray(w_cmp2_v, f32)[0]], 1))
    peT_h = np.ascontiguousarray(np.stack([np.asarray(pe_cmp_k, f32)[0].T, np.asarray(pe_cmp_v, f32)[0].T], 1))
    sgn_h = np.where(np.arange(128) < 64, -1.0, 1.0).astype(f32)
    ckc_h = np.ascontiguousarray(np.asarray(cache_k_cmp, f32)[0]).reshape(2560 * 32, 2048)
    cvc_h = np.ascontiguousarray(np.asarray(cache_v_cmp, f32)[0]).reshape(2560 * 32, 2048)
    cks_h = np.ascontiguousarray(np.asarray(cache_k_slc, f32)[0]).reshape(2560 * 32, 2048)
    cvs_h = np.ascontiguousarray(np.asarray(cache_v_slc, f32)[0]).reshape(2560 * 32, 2048)
    ptab_all = np.asarray(page_table).astype(np.int32)
    oh_h = np.zeros((128, 5), f32)
    oh_h[np.arange(128), np.arange(128) // 32] = 1.0
    oh_h[:, 4] = np.arange(128) % 32
    sang = (16.0 * np.arange(512) + 31.0).astype(f32)[None, :] * inv[np.arange(128) % 64][:, None]
    ccs_h = np.cos(sang).astype(f32)
    scs_h = (np.sin(sang) * sgn_h[:, None]).astype(f32)
    addms_h = np.zeros((4, 129), f32)
    addms_h[:, [0, 127, 128]] = 1e30

    in_maps = []
    for c in range(8):
        b, j = c // 4, c % 4
        pad = 3 - j
        xs = np.zeros((4096, D), f32)
        xs[pad * 128:] = x_prompt[b, :4096 - pad * 128]
        xTs = np.ascontiguousarray(xs.reshape(4, 1024, 32, 128).transpose(0, 3, 2, 1))
        xsm = x_sample[4 * c:4 * c + 4].reshape(16, D)
        xsT = np.ascontiguousarray(xsm.reshape(16, 32, 128).transpose(2, 1, 0))
        own_rows = np.concatenate([np.arange(128) + (4 * i + 3) * 128 for i in range(8)])
        x_own = np.ascontiguousarray(np.concatenate([xs[own_rows], xsm], 0))
        pos = np.concatenate([np.arange(4096) - pad * 128, np.tile(8192 + np.arange(4), 4)]).astype(f32)
        ang = pos[:, None] * inv[None, :]
        own_pos = np.concatenate([pos[own_rows], pos[4096:]])
        rc = np.stack([1.0 / np.minimum(np.maximum(own_pos, 0) + 1.0, float(2 << gi)) for gi in range(4)]).astype(f32)
        rc_h = np.ascontiguousarray(np.broadcast_to(rc[None], (128, 4, TOK)))
        n_ = np.arange(256)
        cpos = (16 * n_ + 31 - pad * 128).astype(f32)
        cang = cpos[None, :] * inv[np.arange(128) % 64][:, None]
        ccmp_h = np.cos(cang).astype(f32)
        scmp_h = (np.sin(cang) * sgn_h[:, None]).astype(f32)
        ccmp_h[:, 255] = 0
        scmp_h[:, 255] = 0
        s_t = ((4 * np.arange(8)[None, :] + 3) * 128 + np.arange(128)[:, None])
        cval_h = ((16 * n_[None, None, :] + 31 <= s_t[:, :, None]) & (16 * n_[None, None, :] >= pad * 128)
                  & (n_[None, None, :] <= 254)).astype(f32)
        blk_ = np.arange(64)[None, None, :]
        cur_ = (s_t // 64)[:, :, None]
        b0_ = 2 * pad
        valid_ = (blk_ >= b0_) & (blk_ <= cur_)
        forced_ = (blk_ == b0_) | (blk_ == cur_) | (blk_ == cur_ - 1)
        addm_h = np.where(forced_, 1e30, np.where(valid_, 0.0, -1e30)).astype(f32)
        vblk_h = valid_.astype(f32)
        padb_h = np.ascontiguousarray(np.broadcast_to(np.where(np.arange(32) < pad, NEGV, 0.0).astype(f32)[None, :], (128, 32)))
        stp_h = np.ascontiguousarray(np.asarray(state_pool, f32)[0, 4 * c:4 * c + 4].reshape(4, 15, 8, 128).transpose(3, 2, 0, 1))
        in_maps.append({
            "xTs": xTs, "xsT": xsT, "x_own": x_own, "w_in": w_in_t, "w_o": w_o_t, "w_g": w_g_t, "w_u": w_u_t,
            "w_d": w_d_t, "cosT": np.cos(ang).astype(f32), "sinT": np.sin(ang).astype(f32), "lnp": lnp,
            "cbc_d": cbc_h, "cba_d": cba_h, "padb_d": padb_h, "w1k_d": w1k_h, "w1v_d": w1v_h, "w2_d": w2_h,
            "peT_d": peT_h, "ccmp_d": ccmp_h, "scmp_d": scmp_h, "cval_d": np.ascontiguousarray(cval_h),
            "addm_d": np.ascontiguousarray(addm_h), "vblk_d": np.ascontiguousarray(vblk_h),
            "ckc_d": ckc_h, "cvc_d": cvc_h, "cks_d": cks_h, "cvs_d": cvs_h,
            "ptab_d": np.ascontiguousarray(ptab_all[4 * c:4 * c + 4]), "oh_d": oh_h, "ccs_d": ccs_h, "scs_d": scs_h,
            "addms_d": addms_h,
            "ident": ident, "rc_d": rc_h, "wp_d": wp_h, "psc_d": psc_h, "stp_d": stp_h,
            "st_kw": np.ascontiguousarray(np.asarray(state_k_win, f32)[0, 4 * c:4 * c + 4].reshape(4, 512, 512)),
            "st_vw": np.ascontiguousarray(np.asarray(state_v_win, f32)[0, 4 * c:4 * c + 4].reshape(4, 512, 512)),
            "st_pool": np.ascontiguousarray(np.asarray(state_pool, f32)[0, 4 * c:4 * c + 4]),
        })
    res = run_bass_kernel_spmd(nc, in_maps, core_ids=list(range(8)))
    R = res.results

    y_prompt = np.zeros((2, 4096, D), f32)
    y_sample = np.zeros((32, 4, D), f32)
    kvp = [np.zeros((1, 2, 4096, 4, 128), f32) for _ in range(6)]
    kvs = [np.zeros((1, 32, 4, 4, 128), f32) for _ in range(6)]
    kw_s = np.zeros((1, 32, 512, 4, 128), f32)
    vw_s = np.zeros((1, 32, 512, 4, 128), f32)
    pool_p = np.zeros((1, 2, 15, 1024), f32)
    pool_s = np.zeros((1, 32, 15, 1024), f32)
    for c in range(8):
        b, j = c // 4, c % 4
        r = R[c]
        yo = r["y_o"]
        for i in range(8):
            g = 4 * i + j
            y_prompt[b, g * 128:(g + 1) * 128] = yo[i * 128:(i + 1) * 128]
        y_sample[4 * c:4 * c + 4] = yo[1024:1040].reshape(4, 4, D)
        for t in range(6):
            kvs[t][0, 4 * c:4 * c + 4] = r["kvout"][t, 4096:4112].reshape(4, 4, 4, 128)
            if j == 3:
                kvp[t][0, b] = r["kvout"][t, 0:4096].reshape(4096, 4, 128)
        kw_s[0, 4 * c:4 * c + 4] = r["kws_o"].reshape(4, 512, 4, 128)
        vw_s[0, 4 * c:4 * c + 4] = r["vws_o"].reshape(4, 512, 4, 128)
        pool_s[0, 4 * c:4 * c + 4] = r["pools_o"]
        if j == 3:
            pool_p[0, b] = r["poolp_o"]
    kc_p, vc_p, ks_p, vs_p, kw_full, vw_full = kvp
    return (y_prompt, y_sample, kc_p, vc_p, ks_p, vs_p,
            np.ascontiguousarray(kw_full[:, :, 4096 - 512:]), np.ascontiguousarray(vw_full[:, :, 4096 - 512:]),
            pool_p, kvs[0], kvs[1], kvs[2], kvs[3], kw_s, vw_s, pool_s)
```

```python
import contextlib
import numpy as np
import concourse.bass as bass
import concourse.mybir as mybir
from concourse.bass_utils import run_bass_kernel_spmd

F32 = mybir.dt.float32
BF16 = mybir.dt.bfloat16
I32 = mybir.dt.int32
AF = mybir.ActivationFunctionType
ALU = mybir.AluOpType
AX = mybir.AxisListType

D = 4096
DFF = 11008
NFC = 86
TOK = 1040
ALPHA = 2.0 ** 0.25
EPS = 1e-5
E_PAD = 7680


class Buf:
    __slots__ = ("name", "w", "r")

    def __init__(self, name=""):
        self.name = name
        self.w = None
        self.r = {}


class Op:
    __slots__ = ("eng", "fn", "deps", "dma", "needed", "sem", "semval", "done", "slot")

    def __init__(self, eng, fn, dma):
        self.eng = eng
        self.fn = fn
        self.deps = []
        self.dma = dma
        self.needed = False
        self.sem = None
        self.semval = None
        self.done = False
        self.slot = None


class Prog:
    ENGS = ("pe", "act", "dve", "pool", "sp")
    NDS = 16

    def __init__(self, nc, st):
        self.nc = nc
        self.ops = {k: [] for k in self.ENGS}
        self.engsem = {k: st.enter_context(nc.semaphore("es_" + k)) for k in self.ENGS}
        self.nds = {"sp": 16, "pool": 8, "act": 4}
        self.dsem = {q: [st.enter_context(nc.semaphore("ds_%s_%d" % (q, i))) for i in range(self.nds[q])]
                     for q in ("sp", "pool", "act")}
        self.cnt = {k: 0 for k in self.ENGS}
        self.dma_n = {q: 0 for q in self.dsem}
        self.dma_uses = {q: [0] * self.nds[q] for q in self.dsem}
        self.dma_last = {q: [None] * self.nds[q] for q in self.dsem}
        self.phase_dma = []

    def clear_sems(self):
        nc = self.nc
        with nc.Block() as block:
            def body(e):
                for s in self.engsem.values():
                    e.sem_clear(s)
                for lst in self.dsem.values():
                    for s in lst:
                        e.sem_clear(s)
            block.sync(body)

    def op(self, eng, fn, reads=(), writes=(), dma=False):
        o = Op(eng, fn, dma)
        deps = []
        for b in reads:
            if b.w is not None:
                deps.append(b.w)
        for b in writes:
            if b.w is not None:
                deps.append(b.w)
            deps.extend(b.r.values())
        if dma:
            s = self.dma_n[eng] % self.nds[eng]
            self.dma_n[eng] += 1
            o.slot = s
            if self.dma_last[eng][s] is not None:
                deps.append(self.dma_last[eng][s])
            self.dma_uses[eng][s] += 1
            o.sem = self.dsem[eng][s]
            o.semval = 16 * self.dma_uses[eng][s]
            self.dma_last[eng][s] = o
            self.phase_dma.append(o)
        seen = set()
        for d in deps:
            if id(d) in seen or d.done:
                continue
            seen.add(id(d))
            if (not d.dma) and d.eng == eng and eng == "pe":
                continue
            d.needed = True
            o.deps.append(d)
        self.ops[eng].append(o)
        for b in writes:
            b.w = o
            b.r = {}
        for b in reads:
            if b.w is o:
                continue
            key = ("dma", eng, o.slot) if dma else eng
            b.r[key] = o
        return o

    def emit_phase(self):
        nc = self.nc
        for k in self.ENGS:
            for o in self.ops[k]:
                if not o.dma and o.needed:
                    self.cnt[k] += 1
                    o.sem = self.engsem[k]
                    o.semval = self.cnt[k]
        finals = {}
        for o in self.phase_dma:
            finals[(o.eng, o.slot)] = o
        finals = list(finals.values())

        def make_body(k):
            def body(e):
                waited = {}

                def wait(d):
                    sid = id(d.sem)
                    if waited.get(sid, 0) >= d.semval:
                        return
                    waited[sid] = d.semval
                    e.wait_ge(d.sem, d.semval)

                for o in self.ops[k]:
                    for d in o.deps:
                        wait(d)
                    ins = o.fn(e)
                    if o.dma:
                        ins.then_inc(o.sem, 16)
                    elif o.needed:
                        ins.then_inc(o.sem, 1)
                if k == "sp":
                    for d in finals:
                        wait(d)
            return body

        with nc.Block() as block:
            block.tensor(make_body("pe"))
            block.scalar(make_body("act"))
            block.vector(make_body("dve"))
            block.gpsimd(make_body("pool"))
            block.sync(make_body("sp"))
        for k in self.ENGS:
            for o in self.ops[k]:
                o.done = True
            self.ops[k] = []
        self.phase_dma = []


def build_program():
    nc = bass.Bass("TRN2", target_bir_lowering=False)

    def din(name, shape, dt=F32):
        return nc.dram_tensor(name, list(shape), dt, kind="ExternalInput").ap()

    def dout(name, shape, dt=F32):
        return nc.dram_tensor(name, list(shape), dt, kind="ExternalOutput").ap()

    _uid = [0]

    def uname(name):
        _uid[0] += 1
        return "%s_%d" % (name, _uid[0])

    def dscr(name, shape, dt=F32):
        return nc.dram_tensor(name, list(shape), dt, kind="Internal").ap()

    xTs = din("xTs", [4, 128, 32, 1024])
    xsT = din("xsT", [128, 32, 16])
    x_own = din("x_own", [TOK, D])
    w_in = din("w_in", [15, 128, 32, 512])
    w_o = din("w_o", [8, 128, 32, 512])
    w_g = din("w_g", [NFC, 128, 32, 128])
    w_u = din("w_u", [NFC, 128, 32, 128])
    w_d = din("w_d", [8, 128, NFC, 512])
    cosT = din("cosT", [4112, 64])
    sinT = din("sinT", [4112, 64])
    lnp = din("lnp", [4, 128, D])
    ident = din("ident", [128, 128])
    st_kw = din("st_kw", [4, 512, 512])
    st_vw = din("st_vw", [4, 512, 512])
    st_pool = din("st_pool", [4, 15, 1024])
    rc_d = din("rc_d", [128, 4, TOK])
    wp_d = din("wp_d", [128, 4, 2, 256])
    psc_d = din("psc_d", [128, 8])
    stp_d = din("stp_d", [128, 8, 4, 15])
    cbc_d = din("cbc_d", [128, 128])
    cba_d = din("cba_d", [128, 128])
    padb_d = din("padb_d", [128, 32])
    w1k_d = din("w1k_d", [128, 32, 128])
    w1v_d = din("w1v_d", [128, 32, 128])
    w2_d = din("w2_d", [128, 3, 128])
    peT_d = din("peT_d", [128, 2, 32])
    ccmp_d = din("ccmp_d", [128, 256])
    scmp_d = din("scmp_d", [128, 256])
    cval_d = din("cval_d", [128, 8, 256])
    addm_d = din("addm_d", [128, 8, 64])
    vblk_d = din("vblk_d", [128, 8, 64])
    ckc_d = din("ckc_d", [2560 * 32, 2048])
    cvc_d = din("cvc_d", [2560 * 32, 2048])
    cks_d = din("cks_d", [2560 * 32, 2048])
    cvs_d = din("cvs_d", [2560 * 32, 2048])
    ptab_d = din("ptab_d", [4, 64], I32)
    oh_d = din("oh_d", [128, 5])
    ccs_d = din("ccs_d", [128, 512])
    scs_d = din("scs_d", [128, 512])
    addms_d = din("addms_d", [4, 129])

    kvout = dout("kvout", [6, 4112, 512])
    kws_o = dout("kws_o", [4, 512, 512])
    vws_o = dout("vws_o", [4, 512, 512])
    poolp_o = dout("poolp_o", [15, 1024])
    pools_o = dout("pools_o", [4, 15, 1024])
    y_o = dout("y_o", [TOK, D])

    mixT_s = dscr("mixT_s", [128, 32, TOK], BF16)
    r_s = dscr("r_s", [TOK, D])
    h_s = dscr("h_s", [TOK, D])
    hT_s = dscr("hT_s", [128, 32, TOK], BF16)
    y_s = dscr("y_s", [TOK, D])
    q_s = dscr("q_s", [TOK, 3072])
    gate_s = dscr("gate_s", [TOK, 72])

    TT9 = [(i * 128, 128) for i in range(8)] + [(1024, 16)]

    with contextlib.ExitStack() as gst:
        P = Prog(nc, gst)
        P.clear_sems()
        psum = [gst.enter_context(nc.psum_tensor("ps%d" % i, [128, 512], F32)) for i in range(7)]
        psb = gst.enter_context(nc.psum_tensor("psb", [128, 1024], BF16))

        with contextlib.ExitStack() as st:
            sb = lambda name, shape, dt: st.enter_context(nc.sbuf_tensor(uname(name), list(shape), dt))
            xb_t = sb("xb_t", [128, 32, 1040], BF16)
            wt = [sb("wt%d" % i, [128, 32, 512], BF16) for i in range(2)]
            cos_t = sb("cos_t", [128, 9, 64], F32)
            sin_t = sb("sin_t", [128, 9, 64], F32)
            o32 = [sb("o32_%d" % i, [128, 512], F32) for i in range(4)]
            ta = sb("ta", [128, 512], F32)
            tb = sb("tb", [128, 512], F32)
            zt = sb("zt", [128, TOK], BF16)
            u32 = sb("u32", [128, 144], F32)
            sAB = [sb("sA", [128, 144], F32), sb("sB", [128, 144], F32)]
            ua = sb("ua", [128, 19], F32)
            dT = sb("dT", [128, 2, 128], BF16)
            mo_t = [sb("mo0", [128, 128], BF16), sb("mo1", [128, 128], BF16)]
            rc_t = sb("rc_t", [128, 4, TOK], F32)
            wp_t = sb("wp_t", [128, 4, 2, 256], BF16)
            psc_t = sb("psc_t", [128, 8], F32)
            stp_t = sb("stp_t", [128, 8, 4, 15], F32)
            Bu32, BsAB, Bua, BdT, Bmo, Brc = Buf(), [Buf(), Buf()], Buf(), Buf(), [Buf(), Buf()], Buf()
            P.op("sp", lambda e: e.dma_start(out=rc_t[:], in_=rc_d[:, :, :]), writes=[Brc], dma=True)
            P.op("pool", lambda e: e.dma_start(out=wp_t[:], in_=wp_d[:, :, :, :]), writes=[Brc], dma=True)
            P.op("sp", lambda e: e.dma_start(out=psc_t[:], in_=psc_d[:, :]), writes=[Brc], dma=True)
            P.op("sp", lambda e: e.dma_start(out=stp_t[:], in_=stp_d[:, :, :, :]), writes=[Brc], dma=True)
            Bps = [Buf() for _ in range(7)]
            Bxb, Bcs = [Buf() for _ in range(5)], Buf()
            Bwt = [[Buf(), Buf()], [Buf(), Buf()]]
            Bo32 = [Buf() for _ in range(4)]
            Bta, Btb, Bzt = Buf(), Buf(), Buf()
            P.op("dve", lambda e: e.memset(zt[:], 0.0), writes=[Bzt])
            for k in range(8, 32):
                P.op("sp", lambda e, k=k: e.dma_start(out=mixT_s[:, k, :], in_=zt[:]), reads=[Bzt], dma=True)
            for sbi in range(4):
                P.op("sp", lambda e, i=sbi: e.dma_start(out=kws_o[i, 0:508, :], in_=st_kw[i, 4:512, :]), dma=True)
                P.op("sp", lambda e, i=sbi: e.dma_start(out=vws_o[i, 0:508, :], in_=st_vw[i, 4:512, :]), dma=True)
                P.op("sp", lambda e, i=sbi: e.dma_start(out=pools_o[i, 0:11, :], in_=st_pool[i, 4:15, :]), dma=True)
            psi = 0
            oi = 0
            wi = 0
            for xb in range(4):
                for kq in range(4):
                    P.op("pool", lambda e, xb=xb, kq=kq: e.dma_start(
                        out=xb_t[:, kq * 8:(kq + 1) * 8, 0:1024], in_=xTs[xb, :, kq * 8:(kq + 1) * 8, :]),
                        writes=[Bxb[kq]], dma=True)
                if xb == 3:
                    P.op("pool", lambda e: e.dma_start(out=xb_t[:, :, 1024:1040], in_=xsT[:, :, :]),
                         writes=[Bxb[4]], dma=True)
                ntile = 9 if xb == 3 else 8
                for (tab_t, tab_d) in ((cos_t, cosT), (sin_t, sinT)):
                    P.op("sp", lambda e, tab_t=tab_t, tab_d=tab_d, xb=xb: e.dma_start(
                        out=tab_t[:, 0:8, :], in_=tab_d[xb * 1024:(xb + 1) * 1024, :].rearrange("(i p) d -> p i d", p=128)),
                        writes=[Bcs], dma=True)
                    if xb == 3:
                        P.op("sp", lambda e, tab_t=tab_t, tab_d=tab_d: e.dma_start(
                            out=tab_t[0:16, 8, :], in_=tab_d[4096:4112, :]), writes=[Bcs], dma=True)
                for eb in list(range(8, 14)) + [0, 1] + list(range(2, 8)) + [14]:
                    w = wi % 2
                    wi += 1
                    for kh in range(2):
                        P.op("pool", lambda e, eb=eb, kh=kh, w=w: e.dma_start(
                            out=wt[w][:, kh * 16:(kh + 1) * 16, :].rearrange("p k f -> p (k f)"),
                            in_=w_in[eb, :, kh * 16:(kh + 1) * 16, :].rearrange("p k f -> p (k f)")),
                            writes=[Bwt[w][kh]], dma=True)
                    for i in range(ntile):
                        m = 128 if i < 8 else 16
                        c0 = i * 128
                        S = xb * 8 + i if i < 8 else 32
                        r0 = S * 128
                        own = (i == 8) or (i % 4 == 3)
                        tok0 = (S // 4) * 128 if i < 8 else 1024
                        if 8 <= eb < 14:
                            do = True
                        elif eb < 2:
                            do = S >= 31
                        else:
                            do = own
                        if not do:
                            continue
                        p = psi % 7
                        psi += 1
                        for k in range(32):
                            P.op("pe", lambda e, p=p, k=k, c0=c0, m=m, w=w: e.matmul(
                                psum[p][0:m, :], lhsT=xb_t[:, k, c0:c0 + m], rhs=wt[w][:, k, :],
                                start=(k == 0), stop=(k == 31)), reads=Bxb + Bwt[w], writes=[Bps[p]])
                        o = oi % 4
                        oi += 1
                        kv = eb - 8
                        if kv in (2, 4) or 2 <= eb < 8:
                            z4 = psum[p][0:m, :].rearrange("p (h two d) -> p h two d", two=2, d=64)
                            a4 = ta[0:m, :].rearrange("p (h two d) -> p h two d", two=2, d=64)
                            b4 = tb[0:m, :].rearrange("p (h two d) -> p h two d", two=2, d=64)
                            cb = cos_t[0:m, i, :].unsqueeze(1).unsqueeze(1).to_broadcast([m, 4, 2, 64])
                            sbh = sin_t[0:m, i, :].unsqueeze(1).to_broadcast([m, 4, 64])
                            P.op("dve", lambda e, a4=a4, z4=z4, cb=cb: e.tensor_tensor(out=a4, in0=z4, in1=cb, op=ALU.mult),
                                 reads=[Bps[p], Bcs], writes=[Bta])
                            P.op("dve", lambda e, b4=b4, z4=z4, sbh=sbh: e.tensor_tensor(
                                out=b4[:, :, 0, :], in0=z4[:, :, 1, :], in1=sbh, op=ALU.mult),
                                reads=[Bps[p], Bcs], writes=[Btb])
                            P.op("dve", lambda e, b4=b4, z4=z4, sbh=sbh: e.tensor_tensor(
                                out=b4[:, :, 1, :], in0=z4[:, :, 0, :], in1=sbh, op=ALU.mult),
                                reads=[Bps[p], Bcs, Btb], writes=[Btb])
                            o4 = o32[o][0:m, :].rearrange("p (h two d) -> p h two d", two=2, d=64)
                            P.op("dve", lambda e, o4=o4, a4=a4, b4=b4: e.tensor_tensor(
                                out=o4[:, :, 0, :], in0=a4[:, :, 0, :], in1=b4[:, :, 0, :], op=ALU.subtract),
                                reads=[Bta, Btb], writes=[Bo32[o]])
                            P.op("dve", lambda e, o4=o4, a4=a4, b4=b4: e.tensor_tensor(
                                out=o4[:, :, 1, :], in0=a4[:, :, 1, :], in1=b4[:, :, 1, :], op=ALU.add),
                                reads=[Bta, Btb, Bo32[o]], writes=[Bo32[o]])
                        elif eb == 14:
                            P.op("act", lambda e, o=o, p=p, m=m: e.activation(out=o32[o][0:m, 0:72], in_=psum[p][0:m, 0:72], func=AF.Sigmoid),
                                 reads=[Bps[p]], writes=[Bo32[o]])
                        else:
                            P.op("act", lambda e, o=o, p=p, m=m: e.activation(out=o32[o][0:m, :], in_=psum[p][0:m, :], func=AF.Copy),
                                 reads=[Bps[p]], writes=[Bo32[o]])
                        if 8 <= eb < 14:
                            P.op("sp", lambda e, kv=kv, r0=r0, m=m, o=o: e.dma_start(
                                out=kvout[kv, r0:r0 + m, :], in_=o32[o][0:m, :]), reads=[Bo32[o]], dma=True)
                            if S == 32 and kv in (4, 5):
                                dst = kws_o if kv == 4 else vws_o
                                for sbi in range(4):
                                    P.op("sp", lambda e, dst=dst, sbi=sbi, o=o: e.dma_start(
                                        out=dst[sbi, 508:512, :], in_=o32[o][sbi * 4:(sbi + 1) * 4, :]),
                                        reads=[Bo32[o]], dma=True)
                        elif eb < 2:
                            if S == 31:
                                P.op("sp", lambda e, eb=eb, o=o: e.dma_start(
                                    out=poolp_o[:, eb * 512:(eb + 1) * 512], in_=o32[o][113:128, :]),
                                    reads=[Bo32[o]], dma=True)
                            else:
                                for sbi in range(4):
                                    P.op("sp", lambda e, eb=eb, sbi=sbi, o=o: e.dma_start(
                                        out=pools_o[sbi, 11:15, eb * 512:(eb + 1) * 512],
                                        in_=o32[o][sbi * 4:(sbi + 1) * 4, :]), reads=[Bo32[o]], dma=True)
                        elif eb < 8:
                            P.op("sp", lambda e, eb=eb, tok0=tok0, m=m, o=o: e.dma_start(
                                out=q_s[tok0:tok0 + m, (eb - 2) * 512:(eb - 1) * 512], in_=o32[o][0:m, :]),
                                reads=[Bo32[o]], dma=True)
                        else:
                            P.op("sp", lambda e, tok0=tok0, m=m, o=o: e.dma_start(
                                out=gate_s[tok0:tok0 + m, :], in_=o32[o][0:m, 0:72]), reads=[Bo32[o]], dma=True)
                    if eb >= 2:
                        continue
                    units = [(3, False), (7, False)] + ([(8, True)] if xb == 3 else [])
                    for (i, is_s) in units:
                        S = xb * 8 + i if not is_s else 32
                        tok0 = (S // 4) * 128 if not is_s else 1024
                        m = 16 if is_s else 128
                        for cc4 in range(4):
                            cg = eb * 4 + cc4
                            gi, cc = cg // 2, cg % 2
                            wsz = 2 << gi
                            p = psi % 7
                            psi += 1
                            if not is_s:
                                n = 144
                                cs0 = i * 128 - 16
                            else:
                                n = 16
                                cs0 = 1024
                            for k in range(32):
                                P.op("pe", lambda e, p=p, k=k, cs0=cs0, n=n, w=w, cc4=cc4: e.matmul(
                                    psum[p][:, 0:n], lhsT=wt[w][:, k, cc4 * 128:(cc4 + 1) * 128], rhs=xb_t[:, k, cs0:cs0 + n],
                                    start=(k == 0), stop=(k == 31)), reads=Bxb + Bwt[w], writes=[Bps[p]])
                            P.op("act", lambda e, p=p, n=n: e.activation(out=u32[:, 0:n], in_=psum[p][:, 0:n], func=AF.Copy),
                                 reads=[Bps[p]], writes=[Bu32])
                            if not is_s:
                                cur, Bcur = u32, Bu32
                                for lv in range(gi + 1):
                                    sh = 1 << lv
                                    nxt, Bnxt = sAB[lv % 2], BsAB[lv % 2]
                                    P.op("dve", lambda e, nxt=nxt, cur=cur, sh=sh: e.tensor_tensor(
                                        out=nxt[:, sh:144], in0=cur[:, sh:144], in1=cur[:, 0:144 - sh], op=ALU.add),
                                        reads=[Bcur], writes=[Bnxt])
                                    cur, Bcur = nxt, Bnxt
                                oth, Both = sAB[(gi + 1) % 2], BsAB[(gi + 1) % 2]
                                P.op("dve", lambda e, oth=oth, cur=cur, gi=gi, tok0=tok0: e.tensor_tensor(
                                    out=oth[:, 0:128], in0=cur[:, 16:144], in1=rc_t[:, gi, tok0:tok0 + 128], op=ALU.mult),
                                    reads=[Bcur, Brc], writes=[Both])
                                P.op("dve", lambda e, oth=oth, cc=cc: e.tensor_tensor(
                                    out=dT[:, cc, 0:128], in0=oth[:, 0:128], in1=u32[:, 16:144], op=ALU.subtract),
                                    reads=[Both, Bu32], writes=[BdT])
                            else:
                                for sbi in range(4):
                                    P.op("dve", lambda e, cg=cg, sbi=sbi: e.tensor_copy(out=ua[:, 0:15], in_=stp_t[:, cg, sbi, :]),
                                         reads=[Brc], writes=[Bua])
                                    P.op("dve", lambda e, sbi=sbi: e.tensor_copy(out=ua[:, 15:19], in_=u32[:, sbi * 4:(sbi + 1) * 4]),
                                         reads=[Bu32, Bua], writes=[Bua])
                                    cur, Bcur = ua, Bua
                                    for lv in range(gi + 1):
                                        sh = 1 << lv
                                        nxt, Bnxt = sAB[lv % 2], BsAB[lv % 2]
                                        P.op("dve", lambda e, nxt=nxt, cur=cur, sh=sh: e.tensor_tensor(
                                            out=nxt[:, sh:19], in0=cur[:, sh:19], in1=cur[:, 0:19 - sh], op=ALU.add),
                                            reads=[Bcur], writes=[Bnxt])
                                        cur, Bcur = nxt, Bnxt
                                    P.op("dve", lambda e, cur=cur, wsz=wsz, cc=cc, sbi=sbi: e.scalar_tensor_tensor(
                                        out=dT[:, cc, sbi * 4:(sbi + 1) * 4], in0=cur[:, 15:19], scalar=1.0 / wsz, in1=ua[:, 15:19],
                                        op0=ALU.mult, op1=ALU.subtract), reads=[Bcur, Bua, BdT], writes=[BdT])
                            if cc == 1:
                                for ec in range(2):
                                    p2 = psi % 7
                                    psi += 1
                                    for c2 in range(2):
                                        P.op("pe", lambda e, p2=p2, gi=gi, c2=c2, ec=ec, m=m: e.matmul(
                                            psum[p2][:, 0:m], lhsT=wp_t[:, gi, c2, ec * 128:(ec + 1) * 128], rhs=dT[:, c2, 0:m],
                                            start=(c2 == 0), stop=(c2 == 1)), reads=[BdT, Brc], writes=[Bps[p2]])
                                    mo = mo_t[ec]
                                    P.op("dve", lambda e, mo=mo, p2=p2, m=m, gi=gi, ec=ec: e.tensor_scalar(
                                        out=mo[:, 0:m], in0=psum[p2][:, 0:m], scalar1=psc_t[:, 2 * gi + ec:2 * gi + ec + 1], scalar2=None,
                                        op0=ALU.mult), reads=[Bps[p2], Brc], writes=[Bmo[ec]])
                                    P.op("sp", lambda e, mo=mo, gi=gi, ec=ec, tok0=tok0, m=m: e.dma_start(
                                        out=mixT_s[:, 2 * gi + ec, tok0:tok0 + m], in_=mo[:, 0:m]), reads=[Bmo[ec]], dma=True)
            P.emit_phase()

        SCALE = 128.0 ** -0.5
        NEG = -30000.0

        def gelu_cols(Pq, ps_ap, hid0_ap, n, x32, x2, sg_t, g_out, Bx, Bpsx, Bg, rd=()):
            Pq.op("act", lambda e: e.activation(out=x32[:, 0:n], in_=ps_ap, func=AF.Identity, bias=hid0_ap, scale=1.0),
                  reads=[Bpsx] + list(rd), writes=[Bx])
            Pq.op("dve", lambda e: e.tensor_tensor(out=x2[:, 0:n], in0=x32[:, 0:n], in1=x32[:, 0:n], op=ALU.mult),
                  reads=[Bx], writes=[Bx])
            Pq.op("dve", lambda e: e.tensor_scalar(out=x2[:, 0:n], in0=x2[:, 0:n], scalar1=0.044715, scalar2=1.0,
                                                   op0=ALU.mult, op1=ALU.add), reads=[Bx], writes=[Bx])
            Pq.op("dve", lambda e: e.tensor_tensor(out=x2[:, 0:n], in0=x2[:, 0:n], in1=x32[:, 0:n], op=ALU.mult),
                  reads=[Bx], writes=[Bx])
            Pq.op("act", lambda e: e.activation(out=sg_t[:, 0:n], in_=x2[:, 0:n], func=AF.Sigmoid, scale=1.5957691216057308),
                  reads=[Bx], writes=[Bx])
            Pq.op("dve", lambda e: e.tensor_tensor(out=g_out, in0=sg_t[:, 0:n], in1=x32[:, 0:n], op=ALU.mult),
                  reads=[Bx], writes=[Bg])

        with contextlib.ExitStack() as st:
            sb = lambda name, shape, dt: st.enter_context(nc.sbuf_tensor(uname(name), list(shape), dt))
            identb = sb("a_identb", [128, 384], BF16)
            cb_c = sb("a_cbc", [128, 128], BF16)
            cb_a = sb("a_cba", [128, 128], BF16)
            padb = sb("a_padb", [128, 32], F32)
            w1k = sb("a_w1k", [128, 32, 128], BF16)
            w1v = sb("a_w1v", [128, 32, 128], BF16)
            w2 = sb("a_w2", [128, 3, 128], BF16)
            peT = sb("a_peT", [128, 2, 32], BF16)
            hid0 = sb("a_hid0", [128, 2], F32)
            ccmp = sb("a_ccmp", [128, 256], F32)
            scmp = sb("a_scmp", [128, 256], F32)
            cval = sb("a_cval", [128, 8, 256], F32)
            addm = sb("a_addm", [128, 8, 64], F32)
            vblk = sb("a_vblk", [128, 8, 64], F32)
            tm = [sb("a_tm%d" % i, [128, 32, 128], BF16) for i in range(2)]
            XT = {t: sb("a_XT%d" % t, [128, 4096], BF16) for t in (0, 1, 2, 4)}
            V1 = {t: sb("a_V1%d" % t, [128, 32, 130], BF16) for t in (3, 5)}
            kcbT = sb("a_kcbT", [128, 256], BF16)
            vcb = sb("a_vcb", [128, 2, 128], BF16)
            x32 = sb("a_x32", [128, 256], F32)
            x2 = sb("a_x2", [128, 256], F32)
            sgt = sb("a_sgt", [128, 256], F32)
            Gk = sb("a_Gk", [128, 256], BF16)
            Gv = sb("a_Gv", [128, 256], BF16)
            qtm = sb("a_qtm", [128, 768], BF16)
            qT = sb("a_qT", [128, 768], BF16)
            gt = sb("a_gt", [128, 18], F32)
            e32 = sb("a_e32", [128, 256], F32)
            ev = sb("a_ev", [128, 256], F32)
            Pp = sb("a_Pp", [128, 260], F32)
            Pb = sb("a_Pb", [128, 256], BF16)
            PT = sb("a_PT", [128, 2, 128], BF16)
            sm = sb("a_sm", [128, 8], F32)
            imp = sb("a_imp", [128, 64], F32)
            imw = sb("a_imw", [128, 64], F32)
            m8 = sb("a_m8", [128, 16], F32)
            negb = sb("a_negb", [128, 64], BF16)
            negx = sb("a_negx", [128, 4096], BF16)
            PTs = [sb("a_PTs%d" % i, [128, 384], BF16) for i in range(4)]
            acc = sb("a_acc", [128, 768], F32)
            accb = sb("a_accb", [128, 768], BF16)
            mixo = sb("a_mixo", [128, 6, 128], BF16)
            sc6 = sb("a_sc6", [128, 8], F32)

            Bc = Buf()
            Btm = [Buf(), Buf()]
            BXT = {t: Buf() for t in (0, 1, 2, 4)}
            BV1 = {t: Buf() for t in (3, 5)}
            Bkcb, Bvcb, Bx, BGk, BGv = Buf(), Buf(), Buf(), Buf(), Buf()
            Bq, BqT, Bgt, Be, BPp, BPb, BPT, Bsm, Bimp, Bneg, Bnegx = (Buf() for _ in range(11))
            BPTs = [Buf() for _ in range(4)]
            Bacc, Baccb, Bmixo, Bsc6 = Buf(), Buf(), Buf(), Buf()
            Bps = [Buf() for _ in range(7)]
            Bpsb = Buf()

            for j3 in range(3):
                P.op("pool", lambda e, j3=j3: e.dma_start(out=identb[:, j3 * 128:(j3 + 1) * 128], in_=ident[:, :]), writes=[Bc], dma=True)
            P.op("pool", lambda e: e.dma_start(out=cb_c[:], in_=cbc_d[:, :]), writes=[Bc], dma=True)
            P.op("pool", lambda e: e.dma_start(out=cb_a[:], in_=cba_d[:, :]), writes=[Bc], dma=True)
            P.op("sp", lambda e: e.dma_start(out=padb[:], in_=padb_d[:, :]), writes=[Bc], dma=True)
            P.op("pool", lambda e: e.dma_start(out=w1k[:], in_=w1k_d[:, :, :]), writes=[Bc], dma=True)
            P.op("pool", lambda e: e.dma_start(out=w1v[:], in_=w1v_d[:, :, :]), writes=[Bc], dma=True)
            P.op("pool", lambda e: e.dma_start(out=w2[:], in_=w2_d[:, :, :]), writes=[Bc], dma=True)
            P.op("pool", lambda e: e.dma_start(out=peT[:], in_=peT_d[:, :, :]), writes=[Bc], dma=True)
            P.op("sp", lambda e: e.dma_start(out=ccmp[:], in_=ccmp_d[:, :]), writes=[Bc], dma=True)
            P.op("sp", lambda e: e.dma_start(out=scmp[:], in_=scmp_d[:, :]), writes=[Bc], dma=True)
            P.op("sp", lambda e: e.dma_start(out=cval[:], in_=cval_d[:, :, :]), writes=[Bc], dma=True)
            P.op("sp", lambda e: e.dma_start(out=addm[:], in_=addm_d[:, :, :]), writes=[Bc], dma=True)
            P.op("sp", lambda e: e.dma_start(out=vblk[:], in_=vblk_d[:, :, :]), writes=[Bc], dma=True)
            for t in (3, 5):
                P.op("dve", lambda e, t=t: e.memset(V1[t][:, :, 128:130], 1.0), writes=[BV1[t]])
            P.op("dve", lambda e: e.memset(Pp[:], 0.0), writes=[BPp])
            P.op("dve", lambda e: e.memset(vcb[:], 0.0), writes=[Bvcb])
            for vi, w1t in enumerate((w1k, w1v)):
                for pp in range(32):
                    P.op("pe", lambda e, vi=vi, w1t=w1t, pp=pp: e.matmul(
                        psum[0][:, vi:vi + 1], lhsT=w1t[:, pp, :], rhs=peT[:, vi, pp:pp + 1],
                        start=(pp == 0 and vi == 0), stop=(pp == 31), skip_group_check=True), reads=[Bc], writes=[Bps[0]])
            P.op("act", lambda e: e.activation(out=hid0[:, 0:2], in_=psum[0][:, 0:2], func=AF.Copy), reads=[Bps[0]], writes=[Bc])

            tmi = 0
            for g in range(4):
                for t in (0, 1, 2, 4):
                    tb_ = tmi % 2
                    tmi += 1
                    for qd in range(4):
                        P.op("pool", lambda e, t=t, g=g, qd=qd, tb_=tb_: e.dma_start(
                            out=tm[tb_][:, qd * 8:(qd + 1) * 8, :],
                            in_=kvout[t, qd * 1024:(qd + 1) * 1024, g * 128:(g + 1) * 128].rearrange("(i p) d -> p i d", p=128)),
                            writes=[Btm[tb_]], dma=True)
                    for b8 in range(4):
                        for kk in range(8):
                            it = b8 * 8 + kk
                            P.op("pe", lambda e, tb_=tb_, it=it, kk=kk: e.transpose(
                                out=psb[:, kk * 128:(kk + 1) * 128], in_=tm[tb_][:, it, :], identity=identb[:, 0:128]),
                                reads=[Btm[tb_], Bc], writes=[Bpsb])
                        eng = "act" if b8 % 2 == 0 else "dve"
                        if eng == "act":
                            P.op("act", lambda e, t=t, b8=b8: e.activation(out=XT[t][:, b8 * 1024:(b8 + 1) * 1024], in_=psb[:, :], func=AF.Copy),
                                 reads=[Bpsb], writes=[BXT[t]])
                        else:
                            P.op("dve", lambda e, t=t, b8=b8: e.tensor_copy(out=XT[t][:, b8 * 1024:(b8 + 1) * 1024], in_=psb[:, :]),
                                 reads=[Bpsb], writes=[BXT[t]])
                for t in (3, 5):
                    for qd in range(4):
                        P.op("pool", lambda e, t=t, g=g, qd=qd: e.dma_start(
                            out=V1[t][:, qd * 8:(qd + 1) * 8, 0:128],
                            in_=kvout[t, qd * 1024:(qd + 1) * 1024, g * 128:(g + 1) * 128].rearrange("(i p) d -> p i d", p=128)),
                            writes=[BV1[t]], dma=True)
                for vi, (w1t, srcT, Gt, BG) in enumerate(((w1k, XT[0], Gk, BGk), (w1v, XT[1], Gv, BGv))):
                    for ap_ in range(32):
                        a_, p16 = ap_ // 16, ap_ % 16
                        c_0 = 16 * a_ + p16
                        P.op("pe", lambda e, w1t=w1t, srcT=srcT, ap_=ap_, c_0=c_0: e.matmul(
                            psum[1][:, 0:255], lhsT=w1t[:, ap_, :], rhs=srcT[:, c_0:c_0 + 16 * 254 + 1:16],
                            start=(ap_ == 0), stop=(ap_ == 31)), reads=[Bc, BXT[vi]], writes=[Bps[1]])
                    gelu_cols(P, psum[1][:, 0:255], hid0[:, vi:vi + 1], 255, x32, x2, sgt, Gt[:, 0:255], Bx, Bps[1], BG, rd=[Bc])
                P.op("pe", lambda e: e.matmul(psum[2][:, 0:255], lhsT=w2[:, 0, :], rhs=Gk[:, 0:255], start=True, stop=True),
                     reads=[Bc, BGk], writes=[Bps[2]])
                P.op("pe", lambda e: e.matmul(psum[3][:, 0:255], lhsT=w2[:, 1, :], rhs=Gk[:, 0:255], start=True, stop=True),
                     reads=[Bc, BGk], writes=[Bps[3]])
                P.op("dve", lambda e: e.tensor_tensor(out=x32[:, 0:255], in0=psum[2][:, 0:255], in1=ccmp[:, 0:255], op=ALU.mult),
                     reads=[Bps[2], Bc, Bx], writes=[Bx])
                P.op("dve", lambda e: e.tensor_tensor(out=x2[:, 0:255], in0=psum[3][:, 0:255], in1=scmp[:, 0:255], op=ALU.mult),
                     reads=[Bps[3], Bc, Bx], writes=[Bx])
                P.op("dve", lambda e: e.tensor_tensor(out=kcbT[:, 0:255], in0=x32[:, 0:255], in1=x2[:, 0:255], op=ALU.add),
                     reads=[Bx], writes=[Bkcb])
                for nt_, nn in ((0, 128), (1, 127)):
                    P.op("pe", lambda e, nt_=nt_, nn=nn: e.matmul(
                        psum[4][0:nn, nt_ * 128:(nt_ + 1) * 128], lhsT=Gv[:, nt_ * 128:nt_ * 128 + nn], rhs=w2[:, 2, :],
                        start=(nt_ == 0), stop=True, skip_group_check=True), reads=[Bc, BGv], writes=[Bps[4]])
                P.op("act", lambda e: e.activation(out=vcb[:, 0, :], in_=psum[4][:, 0:128], func=AF.Copy), reads=[Bps[4]], writes=[Bvcb])
                P.op("act", lambda e: e.activation(out=vcb[0:127, 1, :], in_=psum[4][0:127, 128:256], func=AF.Copy), reads=[Bps[4]], writes=[Bvcb])

                for i in range(8):
                    S = 4 * i + 3
                    tok0 = i * 128
                    P.op("pool", lambda e, tok0=tok0, g=g: e.dma_start(out=qtm[:], in_=q_s[tok0:tok0 + 128, g * 768:(g + 1) * 768]),
                         writes=[Bq], dma=True)
                    P.op("sp", lambda e, tok0=tok0, g=g: e.dma_start(out=gt[:], in_=gate_s[tok0:tok0 + 128, g * 18:(g + 1) * 18]),
                         writes=[Bgt], dma=True)
                    for h in range(6):
                        P.op("pe", lambda e, h=h: e.transpose(out=psb[:, h * 128:(h + 1) * 128], in_=qtm[:, h * 128:(h + 1) * 128],
                                                              identity=identb[:, 0:128]), reads=[Bq, Bc], writes=[Bpsb])
                    P.op("act", lambda e: e.activation(out=qT[:], in_=psb[:, 0:768], func=AF.Copy), reads=[Bpsb], writes=[BqT])
                    for h in range(6):
                        P.op("pe", lambda e, h=h: e.matmul(psum[0][:, 0:255], lhsT=qT[:, h * 128:(h + 1) * 128], rhs=kcbT[:, 0:255],
                                                           start=True, stop=True), reads=[BqT, Bkcb], writes=[Bps[0]])
                        P.op("dve", lambda e: e.reduce_max(out=sm[:, 0:1], in_=psum[0][:, 0:255], axis=AX.X), reads=[Bps[0]], writes=[Bsm])
                        P.op("dve", lambda e: e.tensor_scalar(out=sm[:, 1:2], in0=sm[:, 0:1], scalar1=-SCALE, scalar2=None, op0=ALU.mult),
                             reads=[Bsm], writes=[Bsm])
                        P.op("act", lambda e: e.activation(out=e32[:, 0:255], in_=psum[0][:, 0:255], func=AF.Exp, bias=sm[:, 1:2], scale=SCALE),
                             reads=[Bps[0], Bsm], writes=[Be])
                        P.op("dve", lambda e, i=i: e.tensor_tensor(out=ev[:, 0:255], in0=e32[:, 0:255], in1=cval[:, i, 0:255], op=ALU.mult),
                             reads=[Be, Bc], writes=[Be])
                        P.op("dve", lambda e: e.reduce_sum(out=sm[:, 2:3], in_=ev[:, 0:255], axis=AX.X), reads=[Be], writes=[Bsm])
                        P.op("dve", lambda e: e.tensor_scalar(out=sm[:, 2:3], in0=sm[:, 2:3], scalar1=1e-30, scalar2=None, op0=ALU.max),
                             reads=[Bsm], writes=[Bsm])
                        P.op("dve", lambda e: e.reciprocal(out=sm[:, 3:4], in_=sm[:, 2:3]), reads=[Bsm], writes=[Bsm])
                        P.op("dve", lambda e: e.tensor_scalar(out=ev[:, 0:255], in0=ev[:, 0:255], scalar1=sm[:, 3:4], scalar2=None, op0=ALU.mult),
                             reads=[Be, Bsm], writes=[Be])
                        if h == 0:
                            P.op("dve", lambda e: e.tensor_copy(out=Pp[:, 1:256], in_=ev[:, 0:255]), reads=[Be], writes=[BPp])
                        else:
                            P.op("dve", lambda e: e.tensor_tensor(out=Pp[:, 1:256], in0=Pp[:, 1:256], in1=ev[:, 0:255], op=ALU.add),
                                 reads=[Be, BPp], writes=[BPp])
                        P.op("act", lambda e: e.activation(out=Pb[:, 0:255], in_=ev[:, 0:255], func=AF.Copy), reads=[Be], writes=[BPb])
                        for nt_, nn in ((0, 128), (1, 127)):
                            P.op("pe", lambda e, nt_=nt_, nn=nn: e.transpose(out=psb[0:nn, nt_ * 128:(nt_ + 1) * 128],
                                                                             in_=Pb[:, nt_ * 128:nt_ * 128 + nn], identity=identb[:, 0:128]),
                                 reads=[BPb, Bc], writes=[Bpsb])
                        P.op("act", lambda e: e.activation(out=PT[:, 0, :], in_=psb[:, 0:128], func=AF.Copy), reads=[Bpsb], writes=[BPT])
                        P.op("act", lambda e: e.activation(out=PT[0:127, 1, :], in_=psb[0:127, 128:256], func=AF.Copy), reads=[Bpsb, BPT], writes=[BPT])
                        for nt_, nn in ((0, 128), (1, 127)):
                            P.op("pe", lambda e, nt_=nt_, nn=nn: e.matmul(psum[1][:, 0:128], lhsT=PT[0:nn, nt_, :], rhs=vcb[0:nn, nt_, :],
                                                                          start=(nt_ == 0), stop=(nt_ == 1)), reads=[BPT, Bvcb], writes=[Bps[1]])
                        P.op("dve", lambda e, h=h: e.tensor_scalar(out=acc[:, h * 128:(h + 1) * 128], in0=psum[1][:, 0:128],
                                                                   scalar1=gt[:, 3 * h:3 * h + 1], scalar2=None, op0=ALU.mult),
                             reads=[Bps[1], Bgt], writes=[Bacc])
                    P.op("dve", lambda e: e.tensor_reduce(out=imp[:, :], in_=Pp[:, 0:256].rearrange("p (s j) -> p s j", j=4), axis=AX.X, op=ALU.add),
                         reads=[BPp], writes=[Bimp])
                    P.op("dve", lambda e: e.tensor_tensor(out=imp[:, :], in0=imp[:, :], in1=Pp[:, 4:260:4], op=ALU.add), reads=[BPp, Bimp], writes=[Bimp])
                    P.op("dve", lambda e, i=i: e.tensor_tensor(out=imp[:, :], in0=imp[:, :], in1=addm[:, i, :], op=ALU.add), reads=[Bimp, Bc], writes=[Bimp])
                    P.op("dve", lambda e: e.max(out=m8[:, 0:8], in_=imp[:, :]), reads=[Bimp], writes=[Bsm])
                    P.op("dve", lambda e: e.match_replace(out=imw[:, :], in_to_replace=m8[:, 0:8], in_values=imp[:, :], imm_value=-3.0e38),
                         reads=[Bimp, Bsm], writes=[Bimp])
                    P.op("dve", lambda e: e.max(out=m8[:, 8:16], in_=imw[:, :]), reads=[Bimp], writes=[Bsm])
                    P.op("dve", lambda e: e.tensor_scalar(out=imw[:, :], in0=imp[:, :], scalar1=m8[:, 15:16], scalar2=None, op0=ALU.is_ge),
                         reads=[Bimp, Bsm], writes=[Bimp])
                    P.op("dve", lambda e, i=i: e.tensor_tensor(out=imw[:, :], in0=imw[:, :], in1=vblk[:, i, :], op=ALU.mult), reads=[Bimp, Bc], writes=[Bimp])
                    P.op("dve", lambda e: e.tensor_scalar(out=negb[:, :], in0=imw[:, :], scalar1=-1.0, scalar2=-NEG, op0=ALU.add, op1=ALU.mult),
                         reads=[Bimp], writes=[Bneg])
                    P.op("dve", lambda e: e.tensor_copy(out=negx[:, :].rearrange("p (s j) -> p s j", j=64),
                                                        in_=negb[:, :].unsqueeze(2).to_broadcast([128, 64, 64])), reads=[Bneg], writes=[Bnegx])
                    pti = 0
                    for br in (1, 2):
                        if br == 1:
                            kts = list(range(0, S + 1))
                            KT, VV, BK, BV = XT[2], V1[3], BXT[2], BV1[3]
                        else:
                            kts = list(range(max(S - 4, 0), S + 1))
                            KT, VV, BK, BV = XT[4], V1[5], BXT[4], BV1[5]
                        items = []
                        for ki, kt in enumerate(kts):
                            biases = []
                            if br == 1:
                                biases.append(negx[:, kt * 128:(kt + 1) * 128])
                            if kt == S:
                                biases.append(cb_c[:, :])
                            if br == 2 and kt == S - 4:
                                biases.append(cb_a[:, :])
                            for hh in range(2):
                                items.append((ki, kt, hh, biases))
                        pti0 = pti
                        pti += len(items)

                        def qk(n, items=items, KT=KT, BK=BK):
                            ki, kt, hh, biases = items[n]
                            ps_s = 2 + (n % 3)
                            P.op("pe", lambda e, ps_s=ps_s, kt=kt, hh=hh, KT=KT, nb=len(biases): e.matmul(
                                psum[ps_s][:, 0:384], lhsT=KT[:, kt * 128:(kt + 1) * 128], rhs=qT[:, hh * 384:(hh + 1) * 384],
                                start=True, stop=(nb == 0)), reads=[BK, BqT], writes=[Bps[ps_s]])
                            for bi, bias_ap in enumerate(biases):
                                P.op("pe", lambda e, ps_s=ps_s, bias_ap=bias_ap, last=(bi == len(biases) - 1): e.matmul(
                                    psum[ps_s][:, 0:384], lhsT=bias_ap, rhs=identb[:, 0:384], start=False, stop=last),
                                    reads=[Bnegx, Bc], writes=[Bps[ps_s]])

                        def expv(n, items=items, VV=VV, BV=BV, pti0=pti0, nk_=len(kts)):
                            ki, kt, hh, biases = items[n]
                            ps_s = 2 + (n % 3)
                            pt_ = (pti0 + n) % 4
                            P.op("act", lambda e, ps_s=ps_s, pt_=pt_, kt=kt: e.activation(
                                out=PTs[pt_][:, :], in_=psum[ps_s][:, 0:384], func=AF.Exp, bias=padb[:, kt:kt + 1], scale=SCALE),
                                reads=[Bps[ps_s], Bc], writes=[BPTs[pt_]])
                            for hl in range(3):
                                P.op("pe", lambda e, hh=hh, hl=hl, pt_=pt_, kt=kt, VV=VV, first=(ki == 0 and hl == 0), last=(ki == nk_ - 1): e.matmul(
                                    psum[5 + hh][:, hl * 130:(hl + 1) * 130], lhsT=PTs[pt_][:, hl * 128:(hl + 1) * 128], rhs=VV[:, kt, :],
                                    start=first, stop=last, skip_group_check=True), reads=[BPTs[pt_], BV], writes=[Bps[5 + hh]])

                        for n in range(min(2, len(items))):
                            qk(n)
                        for n in range(len(items)):
                            if n + 2 < len(items):
                                qk(n + 2)
                            expv(n)
                        for hh in range(2):
                            for hl in range(3):
                                h = hh * 3 + hl
                                P.op("dve", lambda e, hh=hh, hl=hl, h=h: e.reciprocal(out=sc6[:, h:h + 1], in_=psum[5 + hh][:, hl * 130 + 128:hl * 130 + 129]),
                                     reads=[Bps[5 + hh]], writes=[Bsc6])
                                P.op("dve", lambda e, h=h, br=br: e.tensor_tensor(out=sc6[:, h:h + 1], in0=sc6[:, h:h + 1], in1=gt[:, 3 * h + br:3 * h + br + 1], op=ALU.mult),
                                     reads=[Bsc6, Bgt], writes=[Bsc6])
                                P.op("dve", lambda e, hh=hh, hl=hl, h=h: e.scalar_tensor_tensor(
                                    out=acc[:, h * 128:(h + 1) * 128], in0=psum[5 + hh][:, hl * 130:hl * 130 + 128], scalar=sc6[:, h:h + 1],
                                    in1=acc[:, h * 128:(h + 1) * 128], op0=ALU.mult, op1=ALU.add), reads=[Bps[5 + hh], Bsc6, Bacc], writes=[Bacc])
                    P.op("act", lambda e: e.activation(out=accb[:], in_=acc[:], func=AF.Copy), reads=[Bacc], writes=[Baccb])
                    for h in range(6):
                        P.op("pe", lambda e, h=h: e.transpose(out=psb[:, h * 128:(h + 1) * 128], in_=accb[:, h * 128:(h + 1) * 128],
                                                              identity=identb[:, 0:128]), reads=[Baccb, Bc], writes=[Bpsb])
                    P.op("act", lambda e: e.activation(out=mixo[:, :, :].rearrange("p a b -> p (a b)"), in_=psb[:, 0:768], func=AF.Copy),
                         reads=[Bpsb], writes=[Bmixo])
                    P.op("sp", lambda e, g=g, tok0=tok0: e.dma_start(out=mixT_s[:, 8 + 6 * g:14 + 6 * g, tok0:tok0 + 128], in_=mixo[:, :, :]),
                         reads=[Bmixo], dma=True)
            P.emit_phase()

        with contextlib.ExitStack() as st:
            sb = lambda name, shape, dt: st.enter_context(nc.sbuf_tensor(uname(name), list(shape), dt))
            identb = sb("s_identb", [128, 128], BF16)
            identf = sb("s_identf", [128, 128], F32)
            id4x3 = sb("s_id4x3", [4, 12], BF16)
            cb_c = sb("s_cbc", [128, 128], BF16)
            cb_a = sb("s_cba", [128, 128], BF16)
            w1k = sb("s_w1k", [128, 32, 128], BF16)
            w1v = sb("s_w1v", [128, 32, 128], BF16)
            w2 = sb("s_w2", [128, 3, 128], BF16)
            peT = sb("s_peT", [128, 2, 32], BF16)
            hid0 = sb("s_hid0", [128, 2], F32)
            ccs = sb("s_ccs", [128, 512], F32)
            scs = sb("s_scs", [128, 512], F32)
            addms = sb("s_addms", [4, 129], F32)
            oh = sb("s_oh", [128, 5], F32)
            ptb_i = sb("s_ptb_i", [128, 64], I32)
            ptb_f = sb("s_ptb_f", [128, 64], F32)
            ptmp = sb("s_ptmp", [128, 64], F32)
            psel = sb("s_psel", [128, 16], F32)
            idx = sb("s_idx", [128, 16], I32)
            G32 = [sb("s_G32_%d" % i, [128, 2048], F32) for i in range(6)]
            XT = sb("s_XT", [128, 2, 4, 2048], BF16)
            V1 = sb("s_V1", [128, 2, 65, 130], BF16)
            XTn = sb("s_XTn", [128, 2, 4], BF16)
            kwT = sb("s_kwT", [128, 2, 516], BF16)
            V1w = sb("s_V1w", [128, 2, 5, 130], BF16)
            wtm = sb("s_wtm", [128, 4, 256], BF16)
            ntm = sb("s_ntm", [4, 256], BF16)
            kcbT = sb("s_kcbT", [128, 2, 512], BF16)
            vcb = sb("s_vcb", [128, 2, 4, 128], BF16)
            x32 = sb("s_x32", [128, 512], F32)
            x2 = sb("s_x2", [128, 512], F32)
            sgt = sb("s_sgt", [128, 512], F32)
            Gk = sb("s_Gk", [128, 512], BF16)
            Gv = sb("s_Gv", [128, 512], BF16)
            qtm = sb("s_qtm", [4, 768], BF16)
            qT = sb("s_qT", [128, 24], BF16)
            gt = sb("s_gt", [4, 18], F32)
            e32 = sb("s_e32", [4, 512], F32)
            ev = sb("s_ev", [4, 512], F32)
            Pp = sb("s_Pp", [4, 520], F32)
            Pb = sb("s_Pb", [4, 512], BF16)
            PT = sb("s_PT", [128, 4, 4], BF16)
            sm = sb("s_sm", [4, 8], F32)
            imp = sb("s_imp", [4, 129], F32)
            imw = sb("s_imw", [4, 129], F32)
            m8 = sb("s_m8", [4, 16], F32)
            negb = sb("s_negb", [4, 129], BF16)
            negx = sb("s_negx", [4, 16, 128], BF16)
            PTs = [sb("s_PTs%d" % i, [128, 12], BF16) for i in range(4)]
            acc = sb("s_acc", [4, 768], F32)
            accb = sb("s_accb", [4, 768], BF16)
            mixo = sb("s_mixo", [128, 6, 4], BF16)
            sc6 = sb("s_sc6", [4, 8], F32)

            Bc, Bidx = Buf(), Buf()
            BG32 = [Buf() for _ in range(6)]
            BXT, BV1, BXTn, BkwT, BV1w, Bwtm, Bntm = (Buf() for _ in range(7))
            Bkcb, Bvcb, Bx, BGk, BGv = Buf(), Buf(), Buf(), Buf(), Buf()
            Bq, BqT, Bgt, Be, BPp, BPb, BPT, Bsm, Bimp, Bneg, Bnegx = (Buf() for _ in range(11))
            BPTs = [Buf() for _ in range(4)]
            Bacc, Baccb, Bmixo, Bsc6 = Buf(), Buf(), Buf(), Buf()
            Bps = [Buf() for _ in range(7)]
            Bpsb = Buf()

            P.op("pool", lambda e: e.dma_start(out=identb[:], in_=ident[:, :]), writes=[Bc], dma=True)
            P.op("sp", lambda e: e.dma_start(out=identf[:], in_=ident[:, :]), writes=[Bc], dma=True)
            for j3 in range(3):
                P.op("pool", lambda e, j3=j3: e.dma_start(out=id4x3[:, j3 * 4:(j3 + 1) * 4], in_=ident[0:4, 0:4]), writes=[Bc], dma=True)
            P.op("pool", lambda e: e.dma_start(out=cb_c[:], in_=cbc_d[:, :]), writes=[Bc], dma=True)
            P.op("pool", lambda e: e.dma_start(out=cb_a[:], in_=cba_d[:, :]), writes=[Bc], dma=True)
            P.op("pool", lambda e: e.dma_start(out=w1k[:], in_=w1k_d[:, :, :]), writes=[Bc], dma=True)
            P.op("pool", lambda e: e.dma_start(out=w1v[:], in_=w1v_d[:, :, :]), writes=[Bc], dma=True)
            P.op("pool", lambda e: e.dma_start(out=w2[:], in_=w2_d[:, :, :]), writes=[Bc], dma=True)
            P.op("pool", lambda e: e.dma_start(out=peT[:], in_=peT_d[:, :, :]), writes=[Bc], dma=True)
            P.op("sp", lambda e: e.dma_start(out=ccs[:], in_=ccs_d[:, :]), writes=[Bc], dma=True)
            P.op("sp", lambda e: e.dma_start(out=scs[:], in_=scs_d[:, :]), writes=[Bc], dma=True)
            P.op("sp", lambda e: e.dma_start(out=addms[:], in_=addms_d[:, :]), writes=[Bc], dma=True)
            P.op("sp", lambda e: e.dma_start(out=oh[:], in_=oh_d[:, :]), writes=[Bc], dma=True)
            P.op("dve", lambda e: e.memset(V1[:, :, :, 128:130], 1.0), writes=[BV1])
            P.op("dve", lambda e: e.memset(V1w[:, :, :, 128:130], 1.0), writes=[BV1w])
            P.op("dve", lambda e: e.memset(Pp[:], 0.0), writes=[BPp])
            P.op("dve", lambda e: e.memset(vcb[:], 0.0), writes=[Bvcb])
            for vi, w1t in enumerate((w1k, w1v)):
                for pp in range(32):
                    P.op("pe", lambda e, vi=vi, w1t=w1t, pp=pp: e.matmul(
                        psum[0][:, vi:vi + 1], lhsT=w1t[:, pp, :], rhs=peT[:, vi, pp:pp + 1],
                        start=(pp == 0 and vi == 0), stop=(pp == 31), skip_group_check=True), reads=[Bc], writes=[Bps[0]])
            P.op("act", lambda e: e.activation(out=hid0[:, 0:2], in_=psum[0][:, 0:2], func=AF.Copy), reads=[Bps[0]], writes=[Bc])

            caches = (ckc_d, cvc_d, cks_d, cvs_d)
            bc_reg = {}
            gi_ = 0
            tpi = 0
            for sbi in range(4):
                tokS = 1024 + 4 * sbi
                rowS = 4096 + 4 * sbi
                P.op("sp", lambda e, sbi=sbi: e.dma_start(out=ptb_i[:], in_=ptab_d[sbi, :].partition_broadcast(128)), writes=[Bidx], dma=True)
                P.op("dve", lambda e: e.tensor_copy(out=ptb_f[:], in_=ptb_i[:]), reads=[Bidx], writes=[Bidx])
                P.op("dve", lambda e: e.tensor_tensor(out=ptmp[:, :].rearrange("p (c a) -> p c a", a=4),
                                                      in0=ptb_f[:, :].rearrange("p (c a) -> p c a", a=4),
                                                      in1=oh[:, 0:4].unsqueeze(1).to_broadcast([128, 16, 4]), op=ALU.mult),
                     reads=[Bidx, Bc], writes=[Bidx])
                P.op("dve", lambda e: e.tensor_reduce(out=psel[:, :], in_=ptmp[:, :].rearrange("p (c a) -> p c a", a=4), axis=AX.X, op=ALU.add),
                     reads=[Bidx], writes=[Bidx])
                P.op("dve", lambda e: e.tensor_scalar(out=psel[:, :], in0=psel[:, :], scalar1=32.0, scalar2=oh[:, 4:5], op0=ALU.mult, op1=ALU.add),
                     reads=[Bidx, Bc], writes=[Bidx])
                P.op("dve", lambda e: e.tensor_copy(out=idx[:, :], in_=psel[:, :]), reads=[Bidx], writes=[Bidx])
                for gp in range(2):
                    for ci, cache_d in enumerate(caches):
                        for c in range(16):
                            gb = gi_ % 6
                            gi_ += 1
                            def gather_fn(e, gb=gb, c=c, cache_d=cache_d):
                                if "r" not in bc_reg:
                                    bc_reg["r"] = e.to_reg(2560 * 32 - 1)
                                return e.indirect_dma_start(
                                    out=G32[gb][:, :], out_offset=None, in_=cache_d[:, :],
                                    in_offset=bass.IndirectOffsetOnAxis(ap=idx[:, c:c + 1], axis=0),
                                    bounds_check=bc_reg["r"], oob_is_err=False)
                            P.op("pool", gather_fn, reads=[Bidx], writes=[BG32[gb]], dma=True)
                            if ci < 3:
                                for gl in range(2):
                                    g = 2 * gp + gl
                                    pb_ = 2 + (tpi % 2)
                                    tpi += 1
                                    for r4 in range(4):
                                        P.op("pe", lambda e, pb_=pb_, r4=r4, g=g, gb=gb: e.transpose(
                                            out=psum[pb_][:, r4 * 128:(r4 + 1) * 128], in_=G32[gb][:, r4 * 512 + g * 128:r4 * 512 + (g + 1) * 128],
                                            identity=identf[:, :]), reads=[BG32[gb], Bc], writes=[Bps[pb_]])
                                    eng = "act" if tpi % 2 == 0 else "dve"
                                    src = psum[pb_][:, :].rearrange("p (r q) -> p r q", q=128)
                                    if eng == "act":
                                        P.op("act", lambda e, gl=gl, c=c, src=src: e.activation(out=XT[:, gl, :, 128 * c:128 * (c + 1)], in_=src, func=AF.Copy),
                                             reads=[Bps[pb_]], writes=[BXT])
                                    else:
                                        P.op("dve", lambda e, gl=gl, c=c, src=src: e.tensor_copy(out=XT[:, gl, :, 128 * c:128 * (c + 1)], in_=src),
                                             reads=[Bps[pb_]], writes=[BXT])
                            else:
                                for gl in range(2):
                                    g = 2 * gp + gl
                                    src = G32[gb][:, :].rearrange("p (r q) -> p r q", q=512)[:, :, g * 128:(g + 1) * 128]
                                    eng = "act" if gl == 0 else "dve"
                                    if eng == "act":
                                        P.op("act", lambda e, gl=gl, c=c, src=src: e.activation(out=V1[:, gl, 4 * c:4 * c + 4, 0:128], in_=src, func=AF.Copy),
                                             reads=[BG32[gb]], writes=[BV1])
                                    else:
                                        P.op("dve", lambda e, gl=gl, c=c, src=src: e.tensor_copy(out=V1[:, gl, 4 * c:4 * c + 4, 0:128], in_=src),
                                             reads=[BG32[gb]], writes=[BV1])
                        if ci < 2:
                            w1t = w1k if ci == 0 else w1v
                            Gt, BG = (Gk, BGk) if ci == 0 else (Gv, BGv)
                            for gl in range(2):
                                mi = 0
                                for a_ in range(2):
                                    for j4 in range(4):
                                        for r4 in range(4):
                                            c_0 = 4 * a_ + j4
                                            P.op("pe", lambda e, w1t=w1t, a_=a_, j4=j4, r4=r4, gl=gl, c_0=c_0, mi=mi: e.matmul(
                                                psum[1][:, 0:511], lhsT=w1t[:, a_ * 16 + 4 * j4 + r4, :],
                                                rhs=XT[:, gl, r4, c_0:c_0 + 4 * 510 + 1:4], start=(mi == 0), stop=(mi == 31)),
                                                reads=[Bc, BXT], writes=[Bps[1]])
                                            mi += 1
                                gelu_cols(P, psum[1][:, 0:511], hid0[:, ci:ci + 1], 511, x32, x2, sgt, Gt[:, 0:511], Bx, Bps[1], BG, rd=[Bc])
                                if ci == 0:
                                    P.op("pe", lambda e: e.matmul(psum[4][:, 0:511], lhsT=w2[:, 0, :], rhs=Gk[:, 0:511], start=True, stop=True),
                                         reads=[Bc, BGk], writes=[Bps[4]])
                                    P.op("pe", lambda e: e.matmul(psum[5][:, 0:511], lhsT=w2[:, 1, :], rhs=Gk[:, 0:511], start=True, stop=True),
                                         reads=[Bc, BGk], writes=[Bps[5]])
                                    P.op("dve", lambda e: e.tensor_tensor(out=x32[:, 0:511], in0=psum[4][:, 0:511], in1=ccs[:, 0:511], op=ALU.mult),
                                         reads=[Bps[4], Bc, Bx], writes=[Bx])
                                    P.op("dve", lambda e: e.tensor_tensor(out=x2[:, 0:511], in0=psum[5][:, 0:511], in1=scs[:, 0:511], op=ALU.mult),
                                         reads=[Bps[5], Bc, Bx], writes=[Bx])
                                    P.op("dve", lambda e, gl=gl: e.tensor_tensor(out=kcbT[:, gl, 0:511], in0=x32[:, 0:511], in1=x2[:, 0:511], op=ALU.add),
                                         reads=[Bx], writes=[Bkcb])
                                else:
                                    for nt_, nn in ((0, 128), (1, 128), (2, 128), (3, 127)):
                                        P.op("pe", lambda e, nt_=nt_, nn=nn: e.matmul(
                                            psum[4][0:nn, nt_ * 128:(nt_ + 1) * 128], lhsT=Gv[:, nt_ * 128:nt_ * 128 + nn], rhs=w2[:, 2, :],
                                            start=(nt_ == 0), stop=True, skip_group_check=True), reads=[Bc, BGv], writes=[Bps[4]])
                                    for nt_, nn in ((0, 128), (1, 128), (2, 128), (3, 127)):
                                        P.op("act", lambda e, nt_=nt_, nn=nn, gl=gl: e.activation(
                                            out=vcb[0:nn, gl, nt_, :], in_=psum[4][0:nn, nt_ * 128:(nt_ + 1) * 128], func=AF.Copy),
                                            reads=[Bps[4], Bvcb], writes=[Bvcb])
                    P.op("pool", lambda e, rowS=rowS, gp=gp: e.dma_start(out=ntm[:, :], in_=kvout[2, rowS:rowS + 4, gp * 256:(gp + 1) * 256]),
                         writes=[Bntm], dma=True)
                    for gl in range(2):
                        P.op("pe", lambda e, gl=gl: e.transpose(out=psb[:, gl * 4:gl * 4 + 4], in_=ntm[0:4, gl * 128:(gl + 1) * 128], identity=identb[0:4, 0:4]),
                             reads=[Bntm, Bc], writes=[Bpsb])
                    P.op("act", lambda e: e.activation(out=XTn[:, :, :].rearrange("p a b -> p (a b)"), in_=psb[:, 0:8], func=AF.Copy),
                         reads=[Bpsb], writes=[BXTn])
                    for gl in range(2):
                        g = 2 * gp + gl
                        P.op("pool", lambda e, rowS=rowS, g=g, gl=gl: e.dma_start(out=V1[0:4, gl, 64, 0:128], in_=kvout[3, rowS:rowS + 4, g * 128:(g + 1) * 128]),
                             writes=[BV1], dma=True)
                        P.op("pool", lambda e, rowS=rowS, g=g, gl=gl: e.dma_start(out=V1w[0:4, gl, 4, 0:128], in_=kvout[5, rowS:rowS + 4, g * 128:(g + 1) * 128]),
                             writes=[BV1w], dma=True)
                        P.op("pool", lambda e, sbi=sbi, g=g, gl=gl: e.dma_start(
                            out=V1w[:, gl, 0:4, 0:128], in_=st_vw[sbi, :, g * 128:(g + 1) * 128].rearrange("(i p) d -> p i d", p=128)),
                            writes=[BV1w], dma=True)
                    P.op("pool", lambda e, sbi=sbi, gp=gp: e.dma_start(
                        out=wtm[:, :, :], in_=st_kw[sbi, :, gp * 256:(gp + 1) * 256].rearrange("(i p) d -> p i d", p=128)),
                        writes=[Bwtm], dma=True)
                    for gl in range(2):
                        for w_ in range(4):
                            P.op("pe", lambda e, gl=gl, w_=w_: e.transpose(out=psb[:, (gl * 4 + w_) * 128:(gl * 4 + w_ + 1) * 128],
                                                                           in_=wtm[:, w_, gl * 128:(gl + 1) * 128], identity=identb[:, :]),
                                 reads=[Bwtm, Bc], writes=[Bpsb])
                    P.op("act", lambda e: e.activation(out=kwT[:, :, 0:512], in_=psb[:, :].rearrange("p (a b) -> p a b", a=2), func=AF.Copy),
                         reads=[Bpsb], writes=[BkwT])
                    P.op("pool", lambda e, rowS=rowS, gp=gp: e.dma_start(out=ntm[:, :], in_=kvout[4, rowS:rowS + 4, gp * 256:(gp + 1) * 256]),
                         writes=[Bntm], dma=True)
                    for gl in range(2):
                        P.op("pe", lambda e, gl=gl: e.transpose(out=psb[:, gl * 4:gl * 4 + 4], in_=ntm[0:4, gl * 128:(gl + 1) * 128], identity=identb[0:4, 0:4]),
                             reads=[Bntm, Bc], writes=[Bpsb])
                    P.op("act", lambda e: e.activation(out=kwT[:, :, 512:516], in_=psb[:, 0:8].rearrange("p (a b) -> p a b", a=2), func=AF.Copy),
                         reads=[Bpsb, BkwT], writes=[BkwT])

                    for gl in range(2):
                        g = 2 * gp + gl
                        P.op("pool", lambda e, tokS=tokS, g=g: e.dma_start(out=qtm[:], in_=q_s[tokS:tokS + 4, g * 768:(g + 1) * 768]),
                             writes=[Bq], dma=True)
                        P.op("sp", lambda e, tokS=tokS, g=g: e.dma_start(out=gt[:], in_=gate_s[tokS:tokS + 4, g * 18:(g + 1) * 18]),
                             writes=[Bgt], dma=True)
                        for h in range(6):
                            P.op("pe", lambda e, h=h: e.transpose(out=psb[:, h * 4:(h + 1) * 4], in_=qtm[0:4, h * 128:(h + 1) * 128],
                                                                  identity=identb[0:4, 0:4]), reads=[Bq, Bc], writes=[Bpsb])
                        P.op("act", lambda e: e.activation(out=qT[:], in_=psb[:, 0:24], func=AF.Copy), reads=[Bpsb], writes=[BqT])
                        for h in range(6):
                            P.op("pe", lambda e, h=h, gl=gl: e.matmul(psum[0][0:4, 0:511], lhsT=qT[:, h * 4:(h + 1) * 4], rhs=kcbT[:, gl, 0:511],
                                                                      start=True, stop=True), reads=[BqT, Bkcb], writes=[Bps[0]])
                            P.op("dve", lambda e: e.reduce_max(out=sm[:, 0:1], in_=psum[0][0:4, 0:511], axis=AX.X), reads=[Bps[0]], writes=[Bsm])
                            P.op("dve", lambda e: e.tensor_scalar(out=sm[:, 1:2], in0=sm[:, 0:1], scalar1=-SCALE, scalar2=None, op0=ALU.mult),
                                 reads=[Bsm], writes=[Bsm])
                            P.op("act", lambda e: e.activation(out=ev[:, 0:511], in_=psum[0][0:4, 0:511], func=AF.Exp, bias=sm[:, 1:2], scale=SCALE),
                                 reads=[Bps[0], Bsm], writes=[Be])
                            P.op("dve", lambda e: e.reduce_sum(out=sm[:, 2:3], in_=ev[:, 0:511], axis=AX.X), reads=[Be], writes=[Bsm])
                            P.op("dve", lambda e: e.tensor_scalar(out=sm[:, 2:3], in0=sm[:, 2:3], scalar1=1e-30, scalar2=None, op0=ALU.max),
                                 reads=[Bsm], writes=[Bsm])
                            P.op("dve", lambda e: e.reciprocal(out=sm[:, 3:4], in_=sm[:, 2:3]), reads=[Bsm], writes=[Bsm])
                            P.op("dve", lambda e: e.tensor_scalar(out=ev[:, 0:511], in0=ev[:, 0:511], scalar1=sm[:, 3:4], scalar2=None, op0=ALU.mult),
                                 reads=[Be, Bsm], writes=[Be])
                            if h == 0:
                                P.op("dve", lambda e: e.tensor_copy(out=Pp[:, 1:512], in_=ev[:, 0:511]), reads=[Be], writes=[BPp])
                            else:
                                P.op("dve", lambda e: e.tensor_tensor(out=Pp[:, 1:512], in0=Pp[:, 1:512], in1=ev[:, 0:511], op=ALU.add),
                                     reads=[Be, BPp], writes=[BPp])
                            P.op("act", lambda e: e.activation(out=Pb[:, 0:511], in_=ev[:, 0:511], func=AF.Copy), reads=[Be], writes=[BPb])
                            for nt_, nn in ((0, 128), (1, 128), (2, 128), (3, 127)):
                                P.op("pe", lambda e, nt_=nt_, nn=nn: e.transpose(out=psb[0:nn, nt_ * 4:nt_ * 4 + 4],
                                                                                 in_=Pb[0:4, nt_ * 128:nt_ * 128 + nn], identity=identb[0:4, 0:4]),
                                     reads=[BPb, Bc], writes=[Bpsb])
                            for nt_, nn in ((0, 128), (1, 128), (2, 128), (3, 127)):
                                P.op("act", lambda e, nt_=nt_, nn=nn: e.activation(out=PT[0:nn, nt_, :], in_=psb[0:nn, nt_ * 4:nt_ * 4 + 4], func=AF.Copy),
                                     reads=[Bpsb, BPT], writes=[BPT])
                            for nt_, nn in ((0, 128), (1, 128), (2, 128), (3, 127)):
                                P.op("pe", lambda e, nt_=nt_, nn=nn, gl=gl: e.matmul(psum[1][0:4, 0:128], lhsT=PT[0:nn, nt_, :], rhs=vcb[0:nn, gl, nt_, :],
                                                                                     start=(nt_ == 0), stop=(nt_ == 3)), reads=[BPT, Bvcb], writes=[Bps[1]])
                            P.op("dve", lambda e, h=h: e.tensor_scalar(out=acc[:, h * 128:(h + 1) * 128], in0=psum[1][0:4, 0:128],
                                                                       scalar1=gt[:, 3 * h:3 * h + 1], scalar2=None, op0=ALU.mult),
                                 reads=[Bps[1], Bgt], writes=[Bacc])
                        P.op("dve", lambda e: e.tensor_reduce(out=imp[:, :], in_=Pp[:, 0:516].rearrange("p (s j) -> p s j", j=4), axis=AX.X, op=ALU.add),
                             reads=[BPp], writes=[Bimp])
                        P.op("dve", lambda e: e.tensor_tensor(out=imp[:, :], in0=imp[:, :], in1=Pp[:, 4:517:4], op=ALU.add), reads=[BPp, Bimp], writes=[Bimp])
                        P.op("dve", lambda e: e.tensor_tensor(out=imp[:, :], in0=imp[:, :], in1=addms[:, :], op=ALU.add), reads=[Bimp, Bc], writes=[Bimp])
                        P.op("dve", lambda e: e.max(out=m8[:, 0:8], in_=imp[:, :]), reads=[Bimp], writes=[Bsm])
                        P.op("dve", lambda e: e.match_replace(out=imw[:, :], in_to_replace=m8[:, 0:8], in_values=imp[:, :], imm_value=-3.0e38),
                             reads=[Bimp, Bsm], writes=[Bimp])
                        P.op("dve", lambda e: e.max(out=m8[:, 8:16], in_=imw[:, :]), reads=[Bimp], writes=[Bsm])
                        P.op("dve", lambda e: e.tensor_scalar(out=imw[:, :], in0=imp[:, :], scalar1=m8[:, 15:16], scalar2=None, op0=ALU.is_ge),
                             reads=[Bimp, Bsm], writes=[Bimp])
                        P.op("dve", lambda e: e.tensor_scalar(out=negb[:, :], in0=imw[:, :], scalar1=-1.0, scalar2=-NEG, op0=ALU.add, op1=ALU.mult),
                             reads=[Bimp], writes=[Bneg])
                        P.op("dve", lambda e: e.tensor_copy(out=negx[:, :, :].rearrange("p c (b k) -> p c b k", k=16),
                                                            in_=negb[:, 0:128].rearrange("p (c b) -> p c b", b=8).unsqueeze(3).to_broadcast([4, 16, 8, 16])),
                             reads=[Bneg], writes=[Bnegx])
                        pti = 0
                        for br in (1, 2):
                            tiles = []
                            if br == 1:
                                for c in range(16):
                                    for r4 in range(4):
                                        tiles.append((XT[:, gl, r4, 128 * c:128 * (c + 1)], V1[:, gl, 4 * c + r4, :], 128, [negx[0:4, c, :]]))
                                tiles.append((XTn[:, gl, :], V1[0:4, gl, 64, :], 4, [cb_c[0:4, 0:4]]))
                                BK, BV = [BXT, BXTn], BV1
                            else:
                                for w_ in range(4):
                                    tiles.append((kwT[:, gl, 128 * w_:128 * (w_ + 1)], V1w[:, gl, w_, :], 128, [cb_a[0:4, 0:128]] if w_ == 0 else []))
                                tiles.append((kwT[:, gl, 512:516], V1w[0:4, gl, 4, :], 4, [cb_c[0:4, 0:4]]))
                                BK, BV = [BkwT], BV1w
                            items = []
                            for ki, (kap, vap, nk, biases) in enumerate(tiles):
                                for hh in range(2):
                                    items.append((ki, kap, vap, nk, biases, hh))
                            pti0 = pti
                            pti += len(items)

                            def qk(n, items=items, BK=BK):
                                ki, kap, vap, nk, biases, hh = items[n]
                                ps_s = 2 + (n % 3)
                                P.op("pe", lambda e, ps_s=ps_s, kap=kap, nk=nk, hh=hh, nb=len(biases): e.matmul(
                                    psum[ps_s][0:nk, 0:12], lhsT=kap, rhs=qT[:, hh * 12:(hh + 1) * 12],
                                    start=True, stop=(nb == 0)), reads=BK + [BqT], writes=[Bps[ps_s]])
                                for bi, bias_ap in enumerate(biases):
                                    P.op("pe", lambda e, ps_s=ps_s, nk=nk, bias_ap=bias_ap, last=(bi == len(biases) - 1): e.matmul(
                                        psum[ps_s][0:nk, 0:12], lhsT=bias_ap, rhs=id4x3[:, :], start=False, stop=last),
                                        reads=[Bnegx, Bc], writes=[Bps[ps_s]])

                            def expv(n, items=items, BV=BV, pti0=pti0, nt_=len(tiles)):
                                ki, kap, vap, nk, biases, hh = items[n]
                                ps_s = 2 + (n % 3)
                                pt_ = (pti0 + n) % 4
                                P.op("act", lambda e, ps_s=ps_s, pt_=pt_, nk=nk: e.activation(
                                    out=PTs[pt_][0:nk, :], in_=psum[ps_s][0:nk, 0:12], func=AF.Exp, scale=SCALE),
                                    reads=[Bps[ps_s]], writes=[BPTs[pt_]])
                                for hl in range(3):
                                    P.op("pe", lambda e, hh=hh, hl=hl, pt_=pt_, nk=nk, vap=vap, first=(ki == 0 and hl == 0), last=(ki == nt_ - 1): e.matmul(
                                        psum[5 + hh][0:4, hl * 130:(hl + 1) * 130], lhsT=PTs[pt_][0:nk, hl * 4:(hl + 1) * 4], rhs=vap,
                                        start=first, stop=last, skip_group_check=True), reads=[BPTs[pt_], BV], writes=[Bps[5 + hh]])

                            for n in range(min(2, len(items))):
                                qk(n)
                            for n in range(len(items)):
                                if n + 2 < len(items):
                                    qk(n + 2)
                                expv(n)
                            for hh in range(2):
                                for hl in range(3):
                                    h = hh * 3 + hl
                                    P.op("dve", lambda e, hh=hh, hl=hl, h=h: e.reciprocal(out=sc6[:, h:h + 1], in_=psum[5 + hh][0:4, hl * 130 + 128:hl * 130 + 129]),
                                         reads=[Bps[5 + hh]], writes=[Bsc6])
                                    P.op("dve", lambda e, h=h, br=br: e.tensor_tensor(out=sc6[:, h:h + 1], in0=sc6[:, h:h + 1], in1=gt[:, 3 * h + br:3 * h + br + 1], op=ALU.mult),
                                         reads=[Bsc6, Bgt], writes=[Bsc6])
                                    P.op("dve", lambda e, hh=hh, hl=hl, h=h: e.scalar_tensor_tensor(
                                        out=acc[:, h * 128:(h + 1) * 128], in0=psum[5 + hh][0:4, hl * 130:hl * 130 + 128], scalar=sc6[:, h:h + 1],
                                        in1=acc[:, h * 128:(h + 1) * 128], op0=ALU.mult, op1=ALU.add), reads=[Bps[5 + hh], Bsc6, Bacc], writes=[Bacc])
                        P.op("act", lambda e: e.activation(out=accb[:], in_=acc[:], func=AF.Copy), reads=[Bacc], writes=[Baccb])
                        for h in range(6):
                            P.op("pe", lambda e, h=h: e.transpose(out=psb[:, h * 4:(h + 1) * 4], in_=accb[0:4, h * 128:(h + 1) * 128],
                                                                  identity=identb[0:4, 0:4]), reads=[Baccb, Bc], writes=[Bpsb])
                        P.op("act", lambda e: e.activation(out=mixo[:, :, :].rearrange("p a b -> p (a b)"), in_=psb[:, 0:24], func=AF.Copy),
                             reads=[Bpsb], writes=[Bmixo])
                        P.op("sp", lambda e, g=g, tokS=tokS: e.dma_start(out=mixT_s[:, 8 + 6 * g:14 + 6 * g, tokS:tokS + 4], in_=mixo[:, :, :]),
                             reads=[Bmixo], dma=True)
            P.emit_phase()

        with contextlib.ExitStack() as st:
            sb = lambda name, shape, dt: st.enter_context(nc.sbuf_tensor(uname(name), list(shape), dt))
            mixT = sb("mixT", [128, 32, TOK], BF16)
            wt = [sb("wo%d" % i, [128, 32, 512], BF16) for i in range(2)]
            xr = [sb("xr%d" % i, [128, 512], F32) for i in range(3)]
            Bmix = Buf()
            Bwt = [Buf(), Buf()]
            Bxr = [Buf() for _ in range(3)]
            Bps = [Buf() for _ in range(7)]
            for kq in range(4):
                P.op("sp", lambda e, kq=kq: e.dma_start(out=mixT[:, kq * 8:(kq + 1) * 8, :], in_=mixT_s[:, kq * 8:(kq + 1) * 8, :]),
                     writes=[Bmix], dma=True)
            psi = 0
            xi = 0
            for db in range(8):
                w = db % 2
                for kh in range(2):
                    P.op("pool", lambda e, db=db, kh=kh, w=w: e.dma_start(
                        out=wt[w][:, kh * 16:(kh + 1) * 16, :].rearrange("p k f -> p (k f)"),
                        in_=w_o[db, :, kh * 16:(kh + 1) * 16, :].rearrange("p k f -> p (k f)")),
                        writes=[Bwt[w]], dma=True)
                for (r0, m) in TT9:
                    p = psi % 7
                    psi += 1
                    x = xi % 3
                    xi += 1
                    P.op("sp", lambda e, x=x, r0=r0, m=m, db=db: e.dma_start(
                        out=xr[x][0:m, :], in_=x_own[r0:r0 + m, db * 512:(db + 1) * 512]), writes=[Bxr[x]], dma=True)
                    for k in range(32):
                        P.op("pe", lambda e, p=p, k=k, r0=r0, m=m, w=w: e.matmul(
                            psum[p][0:m, :], lhsT=mixT[:, k, r0:r0 + m], rhs=wt[w][:, k, :],
                            start=(k == 0), stop=(k == 31)), reads=[Bmix, Bwt[w]], writes=[Bps[p]])
                    P.op("dve", lambda e, x=x, p=p, m=m: e.scalar_tensor_tensor(
                        out=xr[x][0:m, :], in0=xr[x][0:m, :], scalar=ALPHA, in1=psum[p][0:m, :],
                        op0=ALU.mult, op1=ALU.add), reads=[Bps[p], Bxr[x]], writes=[Bxr[x]])
                    P.op("sp", lambda e, x=x, r0=r0, m=m, db=db: e.dma_start(
                        out=r_s[r0:r0 + m, db * 512:(db + 1) * 512], in_=xr[x][0:m, :]), reads=[Bxr[x]], dma=True)
            P.emit_phase()

        def ln_phase(src, g_idx, dst_f32, dst_T):
            with contextlib.ExitStack() as st:
                sb = lambda name, shape, dt: st.enter_context(nc.sbuf_tensor(uname(name), list(shape), dt))
                gt = sb("ln_g", [128, D], F32)
                bt = sb("ln_b", [128, D], F32)
                idb = sb("ln_idb", [128, 128], BF16)
                idf = sb("ln_idf", [128, 128], F32)
                rt = [sb("ln_r%d" % i, [128, D], F32) for i in range(2)]
                hb = sb("ln_hb", [128, D], BF16)
                hT = sb("ln_hT", [128, 32, 128], BF16)
                stats = sb("ln_stats", [128, 8, 6], F32)
                mv = sb("ln_mv", [128, 2], F32)
                rstd = sb("ln_rstd", [128, 1], F32)
                Bg, Bid = Buf(), Buf()
                Brt = [Buf(), Buf()]
                Bhb, BhT, Bst, Bmv, Brs = Buf(), Buf(), Buf(), Buf(), Buf()
                Bpsb = Buf()
                P.op("sp", lambda e: e.dma_start(out=gt[:], in_=lnp[g_idx]), writes=[Bg], dma=True)
                P.op("sp", lambda e: e.dma_start(out=bt[:], in_=lnp[g_idx + 1]), writes=[Bg], dma=True)
                P.op("sp", lambda e: e.dma_start(out=idf[:], in_=ident[:, :]), writes=[Bid], dma=True)
                P.op("dve", lambda e: e.tensor_copy(out=idb[:], in_=idf[:]), reads=[Bid], writes=[Bid])
                for ti, (r0, m) in enumerate(TT9):
                    r = ti % 2
                    P.op("sp", lambda e, r=r, r0=r0, m=m: e.dma_start(out=rt[r][0:m, :], in_=src[r0:r0 + m, :]),
                         writes=[Brt[r]], dma=True)
                    for c in range(8):
                        P.op("dve", lambda e, r=r, m=m, c=c: e.bn_stats(out=stats[0:m, c, :], in_=rt[r][0:m, c * 512:(c + 1) * 512]),
                             reads=[Brt[r]], writes=[Bst])
                    P.op("dve", lambda e, m=m: e.bn_aggr(out=mv[0:m, :], in_=stats[0:m, :, :].rearrange("p a b -> p (a b)")),
                         reads=[Bst], writes=[Bmv])
                    P.op("act", lambda e, m=m: e.activation(out=rstd[0:m, :], in_=mv[0:m, 1:2], func=AF.Sqrt, bias=EPS, scale=1.0),
                         reads=[Bmv], writes=[Brs])
                    P.op("dve", lambda e, m=m: e.reciprocal(out=rstd[0:m, :], in_=rstd[0:m, :]), reads=[Brs], writes=[Brs])
                    P.op("dve", lambda e, r=r, m=m: e.tensor_scalar(
                        out=rt[r][0:m, :], in0=rt[r][0:m, :], scalar1=mv[0:m, 0:1], scalar2=rstd[0:m, 0:1],
                        op0=ALU.subtract, op1=ALU.mult), reads=[Brt[r], Bmv, Brs], writes=[Brt[r]])
                    P.op("dve", lambda e, r=r, m=m: e.tensor_tensor(out=rt[r][0:m, :], in0=rt[r][0:m, :], in1=gt[0:m, :], op=ALU.mult),
                         reads=[Brt[r], Bg], writes=[Brt[r]])
                    P.op("dve", lambda e, r=r, m=m: e.tensor_tensor(out=rt[r][0:m, :], in0=rt[r][0:m, :], in1=bt[0:m, :], op=ALU.add),
                         reads=[Brt[r], Bg], writes=[Brt[r]])
                    P.op("sp", lambda e, r=r, r0=r0, m=m: e.dma_start(out=dst_f32[r0:r0 + m, :], in_=rt[r][0:m, :]),
                         reads=[Brt[r]], dma=True)
                    if dst_T is not None:
                        P.op("act", lambda e, r=r, m=m: e.activation(out=hb[0:m, :], in_=rt[r][0:m, :], func=AF.Copy),
                             reads=[Brt[r]], writes=[Bhb])
                        for kq in range(4):
                            for kk in range(8):
                                k = kq * 8 + kk
                                P.op("pe", lambda e, k=k, kk=kk, m=m: e.transpose(
                                    out=psb[:, kk * 128:kk * 128 + m], in_=hb[0:m, k * 128:(k + 1) * 128], identity=idb[0:m, 0:m]),
                                    reads=[Bhb, Bid], writes=[Bpsb])
                            pv = psb[:, :].rearrange("p (a b) -> p a b", b=128)
                            P.op("act", lambda e, kq=kq, m=m, pv=pv: e.activation(
                                out=hT[:, kq * 8:(kq + 1) * 8, 0:m], in_=pv[:, :, 0:m], func=AF.Copy),
                                reads=[Bpsb], writes=[BhT])
                        P.op("sp", lambda e, r0=r0, m=m: e.dma_start(out=dst_T[:, :, r0:r0 + m], in_=hT[:, :, 0:m]),
                             reads=[BhT], dma=True)
                P.emit_phase()

        ln_phase(r_s, 0, h_s, hT_s)

        for (t0, nt) in ((0, 512), (512, 528)):
            with contextlib.ExitStack() as st:
                sb = lambda name, shape, dt: st.enter_context(nc.sbuf_tensor(uname(name), list(shape), dt))
                hT = sb("f_hT", [128, 32, 528], BF16)
                ffT = sb("f_ffT", [128, NFC, 528], BF16)
                wg = [sb("f_wg%d" % i, [128, 32, 128], BF16) for i in range(2)]
                wu = [sb("f_wu%d" % i, [128, 32, 128], BF16) for i in range(2)]
                sg = [sb("f_sg%d" % i, [128, 528], F32) for i in range(2)]
                BhT, Bff = Buf(), [Buf() for _ in range(NFC)]
                Bwg, Bwu, Bsg = [Buf(), Buf()], [Buf(), Buf()], [Buf(), Buf()]
                Bps = [Buf() for _ in range(7)]
                Bpsb = Buf()
                for kq in range(4):
                    P.op("sp", lambda e, kq=kq: e.dma_start(out=hT[:, kq * 8:(kq + 1) * 8, 0:nt],
                                                            in_=hT_s[:, kq * 8:(kq + 1) * 8, t0:t0 + nt]),
                         writes=[BhT], dma=True)
                segs = [(0, 512)] + ([(512, 16)] if nt > 512 else [])
                for f in range(NFC):
                    w = f % 2
                    P.op("pool", lambda e, f=f, w=w: e.dma_start(out=wg[w][:, :, :].rearrange("p k f -> p (k f)"),
                                                               in_=w_g[f].rearrange("p k f -> p (k f)")), writes=[Bwg[w]], dma=True)
                    P.op("pool", lambda e, f=f, w=w: e.dma_start(out=wu[w][:, :, :].rearrange("p k f -> p (k f)"),
                                                               in_=w_u[f].rearrange("p k f -> p (k f)")), writes=[Bwu[w]], dma=True)
                    pb = 3 * (f % 2)
                    for si, (c0, n) in enumerate(segs):
                        if si == 0:
                            pg, pu, g0, u0 = pb, pb + 1, 0, 0
                        else:
                            pg, pu, g0, u0 = pb + 2, pb + 2, 0, 16
                        for k in range(32):
                            P.op("pe", lambda e, pg=pg, g0=g0, k=k, c0=c0, n=n, w=w: e.matmul(
                                psum[pg][:, g0:g0 + n], lhsT=wg[w][:, k, :], rhs=hT[:, k, c0:c0 + n],
                                start=(k == 0), stop=(k == 31)), reads=[BhT, Bwg[w]], writes=[Bps[pg]])
                        P.op("act", lambda e, pg=pg, g0=g0, c0=c0, n=n, w=w: e.activation(
                            out=sg[w][:, c0:c0 + n], in_=psum[pg][:, g0:g0 + n], func=AF.Silu), reads=[Bps[pg]], writes=[Bsg[w]])
                        for k in range(32):
                            P.op("pe", lambda e, pu=pu, u0=u0, k=k, c0=c0, n=n, w=w: e.matmul(
                                psum[pu][:, u0:u0 + n], lhsT=wu[w][:, k, :], rhs=hT[:, k, c0:c0 + n],
                                start=(k == 0), stop=(k == 31)), reads=[BhT, Bwu[w]], writes=[Bps[pu]])
                        P.op("dve", lambda e, pu=pu, u0=u0, c0=c0, n=n, w=w, f=f: e.tensor_tensor(
                            out=ffT[:, f, c0:c0 + n], in0=sg[w][:, c0:c0 + n], in1=psum[pu][:, u0:u0 + n], op=ALU.mult),
                            reads=[Bps[pu], Bsg[w]], writes=[Bff[f]])
                wd = [sb("f_wd%d" % i, [128, 6, 512], BF16) for i in range(2)]
                hr = [sb("f_hr%d" % i, [128, 512], F32) for i in range(3)]
                Bwd = [Buf(), Buf()]
                Bhr = [Buf() for _ in range(3)]
                tts = [(i * 128, 128) for i in range(4)] + ([(512, 16)] if nt > 512 else [])
                FG = [(f0, min(6, NFC - f0)) for f0 in range(0, NFC, 6)]
                wi = 0
                hi = 0
                for db in range(8):
                    for (f0, nf) in FG:
                        w = wi % 2
                        wi += 1
                        P.op("pool", lambda e, db=db, f0=f0, nf=nf, w=w: e.dma_start(
                            out=wd[w][:, 0:nf, :].rearrange("p k f -> p (k f)"),
                            in_=w_d[db, :, f0:f0 + nf, :].rearrange("p k f -> p (k f)")), writes=[Bwd[w]], dma=True)
                        for fi in range(nf):
                            f = f0 + fi
                            for ti, (c0, m) in enumerate(tts):
                                P.op("pe", lambda e, ti=ti, c0=c0, m=m, f=f, fi=fi, w=w: e.matmul(
                                    psum[ti][0:m, :], lhsT=ffT[:, f, c0:c0 + m], rhs=wd[w][:, fi, :],
                                    start=(f == 0), stop=(f == NFC - 1)), reads=[Bff[f], Bwd[w]], writes=[Bps[ti]])
                    for ti, (c0, m) in enumerate(tts):
                        x = hi % 3
                        hi += 1
                        r0 = t0 + c0
                        P.op("sp", lambda e, x=x, r0=r0, m=m, db=db: e.dma_start(
                            out=hr[x][0:m, :], in_=h_s[r0:r0 + m, db * 512:(db + 1) * 512]), writes=[Bhr[x]], dma=True)
                        P.op("dve", lambda e, x=x, ti=ti, m=m: e.scalar_tensor_tensor(
                            out=hr[x][0:m, :], in0=hr[x][0:m, :], scalar=ALPHA, in1=psum[ti][0:m, :],
                            op0=ALU.mult, op1=ALU.add), reads=[Bps[ti], Bhr[x]], writes=[Bhr[x]])
                        P.op("sp", lambda e, x=x, r0=r0, m=m, db=db: e.dma_start(
                            out=y_s[r0:r0 + m, db * 512:(db + 1) * 512], in_=hr[x][0:m, :]), reads=[Bhr[x]], dma=True)
                P.emit_phase()

        ln_phase(y_s, 2, y_o, None)
    return nc


_NC_CACHE = {}


def _tile_w(w, nblk, blk):
    return np.ascontiguousarray(w.reshape(32, 128, nblk, blk).transpose(2, 1, 0, 3))


def kernel(x_prompt, x_sample, cache_k_cmp, cache_v_cmp, cache_k_slc, cache_v_slc,
           state_k_win, state_v_win, state_pool, page_table, w_in,
           w_cmp1_k, pe_cmp_k, w_cmp2_k, w_cmp1_v, pe_cmp_v, w_cmp2_v,
           w_pool, pool_scale, w_o, ln1_g, ln1_b, w_gate, w_up, w_down, ln2_g, ln2_b):
    f32 = np.float32
    x_prompt = np.asarray(x_prompt, f32)
    x_sample = np.asarray(x_sample, f32)
    if "nc" not in _NC_CACHE:
        _NC_CACHE["nc"] = build_program()
    nc = _NC_CACHE["nc"]

    w_in_p = np.zeros((D, E_PAD), f32)
    w_in_p[:, :7240] = np.asarray(w_in, f32)[0]
    w_in_t = _tile_w(w_in_p, 15, 512)
    w_o_t = _tile_w(np.asarray(w_o, f32)[0], 8, 512)
    w_g_t = _tile_w(np.asarray(w_gate, f32)[0], NFC, 128)
    w_u_t = _tile_w(np.asarray(w_up, f32)[0], NFC, 128)
    w_d_t = np.ascontiguousarray(np.asarray(w_down, f32)[0].reshape(NFC, 128, 8, 512).transpose(2, 1, 0, 3))
    lnp = np.ascontiguousarray(np.broadcast_to(
        np.stack([np.asarray(a, f32)[0] for a in (ln1_g, ln1_b, ln2_g, ln2_b)])[:, None, :], (4, 128, D)))
    ident = np.eye(128, dtype=f32)
    wp_h = np.ascontiguousarray(np.asarray(w_pool, f32)[0].reshape(4, 2, 128, 256).transpose(2, 0, 1, 3))
    psc_h = np.ascontiguousarray(np.asarray(pool_scale, f32)[0].reshape(8, 128).T)
    half = 64
    inv = (10000.0 ** (-np.arange(half, dtype=f32) / half)).astype(f32)
    NEGV = -30000.0
    tt_, kk_ = np.meshgrid(np.arange(128), np.arange(128), indexing="ij")
    cbc_h = np.where(kk_ <= tt_, 0.0, NEGV).astype(f32)
    cba_h = np.where(kk_ > tt_, 0.0, NEGV).astype(f32)
    w1k_h = np.ascontiguousarray(np.asarray(w_cmp1_k, f32)[0].transpose(1, 0, 2))
    w1v_h = np.ascontiguousarray(np.asarray(w_cmp1_v, f32)[0].transpose(1, 0, 2))
    w2k_ = np.asarray(w_cmp2_k, f32)[0]
    w2_h = np.ascontiguousarray(np.stack([w2k_, np.roll(w2k_, 64, axis=1), np.asarray(w_cmp2_v, f32)[0]], 1))
    peT_h = np.ascontiguousarray(np.stack([np.asarray(pe_cmp_k, f32)[0].T, np.asarray(pe_cmp_v, f32)[0].T], 1))
    sgn_h = np.where(np.arange(128) < 64, -1.0, 1.0).astype(f32)
    ckc_h = np.ascontiguousarray(np.asarray(cache_k_cmp, f32)[0]).reshape(2560 * 32, 2048)
    cvc_h = np.ascontiguousarray(np.asarray(cache_v_cmp, f32)[0]).reshape(2560 * 32, 2048)
    cks_h = np.ascontiguousarray(np.asarray(cache_k_slc, f32)[0]).reshape(2560 * 32, 2048)
    cvs_h = np.ascontiguousarray(np.asarray(cache_v_slc, f32)[0]).reshape(2560 * 32, 2048)
    ptab_all = np.asarray(page_table).astype(np.int32)
    oh_h = np.zeros((128, 5), f32)
    oh_h[np.arange(128), np.arange(128) // 32] = 1.0
    oh_h[:, 4] = np.arange(128) % 32
    sang = (16.0 * np.arange(512) + 31.0).astype(f32)[None, :] * inv[np.arange(128) % 64][:, None]
    ccs_h = np.cos(sang).astype(f32)
    scs_h = (np.sin(sang) * sgn_h[:, None]).astype(f32)
    addms_h = np.zeros((4, 129), f32)
    addms_h[:, [0, 127, 128]] = 1e30

    in_maps = []
    for c in range(8):
        b, j = c // 4, c % 4
        pad = 3 - j
        xs = np.zeros((4096, D), f32)
        xs[pad * 128:] = x_prompt[b, :4096 - pad * 128]
        xTs = np.ascontiguousarray(xs.reshape(4, 1024, 32, 128).transpose(0, 3, 2, 1))
        xsm = x_sample[4 * c:4 * c + 4].reshape(16, D)
        xsT = np.ascontiguousarray(xsm.reshape(16, 32, 128).transpose(2, 1, 0))
        own_rows = np.concatenate([np.arange(128) + (4 * i + 3) * 128 for i in range(8)])
        x_own = np.ascontiguousarray(np.concatenate([xs[own_rows], xsm], 0))
        pos = np.concatenate([np.arange(4096) - pad * 128, np.tile(8192 + np.arange(4), 4)]).astype(f32)
        ang = pos[:, None] * inv[None, :]
        own_pos = np.concatenate([pos[own_rows], pos[4096:]])
        rc = np.stack([1.0 / np.minimum(np.maximum(own_pos, 0) + 1.0, float(2 << gi)) for gi in range(4)]).astype(f32)
        rc_h = np.ascontiguousarray(np.broadcast_to(rc[None], (128, 4, TOK)))
        n_ = np.arange(256)
        cpos = (16 * n_ + 31 - pad * 128).astype(f32)
        cang = cpos[None, :] * inv[np.arange(128) % 64][:, None]
        ccmp_h = np.cos(cang).astype(f32)
        scmp_h = (np.sin(cang) * sgn_h[:, None]).astype(f32)
        ccmp_h[:, 255] = 0
        scmp_h[:, 255] = 0
        s_t = ((4 * np.arange(8)[None, :] + 3) * 128 + np.arange(128)[:, None])
        cval_h = ((16 * n_[None, None, :] + 31 <= s_t[:, :, None]) & (16 * n_[None, None, :] >= pad * 128)
                  & (n_[None, None, :] <= 254)).astype(f32)
        blk_ = np.arange(64)[None, None, :]
        cur_ = (s_t // 64)[:, :, None]
        b0_ = 2 * pad
        valid_ = (blk_ >= b0_) & (blk_ <= cur_)
        forced_ = (blk_ == b0_) | (blk_ == cur_) | (blk_ == cur_ - 1)
        addm_h = np.where(forced_, 1e30, np.where(valid_, 0.0, -1e30)).astype(f32)
        vblk_h = valid_.astype(f32)
        padb_h = np.ascontiguousarray(np.broadcast_to(np.where(np.arange(32) < pad, NEGV, 0.0).astype(f32)[None, :], (128, 32)))
        stp_h = np.ascontiguousarray(np.asarray(state_pool, f32)[0, 4 * c:4 * c + 4].reshape(4, 15, 8, 128).transpose(3, 2, 0, 1))
        in_maps.append({
            "xTs": xTs, "xsT": xsT, "x_own": x_own, "w_in": w_in_t, "w_o": w_o_t, "w_g": w_g_t, "w_u": w_u_t,
            "w_d": w_d_t, "cosT": np.cos(ang).astype(f32), "sinT": np.sin(ang).astype(f32), "lnp": lnp,
            "cbc_d": cbc_h, "cba_d": cba_h, "padb_d": padb_h, "w1k_d": w1k_h, "w1v_d": w1v_h, "w2_d": w2_h,
            "peT_d": peT_h, "ccmp_d": ccmp_h, "scmp_d": scmp_h, "cval_d": np.ascontiguousarray(cval_h),
            "addm_d": np.ascontiguousarray(addm_h), "vblk_d": np.ascontiguousarray(vblk_h),
            "ckc_d": ckc_h, "cvc_d": cvc_h, "cks_d": cks_h, "cvs_d": cvs_h,
            "ptab_d": np.ascontiguousarray(ptab_all[4 * c:4 * c + 4]), "oh_d": oh_h, "ccs_d": ccs_h, "scs_d": scs_h,
            "addms_d": addms_h,
            "ident": ident, "rc_d": rc_h, "wp_d": wp_h, "psc_d": psc_h, "stp_d": stp_h,
            "st_kw": np.ascontiguousarray(np.asarray(state_k_win, f32)[0, 4 * c:4 * c + 4].reshape(4, 512, 512)),
            "st_vw": np.ascontiguousarray(np.asarray(state_v_win, f32)[0, 4 * c:4 * c + 4].reshape(4, 512, 512)),
            "st_pool": np.ascontiguousarray(np.asarray(state_pool, f32)[0, 4 * c:4 * c + 4]),
        })
    res = run_bass_kernel_spmd(nc, in_maps, core_ids=list(range(8)))
    R = res.results

    y_prompt = np.zeros((2, 4096, D), f32)
    y_sample = np.zeros((32, 4, D), f32)
    kvp = [np.zeros((1, 2, 4096, 4, 128), f32) for _ in range(6)]
    kvs = [np.zeros((1, 32, 4, 4, 128), f32) for _ in range(6)]
    kw_s = np.zeros((1, 32, 512, 4, 128), f32)
    vw_s = np.zeros((1, 32, 512, 4, 128), f32)
    pool_p = np.zeros((1, 2, 15, 1024), f32)
    pool_s = np.zeros((1, 32, 15, 1024), f32)
    for c in range(8):
        b, j = c // 4, c % 4
        r = R[c]
        yo = r["y_o"]
        for i in range(8):
            g = 4 * i + j
            y_prompt[b, g * 128:(g + 1) * 128] = yo[i * 128:(i + 1) * 128]
        y_sample[4 * c:4 * c + 4] = yo[1024:1040].reshape(4, 4, D)
        for t in range(6):
            kvs[t][0, 4 * c:4 * c + 4] = r["kvout"][t, 4096:4112].reshape(4, 4, 4, 128)
            if j == 3:
                kvp[t][0, b] = r["kvout"][t, 0:4096].reshape(4096, 4, 128)
        kw_s[0, 4 * c:4 * c + 4] = r["kws_o"].reshape(4, 512, 4, 128)
        vw_s[0, 4 * c:4 * c + 4] = r["vws_o"].reshape(4, 512, 4, 128)
        pool_s[0, 4 * c:4 * c + 4] = r["pools_o"]
        if j == 3:
            pool_p[0, b] = r["poolp_o"]
    kc_p, vc_p, ks_p, vs_p, kw_full, vw_full = kvp
    return (y_prompt, y_sample, kc_p, vc_p, ks_p, vs_p,
            np.ascontiguousarray(kw_full[:, :, 4096 - 512:]), np.ascontiguousarray(vw_full[:, :, 4096 - 512:]),
            pool_p, kvs[0], kvs[1], kvs[2], kvs[3], kw_s, vw_s, pool_s)
```

```python
import contextlib
import numpy as np
import concourse.bass as bass
import concourse.mybir as mybir
from concourse.bass_utils import run_bass_kernel_spmd

F32 = mybir.dt.float32
BF16 = mybir.dt.bfloat16
I32 = mybir.dt.int32
AF = mybir.ActivationFunctionType
ALU = mybir.AluOpType
AX = mybir.AxisListType

D = 4096
DFF = 11008
NFC = 86
TOK = 1040
ALPHA = 2.0 ** 0.25
EPS = 1e-5
E_PAD = 7680


class Buf:
    __slots__ = ("name", "w", "r")

    def __init__(self, name=""):
        self.name = name
        self.w = None
        self.r = {}


class Op:
    __slots__ = ("eng", "fn", "deps", "dma", "needed", "sem", "semval", "done", "slot")

    def __init__(self, eng, fn, dma):
        self.eng = eng
        self.fn = fn
        self.deps = []
        self.dma = dma
        self.needed = False
        self.sem = None
        self.semval = None
        self.done = False
        self.slot = None


class Prog:
    ENGS = ("pe", "act", "dve", "pool", "sp")
    NDS = 16

    def __init__(self, nc, st):
        self.nc = nc
        self.ops = {k: [] for k in self.ENGS}
        self.engsem = {k: st.enter_context(nc.semaphore("es_" + k)) for k in self.ENGS}
        self.nds = {"sp": 16, "pool": 8, "act": 4}
        self.dsem = {q: [st.enter_context(nc.semaphore("ds_%s_%d" % (q, i))) for i in range(self.nds[q])]
                     for q in ("sp", "pool", "act")}
        self.cnt = {k: 0 for k in self.ENGS}
        self.dma_n = {q: 0 for q in self.dsem}
        self.dma_uses = {q: [0] * self.nds[q] for q in self.dsem}
        self.dma_last = {q: [None] * self.nds[q] for q in self.dsem}
        self.phase_dma = []

    def clear_sems(self):
        nc = self.nc
        with nc.Block() as block:
            def body(e):
                for s in self.engsem.values():
                    e.sem_clear(s)
                for lst in self.dsem.values():
                    for s in lst:
                        e.sem_clear(s)
            block.sync(body)

    def op(self, eng, fn, reads=(), writes=(), dma=False):
        o = Op(eng, fn, dma)
        deps = []
        for b in reads:
            if b.w is not None:
                deps.append(b.w)
        for b in writes:
            if b.w is not None:
                deps.append(b.w)
            deps.extend(b.r.values())
        if dma:
            s = self.dma_n[eng] % self.nds[eng]
            self.dma_n[eng] += 1
            o.slot = s
            if self.dma_last[eng][s] is not None:
                deps.append(self.dma_last[eng][s])
            self.dma_uses[eng][s] += 1
            o.sem = self.dsem[eng][s]
            o.semval = 16 * self.dma_uses[eng][s]
            self.dma_last[eng][s] = o
            self.phase_dma.append(o)
        seen = set()
        for d in deps:
            if id(d) in seen or d.done:
                continue
            seen.add(id(d))
            if (not d.dma) and d.eng == eng and eng == "pe":
                continue
            d.needed = True
            o.deps.append(d)
        self.ops[eng].append(o)
        for b in writes:
            b.w = o
            b.r = {}
        for b in reads:
            if b.w is o:
                continue
            key = ("dma", eng, o.slot) if dma else eng
            b.r[key] = o
        return o

    def emit_phase(self):
        nc = self.nc
        for k in self.ENGS:
            for o in self.ops[k]:
                if not o.dma and o.needed:
                    self.cnt[k] += 1
                    o.sem = self.engsem[k]
                    o.semval = self.cnt[k]
        finals = {}
        for o in self.phase_dma:
            finals[(o.eng, o.slot)] = o
        finals = list(finals.values())

        def make_body(k):
            def body(e):
                waited = {}

                def wait(d):
                    sid = id(d.sem)
                    if waited.get(sid, 0) >= d.semval:
                        return
                    waited[sid] = d.semval
                    e.wait_ge(d.sem, d.semval)

                for o in self.ops[k]:
                    for d in o.deps:
                        wait(d)
                    ins = o.fn(e)
                    if o.dma:
                        ins.then_inc(o.sem, 16)
                    elif o.needed:
                        ins.then_inc(o.sem, 1)
                if k == "sp":
                    for d in finals:
                        wait(d)
            return body

        with nc.Block() as block:
            block.tensor(make_body("pe"))
            block.scalar(make_body("act"))
            block.vector(make_body("dve"))
            block.gpsimd(make_body("pool"))
            block.sync(make_body("sp"))
        for k in self.ENGS:
            for o in self.ops[k]:
                o.done = True
            self.ops[k] = []
        self.phase_dma = []


def build_program():
    nc = bass.Bass("TRN2", target_bir_lowering=False)

    def din(name, shape, dt=F32):
        return nc.dram_tensor(name, list(shape), dt, kind="ExternalInput").ap()

    def dout(name, shape, dt=F32):
        return nc.dram_tensor(name, list(shape), dt, kind="ExternalOutput").ap()

    _uid = [0]

    def uname(name):
        _uid[0] += 1
        return "%s_%d" % (name, _uid[0])

    def dscr(name, shape, dt=F32):
        return nc.dram_tensor(name, list(shape), dt, kind="Internal").ap()

    xTs = din("xTs", [4, 128, 32, 1024])
    xsT = din("xsT", [128, 32, 16])
    x_own = din("x_own", [TOK, D])
    w_in = din("w_in", [15, 128, 32, 512])
    w_o = din("w_o", [8, 128, 32, 512])
    w_g = din("w_g", [NFC, 128, 32, 128])
    w_u = din("w_u", [NFC, 128, 32, 128])
    w_d = din("w_d", [8, 128, NFC, 512])
    cosT = din("cosT", [4112, 64])
    sinT = din("sinT", [4112, 64])
    lnp = din("lnp", [4, 128, D])
    ident = din("ident", [128, 128])
    st_kw = din("st_kw", [4, 512, 512])
    st_vw = din("st_vw", [4, 512, 512])
    st_pool = din("st_pool", [4, 15, 1024])
    rc_d = din("rc_d", [128, 4, TOK])
    wp_d = din("wp_d", [128, 4, 2, 256])
    psc_d = din("psc_d", [128, 8])
    stp_d = din("stp_d", [128, 8, 4, 15])
    cbc_d = din("cbc_d", [128, 128])
    cba_d = din("cba_d", [128, 128])
    padb_d = din("padb_d", [128, 32])
    w1k_d = din("w1k_d", [128, 32, 128])
    w1v_d = din("w1v_d", [128, 32, 128])
    w2_d = din("w2_d", [128, 3, 128])
    peT_d = din("peT_d", [128, 2, 32])
    ccmp_d = din("ccmp_d", [128, 256])
    scmp_d = din("scmp_d", [128, 256])
    cval_d = din("cval_d", [128, 8, 256])
    addm_d = din("addm_d", [128, 8, 64])
    vblk_d = din("vblk_d", [128, 8, 64])
    ckc_d = din("ckc_d", [2560 * 32, 2048])
    cvc_d = din("cvc_d", [2560 * 32, 2048])
    cks_d = din("cks_d", [2560 * 32, 2048])
    cvs_d = din("cvs_d", [2560 * 32, 2048])
    ptab_d = din("ptab_d", [4, 64], I32)
    oh_d = din("oh_d", [128, 5])
    ccs_d = din("ccs_d", [128, 512])
    scs_d = din("scs_d", [128, 512])
    addms_d = din("addms_d", [4, 129])

    kvout = dout("kvout", [6, 4112, 512])
    kws_o = dout("kws_o", [4, 512, 512])
    vws_o = dout("vws_o", [4, 512, 512])
    poolp_o = dout("poolp_o", [15, 1024])
    pools_o = dout("pools_o", [4, 15, 1024])
    y_o = dout("y_o", [TOK, D])

    mixT_s = dscr("mixT_s", [128, 32, TOK], BF16)
    r_s = dscr("r_s", [TOK, D])
    h_s = dscr("h_s", [TOK, D])
    hT_s = dscr("hT_s", [128, 32, TOK], BF16)
    y_s = dscr("y_s", [TOK, D])
    q_s = dscr("q_s", [TOK, 3072])
    gate_s = dscr("gate_s", [TOK, 72])

    TT9 = [(i * 128, 128) for i in range(8)] + [(1024, 16)]

    with contextlib.ExitStack() as gst:
        P = Prog(nc, gst)
        P.clear_sems()
        psum = [gst.enter_context(nc.psum_tensor("ps%d" % i, [128, 512], F32)) for i in range(7)]
        psb = gst.enter_context(nc.psum_tensor("psb", [128, 1024], BF16))

        with contextlib.ExitStack() as st:
            sb = lambda name, shape, dt: st.enter_context(nc.sbuf_tensor(uname(name), list(shape), dt))
            xb_t = sb("xb_t", [128, 32, 1040], BF16)
            wt = [sb("wt%d" % i, [128, 32, 512], BF16) for i in range(2)]
            cos_t = sb("cos_t", [128, 9, 64], F32)
            sin_t = sb("sin_t", [128, 9, 64], F32)
            o32 = [sb("o32_%d" % i, [128, 512], F32) for i in range(4)]
            ta = sb("ta", [128, 512], F32)
            tb = sb("tb", [128, 512], F32)
            zt = sb("zt", [128, TOK], BF16)
            u32 = sb("u32", [128, 144], F32)
            sAB = [sb("sA", [128, 144], F32), sb("sB", [128, 144], F32)]
            ua = sb("ua", [128, 19], F32)
            dT = sb("dT", [128, 2, 128], BF16)
            mo_t = [sb("mo0", [128, 128], BF16), sb("mo1", [128, 128], BF16)]
            rc_t = sb("rc_t", [128, 4, TOK], F32)
            wp_t = sb("wp_t", [128, 4, 2, 256], BF16)
            psc_t = sb("psc_t", [128, 8], F32)
            stp_t = sb("stp_t", [128, 8, 4, 15], F32)
            Bu32, BsAB, Bua, BdT, Bmo, Brc = Buf(), [Buf(), Buf()], Buf(), Buf(), [Buf(), Buf()], Buf()
            P.op("sp", lambda e: e.dma_start(out=rc_t[:], in_=rc_d[:, :, :]), writes=[Brc], dma=True)
            P.op("pool", lambda e: e.dma_start(out=wp_t[:], in_=wp_d[:, :, :, :]), writes=[Brc], dma=True)
            P.op("sp", lambda e: e.dma_start(out=psc_t[:], in_=psc_d[:, :]), writes=[Brc], dma=True)
            P.op("sp", lambda e: e.dma_start(out=stp_t[:], in_=stp_d[:, :, :, :]), writes=[Brc], dma=True)
            Bps = [Buf() for _ in range(7)]
            Bxb, Bcs = [Buf() for _ in range(5)], Buf()
            Bwt = [[Buf(), Buf()], [Buf(), Buf()]]
            Bo32 = [Buf() for _ in range(4)]
            Bta, Btb, Bzt = Buf(), Buf(), Buf()
            P.op("dve", lambda e: e.memset(zt[:], 0.0), writes=[Bzt])
            for k in range(8, 32):
                P.op("sp", lambda e, k=k: e.dma_start(out=mixT_s[:, k, :], in_=zt[:]), reads=[Bzt], dma=True)
            for sbi in range(4):
                P.op("sp", lambda e, i=sbi: e.dma_start(out=kws_o[i, 0:508, :], in_=st_kw[i, 4:512, :]), dma=True)
                P.op("sp", lambda e, i=sbi: e.dma_start(out=vws_o[i, 0:508, :], in_=st_vw[i, 4:512, :]), dma=True)
                P.op("sp", lambda e, i=sbi: e.dma_start(out=pools_o[i, 0:11, :], in_=st_pool[i, 4:15, :]), dma=True)
            psi = 0
            oi = 0
            wi = 0
            for xb in range(4):
                for kq in range(4):
                    P.op("pool", lambda e, xb=xb, kq=kq: e.dma_start(
                        out=xb_t[:, kq * 8:(kq + 1) * 8, 0:1024], in_=xTs[xb, :, kq * 8:(kq + 1) * 8, :]),
                        writes=[Bxb[kq]], dma=True)
                if xb == 3:
                    P.op("pool", lambda e: e.dma_start(out=xb_t[:, :, 1024:1040], in_=xsT[:, :, :]),
                         writes=[Bxb[4]], dma=True)
                ntile = 9 if xb == 3 else 8
                for (tab_t, tab_d) in ((cos_t, cosT), (sin_t, sinT)):
                    P.op("sp", lambda e, tab_t=tab_t, tab_d=tab_d, xb=xb: e.dma_start(
                        out=tab_t[:, 0:8, :], in_=tab_d[xb * 1024:(xb + 1) * 1024, :].rearrange("(i p) d -> p i d", p=128)),
                        writes=[Bcs], dma=True)
                    if xb == 3:
                        P.op("sp", lambda e, tab_t=tab_t, tab_d=tab_d: e.dma_start(
                            out=tab_t[0:16, 8, :], in_=tab_d[4096:4112, :]), writes=[Bcs], dma=True)
                for eb in list(range(8, 14)) + [0, 1] + list(range(2, 8)) + [14]:
                    w = wi % 2
                    wi += 1
                    for kh in range(2):
                        P.op("pool", lambda e, eb=eb, kh=kh, w=w: e.dma_start(
                            out=wt[w][:, kh * 16:(kh + 1) * 16, :].rearrange("p k f -> p (k f)"),
                            in_=w_in[eb, :, kh * 16:(kh + 1) * 16, :].rearrange("p k f -> p (k f)")),
                            writes=[Bwt[w][kh]], dma=True)
                    for i in range(ntile):
                        m = 128 if i < 8 else 16
                        c0 = i * 128
                        S = xb * 8 + i if i < 8 else 32
                        r0 = S * 128
                        own = (i == 8) or (i % 4 == 3)
                        tok0 = (S // 4) * 128 if i < 8 else 1024
                        if 8 <= eb < 14:
                            do = True
                        elif eb < 2:
                            do = S >= 31
                        else:
                            do = own
                        if not do:
                            continue
                        p = psi % 7
                        psi += 1
                        for k in range(32):
                            P.op("pe", lambda e, p=p, k=k, c0=c0, m=m, w=w: e.matmul(
                                psum[p][0:m, :], lhsT=xb_t[:, k, c0:c0 + m], rhs=wt[w][:, k, :],
                                start=(k == 0), stop=(k == 31)), reads=Bxb + Bwt[w], writes=[Bps[p]])
                        o = oi % 4
                        oi += 1
                        kv = eb - 8
                        if kv in (2, 4) or 2 <= eb < 8:
                            z4 = psum[p][0:m, :].rearrange("p (h two d) -> p h two d", two=2, d=64)
                            a4 = ta[0:m, :].rearrange("p (h two d) -> p h two d", two=2, d=64)
                            b4 = tb[0:m, :].rearrange("p (h two d) -> p h two d", two=2, d=64)
                            cb = cos_t[0:m, i, :].unsqueeze(1).unsqueeze(1).to_broadcast([m, 4, 2, 64])
                            sbh = sin_t[0:m, i, :].unsqueeze(1).to_broadcast([m, 4, 64])
                            P.op("dve", lambda e, a4=a4, z4=z4, cb=cb: e.tensor_tensor(out=a4, in0=z4, in1=cb, op=ALU.mult),
                                 reads=[Bps[p], Bcs], writes=[Bta])
                            P.op("dve", lambda e, b4=b4, z4=z4, sbh=sbh: e.tensor_tensor(
                                out=b4[:, :, 0, :], in0=z4[:, :, 1, :], in1=sbh, op=ALU.mult),
                                reads=[Bps[p], Bcs], writes=[Btb])
                            P.op("dve", lambda e, b4=b4, z4=z4, sbh=sbh: e.tensor_tensor(
                                out=b4[:, :, 1, :], in0=z4[:, :, 0, :], in1=sbh, op=ALU.mult),
                                reads=[Bps[p], Bcs, Btb], writes=[Btb])
                            o4 = o32[o][0:m, :].rearrange("p (h two d) -> p h two d", two=2, d=64)
                            P.op("dve", lambda e, o4=o4, a4=a4, b4=b4: e.tensor_tensor(
                                out=o4[:, :, 0, :], in0=a4[:, :, 0, :], in1=b4[:, :, 0, :], op=ALU.subtract),
                                reads=[Bta, Btb], writes=[Bo32[o]])
                            P.op("dve", lambda e, o4=o4, a4=a4, b4=b4: e.tensor_tensor(
                                out=o4[:, :, 1, :], in0=a4[:, :, 1, :], in1=b4[:, :, 1, :], op=ALU.add),
                                reads=[Bta, Btb, Bo32[o]], writes=[Bo32[o]])
                        elif eb == 14:
                            P.op("act", lambda e, o=o, p=p, m=m: e.activation(out=o32[o][0:m, 0:72], in_=psum[p][0:m, 0:72], func=AF.Sigmoid),
                                 reads=[Bps[p]], writes=[Bo32[o]])
                        else:
                            P.op("act", lambda e, o=o, p=p, m=m: e.activation(out=o32[o][0:m, :], in_=psum[p][0:m, :], func=AF.Copy),
                                 reads=[Bps[p]], writes=[Bo32[o]])
                        if 8 <= eb < 14:
                            P.op("sp", lambda e, kv=kv, r0=r0, m=m, o=o: e.dma_start(
                                out=kvout[kv, r0:r0 + m, :], in_=o32[o][0:m, :]), reads=[Bo32[o]], dma=True)
                            if S == 32 and kv in (4, 5):
                                dst = kws_o if kv == 4 else vws_o
                                for sbi in range(4):
                                    P.op("sp", lambda e, dst=dst, sbi=sbi, o=o: e.dma_start(
                                        out=dst[sbi, 508:512, :], in_=o32[o][sbi * 4:(sbi + 1) * 4, :]),
                                        reads=[Bo32[o]], dma=True)
                        elif eb < 2:
                            if S == 31:
                                P.op("sp", lambda e, eb=eb, o=o: e.dma_start(
                                    out=poolp_o[:, eb * 512:(eb + 1) * 512], in_=o32[o][113:128, :]),
                                    reads=[Bo32[o]], dma=True)
                            else:
                                for sbi in range(4):
                                    P.op("sp", lambda e, eb=eb, sbi=sbi, o=o: e.dma_start(
                                        out=pools_o[sbi, 11:15, eb * 512:(eb + 1) * 512],
                                        in_=o32[o][sbi * 4:(sbi + 1) * 4, :]), reads=[Bo32[o]], dma=True)
                        elif eb < 8:
                            P.op("sp", lambda e, eb=eb, tok0=tok0, m=m, o=o: e.dma_start(
                                out=q_s[tok0:tok0 + m, (eb - 2) * 512:(eb - 1) * 512], in_=o32[o][0:m, :]),
                                reads=[Bo32[o]], dma=True)
                        else:
                            P.op("sp", lambda e, tok0=tok0, m=m, o=o: e.dma_start(
                                out=gate_s[tok0:tok0 + m, :], in_=o32[o][0:m, 0:72]), reads=[Bo32[o]], dma=True)
                    if eb >= 2:
                        continue
                    units = [(3, False), (7, False)] + ([(8, True)] if xb == 3 else [])
                    for (i, is_s) in units:
                        S = xb * 8 + i if not is_s else 32
                        tok0 = (S // 4) * 128 if not is_s else 1024
                        m = 16 if is_s else 128
                        for cc4 in range(4):
                            cg = eb * 4 + cc4
                            gi, cc = cg // 2, cg % 2
                            wsz = 2 << gi
                            p = psi % 7
                            psi += 1
                            if not is_s:
                                n = 144
                                cs0 = i * 128 - 16
                            else:
                                n = 16
                                cs0 = 1024
                            for k in range(32):
                                P.op("pe", lambda e, p=p, k=k, cs0=cs0, n=n, w=w, cc4=cc4: e.matmul(
                                    psum[p][:, 0:n], lhsT=wt[w][:, k, cc4 * 128:(cc4 + 1) * 128], rhs=xb_t[:, k, cs0:cs0 + n],
                                    start=(k == 0), stop=(k == 31)), reads=Bxb + Bwt[w], writes=[Bps[p]])
                            P.op("act", lambda e, p=p, n=n: e.activation(out=u32[:, 0:n], in_=psum[p][:, 0:n], func=AF.Copy),
                                 reads=[Bps[p]], writes=[Bu32])
                            if not is_s:
                                cur, Bcur = u32, Bu32
                                for lv in range(gi + 1):
                                    sh = 1 << lv
                                    nxt, Bnxt = sAB[lv % 2], BsAB[lv % 2]
                                    P.op("dve", lambda e, nxt=nxt, cur=cur, sh=sh: e.tensor_tensor(
                                        out=nxt[:, sh:144], in0=cur[:, sh:144], in1=cur[:, 0:144 - sh], op=ALU.add),
                                        reads=[Bcur], writes=[Bnxt])
                                    cur, Bcur = nxt, Bnxt
                                oth, Both = sAB[(gi + 1) % 2], BsAB[(gi + 1) % 2]
                                P.op("dve", lambda e, oth=oth, cur=cur, gi=gi, tok0=tok0: e.tensor_tensor(
                                    out=oth[:, 0:128], in0=cur[:, 16:144], in1=rc_t[:, gi, tok0:tok0 + 128], op=ALU.mult),
                                    reads=[Bcur, Brc], writes=[Both])
                                P.op("dve", lambda e, oth=oth, cc=cc: e.tensor_tensor(
                                    out=dT[:, cc, 0:128], in0=oth[:, 0:128], in1=u32[:, 16:144], op=ALU.subtract),
                                    reads=[Both, Bu32], writes=[BdT])
                            else:
                                for sbi in range(4):
                                    P.op("dve", lambda e, cg=cg, sbi=sbi: e.tensor_copy(out=ua[:, 0:15], in_=stp_t[:, cg, sbi, :]),
                                         reads=[Brc], writes=[Bua])
                                    P.op("dve", lambda e, sbi=sbi: e.tensor_copy(out=ua[:, 15:19], in_=u32[:, sbi * 4:(sbi + 1) * 4]),
                                         reads=[Bu32, Bua], writes=[Bua])
                                    cur, Bcur = ua, Bua
                                    for lv in range(gi + 1):
                                        sh = 1 << lv
                                        nxt, Bnxt = sAB[lv % 2], BsAB[lv % 2]
                                        P.op("dve", lambda e, nxt=nxt, cur=cur, sh=sh: e.tensor_tensor(
                                            out=nxt[:, sh:19], in0=cur[:, sh:19], in1=cur[:, 0:19 - sh], op=ALU.add),
                                            reads=[Bcur], writes=[Bnxt])
                                        cur, Bcur = nxt, Bnxt
                                    P.op("dve", lambda e, cur=cur, wsz=wsz, cc=cc, sbi=sbi: e.scalar_tensor_tensor(
                                        out=dT[:, cc, sbi * 4:(sbi + 1) * 4], in0=cur[:, 15:19], scalar=1.0 / wsz, in1=ua[:, 15:19],
                                        op0=ALU.mult, op1=ALU.subtract), reads=[Bcur, Bua, BdT], writes=[BdT])
                            if cc == 1:
                                for ec in range(2):
                                    p2 = psi % 7
                                    psi += 1
                                    for c2 in range(2):
                                        P.op("pe", lambda e, p2=p2, gi=gi, c2=c2, ec=ec, m=m: e.matmul(
                                            psum[p2][:, 0:m], lhsT=wp_t[:, gi, c2, ec * 128:(ec + 1) * 128], rhs=dT[:, c2, 0:m],
                                            start=(c2 == 0), stop=(c2 == 1)), reads=[BdT, Brc], writes=[Bps[p2]])
                                    mo = mo_t[ec]
                                    P.op("dve", lambda e, mo=mo, p2=p2, m=m, gi=gi, ec=ec: e.tensor_scalar(
                                        out=mo[:, 0:m], in0=psum[p2][:, 0:m], scalar1=psc_t[:, 2 * gi + ec:2 * gi + ec + 1], scalar2=None,
                                        op0=ALU.mult), reads=[Bps[p2], Brc], writes=[Bmo[ec]])
                                    P.op("sp", lambda e, mo=mo, gi=gi, ec=ec, tok0=tok0, m=m: e.dma_start(
                                        out=mixT_s[:, 2 * gi + ec, tok0:tok0 + m], in_=mo[:, 0:m]), reads=[Bmo[ec]], dma=True)
            P.emit_phase()

        SCALE = 128.0 ** -0.5
        NEG = -30000.0

        def gelu_cols(Pq, ps_ap, hid0_ap, n, x32, x2, sg_t, g_out, Bx, Bpsx, Bg, rd=()):
            Pq.op("act", lambda e: e.activation(out=x32[:, 0:n], in_=ps_ap, func=AF.Identity, bias=hid0_ap, scale=1.0),
                  reads=[Bpsx] + list(rd), writes=[Bx])
            Pq.op("dve", lambda e: e.tensor_tensor(out=x2[:, 0:n], in0=x32[:, 0:n], in1=x32[:, 0:n], op=ALU.mult),
                  reads=[Bx], writes=[Bx])
            Pq.op("dve", lambda e: e.tensor_scalar(out=x2[:, 0:n], in0=x2[:, 0:n], scalar1=0.044715, scalar2=1.0,
                                                   op0=ALU.mult, op1=ALU.add), reads=[Bx], writes=[Bx])
            Pq.op("dve", lambda e: e.tensor_tensor(out=x2[:, 0:n], in0=x2[:, 0:n], in1=x32[:, 0:n], op=ALU.mult),
                  reads=[Bx], writes=[Bx])
            Pq.op("act", lambda e: e.activation(out=sg_t[:, 0:n], in_=x2[:, 0:n], func=AF.Sigmoid, scale=1.5957691216057308),
                  reads=[Bx], writes=[Bx])
            Pq.op("dve", lambda e: e.tensor_tensor(out=g_out, in0=sg_t[:, 0:n], in1=x32[:, 0:n], op=ALU.mult),
                  reads=[Bx], writes=[Bg])

        with contextlib.ExitStack() as st:
            sb = lambda name, shape, dt: st.enter_context(nc.sbuf_tensor(uname(name), list(shape), dt))
            identb = sb("a_identb", [128, 384], BF16)
            cb_c = sb("a_cbc", [128, 128], BF16)
            cb_a = sb("a_cba", [128, 128], BF16)
            padb = sb("a_padb", [128, 32], F32)
            w1k = sb("a_w1k", [128, 32, 128], BF16)
            w1v = sb("a_w1v", [128, 32, 128], BF16)
            w2 = sb("a_w2", [128, 3, 128], BF16)
            peT = sb("a_peT", [128, 2, 32], BF16)
            hid0 = sb("a_hid0", [128, 2], F32)
            ccmp = sb("a_ccmp", [128, 256], F32)
            scmp = sb("a_scmp", [128, 256], F32)
            cval = sb("a_cval", [128, 8, 256], F32)
            addm = sb("a_addm", [128, 8, 64], F32)
            vblk = sb("a_vblk", [128, 8, 64], F32)
            tm = [sb("a_tm%d" % i, [128, 32, 128], BF16) for i in range(2)]
            XT = {t: sb("a_XT%d" % t, [128, 4096], BF16) for t in (0, 1, 2, 4)}
            V1 = {t: sb("a_V1%d" % t, [128, 32, 130], BF16) for t in (3, 5)}
            kcbT = sb("a_kcbT", [128, 256], BF16)
            vcb = sb("a_vcb", [128, 2, 128], BF16)
            x32 = sb("a_x32", [128, 256], F32)
            x2 = sb("a_x2", [128, 256], F32)
            sgt = sb("a_sgt", [128, 256], F32)
            Gk = sb("a_Gk", [128, 256], BF16)
            Gv = sb("a_Gv", [128, 256], BF16)
            qtm = sb("a_qtm", [128, 768], BF16)
            qT = sb("a_qT", [128, 768], BF16)
            gt = sb("a_gt", [128, 18], F32)
            e32 = sb("a_e32", [128, 256], F32)
            ev = sb("a_ev", [128, 256], F32)
            Pp = sb("a_Pp", [128, 260], F32)
            Pb = sb("a_Pb", [128, 256], BF16)
            PT = sb("a_PT", [128, 2, 128], BF16)
            sm = sb("a_sm", [128, 8], F32)
            imp = sb("a_imp", [128, 64], F32)
            imw = sb("a_imw", [128, 64], F32)
            m8 = sb("a_m8", [128, 16], F32)
            negb = sb("a_negb", [128, 64], BF16)
            negx = sb("a_negx", [128, 4096], BF16)
            PTs = [sb("a_PTs%d" % i, [128, 384], BF16) for i in range(4)]
            acc = sb("a_acc", [128, 768], F32)
            accb = sb("a_accb", [128, 768], BF16)
            mixo = sb("a_mixo", [128, 6, 128], BF16)
            sc6 = sb("a_sc6", [128, 8], F32)
            smh = sb("a_smh", [128, 6, 4], F32)
            evh = sb("a_evh", [128, 6, 256], F32)
            Pbh = sb("a_Pbh", [128, 6, 256], BF16)
            PTh = sb("a_PTh", [128, 6, 256], BF16)
            Bsch, Bsmh, Bevh, BPbh, BPTh, Boch = ([Buf() for _ in range(6)] for _ in range(6))

            Bc = Buf()
            Btm = [Buf(), Buf()]
            BXT = {t: Buf() for t in (0, 1, 2, 4)}
            BV1 = {t: Buf() for t in (3, 5)}
            Bkcb, Bvcb, Bx, BGk, BGv = Buf(), Buf(), Buf(), Buf(), Buf()
            Bq, BqT, Bgt, Be, BPp, BPb, BPT, Bsm, Bimp, Bneg, Bnegx = (Buf() for _ in range(11))
            BPTs = [Buf() for _ in range(4)]
            Bacc, Baccb, Bmixo, Bsc6 = Buf(), Buf(), Buf(), Buf()
            Bps = [Buf() for _ in range(7)]
            Bpsb = Buf()

            for j3 in range(3):
                P.op("pool", lambda e, j3=j3: e.dma_start(out=identb[:, j3 * 128:(j3 + 1) * 128], in_=ident[:, :]), writes=[Bc], dma=True)
            P.op("pool", lambda e: e.dma_start(out=cb_c[:], in_=cbc_d[:, :]), writes=[Bc], dma=True)
            P.op("pool", lambda e: e.dma_start(out=cb_a[:], in_=cba_d[:, :]), writes=[Bc], dma=True)
            P.op("sp", lambda e: e.dma_start(out=padb[:], in_=padb_d[:, :]), writes=[Bc], dma=True)
            P.op("pool", lambda e: e.dma_start(out=w1k[:], in_=w1k_d[:, :, :]), writes=[Bc], dma=True)
            P.op("pool", lambda e: e.dma_start(out=w1v[:], in_=w1v_d[:, :, :]), writes=[Bc], dma=True)
            P.op("pool", lambda e: e.dma_start(out=w2[:], in_=w2_d[:, :, :]), writes=[Bc], dma=True)
            P.op("pool", lambda e: e.dma_start(out=peT[:], in_=peT_d[:, :, :]), writes=[Bc], dma=True)
            P.op("sp", lambda e: e.dma_start(out=ccmp[:], in_=ccmp_d[:, :]), writes=[Bc], dma=True)
            P.op("sp", lambda e: e.dma_start(out=scmp[:], in_=scmp_d[:, :]), writes=[Bc], dma=True)
            P.op("sp", lambda e: e.dma_start(out=cval[:], in_=cval_d[:, :, :]), writes=[Bc], dma=True)
            P.op("sp", lambda e: e.dma_start(out=addm[:], in_=addm_d[:, :, :]), writes=[Bc], dma=True)
            P.op("sp", lambda e: e.dma_start(out=vblk[:], in_=vblk_d[:, :, :]), writes=[Bc], dma=True)
            for t in (3, 5):
                P.op("dve", lambda e, t=t: e.memset(V1[t][:, :, 128:130], 1.0), writes=[BV1[t]])
            P.op("dve", lambda e: e.memset(Pp[:], 0.0), writes=[BPp])
            P.op("dve", lambda e: e.memset(vcb[:], 0.0), writes=[Bvcb])
            for vi, w1t in enumerate((w1k, w1v)):
                for pp in range(32):
                    P.op("pe", lambda e, vi=vi, w1t=w1t, pp=pp: e.matmul(
                        psum[0][:, vi:vi + 1], lhsT=w1t[:, pp, :], rhs=peT[:, vi, pp:pp + 1],
                        start=(pp == 0 and vi == 0), stop=(pp == 31), skip_group_check=True), reads=[Bc], writes=[Bps[0]])
            P.op("act", lambda e: e.activation(out=hid0[:, 0:2], in_=psum[0][:, 0:2], func=AF.Copy), reads=[Bps[0]], writes=[Bc])

            tmi = 0
            for g in range(4):
                for t in (0, 1, 2, 4):
                    tb_ = tmi % 2
                    tmi += 1
                    for qd in range(4):
                        P.op("pool", lambda e, t=t, g=g, qd=qd, tb_=tb_: e.dma_start(
                            out=tm[tb_][:, qd * 8:(qd + 1) * 8, :],
                            in_=kvout[t, qd * 1024:(qd + 1) * 1024, g * 128:(g + 1) * 128].rearrange("(i p) d -> p i d", p=128)),
                            writes=[Btm[tb_]], dma=True)
                    for b8 in range(4):
                        for kk in range(8):
                            it = b8 * 8 + kk
                            P.op("pe", lambda e, tb_=tb_, it=it, kk=kk: e.transpose(
                                out=psb[:, kk * 128:(kk + 1) * 128], in_=tm[tb_][:, it, :], identity=identb[:, 0:128]),
                                reads=[Btm[tb_], Bc], writes=[Bpsb])
                        eng = "act" if b8 % 2 == 0 else "dve"
                        if eng == "act":
                            P.op("act", lambda e, t=t, b8=b8: e.activation(out=XT[t][:, b8 * 1024:(b8 + 1) * 1024], in_=psb[:, :], func=AF.Copy),
                                 reads=[Bpsb], writes=[BXT[t]])
                        else:
                            P.op("dve", lambda e, t=t, b8=b8: e.tensor_copy(out=XT[t][:, b8 * 1024:(b8 + 1) * 1024], in_=psb[:, :]),
                                 reads=[Bpsb], writes=[BXT[t]])
                for t in (3, 5):
                    for qd in range(4):
                        P.op("pool", lambda e, t=t, g=g, qd=qd: e.dma_start(
                            out=V1[t][:, qd * 8:(qd + 1) * 8, 0:128],
                            in_=kvout[t, qd * 1024:(qd + 1) * 1024, g * 128:(g + 1) * 128].rearrange("(i p) d -> p i d", p=128)),
                            writes=[BV1[t]], dma=True)
                for vi, (w1t, srcT, Gt, BG) in enumerate(((w1k, XT[0], Gk, BGk), (w1v, XT[1], Gv, BGv))):
                    for ap_ in range(32):
                        a_, p16 = ap_ // 16, ap_ % 16
                        c_0 = 16 * a_ + p16
                        P.op("pe", lambda e, w1t=w1t, srcT=srcT, ap_=ap_, c_0=c_0: e.matmul(
                            psum[1][:, 0:255], lhsT=w1t[:, ap_, :], rhs=srcT[:, c_0:c_0 + 16 * 254 + 1:16],
                            start=(ap_ == 0), stop=(ap_ == 31)), reads=[Bc, BXT[vi]], writes=[Bps[1]])
                    gelu_cols(P, psum[1][:, 0:255], hid0[:, vi:vi + 1], 255, x32, x2, sgt, Gt[:, 0:255], Bx, Bps[1], BG, rd=[Bc])
                P.op("pe", lambda e: e.matmul(psum[2][:, 0:255], lhsT=w2[:, 0, :], rhs=Gk[:, 0:255], start=True, stop=True),
                     reads=[Bc, BGk], writes=[Bps[2]])
                P.op("pe", lambda e: e.matmul(psum[3][:, 0:255], lhsT=w2[:, 1, :], rhs=Gk[:, 0:255], start=True, stop=True),
                     reads=[Bc, BGk], writes=[Bps[3]])
                P.op("dve", lambda e: e.tensor_tensor(out=x32[:, 0:255], in0=psum[2][:, 0:255], in1=ccmp[:, 0:255], op=ALU.mult),
                     reads=[Bps[2], Bc, Bx], writes=[Bx])
                P.op("dve", lambda e: e.tensor_tensor(out=x2[:, 0:255], in0=psum[3][:, 0:255], in1=scmp[:, 0:255], op=ALU.mult),
                     reads=[Bps[3], Bc, Bx], writes=[Bx])
                P.op("dve", lambda e: e.tensor_tensor(out=kcbT[:, 0:255], in0=x32[:, 0:255], in1=x2[:, 0:255], op=ALU.add),
                     reads=[Bx], writes=[Bkcb])
                for nt_, nn in ((0, 128), (1, 127)):
                    P.op("pe", lambda e, nt_=nt_, nn=nn: e.matmul(
                        psum[4][0:nn, nt_ * 128:(nt_ + 1) * 128], lhsT=Gv[:, nt_ * 128:nt_ * 128 + nn], rhs=w2[:, 2, :],
                        start=(nt_ == 0), stop=True, skip_group_check=True), reads=[Bc, BGv], writes=[Bps[4]])
                P.op("act", lambda e: e.activation(out=vcb[:, 0, :], in_=psum[4][:, 0:128], func=AF.Copy), reads=[Bps[4]], writes=[Bvcb])
                P.op("act", lambda e: e.activation(out=vcb[0:127, 1, :], in_=psum[4][0:127, 128:256], func=AF.Copy), reads=[Bps[4]], writes=[Bvcb])

                for i in range(8):
                    S = 4 * i + 3
                    tok0 = i * 128
                    P.op("pool", lambda e, tok0=tok0, g=g: e.dma_start(out=qtm[:], in_=q_s[tok0:tok0 + 128, g * 768:(g + 1) * 768]),
                         writes=[Bq], dma=True)
                    P.op("sp", lambda e, tok0=tok0, g=g: e.dma_start(out=gt[:], in_=gate_s[tok0:tok0 + 128, g * 18:(g + 1) * 18]),
                         writes=[Bgt], dma=True)
                    for h in range(6):
                        P.op("pe", lambda e, h=h: e.transpose(out=psb[:, h * 128:(h + 1) * 128], in_=qtm[:, h * 128:(h + 1) * 128],
                                                              identity=identb[:, 0:128]), reads=[Bq, Bc], writes=[Bpsb])
                    P.op("act", lambda e: e.activation(out=qT[:], in_=psb[:, 0:768], func=AF.Copy), reads=[Bpsb], writes=[BqT])
                    def scr(h):
                        return psum[h // 2][:, (h % 2) * 256:(h % 2) * 256 + 255]

                    def ocr(h):
                        return psum[3 + h // 4][:, (h % 4) * 128:(h % 4) * 128 + 128]
                    for h in range(6):
                        wr = [Bps[h // 2]]
                        rd = [BqT, Bkcb]
                        P.op("pe", lambda e, h=h: e.matmul(scr(h), lhsT=qT[:, h * 128:(h + 1) * 128], rhs=kcbT[:, 0:255],
                                                           start=True, stop=True, skip_group_check=True), reads=rd, writes=wr)
                    for h in range(6):
                        P.op("dve", lambda e, h=h: e.reduce_max(out=smh[:, h, 0:1], in_=scr(h), axis=AX.X), reads=[Bps[h // 2]], writes=[Bsmh[h]])
                    for h in range(6):
                        P.op("dve", lambda e, h=h: e.tensor_scalar(out=smh[:, h, 1:2], in0=smh[:, h, 0:1], scalar1=-SCALE, scalar2=None, op0=ALU.mult),
                             reads=[Bsmh[h]], writes=[Bsmh[h]])
                    for h in range(6):
                        P.op("act", lambda e, h=h: e.activation(out=evh[:, h, 0:255], in_=scr(h), func=AF.Exp, bias=smh[:, h, 1:2], scale=SCALE),
                             reads=[Bps[h // 2], Bsmh[h]], writes=[Bevh[h]])
                    for h in range(6):
                        P.op("dve", lambda e, h=h, i=i: e.tensor_tensor(out=evh[:, h, 0:255], in0=evh[:, h, 0:255], in1=cval[:, i, 0:255], op=ALU.mult),
                             reads=[Bevh[h], Bc], writes=[Bevh[h]])
                    for h in range(6):
                        P.op("dve", lambda e, h=h: e.reduce_sum(out=smh[:, h, 2:3], in_=evh[:, h, 0:255], axis=AX.X), reads=[Bevh[h]], writes=[Bsmh[h]])
                    for h in range(6):
                        P.op("dve", lambda e, h=h: e.tensor_scalar(out=smh[:, h, 2:3], in0=smh[:, h, 2:3], scalar1=1e-30, scalar2=None, op0=ALU.max),
                             reads=[Bsmh[h]], writes=[Bsmh[h]])
                    for h in range(6):
                        P.op("dve", lambda e, h=h: e.reciprocal(out=smh[:, h, 3:4], in_=smh[:, h, 2:3]), reads=[Bsmh[h]], writes=[Bsmh[h]])
                    for h in range(6):
                        P.op("dve", lambda e, h=h: e.tensor_scalar(out=evh[:, h, 0:255], in0=evh[:, h, 0:255], scalar1=smh[:, h, 3:4], scalar2=None, op0=ALU.mult),
                             reads=[Bevh[h], Bsmh[h]], writes=[Bevh[h]])
                    for h in range(6):
                        P.op("act", lambda e, h=h: e.activation(out=Pbh[:, h, 0:255], in_=evh[:, h, 0:255], func=AF.Copy), reads=[Bevh[h]], writes=[BPbh[h]])
                    for h in range(6):
                        if h == 0:
                            P.op("dve", lambda e: e.tensor_copy(out=Pp[:, 1:256], in_=evh[:, 0, 0:255]), reads=[Bevh[0]], writes=[BPp])
                        else:
                            P.op("dve", lambda e, h=h: e.tensor_tensor(out=Pp[:, 1:256], in0=Pp[:, 1:256], in1=evh[:, h, 0:255], op=ALU.add),
                                 reads=[Bevh[h], BPp], writes=[BPp])
                    for (h0, h1) in ((0, 4), (4, 6)):
                        for h in range(h0, h1):
                            for nt_, nn in ((0, 128), (1, 127)):
                                P.op("pe", lambda e, h=h, h0=h0, nt_=nt_, nn=nn: e.transpose(
                                    out=psb[0:nn, (h - h0) * 256 + nt_ * 128:(h - h0) * 256 + (nt_ + 1) * 128],
                                    in_=Pbh[:, h, nt_ * 128:nt_ * 128 + nn], identity=identb[:, 0:128]), reads=[BPbh[h], Bc], writes=[Bpsb])
                        for h in range(h0, h1):
                            P.op("act", lambda e, h=h, h0=h0: e.activation(out=PTh[:, h, :], in_=psb[:, (h - h0) * 256:(h - h0 + 1) * 256], func=AF.Copy),
                                 reads=[Bpsb], writes=[BPTh[h]])
                    for h in range(6):
                        wr = [Bps[3 + h // 4]]
                        rd = [BPTh[h], Bvcb]
                        for nt_, nn in ((0, 128), (1, 127)):
                            P.op("pe", lambda e, h=h, nt_=nt_, nn=nn: e.matmul(ocr(h), lhsT=PTh[0:nn, h, nt_ * 128:(nt_ + 1) * 128], rhs=vcb[0:nn, nt_, :],
                                                                               start=(nt_ == 0), stop=(nt_ == 1), skip_group_check=True), reads=rd, writes=wr)
                    for h in range(6):
                        P.op("dve", lambda e, h=h: e.tensor_scalar(out=acc[:, h * 128:(h + 1) * 128], in0=ocr(h),
                                                                   scalar1=gt[:, 3 * h:3 * h + 1], scalar2=None, op0=ALU.mult),
                             reads=[Bps[3 + h // 4], Bgt], writes=[Bacc])
                    P.op("dve", lambda e: e.tensor_reduce(out=imp[:, :], in_=Pp[:, 0:256].rearrange("p (s j) -> p s j", j=4), axis=AX.X, op=ALU.add),
                         reads=[BPp], writes=[Bimp])
                    P.op("dve", lambda e: e.tensor_tensor(out=imp[:, :], in0=imp[:, :], in1=Pp[:, 4:260:4], op=ALU.add), reads=[BPp, Bimp], writes=[Bimp])
                    P.op("dve", lambda e, i=i: e.tensor_tensor(out=imp[:, :], in0=imp[:, :], in1=addm[:, i, :], op=ALU.add), reads=[Bimp, Bc], writes=[Bimp])
                    P.op("dve", lambda e: e.max(out=m8[:, 0:8], in_=imp[:, :]), reads=[Bimp], writes=[Bsm])
                    P.op("dve", lambda e: e.match_replace(out=imw[:, :], in_to_replace=m8[:, 0:8], in_values=imp[:, :], imm_value=-3.0e38),
                         reads=[Bimp, Bsm], writes=[Bimp])
                    P.op("dve", lambda e: e.max(out=m8[:, 8:16], in_=imw[:, :]), reads=[Bimp], writes=[Bsm])
                    P.op("dve", lambda e: e.tensor_scalar(out=imw[:, :], in0=imp[:, :], scalar1=m8[:, 15:16], scalar2=None, op0=ALU.is_ge),
                         reads=[Bimp, Bsm], writes=[Bimp])
                    P.op("dve", lambda e, i=i: e.tensor_tensor(out=imw[:, :], in0=imw[:, :], in1=vblk[:, i, :], op=ALU.mult), reads=[Bimp, Bc], writes=[Bimp])
                    P.op("dve", lambda e: e.tensor_scalar(out=negb[:, :], in0=imw[:, :], scalar1=-1.0, scalar2=-NEG, op0=ALU.add, op1=ALU.mult),
                         reads=[Bimp], writes=[Bneg])
                    P.op("dve", lambda e: e.tensor_copy(out=negx[:, :].rearrange("p (s j) -> p s j", j=64),
                                                        in_=negb[:, :].unsqueeze(2).to_broadcast([128, 64, 64])), reads=[Bneg], writes=[Bnegx])
                    pti = 0
                    for br in (1, 2):
                        if br == 1:
                            kts = list(range(0, S + 1))
                            KT, VV, BK, BV = XT[2], V1[3], BXT[2], BV1[3]
                        else:
                            kts = list(range(max(S - 4, 0), S + 1))
                            KT, VV, BK, BV = XT[4], V1[5], BXT[4], BV1[5]
                        items = []
                        for ki, kt in enumerate(kts):
                            biases = []
                            if br == 1:
                                biases.append(negx[:, kt * 128:(kt + 1) * 128])
                            if kt == S:
                                biases.append(cb_c[:, :])
                            if br == 2 and kt == S - 4:
                                biases.append(cb_a[:, :])
                            for hh in range(2):
                                items.append((ki, kt, hh, biases))
                        pti0 = pti
                        pti += len(items)

                        def qk(n, items=items, KT=KT, BK=BK):
                            ki, kt, hh, biases = items[n]
                            ps_s = 2 + (n % 3)
                            P.op("pe", lambda e, ps_s=ps_s, kt=kt, hh=hh, KT=KT, nb=len(biases): e.matmul(
                                psum[ps_s][:, 0:384], lhsT=KT[:, kt * 128:(kt + 1) * 128], rhs=qT[:, hh * 384:(hh + 1) * 384],
                                start=True, stop=(nb == 0)), reads=[BK, BqT], writes=[Bps[ps_s]])
                            for bi, bias_ap in enumerate(biases):
                                P.op("pe", lambda e, ps_s=ps_s, bias_ap=bias_ap, last=(bi == len(biases) - 1): e.matmul(
                                    psum[ps_s][:, 0:384], lhsT=bias_ap, rhs=identb[:, 0:384], start=False, stop=last),
                                    reads=[Bnegx, Bc], writes=[Bps[ps_s]])

                        def expv(n, items=items, VV=VV, BV=BV, pti0=pti0, nk_=len(kts)):
                            ki, kt, hh, biases = items[n]
                            ps_s = 2 + (n % 3)
                            pt_ = (pti0 + n) % 4
                            P.op("act", lambda e, ps_s=ps_s, pt_=pt_, kt=kt: e.activation(
                                out=PTs[pt_][:, :], in_=psum[ps_s][:, 0:384], func=AF.Exp, bias=padb[:, kt:kt + 1], scale=SCALE),
                                reads=[Bps[ps_s], Bc], writes=[BPTs[pt_]])
                            for hl in range(3):
                                P.op("pe", lambda e, hh=hh, hl=hl, pt_=pt_, kt=kt, VV=VV, first=(ki == 0 and hl == 0), last=(ki == nk_ - 1): e.matmul(
                                    psum[5 + hh][:, hl * 130:(hl + 1) * 130], lhsT=PTs[pt_][:, hl * 128:(hl + 1) * 128], rhs=VV[:, kt, :],
                                    start=first, stop=last, skip_group_check=True), reads=[BPTs[pt_], BV], writes=[Bps[5 + hh]])

                        for n in range(min(2, len(items))):
                            qk(n)
                        for n in range(len(items)):
                            if n + 2 < len(items):
                                qk(n + 2)
                            expv(n)
                        for hh in range(2):
                            for hl in range(3):
                                h = hh * 3 + hl
                                P.op("dve", lambda e, hh=hh, hl=hl, h=h: e.reciprocal(out=sc6[:, h:h + 1], in_=psum[5 + hh][:, hl * 130 + 128:hl * 130 + 129]),
                                     reads=[Bps[5 + hh]], writes=[Bsc6])
                                P.op("dve", lambda e, h=h, br=br: e.tensor_tensor(out=sc6[:, h:h + 1], in0=sc6[:, h:h + 1], in1=gt[:, 3 * h + br:3 * h + br + 1], op=ALU.mult),
                                     reads=[Bsc6, Bgt], writes=[Bsc6])
                                P.op("dve", lambda e, hh=hh, hl=hl, h=h: e.scalar_tensor_tensor(
                                    out=acc[:, h * 128:(h + 1) * 128], in0=psum[5 + hh][:, hl * 130:hl * 130 + 128], scalar=sc6[:, h:h + 1],
                                    in1=acc[:, h * 128:(h + 1) * 128], op0=ALU.mult, op1=ALU.add), reads=[Bps[5 + hh], Bsc6, Bacc], writes=[Bacc])
                    P.op("act", lambda e: e.activation(out=accb[:], in_=acc[:], func=AF.Copy), reads=[Bacc], writes=[Baccb])
                    for h in range(6):
                        P.op("pe", lambda e, h=h: e.transpose(out=psb[:, h * 128:(h + 1) * 128], in_=accb[:, h * 128:(h + 1) * 128],
                                                              identity=identb[:, 0:128]), reads=[Baccb, Bc], writes=[Bpsb])
                    P.op("act", lambda e: e.activation(out=mixo[:, :, :].rearrange("p a b -> p (a b)"), in_=psb[:, 0:768], func=AF.Copy),
                         reads=[Bpsb], writes=[Bmixo])
                    P.op("sp", lambda e, g=g, tok0=tok0: e.dma_start(out=mixT_s[:, 8 + 6 * g:14 + 6 * g, tok0:tok0 + 128], in_=mixo[:, :, :]),
                         reads=[Bmixo], dma=True)
            P.emit_phase()

        with contextlib.ExitStack() as st:
            sb = lambda name, shape, dt: st.enter_context(nc.sbuf_tensor(uname(name), list(shape), dt))
            identb = sb("s_identb", [128, 128], BF16)
            identf = sb("s_identf", [128, 128], F32)
            id4x3 = sb("s_id4x3", [4, 12], BF16)
            cb_c = sb("s_cbc", [128, 128], BF16)
            cb_a = sb("s_cba", [128, 128], BF16)
            w1k = sb("s_w1k", [128, 32, 128], BF16)
            w1v = sb("s_w1v", [128, 32, 128], BF16)
            w2 = sb("s_w2", [128, 3, 128], BF16)
            peT = sb("s_peT", [128, 2, 32], BF16)
            hid0 = sb("s_hid0", [128, 2], F32)
            ccs = sb("s_ccs", [128, 512], F32)
            scs = sb("s_scs", [128, 512], F32)
            addms = sb("s_addms", [4, 129], F32)
            oh = sb("s_oh", [128, 5], F32)
            ptb_i = sb("s_ptb_i", [128, 64], I32)
            ptb_f = sb("s_ptb_f", [128, 64], F32)
            ptmp = sb("s_ptmp", [128, 64], F32)
            psel = sb("s_psel", [128, 16], F32)
            idx = sb("s_idx", [128, 16], I32)
            G32 = [sb("s_G32_%d" % i, [128, 2048], F32) for i in range(6)]
            XT = sb("s_XT", [128, 2, 4, 2048], BF16)
            V1 = sb("s_V1", [128, 2, 65, 130], BF16)
            XTn = sb("s_XTn", [128, 2, 4], BF16)
            kwT = sb("s_kwT", [128, 2, 516], BF16)
            V1w = sb("s_V1w", [128, 2, 5, 130], BF16)
            wtm = sb("s_wtm", [128, 4, 256], BF16)
            ntm = sb("s_ntm", [4, 256], BF16)
            kcbT = sb("s_kcbT", [128, 2, 512], BF16)
            vcb = sb("s_vcb", [128, 2, 4, 128], BF16)
            x32 = sb("s_x32", [128, 512], F32)
            x2 = sb("s_x2", [128, 512], F32)
            sgt = sb("s_sgt", [128, 512], F32)
            Gk = sb("s_Gk", [128, 512], BF16)
            Gv = sb("s_Gv", [128, 512], BF16)
            qtm = sb("s_qtm", [4, 768], BF16)
            qT = sb("s_qT", [128, 24], BF16)
            gt = sb("s_gt", [4, 18], F32)
            e32 = sb("s_e32", [4, 512], F32)
            ev = sb("s_ev", [4, 512], F32)
            Pp = sb("s_Pp", [4, 520], F32)
            Pb = sb("s_Pb", [4, 512], BF16)
            PT = sb("s_PT", [128, 4, 4], BF16)
            sm = sb("s_sm", [4, 8], F32)
            imp = sb("s_imp", [4, 129], F32)
            imw = sb("s_imw", [4, 129], F32)
            m8 = sb("s_m8", [4, 16], F32)
            negb = sb("s_negb", [4, 129], BF16)
            negx = sb("s_negx", [4, 16, 128], BF16)
            PTs = [sb("s_PTs%d" % i, [128, 12], BF16) for i in range(4)]
            acc = sb("s_acc", [4, 768], F32)
            accb = sb("s_accb", [4, 768], BF16)
            mixo = sb("s_mixo", [128, 6, 4], BF16)
            sc6 = sb("s_sc6", [4, 8], F32)
            smh = sb("s_smh", [4, 6, 4], F32)
            evh = sb("s_evh", [4, 6, 512], F32)
            Pbh = sb("s_Pbh", [4, 6, 512], BF16)
            PTh = sb("s_PTh", [128, 6, 16], BF16)
            Bsmh, Bevh, BPbh, BPTh, Boch = ([Buf() for _ in range(6)] for _ in range(5))

            Bc, Bidx = Buf(), Buf()
            BG32 = [Buf() for _ in range(6)]
            BXT, BV1, BXTn, BkwT, BV1w, Bwtm, Bntm = (Buf() for _ in range(7))
            Bkcb, Bvcb, Bx, BGk, BGv = Buf(), Buf(), Buf(), Buf(), Buf()
            Bq, BqT, Bgt, Be, BPp, BPb, BPT, Bsm, Bimp, Bneg, Bnegx = (Buf() for _ in range(11))
            BPTs = [Buf() for _ in range(4)]
            Bacc, Baccb, Bmixo, Bsc6 = Buf(), Buf(), Buf(), Buf()
            Bps = [Buf() for _ in range(7)]
            Bpsb = Buf()

            P.op("pool", lambda e: e.dma_start(out=identb[:], in_=ident[:, :]), writes=[Bc], dma=True)
            P.op("sp", lambda e: e.dma_start(out=identf[:], in_=ident[:, :]), writes=[Bc], dma=True)
            for j3 in range(3):
                P.op("pool", lambda e, j3=j3: e.dma_start(out=id4x3[:, j3 * 4:(j3 + 1) * 4], in_=ident[0:4, 0:4]), writes=[Bc], dma=True)
            P.op("pool", lambda e: e.dma_start(out=cb_c[:], in_=cbc_d[:, :]), writes=[Bc], dma=True)
            P.op("pool", lambda e: e.dma_start(out=cb_a[:], in_=cba_d[:, :]), writes=[Bc], dma=True)
            P.op("pool", lambda e: e.dma_start(out=w1k[:], in_=w1k_d[:, :, :]), writes=[Bc], dma=True)
            P.op("pool", lambda e: e.dma_start(out=w1v[:], in_=w1v_d[:, :, :]), writes=[Bc], dma=True)
            P.op("pool", lambda e: e.dma_start(out=w2[:], in_=w2_d[:, :, :]), writes=[Bc], dma=True)
            P.op("pool", lambda e: e.dma_start(out=peT[:], in_=peT_d[:, :, :]), writes=[Bc], dma=True)
            P.op("sp", lambda e: e.dma_start(out=ccs[:], in_=ccs_d[:, :]), writes=[Bc], dma=True)
            P.op("sp", lambda e: e.dma_start(out=scs[:], in_=scs_d[:, :]), writes=[Bc], dma=True)
            P.op("sp", lambda e: e.dma_start(out=addms[:], in_=addms_d[:, :]), writes=[Bc], dma=True)
            P.op("sp", lambda e: e.dma_start(out=oh[:], in_=oh_d[:, :]), writes=[Bc], dma=True)
            P.op("dve", lambda e: e.memset(V1[:, :, :, 128:130], 1.0), writes=[BV1])
            P.op("dve", lambda e: e.memset(V1w[:, :, :, 128:130], 1.0), writes=[BV1w])
            P.op("dve", lambda e: e.memset(Pp[:], 0.0), writes=[BPp])
            P.op("dve", lambda e: e.memset(vcb[:], 0.0), writes=[Bvcb])
            for vi, w1t in enumerate((w1k, w1v)):
                for pp in range(32):
                    P.op("pe", lambda e, vi=vi, w1t=w1t, pp=pp: e.matmul(
                        psum[0][:, vi:vi + 1], lhsT=w1t[:, pp, :], rhs=peT[:, vi, pp:pp + 1],
                        start=(pp == 0 and vi == 0), stop=(pp == 31), skip_group_check=True), reads=[Bc], writes=[Bps[0]])
            P.op("act", lambda e: e.activation(out=hid0[:, 0:2], in_=psum[0][:, 0:2], func=AF.Copy), reads=[Bps[0]], writes=[Bc])

            caches = (ckc_d, cvc_d, cks_d, cvs_d)
            bc_reg = {}
            gi_ = 0
            tpi = 0
            for sbi in range(4):
                tokS = 1024 + 4 * sbi
                rowS = 4096 + 4 * sbi
                P.op("sp", lambda e, sbi=sbi: e.dma_start(out=ptb_i[:], in_=ptab_d[sbi, :].partition_broadcast(128)), writes=[Bidx], dma=True)
                P.op("dve", lambda e: e.tensor_copy(out=ptb_f[:], in_=ptb_i[:]), reads=[Bidx], writes=[Bidx])
                P.op("dve", lambda e: e.tensor_tensor(out=ptmp[:, :].rearrange("p (c a) -> p c a", a=4),
                                                      in0=ptb_f[:, :].rearrange("p (c a) -> p c a", a=4),
                                                      in1=oh[:, 0:4].unsqueeze(1).to_broadcast([128, 16, 4]), op=ALU.mult),
                     reads=[Bidx, Bc], writes=[Bidx])
                P.op("dve", lambda e: e.tensor_reduce(out=psel[:, :], in_=ptmp[:, :].rearrange("p (c a) -> p c a", a=4), axis=AX.X, op=ALU.add),
                     reads=[Bidx], writes=[Bidx])
                P.op("dve", lambda e: e.tensor_scalar(out=psel[:, :], in0=psel[:, :], scalar1=32.0, scalar2=oh[:, 4:5], op0=ALU.mult, op1=ALU.add),
                     reads=[Bidx, Bc], writes=[Bidx])
                P.op("dve", lambda e: e.tensor_copy(out=idx[:, :], in_=psel[:, :]), reads=[Bidx], writes=[Bidx])
                for gp in range(2):
                    for ci, cache_d in enumerate(caches):
                        for c in range(16):
                            gb = gi_ % 6
                            gi_ += 1
                            def gather_fn(e, gb=gb, c=c, cache_d=cache_d):
                                if "r" not in bc_reg:
                                    bc_reg["r"] = e.to_reg(2560 * 32 - 1)
                                return e.indirect_dma_start(
                                    out=G32[gb][:, :], out_offset=None, in_=cache_d[:, :],
                                    in_offset=bass.IndirectOffsetOnAxis(ap=idx[:, c:c + 1], axis=0),
                                    bounds_check=bc_reg["r"], oob_is_err=False)
                            P.op("pool", gather_fn, reads=[Bidx], writes=[BG32[gb]], dma=True)
                            if ci < 3:
                                for gl in range(2):
                                    g = 2 * gp + gl
                                    pb_ = 2 + (tpi % 2)
                                    tpi += 1
                                    for r4 in range(4):
                                        P.op("pe", lambda e, pb_=pb_, r4=r4, g=g, gb=gb: e.transpose(
                                            out=psum[pb_][:, r4 * 128:(r4 + 1) * 128], in_=G32[gb][:, r4 * 512 + g * 128:r4 * 512 + (g + 1) * 128],
                                            identity=identf[:, :]), reads=[BG32[gb], Bc], writes=[Bps[pb_]])
                                    eng = "act" if tpi % 2 == 0 else "dve"
                                    src = psum[pb_][:, :].rearrange("p (r q) -> p r q", q=128)
                                    if eng == "act":
                                        P.op("act", lambda e, gl=gl, c=c, src=src: e.activation(out=XT[:, gl, :, 128 * c:128 * (c + 1)], in_=src, func=AF.Copy),
                                             reads=[Bps[pb_]], writes=[BXT])
                                    else:
                                        P.op("dve", lambda e, gl=gl, c=c, src=src: e.tensor_copy(out=XT[:, gl, :, 128 * c:128 * (c + 1)], in_=src),
                                             reads=[Bps[pb_]], writes=[BXT])
                            else:
                                for gl in range(2):
                                    g = 2 * gp + gl
                                    src = G32[gb][:, :].rearrange("p (r q) -> p r q", q=512)[:, :, g * 128:(g + 1) * 128]
                                    eng = "act" if gl == 0 else "dve"
                                    if eng == "act":
                                        P.op("act", lambda e, gl=gl, c=c, src=src: e.activation(out=V1[:, gl, 4 * c:4 * c + 4, 0:128], in_=src, func=AF.Copy),
                                             reads=[BG32[gb]], writes=[BV1])
                                    else:
                                        P.op("dve", lambda e, gl=gl, c=c, src=src: e.tensor_copy(out=V1[:, gl, 4 * c:4 * c + 4, 0:128], in_=src),
                                             reads=[BG32[gb]], writes=[BV1])
                        if ci < 2:
                            w1t = w1k if ci == 0 else w1v
                            Gt, BG = (Gk, BGk) if ci == 0 else (Gv, BGv)
                            for gl in range(2):
                                mi = 0
                                for a_ in range(2):
                                    for j4 in range(4):
                                        for r4 in range(4):
                                            c_0 = 4 * a_ + j4
                                            P.op("pe", lambda e, w1t=w1t, a_=a_, j4=j4, r4=r4, gl=gl, c_0=c_0, mi=mi: e.matmul(
                                                psum[1][:, 0:511], lhsT=w1t[:, a_ * 16 + 4 * j4 + r4, :],
                                                rhs=XT[:, gl, r4, c_0:c_0 + 4 * 510 + 1:4], start=(mi == 0), stop=(mi == 31)),
                                                reads=[Bc, BXT], writes=[Bps[1]])
                                            mi += 1
                                gelu_cols(P, psum[1][:, 0:511], hid0[:, ci:ci + 1], 511, x32, x2, sgt, Gt[:, 0:511], Bx, Bps[1], BG, rd=[Bc])
                                if ci == 0:
                                    P.op("pe", lambda e: e.matmul(psum[4][:, 0:511], lhsT=w2[:, 0, :], rhs=Gk[:, 0:511], start=True, stop=True),
                                         reads=[Bc, BGk], writes=[Bps[4]])
                                    P.op("pe", lambda e: e.matmul(psum[5][:, 0:511], lhsT=w2[:, 1, :], rhs=Gk[:, 0:511], start=True, stop=True),
                                         reads=[Bc, BGk], writes=[Bps[5]])
                                    P.op("dve", lambda e: e.tensor_tensor(out=x32[:, 0:511], in0=psum[4][:, 0:511], in1=ccs[:, 0:511], op=ALU.mult),
                                         reads=[Bps[4], Bc, Bx], writes=[Bx])
                                    P.op("dve", lambda e: e.tensor_tensor(out=x2[:, 0:511], in0=psum[5][:, 0:511], in1=scs[:, 0:511], op=ALU.mult),
                                         reads=[Bps[5], Bc, Bx], writes=[Bx])
                                    P.op("dve", lambda e, gl=gl: e.tensor_tensor(out=kcbT[:, gl, 0:511], in0=x32[:, 0:511], in1=x2[:, 0:511], op=ALU.add),
                                         reads=[Bx], writes=[Bkcb])
                                else:
                                    for nt_, nn in ((0, 128), (1, 128), (2, 128), (3, 127)):
                                        P.op("pe", lambda e, nt_=nt_, nn=nn: e.matmul(
                                            psum[4][0:nn, nt_ * 128:(nt_ + 1) * 128], lhsT=Gv[:, nt_ * 128:nt_ * 128 + nn], rhs=w2[:, 2, :],
                                            start=(nt_ == 0), stop=True, skip_group_check=True), reads=[Bc, BGv], writes=[Bps[4]])
                                    for nt_, nn in ((0, 128), (1, 128), (2, 128), (3, 127)):
                                        P.op("act", lambda e, nt_=nt_, nn=nn, gl=gl: e.activation(
                                            out=vcb[0:nn, gl, nt_, :], in_=psum[4][0:nn, nt_ * 128:(nt_ + 1) * 128], func=AF.Copy),
                                            reads=[Bps[4], Bvcb], writes=[Bvcb])
                    P.op("pool", lambda e, rowS=rowS, gp=gp: e.dma_start(out=ntm[:, :], in_=kvout[2, rowS:rowS + 4, gp * 256:(gp + 1) * 256]),
                         writes=[Bntm], dma=True)
                    for gl in range(2):
                        P.op("pe", lambda e, gl=gl: e.transpose(out=psb[:, gl * 4:gl * 4 + 4], in_=ntm[0:4, gl * 128:(gl + 1) * 128], identity=identb[0:4, 0:4]),
                             reads=[Bntm, Bc], writes=[Bpsb])
                    P.op("act", lambda e: e.activation(out=XTn[:, :, :].rearrange("p a b -> p (a b)"), in_=psb[:, 0:8], func=AF.Copy),
                         reads=[Bpsb], writes=[BXTn])
                    for gl in range(2):
                        g = 2 * gp + gl
                        P.op("pool", lambda e, rowS=rowS, g=g, gl=gl: e.dma_start(out=V1[0:4, gl, 64, 0:128], in_=kvout[3, rowS:rowS + 4, g * 128:(g + 1) * 128]),
                             writes=[BV1], dma=True)
                        P.op("pool", lambda e, rowS=rowS, g=g, gl=gl: e.dma_start(out=V1w[0:4, gl, 4, 0:128], in_=kvout[5, rowS:rowS + 4, g * 128:(g + 1) * 128]),
                             writes=[BV1w], dma=True)
                        P.op("pool", lambda e, sbi=sbi, g=g, gl=gl: e.dma_start(
                            out=V1w[:, gl, 0:4, 0:128], in_=st_vw[sbi, :, g * 128:(g + 1) * 128].rearrange("(i p) d -> p i d", p=128)),
                            writes=[BV1w], dma=True)
                    P.op("pool", lambda e, sbi=sbi, gp=gp: e.dma_start(
                        out=wtm[:, :, :], in_=st_kw[sbi, :, gp * 256:(gp + 1) * 256].rearrange("(i p) d -> p i d", p=128)),
                        writes=[Bwtm], dma=True)
                    for gl in range(2):
                        for w_ in range(4):
                            P.op("pe", lambda e, gl=gl, w_=w_: e.transpose(out=psb[:, (gl * 4 + w_) * 128:(gl * 4 + w_ + 1) * 128],
                                                                           in_=wtm[:, w_, gl * 128:(gl + 1) * 128], identity=identb[:, :]),
                                 reads=[Bwtm, Bc], writes=[Bpsb])
                    P.op("act", lambda e: e.activation(out=kwT[:, :, 0:512], in_=psb[:, :].rearrange("p (a b) -> p a b", a=2), func=AF.Copy),
                         reads=[Bpsb], writes=[BkwT])
                    P.op("pool", lambda e, rowS=rowS, gp=gp: e.dma_start(out=ntm[:, :], in_=kvout[4, rowS:rowS + 4, gp * 256:(gp + 1) * 256]),
                         writes=[Bntm], dma=True)
                    for gl in range(2):
                        P.op("pe", lambda e, gl=gl: e.transpose(out=psb[:, gl * 4:gl * 4 + 4], in_=ntm[0:4, gl * 128:(gl + 1) * 128], identity=identb[0:4, 0:4]),
                             reads=[Bntm, Bc], writes=[Bpsb])
                    P.op("act", lambda e: e.activation(out=kwT[:, :, 512:516], in_=psb[:, 0:8].rearrange("p (a b) -> p a b", a=2), func=AF.Copy),
                         reads=[Bpsb, BkwT], writes=[BkwT])

                    for gl in range(2):
                        g = 2 * gp + gl
                        P.op("pool", lambda e, tokS=tokS, g=g: e.dma_start(out=qtm[:], in_=q_s[tokS:tokS + 4, g * 768:(g + 1) * 768]),
                             writes=[Bq], dma=True)
                        P.op("sp", lambda e, tokS=tokS, g=g: e.dma_start(out=gt[:], in_=gate_s[tokS:tokS + 4, g * 18:(g + 1) * 18]),
                             writes=[Bgt], dma=True)
                        for h in range(6):
                            P.op("pe", lambda e, h=h: e.transpose(out=psb[:, h * 4:(h + 1) * 4], in_=qtm[0:4, h * 128:(h + 1) * 128],
                                                                  identity=identb[0:4, 0:4]), reads=[Bq, Bc], writes=[Bpsb])
                        P.op("act", lambda e: e.activation(out=qT[:], in_=psb[:, 0:24], func=AF.Copy), reads=[Bpsb], writes=[BqT])
                        NT4 = ((0, 128), (1, 128), (2, 128), (3, 127))
                        for rnd in range(2):
                            hs = [3 * rnd + k for k in range(3)]
                            for h in hs:
                                P.op("pe", lambda e, h=h, gl=gl: e.matmul(psum[h % 3][0:4, 0:511], lhsT=qT[:, h * 4:(h + 1) * 4], rhs=kcbT[:, gl, 0:511],
                                                                          start=True, stop=True), reads=[BqT, Bkcb], writes=[Bps[h % 3]])
                            for h in hs:
                                P.op("dve", lambda e, h=h: e.reduce_max(out=smh[:, h, 0:1], in_=psum[h % 3][0:4, 0:511], axis=AX.X), reads=[Bps[h % 3]], writes=[Bsmh[h]])
                            for h in hs:
                                P.op("dve", lambda e, h=h: e.tensor_scalar(out=smh[:, h, 1:2], in0=smh[:, h, 0:1], scalar1=-SCALE, scalar2=None, op0=ALU.mult),
                                     reads=[Bsmh[h]], writes=[Bsmh[h]])
                            for h in hs:
                                P.op("act", lambda e, h=h: e.activation(out=evh[:, h, 0:511], in_=psum[h % 3][0:4, 0:511], func=AF.Exp, bias=smh[:, h, 1:2], scale=SCALE),
                                     reads=[Bps[h % 3], Bsmh[h]], writes=[Bevh[h]])
                            for h in hs:
                                P.op("dve", lambda e, h=h: e.reduce_sum(out=smh[:, h, 2:3], in_=evh[:, h, 0:511], axis=AX.X), reads=[Bevh[h]], writes=[Bsmh[h]])
                            for h in hs:
                                P.op("dve", lambda e, h=h: e.tensor_scalar(out=smh[:, h, 2:3], in0=smh[:, h, 2:3], scalar1=1e-30, scalar2=None, op0=ALU.max),
                                     reads=[Bsmh[h]], writes=[Bsmh[h]])
                            for h in hs:
                                P.op("dve", lambda e, h=h: e.reciprocal(out=smh[:, h, 3:4], in_=smh[:, h, 2:3]), reads=[Bsmh[h]], writes=[Bsmh[h]])
                            for h in hs:
                                P.op("dve", lambda e, h=h: e.tensor_scalar(out=evh[:, h, 0:511], in0=evh[:, h, 0:511], scalar1=smh[:, h, 3:4], scalar2=None, op0=ALU.mult),
                                     reads=[Bevh[h], Bsmh[h]], writes=[Bevh[h]])
                            for h in hs:
                                P.op("act", lambda e, h=h: e.activation(out=Pbh[:, h, 0:511], in_=evh[:, h, 0:511], func=AF.Copy), reads=[Bevh[h]], writes=[BPbh[h]])
                            for h in hs:
                                if h == 0:
                                    P.op("dve", lambda e: e.tensor_copy(out=Pp[:, 1:512], in_=evh[:, 0, 0:511]), reads=[Bevh[0]], writes=[BPp])
                                else:
                                    P.op("dve", lambda e, h=h: e.tensor_tensor(out=Pp[:, 1:512], in0=Pp[:, 1:512], in1=evh[:, h, 0:511], op=ALU.add),
                                         reads=[Bevh[h], BPp], writes=[BPp])
                            for h in hs:
                                for nt_, nn in NT4:
                                    P.op("pe", lambda e, h=h, nt_=nt_, nn=nn: e.transpose(out=psb[0:nn, h * 16 + nt_ * 4:h * 16 + nt_ * 4 + 4],
                                                                                          in_=Pbh[0:4, h, nt_ * 128:nt_ * 128 + nn], identity=identb[0:4, 0:4]),
                                         reads=[BPbh[h], Bc], writes=[Bpsb])
                            for h in hs:
                                P.op("act", lambda e, h=h: e.activation(out=PTh[:, h, :], in_=psb[:, h * 16:(h + 1) * 16], func=AF.Copy),
                                     reads=[Bpsb], writes=[BPTh[h]])
                            for h in hs:
                                first = (h % 3 == 0)
                                wr = [Bps[3 + rnd]]
                                rd = [BPTh[h], Bvcb]
                                for nt_, nn in NT4:
                                    P.op("pe", lambda e, h=h, rnd=rnd, nt_=nt_, nn=nn, gl=gl: e.matmul(
                                        psum[3 + rnd][0:4, (h % 3) * 128:(h % 3) * 128 + 128], lhsT=PTh[0:nn, h, nt_ * 4:nt_ * 4 + 4], rhs=vcb[0:nn, gl, nt_, :],
                                        start=(nt_ == 0), stop=(nt_ == 3), skip_group_check=True), reads=rd, writes=wr)
                            for h in hs:
                                P.op("dve", lambda e, h=h, rnd=rnd: e.tensor_scalar(out=acc[:, h * 128:(h + 1) * 128], in0=psum[3 + rnd][0:4, (h % 3) * 128:(h % 3) * 128 + 128],
                                                                                   scalar1=gt[:, 3 * h:3 * h + 1], scalar2=None, op0=ALU.mult),
                                     reads=[Bps[3 + rnd], Bgt], writes=[Bacc])
                        P.op("dve", lambda e: e.tensor_reduce(out=imp[:, :], in_=Pp[:, 0:516].rearrange("p (s j) -> p s j", j=4), axis=AX.X, op=ALU.add),
                             reads=[BPp], writes=[Bimp])
                        P.op("dve", lambda e: e.tensor_tensor(out=imp[:, :], in0=imp[:, :], in1=Pp[:, 4:517:4], op=ALU.add), reads=[BPp, Bimp], writes=[Bimp])
                        P.op("dve", lambda e: e.tensor_tensor(out=imp[:, :], in0=imp[:, :], in1=addms[:, :], op=ALU.add), reads=[Bimp, Bc], writes=[Bimp])
                        P.op("dve", lambda e: e.max(out=m8[:, 0:8], in_=imp[:, :]), reads=[Bimp], writes=[Bsm])
                        P.op("dve", lambda e: e.match_replace(out=imw[:, :], in_to_replace=m8[:, 0:8], in_values=imp[:, :], imm_value=-3.0e38),
                             reads=[Bimp, Bsm], writes=[Bimp])
                        P.op("dve", lambda e: e.max(out=m8[:, 8:16], in_=imw[:, :]), reads=[Bimp], writes=[Bsm])
                        P.op("dve", lambda e: e.tensor_scalar(out=imw[:, :], in0=imp[:, :], scalar1=m8[:, 15:16], scalar2=None, op0=ALU.is_ge),
                             reads=[Bimp, Bsm], writes=[Bimp])
                        P.op("dve", lambda e: e.tensor_scalar(out=negb[:, :], in0=imw[:, :], scalar1=-1.0, scalar2=-NEG, op0=ALU.add, op1=ALU.mult),
                             reads=[Bimp], writes=[Bneg])
                        P.op("dve", lambda e: e.tensor_copy(out=negx[:, :, :].rearrange("p c (b k) -> p c b k", k=16),
                                                            in_=negb[:, 0:128].rearrange("p (c b) -> p c b", b=8).unsqueeze(3).to_broadcast([4, 16, 8, 16])),
                             reads=[Bneg], writes=[Bnegx])
                        pti = 0
                        for br in (1, 2):
                            tiles = []
                            if br == 1:
                                for c in range(16):
                                    for r4 in range(4):
                                        tiles.append((XT[:, gl, r4, 128 * c:128 * (c + 1)], V1[:, gl, 4 * c + r4, :], 128, [negx[0:4, c, :]]))
                                tiles.append((XTn[:, gl, :], V1[0:4, gl, 64, :], 4, [cb_c[0:4, 0:4]]))
                                BK, BV = [BXT, BXTn], BV1
                            else:
                                for w_ in range(4):
                                    tiles.append((kwT[:, gl, 128 * w_:128 * (w_ + 1)], V1w[:, gl, w_, :], 128, [cb_a[0:4, 0:128]] if w_ == 0 else []))
                                tiles.append((kwT[:, gl, 512:516], V1w[0:4, gl, 4, :], 4, [cb_c[0:4, 0:4]]))
                                BK, BV = [BkwT], BV1w
                            items = []
                            for ki, (kap, vap, nk, biases) in enumerate(tiles):
                                for hh in range(2):
                                    items.append((ki, kap, vap, nk, biases, hh))
                            pti0 = pti
                            pti += len(items)

                            def qk(n, items=items, BK=BK):
                                ki, kap, vap, nk, biases, hh = items[n]
                                ps_s = 2 + (n % 3)
                                P.op("pe", lambda e, ps_s=ps_s, kap=kap, nk=nk, hh=hh, nb=len(biases): e.matmul(
                                    psum[ps_s][0:nk, 0:12], lhsT=kap, rhs=qT[:, hh * 12:(hh + 1) * 12],
                                    start=True, stop=(nb == 0)), reads=BK + [BqT], writes=[Bps[ps_s]])
                                for bi, bias_ap in enumerate(biases):
                                    P.op("pe", lambda e, ps_s=ps_s, nk=nk, bias_ap=bias_ap, last=(bi == len(biases) - 1): e.matmul(
                                        psum[ps_s][0:nk, 0:12], lhsT=bias_ap, rhs=id4x3[:, :], start=False, stop=last),
                                        reads=[Bnegx, Bc], writes=[Bps[ps_s]])

                            def expv(n, items=items, BV=BV, pti0=pti0, nt_=len(tiles)):
                                ki, kap, vap, nk, biases, hh = items[n]
                                ps_s = 2 + (n % 3)
                                pt_ = (pti0 + n) % 4
                                P.op("act", lambda e, ps_s=ps_s, pt_=pt_, nk=nk: e.activation(
                                    out=PTs[pt_][0:nk, :], in_=psum[ps_s][0:nk, 0:12], func=AF.Exp, scale=SCALE),
                                    reads=[Bps[ps_s]], writes=[BPTs[pt_]])
                                for hl in range(3):
                                    P.op("pe", lambda e, hh=hh, hl=hl, pt_=pt_, nk=nk, vap=vap, first=(ki == 0 and hl == 0), last=(ki == nt_ - 1): e.matmul(
                                        psum[5 + hh][0:4, hl * 130:(hl + 1) * 130], lhsT=PTs[pt_][0:nk, hl * 4:(hl + 1) * 4], rhs=vap,
                                        start=first, stop=last, skip_group_check=True), reads=[BPTs[pt_], BV], writes=[Bps[5 + hh]])

                            for n in range(min(2, len(items))):
                                qk(n)
                            for n in range(len(items)):
                                if n + 2 < len(items):
                                    qk(n + 2)
                                expv(n)
                            for hh in range(2):
                                for hl in range(3):
                                    h = hh * 3 + hl
                                    P.op("dve", lambda e, hh=hh, hl=hl, h=h: e.reciprocal(out=sc6[:, h:h + 1], in_=psum[5 + hh][0:4, hl * 130 + 128:hl * 130 + 129]),
                                         reads=[Bps[5 + hh]], writes=[Bsc6])
                                    P.op("dve", lambda e, h=h, br=br: e.tensor_tensor(out=sc6[:, h:h + 1], in0=sc6[:, h:h + 1], in1=gt[:, 3 * h + br:3 * h + br + 1], op=ALU.mult),
                                         reads=[Bsc6, Bgt], writes=[Bsc6])
                                    P.op("dve", lambda e, hh=hh, hl=hl, h=h: e.scalar_tensor_tensor(
                                        out=acc[:, h * 128:(h + 1) * 128], in0=psum[5 + hh][0:4, hl * 130:hl * 130 + 128], scalar=sc6[:, h:h + 1],
                                        in1=acc[:, h * 128:(h + 1) * 128], op0=ALU.mult, op1=ALU.add), reads=[Bps[5 + hh], Bsc6, Bacc], writes=[Bacc])
                        P.op("act", lambda e: e.activation(out=accb[:], in_=acc[:], func=AF.Copy), reads=[Bacc], writes=[Baccb])
                        for h in range(6):
                            P.op("pe", lambda e, h=h: e.transpose(out=psb[:, h * 4:(h + 1) * 4], in_=accb[0:4, h * 128:(h + 1) * 128],
                                                                  identity=identb[0:4, 0:4]), reads=[Baccb, Bc], writes=[Bpsb])
                        P.op("act", lambda e: e.activation(out=mixo[:, :, :].rearrange("p a b -> p (a b)"), in_=psb[:, 0:24], func=AF.Copy),
                             reads=[Bpsb], writes=[Bmixo])
                        P.op("sp", lambda e, g=g, tokS=tokS: e.dma_start(out=mixT_s[:, 8 + 6 * g:14 + 6 * g, tokS:tokS + 4], in_=mixo[:, :, :]),
                             reads=[Bmixo], dma=True)
            P.emit_phase()

        with contextlib.ExitStack() as st:
            sb = lambda name, shape, dt: st.enter_context(nc.sbuf_tensor(uname(name), list(shape), dt))
            mixT = sb("mixT", [128, 32, TOK], BF16)
            wt = [sb("wo%d" % i, [128, 32, 512], BF16) for i in range(2)]
            xr = [sb("xr%d" % i, [128, 512], F32) for i in range(3)]
            Bmix = Buf()
            Bwt = [Buf(), Buf()]
            Bxr = [Buf() for _ in range(3)]
            Bps = [Buf() for _ in range(7)]
            for kq in range(4):
                P.op("sp", lambda e, kq=kq: e.dma_start(out=mixT[:, kq * 8:(kq + 1) * 8, :], in_=mixT_s[:, kq * 8:(kq + 1) * 8, :]),
                     writes=[Bmix], dma=True)
            psi = 0
            xi = 0
            for db in range(8):
                w = db % 2
                for kh in range(2):
                    P.op("pool", lambda e, db=db, kh=kh, w=w: e.dma_start(
                        out=wt[w][:, kh * 16:(kh + 1) * 16, :].rearrange("p k f -> p (k f)"),
                        in_=w_o[db, :, kh * 16:(kh + 1) * 16, :].rearrange("p k f -> p (k f)")),
                        writes=[Bwt[w]], dma=True)
                for (r0, m) in TT9:
                    p = psi % 7
                    psi += 1
                    x = xi % 3
                    xi += 1
                    P.op("sp", lambda e, x=x, r0=r0, m=m, db=db: e.dma_start(
                        out=xr[x][0:m, :], in_=x_own[r0:r0 + m, db * 512:(db + 1) * 512]), writes=[Bxr[x]], dma=True)
                    for k in range(32):
                        P.op("pe", lambda e, p=p, k=k, r0=r0, m=m, w=w: e.matmul(
                            psum[p][0:m, :], lhsT=mixT[:, k, r0:r0 + m], rhs=wt[w][:, k, :],
                            start=(k == 0), stop=(k == 31)), reads=[Bmix, Bwt[w]], writes=[Bps[p]])
                    P.op("dve", lambda e, x=x, p=p, m=m: e.scalar_tensor_tensor(
                        out=xr[x][0:m, :], in0=xr[x][0:m, :], scalar=ALPHA, in1=psum[p][0:m, :],
                        op0=ALU.mult, op1=ALU.add), reads=[Bps[p], Bxr[x]], writes=[Bxr[x]])
                    P.op("sp", lambda e, x=x, r0=r0, m=m, db=db: e.dma_start(
                        out=r_s[r0:r0 + m, db * 512:(db + 1) * 512], in_=xr[x][0:m, :]), reads=[Bxr[x]], dma=True)
            P.emit_phase()

        def ln_phase(src, g_idx, dst_f32, dst_T):
            with contextlib.ExitStack() as st:
                sb = lambda name, shape, dt: st.enter_context(nc.sbuf_tensor(uname(name), list(shape), dt))
                gt = sb("ln_g", [128, D], F32)
                bt = sb("ln_b", [128, D], F32)
                idb = sb("ln_idb", [128, 128], BF16)
                idf = sb("ln_idf", [128, 128], F32)
                rt = [sb("ln_r%d" % i, [128, D], F32) for i in range(2)]
                hb = sb("ln_hb", [128, D], BF16)
                hT = sb("ln_hT", [128, 32, 128], BF16)
                stats = sb("ln_stats", [128, 8, 6], F32)
                mv = sb("ln_mv", [128, 2], F32)
                rstd = sb("ln_rstd", [128, 1], F32)
                Bg, Bid = Buf(), Buf()
                Brt = [Buf(), Buf()]
                Bhb, BhT, Bst, Bmv, Brs = Buf(), Buf(), Buf(), Buf(), Buf()
                Bpsb = Buf()
                P.op("sp", lambda e: e.dma_start(out=gt[:], in_=lnp[g_idx]), writes=[Bg], dma=True)
                P.op("sp", lambda e: e.dma_start(out=bt[:], in_=lnp[g_idx + 1]), writes=[Bg], dma=True)
                P.op("sp", lambda e: e.dma_start(out=idf[:], in_=ident[:, :]), writes=[Bid], dma=True)
                P.op("dve", lambda e: e.tensor_copy(out=idb[:], in_=idf[:]), reads=[Bid], writes=[Bid])
                for ti, (r0, m) in enumerate(TT9):
                    r = ti % 2
                    P.op("sp", lambda e, r=r, r0=r0, m=m: e.dma_start(out=rt[r][0:m, :], in_=src[r0:r0 + m, :]),
                         writes=[Brt[r]], dma=True)
                    for c in range(8):
                        P.op("dve", lambda e, r=r, m=m, c=c: e.bn_stats(out=stats[0:m, c, :], in_=rt[r][0:m, c * 512:(c + 1) * 512]),
                             reads=[Brt[r]], writes=[Bst])
                    P.op("dve", lambda e, m=m: e.bn_aggr(out=mv[0:m, :], in_=stats[0:m, :, :].rearrange("p a b -> p (a b)")),
                         reads=[Bst], writes=[Bmv])
                    P.op("act", lambda e, m=m: e.activation(out=rstd[0:m, :], in_=mv[0:m, 1:2], func=AF.Sqrt, bias=EPS, scale=1.0),
                         reads=[Bmv], writes=[Brs])
                    P.op("dve", lambda e, m=m: e.reciprocal(out=rstd[0:m, :], in_=rstd[0:m, :]), reads=[Brs], writes=[Brs])
                    P.op("dve", lambda e, r=r, m=m: e.tensor_scalar(
                        out=rt[r][0:m, :], in0=rt[r][0:m, :], scalar1=mv[0:m, 0:1], scalar2=rstd[0:m, 0:1],
                        op0=ALU.subtract, op1=ALU.mult), reads=[Brt[r], Bmv, Brs], writes=[Brt[r]])
                    P.op("dve", lambda e, r=r, m=m: e.tensor_tensor(out=rt[r][0:m, :], in0=rt[r][0:m, :], in1=gt[0:m, :], op=ALU.mult),
                         reads=[Brt[r], Bg], writes=[Brt[r]])
                    P.op("dve", lambda e, r=r, m=m: e.tensor_tensor(out=rt[r][0:m, :], in0=rt[r][0:m, :], in1=bt[0:m, :], op=ALU.add),
                         reads=[Brt[r], Bg], writes=[Brt[r]])
                    P.op("sp", lambda e, r=r, r0=r0, m=m: e.dma_start(out=dst_f32[r0:r0 + m, :], in_=rt[r][0:m, :]),
                         reads=[Brt[r]], dma=True)
                    if dst_T is not None:
                        P.op("act", lambda e, r=r, m=m: e.activation(out=hb[0:m, :], in_=rt[r][0:m, :], func=AF.Copy),
                             reads=[Brt[r]], writes=[Bhb])
                        for kq in range(4):
                            for kk in range(8):
                                k = kq * 8 + kk
                                P.op("pe", lambda e, k=k, kk=kk, m=m: e.transpose(
                                    out=psb[:, kk * 128:kk * 128 + m], in_=hb[0:m, k * 128:(k + 1) * 128], identity=idb[0:m, 0:m]),
                                    reads=[Bhb, Bid], writes=[Bpsb])
                            pv = psb[:, :].rearrange("p (a b) -> p a b", b=128)
                            P.op("act", lambda e, kq=kq, m=m, pv=pv: e.activation(
                                out=hT[:, kq * 8:(kq + 1) * 8, 0:m], in_=pv[:, :, 0:m], func=AF.Copy),
                                reads=[Bpsb], writes=[BhT])
                        P.op("sp", lambda e, r0=r0, m=m: e.dma_start(out=dst_T[:, :, r0:r0 + m], in_=hT[:, :, 0:m]),
                             reads=[BhT], dma=True)
                P.emit_phase()

        ln_phase(r_s, 0, h_s, hT_s)

        for (t0, nt) in ((0, 512), (512, 528)):
            with contextlib.ExitStack() as st:
                sb = lambda name, shape, dt: st.enter_context(nc.sbuf_tensor(uname(name), list(shape), dt))
                hT = sb("f_hT", [128, 32, 528], BF16)
                ffT = sb("f_ffT", [128, NFC, 528], BF16)
                wg = [sb("f_wg%d" % i, [128, 32, 128], BF16) for i in range(2)]
                wu = [sb("f_wu%d" % i, [128, 32, 128], BF16) for i in range(2)]
                sg = [sb("f_sg%d" % i, [128, 528], F32) for i in range(2)]
                BhT, Bff = Buf(), [Buf() for _ in range(NFC)]
                Bwg, Bwu, Bsg = [Buf(), Buf()], [Buf(), Buf()], [Buf(), Buf()]
                Bps = [Buf() for _ in range(7)]
                Bpsb = Buf()
                for kq in range(4):
                    P.op("sp", lambda e, kq=kq: e.dma_start(out=hT[:, kq * 8:(kq + 1) * 8, 0:nt],
                                                            in_=hT_s[:, kq * 8:(kq + 1) * 8, t0:t0 + nt]),
                         writes=[BhT], dma=True)
                segs = [(0, 512)] + ([(512, 16)] if nt > 512 else [])
                for f in range(NFC):
                    w = f % 2
                    P.op("pool", lambda e, f=f, w=w: e.dma_start(out=wg[w][:, :, :].rearrange("p k f -> p (k f)"),
                                                               in_=w_g[f].rearrange("p k f -> p (k f)")), writes=[Bwg[w]], dma=True)
                    P.op("pool", lambda e, f=f, w=w: e.dma_start(out=wu[w][:, :, :].rearrange("p k f -> p (k f)"),
                                                               in_=w_u[f].rearrange("p k f -> p (k f)")), writes=[Bwu[w]], dma=True)
                    pb = 3 * (f % 2)
                    for si, (c0, n) in enumerate(segs):
                        if si == 0:
                            pg, pu, g0, u0 = pb, pb + 1, 0, 0
                        else:
                            pg, pu, g0, u0 = pb + 2, pb + 2, 0, 16
                        for k in range(32):
                            P.op("pe", lambda e, pg=pg, g0=g0, k=k, c0=c0, n=n, w=w: e.matmul(
                                psum[pg][:, g0:g0 + n], lhsT=wg[w][:, k, :], rhs=hT[:, k, c0:c0 + n],
                                start=(k == 0), stop=(k == 31)), reads=[BhT, Bwg[w]], writes=[Bps[pg]])
                        P.op("act", lambda e, pg=pg, g0=g0, c0=c0, n=n, w=w: e.activation(
                            out=sg[w][:, c0:c0 + n], in_=psum[pg][:, g0:g0 + n], func=AF.Silu), reads=[Bps[pg]], writes=[Bsg[w]])
                        for k in range(32):
                            P.op("pe", lambda e, pu=pu, u0=u0, k=k, c0=c0, n=n, w=w: e.matmul(
                                psum[pu][:, u0:u0 + n], lhsT=wu[w][:, k, :], rhs=hT[:, k, c0:c0 + n],
                                start=(k == 0), stop=(k == 31)), reads=[BhT, Bwu[w]], writes=[Bps[pu]])
                        P.op("dve", lambda e, pu=pu, u0=u0, c0=c0, n=n, w=w, f=f: e.tensor_tensor(
                            out=ffT[:, f, c0:c0 + n], in0=sg[w][:, c0:c0 + n], in1=psum[pu][:, u0:u0 + n], op=ALU.mult),
                            reads=[Bps[pu], Bsg[w]], writes=[Bff[f]])
                wd = [sb("f_wd%d" % i, [128, 6, 512], BF16) for i in range(2)]
                hr = [sb("f_hr%d" % i, [128, 512], F32) for i in range(3)]
                Bwd = [Buf(), Buf()]
                Bhr = [Buf() for _ in range(3)]
                tts = [(i * 128, 128) for i in range(4)] + ([(512, 16)] if nt > 512 else [])
                FG = [(f0, min(6, NFC - f0)) for f0 in range(0, NFC, 6)]
                wi = 0
                hi = 0
                for db in range(8):
                    for (f0, nf) in FG:
                        w = wi % 2
                        wi += 1
                        P.op("pool", lambda e, db=db, f0=f0, nf=nf, w=w: e.dma_start(
                            out=wd[w][:, 0:nf, :].rearrange("p k f -> p (k f)"),
                            in_=w_d[db, :, f0:f0 + nf, :].rearrange("p k f -> p (k f)")), writes=[Bwd[w]], dma=True)
                        for fi in range(nf):
                            f = f0 + fi
                            for ti, (c0, m) in enumerate(tts):
                                P.op("pe", lambda e, ti=ti, c0=c0, m=m, f=f, fi=fi, w=w: e.matmul(
                                    psum[ti][0:m, :], lhsT=ffT[:, f, c0:c0 + m], rhs=wd[w][:, fi, :],
                                    start=(f == 0), stop=(f == NFC - 1)), reads=[Bff[f], Bwd[w]], writes=[Bps[ti]])
                    for ti, (c0, m) in enumerate(tts):
                        x = hi % 3
                        hi += 1
                        r0 = t0 + c0
                        P.op("sp", lambda e, x=x, r0=r0, m=m, db=db: e.dma_start(
                            out=hr[x][0:m, :], in_=h_s[r0:r0 + m, db * 512:(db + 1) * 512]), writes=[Bhr[x]], dma=True)
                        P.op("dve", lambda e, x=x, ti=ti, m=m: e.scalar_tensor_tensor(
                            out=hr[x][0:m, :], in0=hr[x][0:m, :], scalar=ALPHA, in1=psum[ti][0:m, :],
                            op0=ALU.mult, op1=ALU.add), reads=[Bps[ti], Bhr[x]], writes=[Bhr[x]])
                        P.op("sp", lambda e, x=x, r0=r0, m=m, db=db: e.dma_start(
                            out=y_s[r0:r0 + m, db * 512:(db + 1) * 512], in_=hr[x][0:m, :]), reads=[Bhr[x]], dma=True)
                P.emit_phase()

        ln_phase(y_s, 2, y_o, None)
    return nc


_NC_CACHE = {}


def _tile_w(w, nblk, blk):
    return np.ascontiguousarray(w.reshape(32, 128, nblk, blk).transpose(2, 1, 0, 3))


def kernel(x_prompt, x_sample, cache_k_cmp, cache_v_cmp, cache_k_slc, cache_v_slc,
           state_k_win, state_v_win, state_pool, page_table, w_in,
           w_cmp1_k, pe_cmp_k, w_cmp2_k, w_cmp1_v, pe_cmp_v, w_cmp2_v,
           w_pool, pool_scale, w_o, ln1_g, ln1_b, w_gate, w_up, w_down, ln2_g, ln2_b):
    f32 = np.float32
    x_prompt = np.asarray(x_prompt, f32)
    x_sample = np.asarray(x_sample, f32)
    if "nc" not in _NC_CACHE:
        _NC_CACHE["nc"] = build_program()
    nc = _NC_CACHE["nc"]

    w_in_p = np.zeros((D, E_PAD), f32)
    w_in_p[:, :7240] = np.asarray(w_in, f32)[0]
    w_in_t = _tile_w(w_in_p, 15, 512)
    w_o_t = _tile_w(np.asarray(w_o, f32)[0], 8, 512)
    w_g_t = _tile_w(np.asarray(w_gate, f32)[0], NFC, 128)
    w_u_t = _tile_w(np.asarray(w_up, f32)[0], NFC, 128)
    w_d_t = np.ascontiguousarray(np.asarray(w_down, f32)[0].reshape(NFC, 128, 8, 512).transpose(2, 1, 0, 3))
    lnp = np.ascontiguousarray(np.broadcast_to(
        np.stack([np.asarray(a, f32)[0] for a in (ln1_g, ln1_b, ln2_g, ln2_b)])[:, None, :], (4, 128, D)))
    ident = np.eye(128, dtype=f32)
    wp_h = np.ascontiguousarray(np.asarray(w_pool, f32)[0].reshape(4, 2, 128, 256).transpose(2, 0, 1, 3))
    psc_h = np.ascontiguousarray(np.asarray(pool_scale, f32)[0].reshape(8, 128).T)
    half = 64
    inv = (10000.0 ** (-np.arange(half, dtype=f32) / half)).astype(f32)
    NEGV = -30000.0
    tt_, kk_ = np.meshgrid(np.arange(128), np.arange(128), indexing="ij")
    cbc_h = np.where(kk_ <= tt_, 0.0, NEGV).astype(f32)
    cba_h = np.where(kk_ > tt_, 0.0, NEGV).astype(f32)
    w1k_h = np.ascontiguousarray(np.asarray(w_cmp1_k, f32)[0].transpose(1, 0, 2))
    w1v_h = np.ascontiguousarray(np.asarray(w_cmp1_v, f32)[0].transpose(1, 0, 2))
    w2k_ = np.asarray(w_cmp2_k, f32)[0]
    w2_h = np.ascontiguousarray(np.stack([w2k_, np.roll(w2k_, 64, axis=1), np.asarray(w_cmp2_v, f32)[0]], 1))
    peT_h = np.ascontiguousarray(np.stack([np.asarray(pe_cmp_k, f32)[0].T, np.asarray(pe_cmp_v, f32)[0].T], 1))
    sgn_h = np.where(np.arange(128) < 64, -1.0, 1.0).astype(f32)
    ckc_h = np.ascontiguousarray(np.asarray(cache_k_cmp, f32)[0]).reshape(2560 * 32, 2048)
    cvc_h = np.ascontiguousarray(np.asarray(cache_v_cmp, f32)[0]).reshape(2560 * 32, 2048)
    cks_h = np.ascontiguousarray(np.asarray(cache_k_slc, f32)[0]).reshape(2560 * 32, 2048)
    cvs_h = np.ascontiguousarray(np.asarray(cache_v_slc, f32)[0]).reshape(2560 * 32, 2048)
    ptab_all = np.asarray(page_table).astype(np.int32)
    oh_h = np.zeros((128, 5), f32)
    oh_h[np.arange(128), np.arange(128) // 32] = 1.0
    oh_h[:, 4] = np.arange(128) % 32
    sang = (16.0 * np.arange(512) + 31.0).astype(f32)[None, :] * inv[np.arange(128) % 64][:, None]
    ccs_h = np.cos(sang).astype(f32)
    scs_h = (np.sin(sang) * sgn_h[:, None]).astype(f32)
    addms_h = np.zeros((4, 129), f32)
    addms_h[:, [0, 127, 128]] = 1e30

    in_maps = []
    for c in range(8):
        b, j = c // 4, c % 4
        pad = 3 - j
        xs = np.zeros((4096, D), f32)
        xs[pad * 128:] = x_prompt[b, :4096 - pad * 128]
        xTs = np.ascontiguousarray(xs.reshape(4, 1024, 32, 128).transpose(0, 3, 2, 1))
        xsm = x_sample[4 * c:4 * c + 4].reshape(16, D)
        xsT = np.ascontiguousarray(xsm.reshape(16, 32, 128).transpose(2, 1, 0))
        own_rows = np.concatenate([np.arange(128) + (4 * i + 3) * 128 for i in range(8)])
        x_own = np.ascontiguousarray(np.concatenate([xs[own_rows], xsm], 0))
        pos = np.concatenate([np.arange(4096) - pad * 128, np.tile(8192 + np.arange(4), 4)]).astype(f32)
        ang = pos[:, None] * inv[None, :]
        own_pos = np.concatenate([pos[own_rows], pos[4096:]])
        rc = np.stack([1.0 / np.minimum(np.maximum(own_pos, 0) + 1.0, float(2 << gi)) for gi in range(4)]).astype(f32)
        rc_h = np.ascontiguousarray(np.broadcast_to(rc[None], (128, 4, TOK)))
        n_ = np.arange(256)
        cpos = (16 * n_ + 31 - pad * 128).astype(f32)
        cang = cpos[None, :] * inv[np.arange(128) % 64][:, None]
        ccmp_h = np.cos(cang).astype(f32)
        scmp_h = (np.sin(cang) * sgn_h[:, None]).astype(f32)
        ccmp_h[:, 255] = 0
        scmp_h[:, 255] = 0
        s_t = ((4 * np.arange(8)[None, :] + 3) * 128 + np.arange(128)[:, None])
        cval_h = ((16 * n_[None, None, :] + 31 <= s_t[:, :, None]) & (16 * n_[None, None, :] >= pad * 128)
                  & (n_[None, None, :] <= 254)).astype(f32)
        blk_ = np.arange(64)[None, None, :]
        cur_ = (s_t // 64)[:, :, None]
        b0_ = 2 * pad
        valid_ = (blk_ >= b0_) & (blk_ <= cur_)
        forced_ = (blk_ == b0_) | (blk_ == cur_) | (blk_ == cur_ - 1)
        addm_h = np.where(forced_, 1e30, np.where(valid_, 0.0, -1e30)).astype(f32)
        vblk_h = valid_.astype(f32)
        padb_h = np.ascontiguousarray(np.broadcast_to(np.where(np.arange(32) < pad, NEGV, 0.0).astype(f32)[None, :], (128, 32)))
        stp_h = np.ascontiguousarray(np.asarray(state_pool, f32)[0, 4 * c:4 * c + 4].reshape(4, 15, 8, 128).transpose(3, 2, 0, 1))
        in_maps.append({
            "xTs": xTs, "xsT": xsT, "x_own": x_own, "w_in": w_in_t, "w_o": w_o_t, "w_g": w_g_t, "w_u": w_u_t,
            "w_d": w_d_t, "cosT": np.cos(ang).astype(f32), "sinT": np.sin(ang).astype(f32), "lnp": lnp,
            "cbc_d": cbc_h, "cba_d": cba_h, "padb_d": padb_h, "w1k_d": w1k_h, "w1v_d": w1v_h, "w2_d": w2_h,
            "peT_d": peT_h, "ccmp_d": ccmp_h, "scmp_d": scmp_h, "cval_d": np.ascontiguousarray(cval_h),
            "addm_d": np.ascontiguousarray(addm_h), "vblk_d": np.ascontiguousarray(vblk_h),
            "ckc_d": ckc_h, "cvc_d": cvc_h, "cks_d": cks_h, "cvs_d": cvs_h,
            "ptab_d": np.ascontiguousarray(ptab_all[4 * c:4 * c + 4]), "oh_d": oh_h, "ccs_d": ccs_h, "scs_d": scs_h,
            "addms_d": addms_h,
            "ident": ident, "rc_d": rc_h, "wp_d": wp_h, "psc_d": psc_h, "stp_d": stp_h,
            "st_kw": np.ascontiguousarray(np.asarray(state_k_win, f32)[0, 4 * c:4 * c + 4].reshape(4, 512, 512)),
            "st_vw": np.ascontiguousarray(np.asarray(state_v_win, f32)[0, 4 * c:4 * c + 4].reshape(4, 512, 512)),
            "st_pool": np.ascontiguousarray(np.asarray(state_pool, f32)[0, 4 * c:4 * c + 4]),
        })
    res = run_bass_kernel_spmd(nc, in_maps, core_ids=list(range(8)))
    R = res.results

    y_prompt = np.zeros((2, 4096, D), f32)
    y_sample = np.zeros((32, 4, D), f32)
    kvp = [np.zeros((1, 2, 4096, 4, 128), f32) for _ in range(6)]
    kvs = [np.zeros((1, 32, 4, 4, 128), f32) for _ in range(6)]
    kw_s = np.zeros((1, 32, 512, 4, 128), f32)
    vw_s = np.zeros((1, 32, 512, 4, 128), f32)
    pool_p = np.zeros((1, 2, 15, 1024), f32)
    pool_s = np.zeros((1, 32, 15, 1024), f32)
    for c in range(8):
        b, j = c // 4, c % 4
        r = R[c]
        yo = r["y_o"]
        for i in range(8):
            g = 4 * i + j
            y_prompt[b, g * 128:(g + 1) * 128] = yo[i * 128:(i + 1) * 128]
        y_sample[4 * c:4 * c + 4] = yo[1024:1040].reshape(4, 4, D)
        for t in range(6):
            kvs[t][0, 4 * c:4 * c + 4] = r["kvout"][t, 4096:4112].reshape(4, 4, 4, 128)
            if j == 3:
                kvp[t][0, b] = r["kvout"][t, 0:4096].reshape(4096, 4, 128)
        kw_s[0, 4 * c:4 * c + 4] = r["kws_o"].reshape(4, 512, 4, 128)
        vw_s[0, 4 * c:4 * c + 4] = r["vws_o"].reshape(4, 512, 4, 128)
        pool_s[0, 4 * c:4 * c + 4] = r["pools_o"]
        if j == 3:
            pool_p[0, b] = r["poolp_o"]
    kc_p, vc_p, ks_p, vs_p, kw_full, vw_full = kvp
    return (y_prompt, y_sample, kc_p, vc_p, ks_p, vs_p,
            np.ascontiguousarray(kw_full[:, :, 4096 - 512:]), np.ascontiguousarray(vw_full[:, :, 4096 - 512:]),
            pool_p, kvs[0], kvs[1], kvs[2], kvs[3], kw_s, vw_s, pool_s)
```

```python
import contextlib
import numpy as np
import concourse.bass as bass
import concourse.mybir as mybir
from concourse.bass_utils import run_bass_kernel_spmd

F32 = mybir.dt.float32
BF16 = mybir.dt.bfloat16
I32 = mybir.dt.int32
AF = mybir.ActivationFunctionType
ALU = mybir.AluOpType
AX = mybir.AxisListType

D = 4096
DFF = 11008
NFC = 86
TOK = 1040
ALPHA = 2.0 ** 0.25
EPS = 1e-5
E_PAD = 7680


class Buf:
    __slots__ = ("name", "w", "r")

    def __init__(self, name=""):
        self.name = name
        self.w = None
        self.r = {}


class Op:
    __slots__ = ("eng", "fn", "deps", "dma", "needed", "sem", "semval", "done", "slot")

    def __init__(self, eng, fn, dma):
        self.eng = eng
        self.fn = fn
        self.deps = []
        self.dma = dma
        self.needed = False
        self.sem = None
        self.semval = None
        self.done = False
        self.slot = None


class Prog:
    ENGS = ("pe", "act", "dve", "pool", "sp")
    NDS = 16

    def __init__(self, nc, st):
        self.nc = nc
        self.ops = {k: [] for k in self.ENGS}
        self.engsem = {k: st.enter_context(nc.semaphore("es_" + k)) for k in self.ENGS}
        self.nds = {"sp": 16, "pool": 8, "act": 4}
        self.dsem = {q: [st.enter_context(nc.semaphore("ds_%s_%d" % (q, i))) for i in range(self.nds[q])]
                     for q in ("sp", "pool", "act")}
        self.cnt = {k: 0 for k in self.ENGS}
        self.dma_n = {q: 0 for q in self.dsem}
        self.dma_uses = {q: [0] * self.nds[q] for q in self.dsem}
        self.dma_last = {q: [None] * self.nds[q] for q in self.dsem}
        self.phase_dma = []

    def clear_sems(self):
        nc = self.nc
        with nc.Block() as block:
            def body(e):
                for s in self.engsem.values():
                    e.sem_clear(s)
                for lst in self.dsem.values():
                    for s in lst:
                        e.sem_clear(s)
            block.sync(body)

    def op(self, eng, fn, reads=(), writes=(), dma=False):
        o = Op(eng, fn, dma)
        deps = []
        for b in reads:
            if b.w is not None:
                deps.append(b.w)
        for b in writes:
            if b.w is not None:
                deps.append(b.w)
            deps.extend(b.r.values())
        if dma:
            s = self.dma_n[eng] % self.nds[eng]
            self.dma_n[eng] += 1
            o.slot = s
            if self.dma_last[eng][s] is not None:
                deps.append(self.dma_last[eng][s])
            self.dma_uses[eng][s] += 1
            o.sem = self.dsem[eng][s]
            o.semval = 16 * self.dma_uses[eng][s]
            self.dma_last[eng][s] = o
            self.phase_dma.append(o)
        seen = set()
        for d in deps:
            if id(d) in seen or d.done:
                continue
            seen.add(id(d))
            if (not d.dma) and d.eng == eng and eng == "pe":
                continue
            d.needed = True
            o.deps.append(d)
        self.ops[eng].append(o)
        for b in writes:
            b.w = o
            b.r = {}
        for b in reads:
            if b.w is o:
                continue
            key = ("dma", eng, o.slot) if dma else eng
            b.r[key] = o
        return o

    def emit_phase(self):
        nc = self.nc
        for k in self.ENGS:
            for o in self.ops[k]:
                if not o.dma and o.needed:
                    self.cnt[k] += 1
                    o.sem = self.engsem[k]
                    o.semval = self.cnt[k]
        finals = {}
        for o in self.phase_dma:
            finals[(o.eng, o.slot)] = o
        finals = list(finals.values())

        def make_body(k):
            def body(e):
                waited = {}

                def wait(d):
                    sid = id(d.sem)
                    if waited.get(sid, 0) >= d.semval:
                        return
                    waited[sid] = d.semval
                    e.wait_ge(d.sem, d.semval)

                for o in self.ops[k]:
                    for d in o.deps:
                        wait(d)
                    ins = o.fn(e)
                    if o.dma:
                        ins.then_inc(o.sem, 16)
                    elif o.needed:
                        ins.then_inc(o.sem, 1)
                if k == "sp":
                    for d in finals:
                        wait(d)
            return body

        with nc.Block() as block:
            block.tensor(make_body("pe"))
            block.scalar(make_body("act"))
            block.vector(make_body("dve"))
            block.gpsimd(make_body("pool"))
            block.sync(make_body("sp"))
        for k in self.ENGS:
            for o in self.ops[k]:
                o.done = True
            self.ops[k] = []
        self.phase_dma = []


def build_program():
    nc = bass.Bass("TRN2", target_bir_lowering=False)

    def din(name, shape, dt=F32):
        return nc.dram_tensor(name, list(shape), dt, kind="ExternalInput").ap()

    def dout(name, shape, dt=F32):
        return nc.dram_tensor(name, list(shape), dt, kind="ExternalOutput").ap()

    _uid = [0]

    def uname(name):
        _uid[0] += 1
        return "%s_%d" % (name, _uid[0])

    def dscr(name, shape, dt=F32):
        return nc.dram_tensor(name, list(shape), dt, kind="Internal").ap()

    xTs = din("xTs", [4, 128, 32, 1024])
    xsT = din("xsT", [128, 32, 16])
    x_own = din("x_own", [TOK, D])
    w_in = din("w_in", [15, 128, 32, 512])
    w_o = din("w_o", [8, 128, 32, 512])
    w_g = din("w_g", [NFC, 128, 32, 128])
    w_u = din("w_u", [NFC, 128, 32, 128])
    w_d = din("w_d", [8, 128, NFC, 512])
    cosT = din("cosT", [4112, 64])
    sinT = din("sinT", [4112, 64])
    lnp = din("lnp", [4, 128, D])
    ident = din("ident", [128, 128])
    st_kw = din("st_kw", [4, 512, 512])
    st_vw = din("st_vw", [4, 512, 512])
    st_pool = din("st_pool", [4, 15, 1024])
    rc_d = din("rc_d", [128, 4, TOK])
    wp_d = din("wp_d", [128, 4, 2, 256])
    psc_d = din("psc_d", [128, 8])
    stp_d = din("stp_d", [128, 8, 4, 15])
    cbc_d = din("cbc_d", [128, 128])
    cba_d = din("cba_d", [128, 128])
    padb_d = din("padb_d", [128, 32])
    w1k_d = din("w1k_d", [128, 32, 128])
    w1v_d = din("w1v_d", [128, 32, 128])
    w2_d = din("w2_d", [128, 3, 128])
    peT_d = din("peT_d", [128, 2, 32])
    ccmp_d = din("ccmp_d", [128, 256])
    scmp_d = din("scmp_d", [128, 256])
    cval_d = din("cval_d", [128, 8, 256])
    addm_d = din("addm_d", [128, 8, 64])
    vblk_d = din("vblk_d", [128, 8, 64])
    ckc_d = din("ckc_d", [2560 * 32, 2048])
    cvc_d = din("cvc_d", [2560 * 32, 2048])
    cks_d = din("cks_d", [2560 * 32, 2048])
    cvs_d = din("cvs_d", [2560 * 32, 2048])
    ptab_d = din("ptab_d", [4, 64], I32)
    oh_d = din("oh_d", [128, 5])
    ccs_d = din("ccs_d", [128, 512])
    scs_d = din("scs_d", [128, 512])
    addms_d = din("addms_d", [4, 129])

    kvout = dout("kvout", [6, 4112, 512])
    kws_o = dout("kws_o", [4, 512, 512])
    vws_o = dout("vws_o", [4, 512, 512])
    poolp_o = dout("poolp_o", [15, 1024])
    pools_o = dout("pools_o", [4, 15, 1024])
    y_o = dout("y_o", [TOK, D])

    mixT_s = dscr("mixT_s", [128, 32, TOK], BF16)
    r_s = dscr("r_s", [TOK, D])
    h_s = dscr("h_s", [TOK, D])
    hT_s = dscr("hT_s", [128, 32, TOK], BF16)
    y_s = dscr("y_s", [TOK, D])
    q_s = dscr("q_s", [TOK, 3072])
    gate_s = dscr("gate_s", [TOK, 72])

    TT9 = [(i * 128, 128) for i in range(8)] + [(1024, 16)]

    with contextlib.ExitStack() as gst:
        P = Prog(nc, gst)
        P.clear_sems()
        psum = [gst.enter_context(nc.psum_tensor("ps%d" % i, [128, 512], F32)) for i in range(7)]
        psb = gst.enter_context(nc.psum_tensor("psb", [128, 1024], BF16))

        with contextlib.ExitStack() as st:
            sb = lambda name, shape, dt: st.enter_context(nc.sbuf_tensor(uname(name), list(shape), dt))
            xb_t = sb("xb_t", [128, 32, 1040], BF16)
            wt = [sb("wt%d" % i, [128, 32, 512], BF16) for i in range(2)]
            cos_t = sb("cos_t", [128, 9, 64], F32)
            sin_t = sb("sin_t", [128, 9, 64], F32)
            o32 = [sb("o32_%d" % i, [128, 512], F32) for i in range(4)]
            ta = sb("ta", [128, 512], F32)
            tb = sb("tb", [128, 512], F32)
            zt = sb("zt", [128, TOK], BF16)
            u32 = sb("u32", [128, 144], F32)
            sAB = [sb("sA", [128, 144], F32), sb("sB", [128, 144], F32)]
            ua = sb("ua", [128, 19], F32)
            dT = sb("dT", [128, 2, 128], BF16)
            mo_t = [sb("mo0", [128, 128], BF16), sb("mo1", [128, 128], BF16)]
            rc_t = sb("rc_t", [128, 4, TOK], F32)
            wp_t = sb("wp_t", [128, 4, 2, 256], BF16)
            psc_t = sb("psc_t", [128, 8], F32)
            stp_t = sb("stp_t", [128, 8, 4, 15], F32)
            Bu32, BsAB, Bua, BdT, Bmo, Brc = Buf(), [Buf(), Buf()], Buf(), Buf(), [Buf(), Buf()], Buf()
            P.op("sp", lambda e: e.dma_start(out=rc_t[:], in_=rc_d[:, :, :]), writes=[Brc], dma=True)
            P.op("pool", lambda e: e.dma_start(out=wp_t[:], in_=wp_d[:, :, :, :]), writes=[Brc], dma=True)
            P.op("sp", lambda e: e.dma_start(out=psc_t[:], in_=psc_d[:, :]), writes=[Brc], dma=True)
            P.op("sp", lambda e: e.dma_start(out=stp_t[:], in_=stp_d[:, :, :, :]), writes=[Brc], dma=True)
            Bps = [Buf() for _ in range(7)]
            Bxb, Bcs = [Buf() for _ in range(5)], Buf()
            Bwt = [[Buf(), Buf()], [Buf(), Buf()]]
            Bo32 = [Buf() for _ in range(4)]
            Bta, Btb, Bzt = Buf(), Buf(), Buf()
            P.op("dve", lambda e: e.memset(zt[:], 0.0), writes=[Bzt])
            for k in range(8, 32):
                P.op("sp", lambda e, k=k: e.dma_start(out=mixT_s[:, k, :], in_=zt[:]), reads=[Bzt], dma=True)
            for sbi in range(4):
                P.op("sp", lambda e, i=sbi: e.dma_start(out=kws_o[i, 0:508, :], in_=st_kw[i, 4:512, :]), dma=True)
                P.op("sp", lambda e, i=sbi: e.dma_start(out=vws_o[i, 0:508, :], in_=st_vw[i, 4:512, :]), dma=True)
                P.op("sp", lambda e, i=sbi: e.dma_start(out=pools_o[i, 0:11, :], in_=st_pool[i, 4:15, :]), dma=True)
            psi = 0
            oi = 0
            wi = 0
            for xb in range(4):
                for kq in range(4):
                    P.op("pool", lambda e, xb=xb, kq=kq: e.dma_start(
                        out=xb_t[:, kq * 8:(kq + 1) * 8, 0:1024], in_=xTs[xb, :, kq * 8:(kq + 1) * 8, :]),
                        writes=[Bxb[kq]], dma=True)
                if xb == 3:
                    P.op("pool", lambda e: e.dma_start(out=xb_t[:, :, 1024:1040], in_=xsT[:, :, :]),
                         writes=[Bxb[4]], dma=True)
                ntile = 9 if xb == 3 else 8
                for (tab_t, tab_d) in ((cos_t, cosT), (sin_t, sinT)):
                    P.op("sp", lambda e, tab_t=tab_t, tab_d=tab_d, xb=xb: e.dma_start(
                        out=tab_t[:, 0:8, :], in_=tab_d[xb * 1024:(xb + 1) * 1024, :].rearrange("(i p) d -> p i d", p=128)),
                        writes=[Bcs], dma=True)
                    if xb == 3:
                        P.op("sp", lambda e, tab_t=tab_t, tab_d=tab_d: e.dma_start(
                            out=tab_t[0:16, 8, :], in_=tab_d[4096:4112, :]), writes=[Bcs], dma=True)
                for eb in list(range(8, 14)) + [0, 1] + list(range(2, 8)) + [14]:
                    w = wi % 2
                    wi += 1
                    for kh in range(2):
                        P.op("pool", lambda e, eb=eb, kh=kh, w=w: e.dma_start(
                            out=wt[w][:, kh * 16:(kh + 1) * 16, :].rearrange("p k f -> p (k f)"),
                            in_=w_in[eb, :, kh * 16:(kh + 1) * 16, :].rearrange("p k f -> p (k f)")),
                            writes=[Bwt[w][kh]], dma=True)
                    for i in range(ntile):
                        m = 128 if i < 8 else 16
                        c0 = i * 128
                        S = xb * 8 + i if i < 8 else 32
                        r0 = S * 128
                        own = (i == 8) or (i % 4 == 3)
                        tok0 = (S // 4) * 128 if i < 8 else 1024
                        if 8 <= eb < 14:
                            do = True
                        elif eb < 2:
                            do = S >= 31
                        else:
                            do = own
                        if not do:
                            continue
                        p = psi % 7
                        psi += 1
                        for k in range(32):
                            P.op("pe", lambda e, p=p, k=k, c0=c0, m=m, w=w: e.matmul(
                                psum[p][0:m, :], lhsT=xb_t[:, k, c0:c0 + m], rhs=wt[w][:, k, :],
                                start=(k == 0), stop=(k == 31)), reads=Bxb + Bwt[w], writes=[Bps[p]])
                        o = oi % 4
                        oi += 1
                        kv = eb - 8
                        if kv in (2, 4) or 2 <= eb < 8:
                            z4 = psum[p][0:m, :].rearrange("p (h two d) -> p h two d", two=2, d=64)
                            a4 = ta[0:m, :].rearrange("p (h two d) -> p h two d", two=2, d=64)
                            b4 = tb[0:m, :].rearrange("p (h two d) -> p h two d", two=2, d=64)
                            cb = cos_t[0:m, i, :].unsqueeze(1).unsqueeze(1).to_broadcast([m, 4, 2, 64])
                            sbh = sin_t[0:m, i, :].unsqueeze(1).to_broadcast([m, 4, 64])
                            P.op("dve", lambda e, a4=a4, z4=z4, cb=cb: e.tensor_tensor(out=a4, in0=z4, in1=cb, op=ALU.mult),
                                 reads=[Bps[p], Bcs], writes=[Bta])
                            P.op("dve", lambda e, b4=b4, z4=z4, sbh=sbh: e.tensor_tensor(
                                out=b4[:, :, 0, :], in0=z4[:, :, 1, :], in1=sbh, op=ALU.mult),
                                reads=[Bps[p], Bcs], writes=[Btb])
                            P.op("dve", lambda e, b4=b4, z4=z4, sbh=sbh: e.tensor_tensor(
                                out=b4[:, :, 1, :], in0=z4[:, :, 0, :], in1=sbh, op=ALU.mult),
                                reads=[Bps[p], Bcs, Btb], writes=[Btb])
                            o4 = o32[o][0:m, :].rearrange("p (h two d) -> p h two d", two=2, d=64)
                            P.op("dve", lambda e, o4=o4, a4=a4, b4=b4: e.tensor_tensor(
                                out=o4[:, :, 0, :], in0=a4[:, :, 0, :], in1=b4[:, :, 0, :], op=ALU.subtract),
                                reads=[Bta, Btb], writes=[Bo32[o]])
                            P.op("dve", lambda e, o4=o4, a4=a4, b4=b4: e.tensor_tensor(
                                out=o4[:, :, 1, :], in0=a4[:, :, 1, :], in1=b4[:, :, 1, :], op=ALU.add),
                                reads=[Bta, Btb, Bo32[o]], writes=[Bo32[o]])
                        elif eb == 14:
                            P.op("act", lambda e, o=o, p=p, m=m: e.activation(out=o32[o][0:m, 0:72], in_=psum[p][0:m, 0:72], func=AF.Sigmoid),
                                 reads=[Bps[p]], writes=[Bo32[o]])
                        else:
                            P.op("act", lambda e, o=o, p=p, m=m: e.activation(out=o32[o][0:m, :], in_=psum[p][0:m, :], func=AF.Copy),
                                 reads=[Bps[p]], writes=[Bo32[o]])
                        if 8 <= eb < 14:
                            P.op("sp", lambda e, kv=kv, r0=r0, m=m, o=o: e.dma_start(
                                out=kvout[kv, r0:r0 + m, :], in_=o32[o][0:m, :]), reads=[Bo32[o]], dma=True)
                            if S == 32 and kv in (4, 5):
                                dst = kws_o if kv == 4 else vws_o
                                for sbi in range(4):
                                    P.op("sp", lambda e, dst=dst, sbi=sbi, o=o: e.dma_start(
                                        out=dst[sbi, 508:512, :], in_=o32[o][sbi * 4:(sbi + 1) * 4, :]),
                                        reads=[Bo32[o]], dma=True)
                        elif eb < 2:
                            if S == 31:
                                P.op("sp", lambda e, eb=eb, o=o: e.dma_start(
                                    out=poolp_o[:, eb * 512:(eb + 1) * 512], in_=o32[o][113:128, :]),
                                    reads=[Bo32[o]], dma=True)
                            else:
                                for sbi in range(4):
                                    P.op("sp", lambda e, eb=eb, sbi=sbi, o=o: e.dma_start(
                                        out=pools_o[sbi, 11:15, eb * 512:(eb + 1) * 512],
                                        in_=o32[o][sbi * 4:(sbi + 1) * 4, :]), reads=[Bo32[o]], dma=True)
                        elif eb < 8:
                            P.op("sp", lambda e, eb=eb, tok0=tok0, m=m, o=o: e.dma_start(
                                out=q_s[tok0:tok0 + m, (eb - 2) * 512:(eb - 1) * 512], in_=o32[o][0:m, :]),
                                reads=[Bo32[o]], dma=True)
                        else:
                            P.op("sp", lambda e, tok0=tok0, m=m, o=o: e.dma_start(
                                out=gate_s[tok0:tok0 + m, :], in_=o32[o][0:m, 0:72]), reads=[Bo32[o]], dma=True)
                    if eb >= 2:
                        continue
                    units = [(3, False), (7, False)] + ([(8, True)] if xb == 3 else [])
                    for (i, is_s) in units:
                        S = xb * 8 + i if not is_s else 32
                        tok0 = (S // 4) * 128 if not is_s else 1024
                        m = 16 if is_s else 128
                        for cc4 in range(4):
                            cg = eb * 4 + cc4
                            gi, cc = cg // 2, cg % 2
                            wsz = 2 << gi
                            p = psi % 7
                            psi += 1
                            if not is_s:
                                n = 144
                                cs0 = i * 128 - 16
                            else:
                                n = 16
                                cs0 = 1024
                            for k in range(32):
                                P.op("pe", lambda e, p=p, k=k, cs0=cs0, n=n, w=w, cc4=cc4: e.matmul(
                                    psum[p][:, 0:n], lhsT=wt[w][:, k, cc4 * 128:(cc4 + 1) * 128], rhs=xb_t[:, k, cs0:cs0 + n],
                                    start=(k == 0), stop=(k == 31)), reads=Bxb + Bwt[w], writes=[Bps[p]])
                            P.op("act", lambda e, p=p, n=n: e.activation(out=u32[:, 0:n], in_=psum[p][:, 0:n], func=AF.Copy),
                                 reads=[Bps[p]], writes=[Bu32])
                            if not is_s:
                                cur, Bcur = u32, Bu32
                                for lv in range(gi + 1):
                                    sh = 1 << lv
                                    nxt, Bnxt = sAB[lv % 2], BsAB[lv % 2]
                                    P.op("dve", lambda e, nxt=nxt, cur=cur, sh=sh: e.tensor_tensor(
                                        out=nxt[:, sh:144], in0=cur[:, sh:144], in1=cur[:, 0:144 - sh], op=ALU.add),
                                        reads=[Bcur], writes=[Bnxt])
                                    cur, Bcur = nxt, Bnxt
                                oth, Both = sAB[(gi + 1) % 2], BsAB[(gi + 1) % 2]
                                P.op("dve", lambda e, oth=oth, cur=cur, gi=gi, tok0=tok0: e.tensor_tensor(
                                    out=oth[:, 0:128], in0=cur[:, 16:144], in1=rc_t[:, gi, tok0:tok0 + 128], op=ALU.mult),
                                    reads=[Bcur, Brc], writes=[Both])
                                P.op("dve", lambda e, oth=oth, cc=cc: e.tensor_tensor(
                                    out=dT[:, cc, 0:128], in0=oth[:, 0:128], in1=u32[:, 16:144], op=ALU.subtract),
                                    reads=[Both, Bu32], writes=[BdT])
                            else:
                                for sbi in range(4):
                                    P.op("dve", lambda e, cg=cg, sbi=sbi: e.tensor_copy(out=ua[:, 0:15], in_=stp_t[:, cg, sbi, :]),
                                         reads=[Brc], writes=[Bua])
                                    P.op("dve", lambda e, sbi=sbi: e.tensor_copy(out=ua[:, 15:19], in_=u32[:, sbi * 4:(sbi + 1) * 4]),
                                         reads=[Bu32, Bua], writes=[Bua])
                                    cur, Bcur = ua, Bua
                                    for lv in range(gi + 1):
                                        sh = 1 << lv
                                        nxt, Bnxt = sAB[lv % 2], BsAB[lv % 2]
                                        P.op("dve", lambda e, nxt=nxt, cur=cur, sh=sh: e.tensor_tensor(
                                            out=nxt[:, sh:19], in0=cur[:, sh:19], in1=cur[:, 0:19 - sh], op=ALU.add),
                                            reads=[Bcur], writes=[Bnxt])
                                        cur, Bcur = nxt, Bnxt
                                    P.op("dve", lambda e, cur=cur, wsz=wsz, cc=cc, sbi=sbi: e.scalar_tensor_tensor(
                                        out=dT[:, cc, sbi * 4:(sbi + 1) * 4], in0=cur[:, 15:19], scalar=1.0 / wsz, in1=ua[:, 15:19],
                                        op0=ALU.mult, op1=ALU.subtract), reads=[Bcur, Bua, BdT], writes=[BdT])
                            if cc == 1:
                                for ec in range(2):
                                    p2 = psi % 7
                                    psi += 1
                                    for c2 in range(2):
                                        P.op("pe", lambda e, p2=p2, gi=gi, c2=c2, ec=ec, m=m: e.matmul(
                                            psum[p2][:, 0:m], lhsT=wp_t[:, gi, c2, ec * 128:(ec + 1) * 128], rhs=dT[:, c2, 0:m],
                                            start=(c2 == 0), stop=(c2 == 1)), reads=[BdT, Brc], writes=[Bps[p2]])
                                    mo = mo_t[ec]
                                    P.op("dve", lambda e, mo=mo, p2=p2, m=m, gi=gi, ec=ec: e.tensor_scalar(
                                        out=mo[:, 0:m], in0=psum[p2][:, 0:m], scalar1=psc_t[:, 2 * gi + ec:2 * gi + ec + 1], scalar2=None,
                                        op0=ALU.mult), reads=[Bps[p2], Brc], writes=[Bmo[ec]])
                                    P.op("sp", lambda e, mo=mo, gi=gi, ec=ec, tok0=tok0, m=m: e.dma_start(
                                        out=mixT_s[:, 2 * gi + ec, tok0:tok0 + m], in_=mo[:, 0:m]), reads=[Bmo[ec]], dma=True)
            P.emit_phase()

        SCALE = 128.0 ** -0.5
        NEG = -30000.0

        def gelu_cols(Pq, ps_ap, hid0_ap, n, x32, x2, sg_t, g_out, Bx, Bpsx, Bg, rd=()):
            Pq.op("act", lambda e: e.activation(out=x32[:, 0:n], in_=ps_ap, func=AF.Identity, bias=hid0_ap, scale=1.0),
                  reads=[Bpsx] + list(rd), writes=[Bx])
            Pq.op("dve", lambda e: e.tensor_tensor(out=x2[:, 0:n], in0=x32[:, 0:n], in1=x32[:, 0:n], op=ALU.mult),
                  reads=[Bx], writes=[Bx])
            Pq.op("dve", lambda e: e.tensor_scalar(out=x2[:, 0:n], in0=x2[:, 0:n], scalar1=0.044715, scalar2=1.0,
                                                   op0=ALU.mult, op1=ALU.add), reads=[Bx], writes=[Bx])
            Pq.op("dve", lambda e: e.tensor_tensor(out=x2[:, 0:n], in0=x2[:, 0:n], in1=x32[:, 0:n], op=ALU.mult),
                  reads=[Bx], writes=[Bx])
            Pq.op("act", lambda e: e.activation(out=sg_t[:, 0:n], in_=x2[:, 0:n], func=AF.Sigmoid, scale=1.5957691216057308),
                  reads=[Bx], writes=[Bx])
            Pq.op("dve", lambda e: e.tensor_tensor(out=g_out, in0=sg_t[:, 0:n], in1=x32[:, 0:n], op=ALU.mult),
                  reads=[Bx], writes=[Bg])

        with contextlib.ExitStack() as st:
            sb = lambda name, shape, dt: st.enter_context(nc.sbuf_tensor(uname(name), list(shape), dt))
            identb = sb("a_identb", [128, 384], BF16)
            cb_c = sb("a_cbc", [128, 128], BF16)
            cb_a = sb("a_cba", [128, 128], BF16)
            padb = sb("a_padb", [128, 32], F32)
            w1k = sb("a_w1k", [128, 32, 128], BF16)
            w1v = sb("a_w1v", [128, 32, 128], BF16)
            w2 = sb("a_w2", [128, 3, 128], BF16)
            peT = sb("a_peT", [128, 2, 32], BF16)
            hid0 = sb("a_hid0", [128, 2], F32)
            ccmp = sb("a_ccmp", [128, 256], F32)
            scmp = sb("a_scmp", [128, 256], F32)
            cval = sb("a_cval", [128, 8, 256], F32)
            addm = sb("a_addm", [128, 8, 64], F32)
            vblk = sb("a_vblk", [128, 8, 64], F32)
            tm = [sb("a_tm%d" % i, [128, 32, 128], BF16) for i in range(2)]
            XT = {t: sb("a_XT%d" % t, [128, 4096], BF16) for t in (0, 1, 2, 4)}
            V1 = {t: sb("a_V1%d" % t, [128, 32, 130], BF16) for t in (3, 5)}
            kcbT = sb("a_kcbT", [128, 256], BF16)
            vcb = sb("a_vcb", [128, 2, 128], BF16)
            x32 = sb("a_x32", [128, 256], F32)
            x2 = sb("a_x2", [128, 256], F32)
            sgt = sb("a_sgt", [128, 256], F32)
            Gk = sb("a_Gk", [128, 256], BF16)
            Gv = sb("a_Gv", [128, 256], BF16)
            qtm = sb("a_qtm", [128, 768], BF16)
            qT = sb("a_qT", [128, 768], BF16)
            gt = sb("a_gt", [128, 18], F32)
            e32 = sb("a_e32", [128, 256], F32)
            ev = sb("a_ev", [128, 256], F32)
            Pp = sb("a_Pp", [128, 260], F32)
            Pb = sb("a_Pb", [128, 256], BF16)
            PT = sb("a_PT", [128, 2, 128], BF16)
            sm = sb("a_sm", [128, 8], F32)
            imp = sb("a_imp", [128, 64], F32)
            imw = sb("a_imw", [128, 64], F32)
            m8 = sb("a_m8", [128, 16], F32)
            negb = sb("a_negb", [128, 64], BF16)
            negx = sb("a_negx", [128, 4096], BF16)
            PTs = [sb("a_PTs%d" % i, [128, 384], BF16) for i in range(4)]
            acc = sb("a_acc", [128, 768], F32)
            accb = sb("a_accb", [128, 768], BF16)
            mixo = sb("a_mixo", [128, 6, 128], BF16)
            sc6 = sb("a_sc6", [128, 8], F32)
            smh = sb("a_smh", [128, 6, 4], F32)
            evh = sb("a_evh", [128, 6, 256], F32)
            Pbh = sb("a_Pbh", [128, 6, 256], BF16)
            PTh = sb("a_PTh", [128, 6, 256], BF16)
            Bsch, Bsmh, Bevh, BPbh, BPTh, Boch = ([Buf() for _ in range(6)] for _ in range(6))

            Bc = Buf()
            Btm = [Buf(), Buf()]
            BXT = {t: Buf() for t in (0, 1, 2, 4)}
            BV1 = {t: Buf() for t in (3, 5)}
            Bkcb, Bvcb, Bx, BGk, BGv = Buf(), Buf(), Buf(), Buf(), Buf()
            Bq, BqT, Bgt, Be, BPp, BPb, BPT, Bsm, Bimp, Bneg, Bnegx = (Buf() for _ in range(11))
            BPTs = [Buf() for _ in range(4)]
            Bacc, Baccb, Bmixo, Bsc6 = Buf(), Buf(), Buf(), Buf()
            Bps = [Buf() for _ in range(7)]
            Bpsb = Buf()

            for j3 in range(3):
                P.op("pool", lambda e, j3=j3: e.dma_start(out=identb[:, j3 * 128:(j3 + 1) * 128], in_=ident[:, :]), writes=[Bc], dma=True)
            P.op("pool", lambda e: e.dma_start(out=cb_c[:], in_=cbc_d[:, :]), writes=[Bc], dma=True)
            P.op("pool", lambda e: e.dma_start(out=cb_a[:], in_=cba_d[:, :]), writes=[Bc], dma=True)
            P.op("sp", lambda e: e.dma_start(out=padb[:], in_=padb_d[:, :]), writes=[Bc], dma=True)
            P.op("pool", lambda e: e.dma_start(out=w1k[:], in_=w1k_d[:, :, :]), writes=[Bc], dma=True)
            P.op("pool", lambda e: e.dma_start(out=w1v[:], in_=w1v_d[:, :, :]), writes=[Bc], dma=True)
            P.op("pool", lambda e: e.dma_start(out=w2[:], in_=w2_d[:, :, :]), writes=[Bc], dma=True)
            P.op("pool", lambda e: e.dma_start(out=peT[:], in_=peT_d[:, :, :]), writes=[Bc], dma=True)
            P.op("sp", lambda e: e.dma_start(out=ccmp[:], in_=ccmp_d[:, :]), writes=[Bc], dma=True)
            P.op("sp", lambda e: e.dma_start(out=scmp[:], in_=scmp_d[:, :]), writes=[Bc], dma=True)
            P.op("sp", lambda e: e.dma_start(out=cval[:], in_=cval_d[:, :, :]), writes=[Bc], dma=True)
            P.op("sp", lambda e: e.dma_start(out=addm[:], in_=addm_d[:, :, :]), writes=[Bc], dma=True)
            P.op("sp", lambda e: e.dma_start(out=vblk[:], in_=vblk_d[:, :, :]), writes=[Bc], dma=True)
            for t in (3, 5):
                P.op("dve", lambda e, t=t: e.memset(V1[t][:, :, 128:130], 1.0), writes=[BV1[t]])
            P.op("dve", lambda e: e.memset(Pp[:], 0.0), writes=[BPp])
            P.op("dve", lambda e: e.memset(vcb[:], 0.0), writes=[Bvcb])
            for vi, w1t in enumerate((w1k, w1v)):
                for pp in range(32):
                    P.op("pe", lambda e, vi=vi, w1t=w1t, pp=pp: e.matmul(
                        psum[0][:, vi:vi + 1], lhsT=w1t[:, pp, :], rhs=peT[:, vi, pp:pp + 1],
                        start=(pp == 0 and vi == 0), stop=(pp == 31), skip_group_check=True), reads=[Bc], writes=[Bps[0]])
            P.op("act", lambda e: e.activation(out=hid0[:, 0:2], in_=psum[0][:, 0:2], func=AF.Copy), reads=[Bps[0]], writes=[Bc])

            tmi = 0
            for g in range(4):
                for t in (0, 1, 2, 4):
                    tb_ = tmi % 2
                    tmi += 1
                    for qd in range(4):
                        P.op("pool", lambda e, t=t, g=g, qd=qd, tb_=tb_: e.dma_start(
                            out=tm[tb_][:, qd * 8:(qd + 1) * 8, :],
                            in_=kvout[t, qd * 1024:(qd + 1) * 1024, g * 128:(g + 1) * 128].rearrange("(i p) d -> p i d", p=128)),
                            writes=[Btm[tb_]], dma=True)
                    for b8 in range(4):
                        for kk in range(8):
                            it = b8 * 8 + kk
                            P.op("pe", lambda e, tb_=tb_, it=it, kk=kk: e.transpose(
                                out=psb[:, kk * 128:(kk + 1) * 128], in_=tm[tb_][:, it, :], identity=identb[:, 0:128]),
                                reads=[Btm[tb_], Bc], writes=[Bpsb])
                        eng = "act" if b8 % 2 == 0 else "dve"
                        if eng == "act":
                            P.op("act", lambda e, t=t, b8=b8: e.activation(out=XT[t][:, b8 * 1024:(b8 + 1) * 1024], in_=psb[:, :], func=AF.Copy),
                                 reads=[Bpsb], writes=[BXT[t]])
                        else:
                            P.op("dve", lambda e, t=t, b8=b8: e.tensor_copy(out=XT[t][:, b8 * 1024:(b8 + 1) * 1024], in_=psb[:, :]),
                                 reads=[Bpsb], writes=[BXT[t]])
                for t in (3, 5):
                    for qd in range(4):
                        P.op("pool", lambda e, t=t, g=g, qd=qd: e.dma_start(
                            out=V1[t][:, qd * 8:(qd + 1) * 8, 0:128],
                            in_=kvout[t, qd * 1024:(qd + 1) * 1024, g * 128:(g + 1) * 128].rearrange("(i p) d -> p i d", p=128)),
                            writes=[BV1[t]], dma=True)
                for vi, (w1t, srcT, Gt, BG) in enumerate(((w1k, XT[0], Gk, BGk), (w1v, XT[1], Gv, BGv))):
                    for ap_ in range(32):
                        a_, p16 = ap_ // 16, ap_ % 16
                        c_0 = 16 * a_ + p16
                        P.op("pe", lambda e, w1t=w1t, srcT=srcT, ap_=ap_, c_0=c_0: e.matmul(
                            psum[1][:, 0:255], lhsT=w1t[:, ap_, :], rhs=srcT[:, c_0:c_0 + 16 * 254 + 1:16],
                            start=(ap_ == 0), stop=(ap_ == 31)), reads=[Bc, BXT[vi]], writes=[Bps[1]])
                    gelu_cols(P, psum[1][:, 0:255], hid0[:, vi:vi + 1], 255, x32, x2, sgt, Gt[:, 0:255], Bx, Bps[1], BG, rd=[Bc])
                P.op("pe", lambda e: e.matmul(psum[2][:, 0:255], lhsT=w2[:, 0, :], rhs=Gk[:, 0:255], start=True, stop=True),
                     reads=[Bc, BGk], writes=[Bps[2]])
                P.op("pe", lambda e: e.matmul(psum[3][:, 0:255], lhsT=w2[:, 1, :], rhs=Gk[:, 0:255], start=True, stop=True),
                     reads=[Bc, BGk], writes=[Bps[3]])
                P.op("dve", lambda e: e.tensor_tensor(out=x32[:, 0:255], in0=psum[2][:, 0:255], in1=ccmp[:, 0:255], op=ALU.mult),
                     reads=[Bps[2], Bc, Bx], writes=[Bx])
                P.op("dve", lambda e: e.tensor_tensor(out=x2[:, 0:255], in0=psum[3][:, 0:255], in1=scmp[:, 0:255], op=ALU.mult),
                     reads=[Bps[3], Bc, Bx], writes=[Bx])
                P.op("dve", lambda e: e.tensor_tensor(out=kcbT[:, 0:255], in0=x32[:, 0:255], in1=x2[:, 0:255], op=ALU.add),
                     reads=[Bx], writes=[Bkcb])
                for nt_, nn in ((0, 128), (1, 127)):
                    P.op("pe", lambda e, nt_=nt_, nn=nn: e.matmul(
                        psum[4][0:nn, nt_ * 128:(nt_ + 1) * 128], lhsT=Gv[:, nt_ * 128:nt_ * 128 + nn], rhs=w2[:, 2, :],
                        start=(nt_ == 0), stop=True, skip_group_check=True), reads=[Bc, BGv], writes=[Bps[4]])
                P.op("act", lambda e: e.activation(out=vcb[:, 0, :], in_=psum[4][:, 0:128], func=AF.Copy), reads=[Bps[4]], writes=[Bvcb])
                P.op("act", lambda e: e.activation(out=vcb[0:127, 1, :], in_=psum[4][0:127, 128:256], func=AF.Copy), reads=[Bps[4]], writes=[Bvcb])

                for i in range(8):
                    S = 4 * i + 3
                    tok0 = i * 128
                    P.op("pool", lambda e, tok0=tok0, g=g: e.dma_start(out=qtm[:], in_=q_s[tok0:tok0 + 128, g * 768:(g + 1) * 768]),
                         writes=[Bq], dma=True)
                    P.op("sp", lambda e, tok0=tok0, g=g: e.dma_start(out=gt[:], in_=gate_s[tok0:tok0 + 128, g * 18:(g + 1) * 18]),
                         writes=[Bgt], dma=True)
                    for h in range(6):
                        P.op("pe", lambda e, h=h: e.transpose(out=psb[:, h * 128:(h + 1) * 128], in_=qtm[:, h * 128:(h + 1) * 128],
                                                              identity=identb[:, 0:128]), reads=[Bq, Bc], writes=[Bpsb])
                    P.op("act", lambda e: e.activation(out=qT[:], in_=psb[:, 0:768], func=AF.Copy), reads=[Bpsb], writes=[BqT])
                    def scr(h):
                        return psum[h // 2][:, (h % 2) * 256:(h % 2) * 256 + 255]

                    def ocr(h):
                        return psum[3 + h // 4][:, (h % 4) * 128:(h % 4) * 128 + 128]
                    for h in range(6):
                        wr = [Bps[h // 2]]
                        rd = [BqT, Bkcb]
                        P.op("pe", lambda e, h=h: e.matmul(scr(h), lhsT=qT[:, h * 128:(h + 1) * 128], rhs=kcbT[:, 0:255],
                                                           start=True, stop=True, skip_group_check=True), reads=rd, writes=wr)
                    for h in range(6):
                        P.op("dve", lambda e, h=h: e.reduce_max(out=smh[:, h, 0:1], in_=scr(h), axis=AX.X), reads=[Bps[h // 2]], writes=[Bsmh[h]])
                    for h in range(6):
                        P.op("dve", lambda e, h=h: e.tensor_scalar(out=smh[:, h, 1:2], in0=smh[:, h, 0:1], scalar1=-SCALE, scalar2=None, op0=ALU.mult),
                             reads=[Bsmh[h]], writes=[Bsmh[h]])
                    for h in range(6):
                        P.op("act", lambda e, h=h: e.activation(out=evh[:, h, 0:255], in_=scr(h), func=AF.Exp, bias=smh[:, h, 1:2], scale=SCALE),
                             reads=[Bps[h // 2], Bsmh[h]], writes=[Bevh[h]])
                    for h in range(6):
                        P.op("dve", lambda e, h=h, i=i: e.tensor_tensor(out=evh[:, h, 0:255], in0=evh[:, h, 0:255], in1=cval[:, i, 0:255], op=ALU.mult),
                             reads=[Bevh[h], Bc], writes=[Bevh[h]])
                    for h in range(6):
                        P.op("dve", lambda e, h=h: e.reduce_sum(out=smh[:, h, 2:3], in_=evh[:, h, 0:255], axis=AX.X), reads=[Bevh[h]], writes=[Bsmh[h]])
                    for h in range(6):
                        P.op("dve", lambda e, h=h: e.tensor_scalar(out=smh[:, h, 2:3], in0=smh[:, h, 2:3], scalar1=1e-30, scalar2=None, op0=ALU.max),
                             reads=[Bsmh[h]], writes=[Bsmh[h]])
                    for h in range(6):
                        P.op("dve", lambda e, h=h: e.reciprocal(out=smh[:, h, 3:4], in_=smh[:, h, 2:3]), reads=[Bsmh[h]], writes=[Bsmh[h]])
                    for h in range(6):
                        P.op("dve", lambda e, h=h: e.tensor_scalar(out=evh[:, h, 0:255], in0=evh[:, h, 0:255], scalar1=smh[:, h, 3:4], scalar2=None, op0=ALU.mult),
                             reads=[Bevh[h], Bsmh[h]], writes=[Bevh[h]])
                    for h in range(6):
                        P.op("act", lambda e, h=h: e.activation(out=Pbh[:, h, 0:255], in_=evh[:, h, 0:255], func=AF.Copy), reads=[Bevh[h]], writes=[BPbh[h]])
                    for h in range(6):
                        if h == 0:
                            P.op("dve", lambda e: e.tensor_copy(out=Pp[:, 1:256], in_=evh[:, 0, 0:255]), reads=[Bevh[0]], writes=[BPp])
                        else:
                            P.op("dve", lambda e, h=h: e.tensor_tensor(out=Pp[:, 1:256], in0=Pp[:, 1:256], in1=evh[:, h, 0:255], op=ALU.add),
                                 reads=[Bevh[h], BPp], writes=[BPp])
                    for (h0, h1) in ((0, 4), (4, 6)):
                        for h in range(h0, h1):
                            for nt_, nn in ((0, 128), (1, 127)):
                                P.op("pe", lambda e, h=h, h0=h0, nt_=nt_, nn=nn: e.transpose(
                                    out=psb[0:nn, (h - h0) * 256 + nt_ * 128:(h - h0) * 256 + (nt_ + 1) * 128],
                                    in_=Pbh[:, h, nt_ * 128:nt_ * 128 + nn], identity=identb[:, 0:128]), reads=[BPbh[h], Bc], writes=[Bpsb])
                        for h in range(h0, h1):
                            P.op("act", lambda e, h=h, h0=h0: e.activation(out=PTh[:, h, :], in_=psb[:, (h - h0) * 256:(h - h0 + 1) * 256], func=AF.Copy),
                                 reads=[Bpsb], writes=[BPTh[h]])
                    for h in range(6):
                        wr = [Bps[3 + h // 4]]
                        rd = [BPTh[h], Bvcb]
                        for nt_, nn in ((0, 128), (1, 127)):
                            P.op("pe", lambda e, h=h, nt_=nt_, nn=nn: e.matmul(ocr(h), lhsT=PTh[0:nn, h, nt_ * 128:(nt_ + 1) * 128], rhs=vcb[0:nn, nt_, :],
                                                                               start=(nt_ == 0), stop=(nt_ == 1), skip_group_check=True), reads=rd, writes=wr)
                    for h in range(6):
                        P.op("dve", lambda e, h=h: e.tensor_scalar(out=acc[:, h * 128:(h + 1) * 128], in0=ocr(h),
                                                                   scalar1=gt[:, 3 * h:3 * h + 1], scalar2=None, op0=ALU.mult),
                             reads=[Bps[3 + h // 4], Bgt], writes=[Bacc])
                    P.op("dve", lambda e: e.tensor_reduce(out=imp[:, :], in_=Pp[:, 0:256].rearrange("p (s j) -> p s j", j=4), axis=AX.X, op=ALU.add),
                         reads=[BPp], writes=[Bimp])
                    P.op("dve", lambda e: e.tensor_tensor(out=imp[:, :], in0=imp[:, :], in1=Pp[:, 4:260:4], op=ALU.add), reads=[BPp, Bimp], writes=[Bimp])
                    P.op("dve", lambda e, i=i: e.tensor_tensor(out=imp[:, :], in0=imp[:, :], in1=addm[:, i, :], op=ALU.add), reads=[Bimp, Bc], writes=[Bimp])
                    P.op("dve", lambda e: e.max(out=m8[:, 0:8], in_=imp[:, :]), reads=[Bimp], writes=[Bsm])
                    P.op("dve", lambda e: e.match_replace(out=imw[:, :], in_to_replace=m8[:, 0:8], in_values=imp[:, :], imm_value=-3.0e38),
                         reads=[Bimp, Bsm], writes=[Bimp])
                    P.op("dve", lambda e: e.max(out=m8[:, 8:16], in_=imw[:, :]), reads=[Bimp], writes=[Bsm])
                    P.op("dve", lambda e: e.tensor_scalar(out=imw[:, :], in0=imp[:, :], scalar1=m8[:, 15:16], scalar2=None, op0=ALU.is_ge),
                         reads=[Bimp, Bsm], writes=[Bimp])
                    P.op("dve", lambda e, i=i: e.tensor_tensor(out=imw[:, :], in0=imw[:, :], in1=vblk[:, i, :], op=ALU.mult), reads=[Bimp, Bc], writes=[Bimp])
                    P.op("dve", lambda e: e.tensor_scalar(out=negb[:, :], in0=imw[:, :], scalar1=-1.0, scalar2=-NEG, op0=ALU.add, op1=ALU.mult),
                         reads=[Bimp], writes=[Bneg])
                    P.op("dve", lambda e: e.tensor_copy(out=negx[:, :].rearrange("p (s j) -> p s j", j=64),
                                                        in_=negb[:, :].unsqueeze(2).to_broadcast([128, 64, 64])), reads=[Bneg], writes=[Bnegx])
                    pti = 0
                    for br in (1, 2):
                        if br == 1:
                            kts = list(range(0, S + 1))
                            KT, VV, BK, BV = XT[2], V1[3], BXT[2], BV1[3]
                        else:
                            kts = list(range(max(S - 4, 0), S + 1))
                            KT, VV, BK, BV = XT[4], V1[5], BXT[4], BV1[5]
                        items = []
                        for ki, kt in enumerate(kts):
                            biases = []
                            if br == 1:
                                biases.append(negx[:, kt * 128:(kt + 1) * 128])
                            if kt == S:
                                biases.append(cb_c[:, :])
                            if br == 2 and kt == S - 4:
                                biases.append(cb_a[:, :])
                            for hh in range(2):
                                items.append((ki, kt, hh, biases))
                        pti0 = pti
                        pti += len(items)

                        def qk(n, items=items, KT=KT, BK=BK):
                            ki, kt, hh, biases = items[n]
                            ps_s = 2 + (n % 3)
                            P.op("pe", lambda e, ps_s=ps_s, kt=kt, hh=hh, KT=KT, nb=len(biases): e.matmul(
                                psum[ps_s][:, 0:384], lhsT=KT[:, kt * 128:(kt + 1) * 128], rhs=qT[:, hh * 384:(hh + 1) * 384],
                                start=True, stop=(nb == 0)), reads=[BK, BqT], writes=[Bps[ps_s]])
                            for bi, bias_ap in enumerate(biases):
                                P.op("pe", lambda e, ps_s=ps_s, bias_ap=bias_ap, last=(bi == len(biases) - 1): e.matmul(
                                    psum[ps_s][:, 0:384], lhsT=bias_ap, rhs=identb[:, 0:384], start=False, stop=last),
                                    reads=[Bnegx, Bc], writes=[Bps[ps_s]])

                        def expv(n, items=items, VV=VV, BV=BV, pti0=pti0, nk_=len(kts)):
                            ki, kt, hh, biases = items[n]
                            ps_s = 2 + (n % 3)
                            pt_ = (pti0 + n) % 4
                            P.op("act", lambda e, ps_s=ps_s, pt_=pt_, kt=kt: e.activation(
                                out=PTs[pt_][:, :], in_=psum[ps_s][:, 0:384], func=AF.Exp, bias=padb[:, kt:kt + 1], scale=SCALE),
                                reads=[Bps[ps_s], Bc], writes=[BPTs[pt_]])
                            for hl in range(3):
                                P.op("pe", lambda e, hh=hh, hl=hl, pt_=pt_, kt=kt, VV=VV, first=(ki == 0 and hl == 0), last=(ki == nk_ - 1): e.matmul(
                                    psum[5 + hh][:, hl * 130:(hl + 1) * 130], lhsT=PTs[pt_][:, hl * 128:(hl + 1) * 128], rhs=VV[:, kt, :],
                                    start=first, stop=last, skip_group_check=True), reads=[BPTs[pt_], BV], writes=[Bps[5 + hh]])

                        for n in range(min(2, len(items))):
                            qk(n)
                        for n in range(len(items)):
                            if n + 2 < len(items):
                                qk(n + 2)
                            expv(n)
                        for hh in range(2):
                            for hl in range(3):
                                h = hh * 3 + hl
                                P.op("dve", lambda e, hh=hh, hl=hl, h=h: e.reciprocal(out=sc6[:, h:h + 1], in_=psum[5 + hh][:, hl * 130 + 128:hl * 130 + 129]),
                                     reads=[Bps[5 + hh]], writes=[Bsc6])
                                P.op("dve", lambda e, h=h, br=br: e.tensor_tensor(out=sc6[:, h:h + 1], in0=sc6[:, h:h + 1], in1=gt[:, 3 * h + br:3 * h + br + 1], op=ALU.mult),
                                     reads=[Bsc6, Bgt], writes=[Bsc6])
                                P.op("dve", lambda e, hh=hh, hl=hl, h=h: e.scalar_tensor_tensor(
                                    out=acc[:, h * 128:(h + 1) * 128], in0=psum[5 + hh][:, hl * 130:hl * 130 + 128], scalar=sc6[:, h:h + 1],
                                    in1=acc[:, h * 128:(h + 1) * 128], op0=ALU.mult, op1=ALU.add), reads=[Bps[5 + hh], Bsc6, Bacc], writes=[Bacc])
                    P.op("act", lambda e: e.activation(out=accb[:], in_=acc[:], func=AF.Copy), reads=[Bacc], writes=[Baccb])
                    for h in range(6):
                        P.op("pe", lambda e, h=h: e.transpose(out=psb[:, h * 128:(h + 1) * 128], in_=accb[:, h * 128:(h + 1) * 128],
                                                              identity=identb[:, 0:128]), reads=[Baccb, Bc], writes=[Bpsb])
                    P.op("act", lambda e: e.activation(out=mixo[:, :, :].rearrange("p a b -> p (a b)"), in_=psb[:, 0:768], func=AF.Copy),
                         reads=[Bpsb], writes=[Bmixo])
                    P.op("sp", lambda e, g=g, tok0=tok0: e.dma_start(out=mixT_s[:, 8 + 6 * g:14 + 6 * g, tok0:tok0 + 128], in_=mixo[:, :, :]),
                         reads=[Bmixo], dma=True)
            P.emit_phase()

        with contextlib.ExitStack() as st:
            sb = lambda name, shape, dt: st.enter_context(nc.sbuf_tensor(uname(name), list(shape), dt))
            identb = sb("s_identb", [128, 128], BF16)
            identf = sb("s_identf", [128, 128], F32)
            id4x3 = sb("s_id4x3", [4, 12], BF16)
            id4x6 = sb("s_id4x6", [4, 24], BF16)
            PTw = [sb("s_PTw%d" % i, [128, 24], BF16) for i in range(4)]
            cb_c = sb("s_cbc", [128, 128], BF16)
            cb_a = sb("s_cba", [128, 128], BF16)
            w1k = sb("s_w1k", [128, 32, 128], BF16)
            w1v = sb("s_w1v", [128, 32, 128], BF16)
            w2 = sb("s_w2", [128, 3, 128], BF16)
            peT = sb("s_peT", [128, 2, 32], BF16)
            hid0 = sb("s_hid0", [128, 2], F32)
            ccs = sb("s_ccs", [128, 512], F32)
            scs = sb("s_scs", [128, 512], F32)
            addms = sb("s_addms", [4, 129], F32)
            oh = sb("s_oh", [128, 5], F32)
            ptb_i = sb("s_ptb_i", [128, 64], I32)
            ptb_f = sb("s_ptb_f", [128, 64], F32)
            ptmp = sb("s_ptmp", [128, 64], F32)
            psel = sb("s_psel", [128, 16], F32)
            idx = sb("s_idx", [128, 16], I32)
            G32 = [sb("s_G32_%d" % i, [128, 2048], F32) for i in range(6)]
            XT = sb("s_XT", [128, 2, 4, 2048], BF16)
            V1 = sb("s_V1", [128, 2, 65, 130], BF16)
            XTn = sb("s_XTn", [128, 2, 4], BF16)
            kwT = sb("s_kwT", [128, 2, 516], BF16)
            V1w = sb("s_V1w", [128, 2, 5, 130], BF16)
            wtm = sb("s_wtm", [128, 4, 256], BF16)
            ntm = sb("s_ntm", [4, 256], BF16)
            kcbT = sb("s_kcbT", [128, 2, 512], BF16)
            vcb = sb("s_vcb", [128, 2, 4, 128], BF16)
            x32 = sb("s_x32", [128, 512], F32)
            x2 = sb("s_x2", [128, 512], F32)
            sgt = sb("s_sgt", [128, 512], F32)
            Gk = sb("s_Gk", [128, 512], BF16)
            Gv = sb("s_Gv", [128, 512], BF16)
            qtm = sb("s_qtm", [4, 768], BF16)
            qT = sb("s_qT", [128, 24], BF16)
            gt = sb("s_gt", [4, 18], F32)
            e32 = sb("s_e32", [4, 512], F32)
            ev = sb("s_ev", [4, 512], F32)
            Pp = sb("s_Pp", [4, 520], F32)
            Pb = sb("s_Pb", [4, 512], BF16)
            PT = sb("s_PT", [128, 4, 4], BF16)
            sm = sb("s_sm", [4, 8], F32)
            imp = sb("s_imp", [4, 129], F32)
            imw = sb("s_imw", [4, 129], F32)
            m8 = sb("s_m8", [4, 16], F32)
            negb = sb("s_negb", [4, 129], BF16)
            negx = sb("s_negx", [4, 16, 128], BF16)
            PTs = [sb("s_PTs%d" % i, [128, 12], BF16) for i in range(4)]
            acc = sb("s_acc", [4, 768], F32)
            accb = sb("s_accb", [4, 768], BF16)
            mixo = sb("s_mixo", [128, 6, 4], BF16)
            sc6 = sb("s_sc6", [4, 8], F32)
            smh = sb("s_smh", [4, 6, 4], F32)
            evh = sb("s_evh", [4, 6, 512], F32)
            Pbh = sb("s_Pbh", [4, 6, 512], BF16)
            PTh = sb("s_PTh", [128, 6, 16], BF16)
            Bsmh, Bevh, BPbh, BPTh, Boch = ([Buf() for _ in range(6)] for _ in range(5))

            Bc, Bidx = Buf(), Buf()
            BG32 = [Buf() for _ in range(6)]
            BXT, BV1, BXTn, BkwT, BV1w, Bwtm, Bntm = (Buf() for _ in range(7))
            Bkcb, Bvcb, Bx, BGk, BGv = Buf(), Buf(), Buf(), Buf(), Buf()
            Bq, BqT, Bgt, Be, BPp, BPb, BPT, Bsm, Bimp, Bneg, Bnegx = (Buf() for _ in range(11))
            BPTs = [Buf() for _ in range(4)]
            Bacc, Baccb, Bmixo, Bsc6 = Buf(), Buf(), Buf(), Buf()
            Bps = [Buf() for _ in range(7)]
            Bpsb = Buf()

            P.op("pool", lambda e: e.dma_start(out=identb[:], in_=ident[:, :]), writes=[Bc], dma=True)
            P.op("sp", lambda e: e.dma_start(out=identf[:], in_=ident[:, :]), writes=[Bc], dma=True)
            for j3 in range(3):
                P.op("pool", lambda e, j3=j3: e.dma_start(out=id4x3[:, j3 * 4:(j3 + 1) * 4], in_=ident[0:4, 0:4]), writes=[Bc], dma=True)
            for j6 in range(6):
                P.op("pool", lambda e, j6=j6: e.dma_start(out=id4x6[:, j6 * 4:(j6 + 1) * 4], in_=ident[0:4, 0:4]), writes=[Bc], dma=True)
            P.op("pool", lambda e: e.dma_start(out=cb_c[:], in_=cbc_d[:, :]), writes=[Bc], dma=True)
            P.op("pool", lambda e: e.dma_start(out=cb_a[:], in_=cba_d[:, :]), writes=[Bc], dma=True)
            P.op("pool", lambda e: e.dma_start(out=w1k[:], in_=w1k_d[:, :, :]), writes=[Bc], dma=True)
            P.op("pool", lambda e: e.dma_start(out=w1v[:], in_=w1v_d[:, :, :]), writes=[Bc], dma=True)
            P.op("pool", lambda e: e.dma_start(out=w2[:], in_=w2_d[:, :, :]), writes=[Bc], dma=True)
            P.op("pool", lambda e: e.dma_start(out=peT[:], in_=peT_d[:, :, :]), writes=[Bc], dma=True)
            P.op("sp", lambda e: e.dma_start(out=ccs[:], in_=ccs_d[:, :]), writes=[Bc], dma=True)
            P.op("sp", lambda e: e.dma_start(out=scs[:], in_=scs_d[:, :]), writes=[Bc], dma=True)
            P.op("sp", lambda e: e.dma_start(out=addms[:], in_=addms_d[:, :]), writes=[Bc], dma=True)
            P.op("sp", lambda e: e.dma_start(out=oh[:], in_=oh_d[:, :]), writes=[Bc], dma=True)
            P.op("dve", lambda e: e.memset(V1[:, :, :, 128:130], 1.0), writes=[BV1])
            P.op("dve", lambda e: e.memset(V1w[:, :, :, 128:130], 1.0), writes=[BV1w])
            P.op("dve", lambda e: e.memset(Pp[:], 0.0), writes=[BPp])
            P.op("dve", lambda e: e.memset(vcb[:], 0.0), writes=[Bvcb])
            for vi, w1t in enumerate((w1k, w1v)):
                for pp in range(32):
                    P.op("pe", lambda e, vi=vi, w1t=w1t, pp=pp: e.matmul(
                        psum[0][:, vi:vi + 1], lhsT=w1t[:, pp, :], rhs=peT[:, vi, pp:pp + 1],
                        start=(pp == 0 and vi == 0), stop=(pp == 31), skip_group_check=True), reads=[Bc], writes=[Bps[0]])
            P.op("act", lambda e: e.activation(out=hid0[:, 0:2], in_=psum[0][:, 0:2], func=AF.Copy), reads=[Bps[0]], writes=[Bc])

            caches = (ckc_d, cvc_d, cks_d, cvs_d)
            bc_reg = {}
            gi_ = 0
            tpi = 0
            for sbi in range(4):
                tokS = 1024 + 4 * sbi
                rowS = 4096 + 4 * sbi
                P.op("sp", lambda e, sbi=sbi: e.dma_start(out=ptb_i[:], in_=ptab_d[sbi, :].partition_broadcast(128)), writes=[Bidx], dma=True)
                P.op("dve", lambda e: e.tensor_copy(out=ptb_f[:], in_=ptb_i[:]), reads=[Bidx], writes=[Bidx])
                P.op("dve", lambda e: e.tensor_tensor(out=ptmp[:, :].rearrange("p (c a) -> p c a", a=4),
                                                      in0=ptb_f[:, :].rearrange("p (c a) -> p c a", a=4),
                                                      in1=oh[:, 0:4].unsqueeze(1).to_broadcast([128, 16, 4]), op=ALU.mult),
                     reads=[Bidx, Bc], writes=[Bidx])
                P.op("dve", lambda e: e.tensor_reduce(out=psel[:, :], in_=ptmp[:, :].rearrange("p (c a) -> p c a", a=4), axis=AX.X, op=ALU.add),
                     reads=[Bidx], writes=[Bidx])
                P.op("dve", lambda e: e.tensor_scalar(out=psel[:, :], in0=psel[:, :], scalar1=32.0, scalar2=oh[:, 4:5], op0=ALU.mult, op1=ALU.add),
                     reads=[Bidx, Bc], writes=[Bidx])
                P.op("dve", lambda e: e.tensor_copy(out=idx[:, :], in_=psel[:, :]), reads=[Bidx], writes=[Bidx])
                for gp in range(2):
                    for ci, cache_d in enumerate(caches):
                        for c in range(16):
                            gb = gi_ % 6
                            gi_ += 1
                            def gather_fn(e, gb=gb, c=c, cache_d=cache_d):
                                if "r" not in bc_reg:
                                    bc_reg["r"] = e.to_reg(2560 * 32 - 1)
                                return e.indirect_dma_start(
                                    out=G32[gb][:, :], out_offset=None, in_=cache_d[:, :],
                                    in_offset=bass.IndirectOffsetOnAxis(ap=idx[:, c:c + 1], axis=0),
                                    bounds_check=bc_reg["r"], oob_is_err=False)
                            P.op("pool", gather_fn, reads=[Bidx], writes=[BG32[gb]], dma=True)
                            if ci < 3:
                                for gl in range(2):
                                    g = 2 * gp + gl
                                    pb_ = 2 + (tpi % 2)
                                    tpi += 1
                                    for r4 in range(4):
                                        P.op("pe", lambda e, pb_=pb_, r4=r4, g=g, gb=gb: e.transpose(
                                            out=psum[pb_][:, r4 * 128:(r4 + 1) * 128], in_=G32[gb][:, r4 * 512 + g * 128:r4 * 512 + (g + 1) * 128],
                                            identity=identf[:, :]), reads=[BG32[gb], Bc], writes=[Bps[pb_]])
                                    eng = "act" if tpi % 2 == 0 else "dve"
                                    src = psum[pb_][:, :].rearrange("p (r q) -> p r q", q=128)
                                    if eng == "act":
                                        P.op("act", lambda e, gl=gl, c=c, src=src: e.activation(out=XT[:, gl, :, 128 * c:128 * (c + 1)], in_=src, func=AF.Copy),
                                             reads=[Bps[pb_]], writes=[BXT])
                                    else:
                                        P.op("dve", lambda e, gl=gl, c=c, src=src: e.tensor_copy(out=XT[:, gl, :, 128 * c:128 * (c + 1)], in_=src),
                                             reads=[Bps[pb_]], writes=[BXT])
                            else:
                                for gl in range(2):
                                    g = 2 * gp + gl
                                    src = G32[gb][:, :].rearrange("p (r q) -> p r q", q=512)[:, :, g * 128:(g + 1) * 128]
                                    eng = "act" if gl == 0 else "dve"
                                    if eng == "act":
                                        P.op("act", lambda e, gl=gl, c=c, src=src: e.activation(out=V1[:, gl, 4 * c:4 * c + 4, 0:128], in_=src, func=AF.Copy),
                                             reads=[BG32[gb]], writes=[BV1])
                                    else:
                                        P.op("dve", lambda e, gl=gl, c=c, src=src: e.tensor_copy(out=V1[:, gl, 4 * c:4 * c + 4, 0:128], in_=src),
                                             reads=[BG32[gb]], writes=[BV1])
                        if ci < 2:
                            w1t = w1k if ci == 0 else w1v
                            Gt, BG = (Gk, BGk) if ci == 0 else (Gv, BGv)
                            for gl in range(2):
                                mi = 0
                                for a_ in range(2):
                                    for j4 in range(4):
                                        for r4 in range(4):
                                            c_0 = 4 * a_ + j4
                                            P.op("pe", lambda e, w1t=w1t, a_=a_, j4=j4, r4=r4, gl=gl, c_0=c_0, mi=mi: e.matmul(
                                                psum[1][:, 0:511], lhsT=w1t[:, a_ * 16 + 4 * j4 + r4, :],
                                                rhs=XT[:, gl, r4, c_0:c_0 + 4 * 510 + 1:4], start=(mi == 0), stop=(mi == 31)),
                                                reads=[Bc, BXT], writes=[Bps[1]])
                                            mi += 1
                                gelu_cols(P, psum[1][:, 0:511], hid0[:, ci:ci + 1], 511, x32, x2, sgt, Gt[:, 0:511], Bx, Bps[1], BG, rd=[Bc])
                                if ci == 0:
                                    P.op("pe", lambda e: e.matmul(psum[4][:, 0:511], lhsT=w2[:, 0, :], rhs=Gk[:, 0:511], start=True, stop=True),
                                         reads=[Bc, BGk], writes=[Bps[4]])
                                    P.op("pe", lambda e: e.matmul(psum[5][:, 0:511], lhsT=w2[:, 1, :], rhs=Gk[:, 0:511], start=True, stop=True),
                                         reads=[Bc, BGk], writes=[Bps[5]])
                                    P.op("dve", lambda e: e.tensor_tensor(out=x32[:, 0:511], in0=psum[4][:, 0:511], in1=ccs[:, 0:511], op=ALU.mult),
                                         reads=[Bps[4], Bc, Bx], writes=[Bx])
                                    P.op("dve", lambda e: e.tensor_tensor(out=x2[:, 0:511], in0=psum[5][:, 0:511], in1=scs[:, 0:511], op=ALU.mult),
                                         reads=[Bps[5], Bc, Bx], writes=[Bx])
                                    P.op("dve", lambda e, gl=gl: e.tensor_tensor(out=kcbT[:, gl, 0:511], in0=x32[:, 0:511], in1=x2[:, 0:511], op=ALU.add),
                                         reads=[Bx], writes=[Bkcb])
                                else:
                                    for nt_, nn in ((0, 128), (1, 128), (2, 128), (3, 127)):
                                        P.op("pe", lambda e, nt_=nt_, nn=nn: e.matmul(
                                            psum[4][0:nn, nt_ * 128:(nt_ + 1) * 128], lhsT=Gv[:, nt_ * 128:nt_ * 128 + nn], rhs=w2[:, 2, :],
                                            start=(nt_ == 0), stop=True, skip_group_check=True), reads=[Bc, BGv], writes=[Bps[4]])
                                    for nt_, nn in ((0, 128), (1, 128), (2, 128), (3, 127)):
                                        P.op("act", lambda e, nt_=nt_, nn=nn, gl=gl: e.activation(
                                            out=vcb[0:nn, gl, nt_, :], in_=psum[4][0:nn, nt_ * 128:(nt_ + 1) * 128], func=AF.Copy),
                                            reads=[Bps[4], Bvcb], writes=[Bvcb])
                    P.op("pool", lambda e, rowS=rowS, gp=gp: e.dma_start(out=ntm[:, :], in_=kvout[2, rowS:rowS + 4, gp * 256:(gp + 1) * 256]),
                         writes=[Bntm], dma=True)
                    for gl in range(2):
                        P.op("pe", lambda e, gl=gl: e.transpose(out=psb[:, gl * 4:gl * 4 + 4], in_=ntm[0:4, gl * 128:(gl + 1) * 128], identity=identb[0:4, 0:4]),
                             reads=[Bntm, Bc], writes=[Bpsb])
                    P.op("act", lambda e: e.activation(out=XTn[:, :, :].rearrange("p a b -> p (a b)"), in_=psb[:, 0:8], func=AF.Copy),
                         reads=[Bpsb], writes=[BXTn])
                    for gl in range(2):
                        g = 2 * gp + gl
                        P.op("pool", lambda e, rowS=rowS, g=g, gl=gl: e.dma_start(out=V1[0:4, gl, 64, 0:128], in_=kvout[3, rowS:rowS + 4, g * 128:(g + 1) * 128]),
                             writes=[BV1], dma=True)
                        P.op("pool", lambda e, rowS=rowS, g=g, gl=gl: e.dma_start(out=V1w[0:4, gl, 4, 0:128], in_=kvout[5, rowS:rowS + 4, g * 128:(g + 1) * 128]),
                             writes=[BV1w], dma=True)
                        P.op("pool", lambda e, sbi=sbi, g=g, gl=gl: e.dma_start(
                            out=V1w[:, gl, 0:4, 0:128], in_=st_vw[sbi, :, g * 128:(g + 1) * 128].rearrange("(i p) d -> p i d", p=128)),
                            writes=[BV1w], dma=True)
                    P.op("pool", lambda e, sbi=sbi, gp=gp: e.dma_start(
                        out=wtm[:, :, :], in_=st_kw[sbi, :, gp * 256:(gp + 1) * 256].rearrange("(i p) d -> p i d", p=128)),
                        writes=[Bwtm], dma=True)
                    for gl in range(2):
                        for w_ in range(4):
                            P.op("pe", lambda e, gl=gl, w_=w_: e.transpose(out=psb[:, (gl * 4 + w_) * 128:(gl * 4 + w_ + 1) * 128],
                                                                           in_=wtm[:, w_, gl * 128:(gl + 1) * 128], identity=identb[:, :]),
                                 reads=[Bwtm, Bc], writes=[Bpsb])
                    P.op("act", lambda e: e.activation(out=kwT[:, :, 0:512], in_=psb[:, :].rearrange("p (a b) -> p a b", a=2), func=AF.Copy),
                         reads=[Bpsb], writes=[BkwT])
                    P.op("pool", lambda e, rowS=rowS, gp=gp: e.dma_start(out=ntm[:, :], in_=kvout[4, rowS:rowS + 4, gp * 256:(gp + 1) * 256]),
                         writes=[Bntm], dma=True)
                    for gl in range(2):
                        P.op("pe", lambda e, gl=gl: e.transpose(out=psb[:, gl * 4:gl * 4 + 4], in_=ntm[0:4, gl * 128:(gl + 1) * 128], identity=identb[0:4, 0:4]),
                             reads=[Bntm, Bc], writes=[Bpsb])
                    P.op("act", lambda e: e.activation(out=kwT[:, :, 512:516], in_=psb[:, 0:8].rearrange("p (a b) -> p a b", a=2), func=AF.Copy),
                         reads=[Bpsb, BkwT], writes=[BkwT])

                    for gl in range(2):
                        g = 2 * gp + gl
                        P.op("pool", lambda e, tokS=tokS, g=g: e.dma_start(out=qtm[:], in_=q_s[tokS:tokS + 4, g * 768:(g + 1) * 768]),
                             writes=[Bq], dma=True)
                        P.op("sp", lambda e, tokS=tokS, g=g: e.dma_start(out=gt[:], in_=gate_s[tokS:tokS + 4, g * 18:(g + 1) * 18]),
                             writes=[Bgt], dma=True)
                        for h in range(6):
                            P.op("pe", lambda e, h=h: e.transpose(out=psb[:, h * 4:(h + 1) * 4], in_=qtm[0:4, h * 128:(h + 1) * 128],
                                                                  identity=identb[0:4, 0:4]), reads=[Bq, Bc], writes=[Bpsb])
                        P.op("act", lambda e: e.activation(out=qT[:], in_=psb[:, 0:24], func=AF.Copy), reads=[Bpsb], writes=[BqT])
                        NT4 = ((0, 128), (1, 128), (2, 128), (3, 127))
                        for rnd in range(2):
                            hs = [3 * rnd + k for k in range(3)]
                            for h in hs:
                                P.op("pe", lambda e, h=h, gl=gl: e.matmul(psum[h % 3][0:4, 0:511], lhsT=qT[:, h * 4:(h + 1) * 4], rhs=kcbT[:, gl, 0:511],
                                                                          start=True, stop=True), reads=[BqT, Bkcb], writes=[Bps[h % 3]])
                            for h in hs:
                                P.op("dve", lambda e, h=h: e.reduce_max(out=smh[:, h, 0:1], in_=psum[h % 3][0:4, 0:511], axis=AX.X), reads=[Bps[h % 3]], writes=[Bsmh[h]])
                            for h in hs:
                                P.op("dve", lambda e, h=h: e.tensor_scalar(out=smh[:, h, 1:2], in0=smh[:, h, 0:1], scalar1=-SCALE, scalar2=None, op0=ALU.mult),
                                     reads=[Bsmh[h]], writes=[Bsmh[h]])
                            for h in hs:
                                P.op("act", lambda e, h=h: e.activation(out=evh[:, h, 0:511], in_=psum[h % 3][0:4, 0:511], func=AF.Exp, bias=smh[:, h, 1:2], scale=SCALE),
                                     reads=[Bps[h % 3], Bsmh[h]], writes=[Bevh[h]])
                            for h in hs:
                                P.op("dve", lambda e, h=h: e.reduce_sum(out=smh[:, h, 2:3], in_=evh[:, h, 0:511], axis=AX.X), reads=[Bevh[h]], writes=[Bsmh[h]])
                            for h in hs:
                                P.op("dve", lambda e, h=h: e.tensor_scalar(out=smh[:, h, 2:3], in0=smh[:, h, 2:3], scalar1=1e-30, scalar2=None, op0=ALU.max),
                                     reads=[Bsmh[h]], writes=[Bsmh[h]])
                            for h in hs:
                                P.op("dve", lambda e, h=h: e.reciprocal(out=smh[:, h, 3:4], in_=smh[:, h, 2:3]), reads=[Bsmh[h]], writes=[Bsmh[h]])
                            for h in hs:
                                P.op("dve", lambda e, h=h: e.tensor_scalar(out=evh[:, h, 0:511], in0=evh[:, h, 0:511], scalar1=smh[:, h, 3:4], scalar2=None, op0=ALU.mult),
                                     reads=[Bevh[h], Bsmh[h]], writes=[Bevh[h]])
                            for h in hs:
                                P.op("act", lambda e, h=h: e.activation(out=Pbh[:, h, 0:511], in_=evh[:, h, 0:511], func=AF.Copy), reads=[Bevh[h]], writes=[BPbh[h]])
                            for h in hs:
                                if h == 0:
                                    P.op("dve", lambda e: e.tensor_copy(out=Pp[:, 1:512], in_=evh[:, 0, 0:511]), reads=[Bevh[0]], writes=[BPp])
                                else:
                                    P.op("dve", lambda e, h=h: e.tensor_tensor(out=Pp[:, 1:512], in0=Pp[:, 1:512], in1=evh[:, h, 0:511], op=ALU.add),
                                         reads=[Bevh[h], BPp], writes=[BPp])
                            for h in hs:
                                for nt_, nn in NT4:
                                    P.op("pe", lambda e, h=h, nt_=nt_, nn=nn: e.transpose(out=psb[0:nn, h * 16 + nt_ * 4:h * 16 + nt_ * 4 + 4],
                                                                                          in_=Pbh[0:4, h, nt_ * 128:nt_ * 128 + nn], identity=identb[0:4, 0:4]),
                                         reads=[BPbh[h], Bc], writes=[Bpsb])
                            for h in hs:
                                P.op("act", lambda e, h=h: e.activation(out=PTh[:, h, :], in_=psb[:, h * 16:(h + 1) * 16], func=AF.Copy),
                                     reads=[Bpsb], writes=[BPTh[h]])
                            for h in hs:
                                first = (h % 3 == 0)
                                wr = [Bps[3 + rnd]]
                                rd = [BPTh[h], Bvcb]
                                for nt_, nn in NT4:
                                    P.op("pe", lambda e, h=h, rnd=rnd, nt_=nt_, nn=nn, gl=gl: e.matmul(
                                        psum[3 + rnd][0:4, (h % 3) * 128:(h % 3) * 128 + 128], lhsT=PTh[0:nn, h, nt_ * 4:nt_ * 4 + 4], rhs=vcb[0:nn, gl, nt_, :],
                                        start=(nt_ == 0), stop=(nt_ == 3), skip_group_check=True), reads=rd, writes=wr)
                            for h in hs:
                                P.op("dve", lambda e, h=h, rnd=rnd: e.tensor_scalar(out=acc[:, h * 128:(h + 1) * 128], in0=psum[3 + rnd][0:4, (h % 3) * 128:(h % 3) * 128 + 128],
                                                                                   scalar1=gt[:, 3 * h:3 * h + 1], scalar2=None, op0=ALU.mult),
                                     reads=[Bps[3 + rnd], Bgt], writes=[Bacc])
                        P.op("dve", lambda e: e.tensor_reduce(out=imp[:, :], in_=Pp[:, 0:516].rearrange("p (s j) -> p s j", j=4), axis=AX.X, op=ALU.add),
                             reads=[BPp], writes=[Bimp])
                        P.op("dve", lambda e: e.tensor_tensor(out=imp[:, :], in0=imp[:, :], in1=Pp[:, 4:517:4], op=ALU.add), reads=[BPp, Bimp], writes=[Bimp])
                        P.op("dve", lambda e: e.tensor_tensor(out=imp[:, :], in0=imp[:, :], in1=addms[:, :], op=ALU.add), reads=[Bimp, Bc], writes=[Bimp])
                        P.op("dve", lambda e: e.max(out=m8[:, 0:8], in_=imp[:, :]), reads=[Bimp], writes=[Bsm])
                        P.op("dve", lambda e: e.match_replace(out=imw[:, :], in_to_replace=m8[:, 0:8], in_values=imp[:, :], imm_value=-3.0e38),
                             reads=[Bimp, Bsm], writes=[Bimp])
                        P.op("dve", lambda e: e.max(out=m8[:, 8:16], in_=imw[:, :]), reads=[Bimp], writes=[Bsm])
                        P.op("dve", lambda e: e.tensor_scalar(out=imw[:, :], in0=imp[:, :], scalar1=m8[:, 15:16], scalar2=None, op0=ALU.is_ge),
                             reads=[Bimp, Bsm], writes=[Bimp])
                        P.op("dve", lambda e: e.tensor_scalar(out=negb[:, :], in0=imw[:, :], scalar1=-1.0, scalar2=-NEG, op0=ALU.add, op1=ALU.mult),
                             reads=[Bimp], writes=[Bneg])
                        P.op("dve", lambda e: e.tensor_copy(out=negx[:, :, :].rearrange("p c (b k) -> p c b k", k=16),
                                                            in_=negb[:, 0:128].rearrange("p (c b) -> p c b", b=8).unsqueeze(3).to_broadcast([4, 16, 8, 16])),
                             reads=[Bneg], writes=[Bnegx])
                        pti = 0
                        for br in (1, 2):
                            tiles = []
                            if br == 1:
                                for c in range(16):
                                    for r4 in range(4):
                                        tiles.append((XT[:, gl, r4, 128 * c:128 * (c + 1)], V1[:, gl, 4 * c + r4, :], 128, [negx[0:4, c, :]]))
                                tiles.append((XTn[:, gl, :], V1[0:4, gl, 64, :], 4, [cb_c[0:4, 0:4]]))
                                BK, BV = [BXT, BXTn], BV1
                            else:
                                for w_ in range(4):
                                    tiles.append((kwT[:, gl, 128 * w_:128 * (w_ + 1)], V1w[:, gl, w_, :], 128, [cb_a[0:4, 0:128]] if w_ == 0 else []))
                                tiles.append((kwT[:, gl, 512:516], V1w[0:4, gl, 4, :], 4, [cb_c[0:4, 0:4]]))
                                BK, BV = [BkwT], BV1w
                            items = list(enumerate(tiles))
                            pti0 = pti
                            pti += len(items)

                            def qk(n, items=items, BK=BK):
                                ki, (kap, vap, nk, biases) = items[n]
                                ps_s = 2 + (n % 3)
                                P.op("pe", lambda e, ps_s=ps_s, kap=kap, nk=nk, nb=len(biases): e.matmul(
                                    psum[ps_s][0:nk, 0:24], lhsT=kap, rhs=qT[:, 0:24],
                                    start=True, stop=(nb == 0)), reads=BK + [BqT], writes=[Bps[ps_s]])
                                for bi, bias_ap in enumerate(biases):
                                    P.op("pe", lambda e, ps_s=ps_s, nk=nk, bias_ap=bias_ap, last=(bi == len(biases) - 1): e.matmul(
                                        psum[ps_s][0:nk, 0:24], lhsT=bias_ap, rhs=id4x6[:, :], start=False, stop=last),
                                        reads=[Bnegx, Bc], writes=[Bps[ps_s]])

                            def expv(n, items=items, BV=BV, pti0=pti0, nt_=len(tiles)):
                                ki, (kap, vap, nk, biases) = items[n]
                                ps_s = 2 + (n % 3)
                                pt_ = (pti0 + n) % 4
                                P.op("act", lambda e, ps_s=ps_s, pt_=pt_, nk=nk: e.activation(
                                    out=PTw[pt_][0:nk, :], in_=psum[ps_s][0:nk, 0:24], func=AF.Exp, scale=SCALE),
                                    reads=[Bps[ps_s]], writes=[BPTs[pt_]])
                                for h in range(6):
                                    hh, hl = h // 3, h % 3
                                    P.op("pe", lambda e, hh=hh, hl=hl, h=h, pt_=pt_, nk=nk, vap=vap, first=(ki == 0 and hl == 0), last=(ki == nt_ - 1): e.matmul(
                                        psum[5 + hh][0:4, hl * 130:(hl + 1) * 130], lhsT=PTw[pt_][0:nk, h * 4:(h + 1) * 4], rhs=vap,
                                        start=first, stop=last, skip_group_check=True), reads=[BPTs[pt_], BV], writes=[Bps[5 + hh]])

                            for n in range(min(2, len(items))):
                                qk(n)
                            for n in range(len(items)):
                                if n + 2 < len(items):
                                    qk(n + 2)
                                expv(n)
                            for hh in range(2):
                                for hl in range(3):
                                    h = hh * 3 + hl
                                    P.op("dve", lambda e, hh=hh, hl=hl, h=h: e.reciprocal(out=sc6[:, h:h + 1], in_=psum[5 + hh][0:4, hl * 130 + 128:hl * 130 + 129]),
                                         reads=[Bps[5 + hh]], writes=[Bsc6])
                                    P.op("dve", lambda e, h=h, br=br: e.tensor_tensor(out=sc6[:, h:h + 1], in0=sc6[:, h:h + 1], in1=gt[:, 3 * h + br:3 * h + br + 1], op=ALU.mult),
                                         reads=[Bsc6, Bgt], writes=[Bsc6])
                                    P.op("dve", lambda e, hh=hh, hl=hl, h=h: e.scalar_tensor_tensor(
                                        out=acc[:, h * 128:(h + 1) * 128], in0=psum[5 + hh][0:4, hl * 130:hl * 130 + 128], scalar=sc6[:, h:h + 1],
                                        in1=acc[:, h * 128:(h + 1) * 128], op0=ALU.mult, op1=ALU.add), reads=[Bps[5 + hh], Bsc6, Bacc], writes=[Bacc])
                        P.op("act", lambda e: e.activation(out=accb[:], in_=acc[:], func=AF.Copy), reads=[Bacc], writes=[Baccb])
                        for h in range(6):
                            P.op("pe", lambda e, h=h: e.transpose(out=psb[:, h * 4:(h + 1) * 4], in_=accb[0:4, h * 128:(h + 1) * 128],
                                                                  identity=identb[0:4, 0:4]), reads=[Baccb, Bc], writes=[Bpsb])
                        P.op("act", lambda e: e.activation(out=mixo[:, :, :].rearrange("p a b -> p (a b)"), in_=psb[:, 0:24], func=AF.Copy),
                             reads=[Bpsb], writes=[Bmixo])
                        P.op("sp", lambda e, g=g, tokS=tokS: e.dma_start(out=mixT_s[:, 8 + 6 * g:14 + 6 * g, tokS:tokS + 4], in_=mixo[:, :, :]),
                             reads=[Bmixo], dma=True)
            P.emit_phase()

        with contextlib.ExitStack() as st:
            sb = lambda name, shape, dt: st.enter_context(nc.sbuf_tensor(uname(name), list(shape), dt))
            mixT = sb("mixT", [128, 32, TOK], BF16)
            wt = [sb("wo%d" % i, [128, 32, 512], BF16) for i in range(2)]
            xr = [sb("xr%d" % i, [128, 512], F32) for i in range(3)]
            Bmix = Buf()
            Bwt = [Buf(), Buf()]
            Bxr = [Buf() for _ in range(3)]
            Bps = [Buf() for _ in range(7)]
            for kq in range(4):
                P.op("sp", lambda e, kq=kq: e.dma_start(out=mixT[:, kq * 8:(kq + 1) * 8, :], in_=mixT_s[:, kq * 8:(kq + 1) * 8, :]),
                     writes=[Bmix], dma=True)
            psi = 0
            xi = 0
            for db in range(8):
                w = db % 2
                for kh in range(2):
                    P.op("pool", lambda e, db=db, kh=kh, w=w: e.dma_start(
                        out=wt[w][:, kh * 16:(kh + 1) * 16, :].rearrange("p k f -> p (k f)"),
                        in_=w_o[db, :, kh * 16:(kh + 1) * 16, :].rearrange("p k f -> p (k f)")),
                        writes=[Bwt[w]], dma=True)
                for (r0, m) in TT9:
                    p = psi % 7
                    psi += 1
                    x = xi % 3
                    xi += 1
                    P.op("sp", lambda e, x=x, r0=r0, m=m, db=db: e.dma_start(
                        out=xr[x][0:m, :], in_=x_own[r0:r0 + m, db * 512:(db + 1) * 512]), writes=[Bxr[x]], dma=True)
                    for k in range(32):
                        P.op("pe", lambda e, p=p, k=k, r0=r0, m=m, w=w: e.matmul(
                            psum[p][0:m, :], lhsT=mixT[:, k, r0:r0 + m], rhs=wt[w][:, k, :],
                            start=(k == 0), stop=(k == 31)), reads=[Bmix, Bwt[w]], writes=[Bps[p]])
                    P.op("dve", lambda e, x=x, p=p, m=m: e.scalar_tensor_tensor(
                        out=xr[x][0:m, :], in0=xr[x][0:m, :], scalar=ALPHA, in1=psum[p][0:m, :],
                        op0=ALU.mult, op1=ALU.add), reads=[Bps[p], Bxr[x]], writes=[Bxr[x]])
                    P.op("sp", lambda e, x=x, r0=r0, m=m, db=db: e.dma_start(
                        out=r_s[r0:r0 + m, db * 512:(db + 1) * 512], in_=xr[x][0:m, :]), reads=[Bxr[x]], dma=True)
            P.emit_phase()

        def ln_phase(src, g_idx, dst_f32, dst_T):
            with contextlib.ExitStack() as st:
                sb = lambda name, shape, dt: st.enter_context(nc.sbuf_tensor(uname(name), list(shape), dt))
                gt = sb("ln_g", [128, D], F32)
                bt = sb("ln_b", [128, D], F32)
                idb = sb("ln_idb", [128, 128], BF16)
                idf = sb("ln_idf", [128, 128], F32)
                rt = [sb("ln_r%d" % i, [128, D], F32) for i in range(2)]
                hb = sb("ln_hb", [128, D], BF16)
                hT = sb("ln_hT", [128, 32, 128], BF16)
                stats = sb("ln_stats", [128, 8, 6], F32)
                mv = sb("ln_mv", [128, 2], F32)
                rstd = sb("ln_rstd", [128, 1], F32)
                Bg, Bid = Buf(), Buf()
                Brt = [Buf(), Buf()]
                Bhb, BhT, Bst, Bmv, Brs = Buf(), Buf(), Buf(), Buf(), Buf()
                Bpsb = Buf()
                P.op("sp", lambda e: e.dma_start(out=gt[:], in_=lnp[g_idx]), writes=[Bg], dma=True)
                P.op("sp", lambda e: e.dma_start(out=bt[:], in_=lnp[g_idx + 1]), writes=[Bg], dma=True)
                P.op("sp", lambda e: e.dma_start(out=idf[:], in_=ident[:, :]), writes=[Bid], dma=True)
                P.op("dve", lambda e: e.tensor_copy(out=idb[:], in_=idf[:]), reads=[Bid], writes=[Bid])
                for ti, (r0, m) in enumerate(TT9):
                    r = ti % 2
                    P.op("sp", lambda e, r=r, r0=r0, m=m: e.dma_start(out=rt[r][0:m, :], in_=src[r0:r0 + m, :]),
                         writes=[Brt[r]], dma=True)
                    for c in range(8):
                        P.op("dve", lambda e, r=r, m=m, c=c: e.bn_stats(out=stats[0:m, c, :], in_=rt[r][0:m, c * 512:(c + 1) * 512]),
                             reads=[Brt[r]], writes=[Bst])
                    P.op("dve", lambda e, m=m: e.bn_aggr(out=mv[0:m, :], in_=stats[0:m, :, :].rearrange("p a b -> p (a b)")),
                         reads=[Bst], writes=[Bmv])
                    P.op("act", lambda e, m=m: e.activation(out=rstd[0:m, :], in_=mv[0:m, 1:2], func=AF.Sqrt, bias=EPS, scale=1.0),
                         reads=[Bmv], writes=[Brs])
                    P.op("dve", lambda e, m=m: e.reciprocal(out=rstd[0:m, :], in_=rstd[0:m, :]), reads=[Brs], writes=[Brs])
                    P.op("dve", lambda e, r=r, m=m: e.tensor_scalar(
                        out=rt[r][0:m, :], in0=rt[r][0:m, :], scalar1=mv[0:m, 0:1], scalar2=rstd[0:m, 0:1],
                        op0=ALU.subtract, op1=ALU.mult), reads=[Brt[r], Bmv, Brs], writes=[Brt[r]])
                    P.op("dve", lambda e, r=r, m=m: e.tensor_tensor(out=rt[r][0:m, :], in0=rt[r][0:m, :], in1=gt[0:m, :], op=ALU.mult),
                         reads=[Brt[r], Bg], writes=[Brt[r]])
                    P.op("dve", lambda e, r=r, m=m: e.tensor_tensor(out=rt[r][0:m, :], in0=rt[r][0:m, :], in1=bt[0:m, :], op=ALU.add),
                         reads=[Brt[r], Bg], writes=[Brt[r]])
                    P.op("sp", lambda e, r=r, r0=r0, m=m: e.dma_start(out=dst_f32[r0:r0 + m, :], in_=rt[r][0:m, :]),
                         reads=[Brt[r]], dma=True)
                    if dst_T is not None:
                        P.op("act", lambda e, r=r, m=m: e.activation(out=hb[0:m, :], in_=rt[r][0:m, :], func=AF.Copy),
                             reads=[Brt[r]], writes=[Bhb])
                        for kq in range(4):
                            for kk in range(8):
                                k = kq * 8 + kk
                                P.op("pe", lambda e, k=k, kk=kk, m=m: e.transpose(
                                    out=psb[:, kk * 128:kk * 128 + m], in_=hb[0:m, k * 128:(k + 1) * 128], identity=idb[0:m, 0:m]),
                                    reads=[Bhb, Bid], writes=[Bpsb])
                            pv = psb[:, :].rearrange("p (a b) -> p a b", b=128)
                            P.op("act", lambda e, kq=kq, m=m, pv=pv: e.activation(
                                out=hT[:, kq * 8:(kq + 1) * 8, 0:m], in_=pv[:, :, 0:m], func=AF.Copy),
                                reads=[Bpsb], writes=[BhT])
                        P.op("sp", lambda e, r0=r0, m=m: e.dma_start(out=dst_T[:, :, r0:r0 + m], in_=hT[:, :, 0:m]),
                             reads=[BhT], dma=True)
                P.emit_phase()

        ln_phase(r_s, 0, h_s, hT_s)

        for (t0, nt) in ((0, 512), (512, 528)):
            with contextlib.ExitStack() as st:
                sb = lambda name, shape, dt: st.enter_context(nc.sbuf_tensor(uname(name), list(shape), dt))
                hT = sb("f_hT", [128, 32, 528], BF16)
                ffT = sb("f_ffT", [128, NFC, 528], BF16)
                wg = [sb("f_wg%d" % i, [128, 32, 128], BF16) for i in range(2)]
                wu = [sb("f_wu%d" % i, [128, 32, 128], BF16) for i in range(2)]
                sg = [sb("f_sg%d" % i, [128, 528], F32) for i in range(2)]
                BhT, Bff = Buf(), [Buf() for _ in range(NFC)]
                Bwg, Bwu, Bsg = [Buf(), Buf()], [Buf(), Buf()], [Buf(), Buf()]
                Bps = [Buf() for _ in range(7)]
                Bpsb = Buf()
                for kq in range(4):
                    P.op("sp", lambda e, kq=kq: e.dma_start(out=hT[:, kq * 8:(kq + 1) * 8, 0:nt],
                                                            in_=hT_s[:, kq * 8:(kq + 1) * 8, t0:t0 + nt]),
                         writes=[BhT], dma=True)
                segs = [(0, 512)] + ([(512, 16)] if nt > 512 else [])
                for f in range(NFC):
                    w = f % 2
                    P.op("pool", lambda e, f=f, w=w: e.dma_start(out=wg[w][:, :, :].rearrange("p k f -> p (k f)"),
                                                               in_=w_g[f].rearrange("p k f -> p (k f)")), writes=[Bwg[w]], dma=True)
                    P.op("pool", lambda e, f=f, w=w: e.dma_start(out=wu[w][:, :, :].rearrange("p k f -> p (k f)"),
                                                               in_=w_u[f].rearrange("p k f -> p (k f)")), writes=[Bwu[w]], dma=True)
                    pb = 3 * (f % 2)
                    for si, (c0, n) in enumerate(segs):
                        if si == 0:
                            pg, pu, g0, u0 = pb, pb + 1, 0, 0
                        else:
                            pg, pu, g0, u0 = pb + 2, pb + 2, 0, 16
                        for k in range(32):
                            P.op("pe", lambda e, pg=pg, g0=g0, k=k, c0=c0, n=n, w=w: e.matmul(
                                psum[pg][:, g0:g0 + n], lhsT=wg[w][:, k, :], rhs=hT[:, k, c0:c0 + n],
                                start=(k == 0), stop=(k == 31)), reads=[BhT, Bwg[w]], writes=[Bps[pg]])
                        P.op("act", lambda e, pg=pg, g0=g0, c0=c0, n=n, w=w: e.activation(
                            out=sg[w][:, c0:c0 + n], in_=psum[pg][:, g0:g0 + n], func=AF.Silu), reads=[Bps[pg]], writes=[Bsg[w]])
                        for k in range(32):
                            P.op("pe", lambda e, pu=pu, u0=u0, k=k, c0=c0, n=n, w=w: e.matmul(
                                psum[pu][:, u0:u0 + n], lhsT=wu[w][:, k, :], rhs=hT[:, k, c0:c0 + n],
                                start=(k == 0), stop=(k == 31)), reads=[BhT, Bwu[w]], writes=[Bps[pu]])
                        P.op("dve", lambda e, pu=pu, u0=u0, c0=c0, n=n, w=w, f=f: e.tensor_tensor(
                            out=ffT[:, f, c0:c0 + n], in0=sg[w][:, c0:c0 + n], in1=psum[pu][:, u0:u0 + n], op=ALU.mult),
                            reads=[Bps[pu], Bsg[w]], writes=[Bff[f]])
                wd = [sb("f_wd%d" % i, [128, 6, 512], BF16) for i in range(2)]
                hr = [sb("f_hr%d" % i, [128, 512], F32) for i in range(3)]
                Bwd = [Buf(), Buf()]
                Bhr = [Buf() for _ in range(3)]
                tts = [(i * 128, 128) for i in range(4)] + ([(512, 16)] if nt > 512 else [])
                FG = [(f0, min(6, NFC - f0)) for f0 in range(0, NFC, 6)]
                wi = 0
                hi = 0
                for db in range(8):
                    for (f0, nf) in FG:
                        w = wi % 2
                        wi += 1
                        P.op("pool", lambda e, db=db, f0=f0, nf=nf, w=w: e.dma_start(
                            out=wd[w][:, 0:nf, :].rearrange("p k f -> p (k f)"),
                            in_=w_d[db, :, f0:f0 + nf, :].rearrange("p k f -> p (k f)")), writes=[Bwd[w]], dma=True)
                        for fi in range(nf):
                            f = f0 + fi
                            for ti, (c0, m) in enumerate(tts):
                                P.op("pe", lambda e, ti=ti, c0=c0, m=m, f=f, fi=fi, w=w: e.matmul(
                                    psum[ti][0:m, :], lhsT=ffT[:, f, c0:c0 + m], rhs=wd[w][:, fi, :],
                                    start=(f == 0), stop=(f == NFC - 1)), reads=[Bff[f], Bwd[w]], writes=[Bps[ti]])
                    for ti, (c0, m) in enumerate(tts):
                        x = hi % 3
                        hi += 1
                        r0 = t0 + c0
                        P.op("sp", lambda e, x=x, r0=r0, m=m, db=db: e.dma_start(
                            out=hr[x][0:m, :], in_=h_s[r0:r0 + m, db * 512:(db + 1) * 512]), writes=[Bhr[x]], dma=True)
                        P.op("dve", lambda e, x=x, ti=ti, m=m: e.scalar_tensor_tensor(
                            out=hr[x][0:m, :], in0=hr[x][0:m, :], scalar=ALPHA, in1=psum[ti][0:m, :],
                            op0=ALU.mult, op1=ALU.add), reads=[Bps[ti], Bhr[x]], writes=[Bhr[x]])
                        P.op("sp", lambda e, x=x, r0=r0, m=m, db=db: e.dma_start(
                            out=y_s[r0:r0 + m, db * 512:(db + 1) * 512], in_=hr[x][0:m, :]), reads=[Bhr[x]], dma=True)
                P.emit_phase()

        ln_phase(y_s, 2, y_o, None)
    return nc


_NC_CACHE = {}


def _tile_w(w, nblk, blk):
    return np.ascontiguousarray(w.reshape(32, 128, nblk, blk).transpose(2, 1, 0, 3))


def kernel(x_prompt, x_sample, cache_k_cmp, cache_v_cmp, cache_k_slc, cache_v_slc,
           state_k_win, state_v_win, state_pool, page_table, w_in,
           w_cmp1_k, pe_cmp_k, w_cmp2_k, w_cmp1_v, pe_cmp_v, w_cmp2_v,
           w_pool, pool_scale, w_o, ln1_g, ln1_b, w_gate, w_up, w_down, ln2_g, ln2_b):
    f32 = np.float32
    x_prompt = np.asarray(x_prompt, f32)
    x_sample = np.asarray(x_sample, f32)
    if "nc" not in _NC_CACHE:
        _NC_CACHE["nc"] = build_program()
    nc = _NC_CACHE["nc"]

    w_in_p = np.zeros((D, E_PAD), f32)
    w_in_p[:, :7240] = np.asarray(w_in, f32)[0]
    w_in_t = _tile_w(w_in_p, 15, 512)
    w_o_t = _tile_w(np.asarray(w_o, f32)[0], 8, 512)
    w_g_t = _tile_w(np.asarray(w_gate, f32)[0], NFC, 128)
    w_u_t = _tile_w(np.asarray(w_up, f32)[0], NFC, 128)
    w_d_t = np.ascontiguousarray(np.asarray(w_down, f32)[0].reshape(NFC, 128, 8, 512).transpose(2, 1, 0, 3))
    lnp = np.ascontiguousarray(np.broadcast_to(
        np.stack([np.asarray(a, f32)[0] for a in (ln1_g, ln1_b, ln2_g, ln2_b)])[:, None, :], (4, 128, D)))
    ident = np.eye(128, dtype=f32)
    wp_h = np.ascontiguousarray(np.asarray(w_pool, f32)[0].reshape(4, 2, 128, 256).transpose(2, 0, 1, 3))
    psc_h = np.ascontiguousarray(np.asarray(pool_scale, f32)[0].reshape(8, 128).T)
    half = 64
    inv = (10000.0 ** (-np.arange(half, dtype=f32) / half)).astype(f32)
    NEGV = -30000.0
    tt_, kk_ = np.meshgrid(np.arange(128), np.arange(128), indexing="ij")
    cbc_h = np.where(kk_ <= tt_, 0.0, NEGV).astype(f32)
    cba_h = np.where(kk_ > tt_, 0.0, NEGV).astype(f32)
    w1k_h = np.ascontiguousarray(np.asarray(w_cmp1_k, f32)[0].transpose(1, 0, 2))
    w1v_h = np.ascontiguousarray(np.asarray(w_cmp1_v, f32)[0].transpose(1, 0, 2))
    w2k_ = np.asarray(w_cmp2_k, f32)[0]
    w2_h = np.ascontiguousarray(np.stack([w2k_, np.roll(w2k_, 64, axis=1), np.asarray(w_cmp2_v, f32)[0]], 1))
    peT_h = np.ascontiguousarray(np.stack([np.asarray(pe_cmp_k, f32)[0].T, np.asarray(pe_cmp_v, f32)[0].T], 1))
    sgn_h = np.where(np.arange(128) < 64, -1.0, 1.0).astype(f32)
    ckc_h = np.ascontiguousarray(np.asarray(cache_k_cmp, f32)[0]).reshape(2560 * 32, 2048)
    cvc_h = np.ascontiguousarray(np.asarray(cache_v_cmp, f32)[0]).reshape(2560 * 32, 2048)
    cks_h = np.ascontiguousarray(np.asarray(cache_k_slc, f32)[0]).reshape(2560 * 32, 2048)
    cvs_h = np.ascontiguousarray(np.asarray(cache_v_slc, f32)[0]).reshape(2560 * 32, 2048)
    ptab_all = np.asarray(page_table).astype(np.int32)
    oh_h = np.zeros((128, 5), f32)
    oh_h[np.arange(128), np.arange(128) // 32] = 1.0
    oh_h[:, 4] = np.arange(128) % 32
    sang = (16.0 * np.arange(512) + 31.0).astype(f32)[None, :] * inv[np.arange(128) % 64][:, None]
    ccs_h = np.cos(sang).astype(f32)
    scs_h = (np.sin(sang) * sgn_h[:, None]).astype(f32)
    addms_h = np.zeros((4, 129), f32)
    addms_h[:, [0, 127, 128]] = 1e30

    in_maps = []
    for c in range(8):
        b, j = c // 4, c % 4
        pad = 3 - j
        xs = np.zeros((4096, D), f32)
        xs[pad * 128:] = x_prompt[b, :4096 - pad * 128]
        xTs = np.ascontiguousarray(xs.reshape(4, 1024, 32, 128).transpose(0, 3, 2, 1))
        xsm = x_sample[4 * c:4 * c + 4].reshape(16, D)
        xsT = np.ascontiguousarray(xsm.reshape(16, 32, 128).transpose(2, 1, 0))
        own_rows = np.concatenate([np.arange(128) + (4 * i + 3) * 128 for i in range(8)])
        x_own = np.ascontiguousarray(np.concatenate([xs[own_rows], xsm], 0))
        pos = np.concatenate([np.arange(4096) - pad * 128, np.tile(8192 + np.arange(4), 4)]).astype(f32)
        ang = pos[:, None] * inv[None, :]
        own_pos = np.concatenate([pos[own_rows], pos[4096:]])
        rc = np.stack([1.0 / np.minimum(np.maximum(own_pos, 0) + 1.0, float(2 << gi)) for gi in range(4)]).astype(f32)
        rc_h = np.ascontiguousarray(np.broadcast_to(rc[None], (128, 4, TOK)))
        n_ = np.arange(256)
        cpos = (16 * n_ + 31 - pad * 128).astype(f32)
        cang = cpos[None, :] * inv[np.arange(128) % 64][:, None]
        ccmp_h = np.cos(cang).astype(f32)
        scmp_h = (np.sin(cang) * sgn_h[:, None]).astype(f32)
        ccmp_h[:, 255] = 0
        scmp_h[:, 255] = 0
        s_t = ((4 * np.arange(8)[None, :] + 3) * 128 + np.arange(128)[:, None])
        cval_h = ((16 * n_[None, None, :] + 31 <= s_t[:, :, None]) & (16 * n_[None, None, :] >= pad * 128)
                  & (n_[None, None, :] <= 254)).astype(f32)
        blk_ = np.arange(64)[None, None, :]
        cur_ = (s_t // 64)[:, :, None]
        b0_ = 2 * pad
        valid_ = (blk_ >= b0_) & (blk_ <= cur_)
        forced_ = (blk_ == b0_) | (blk_ == cur_) | (blk_ == cur_ - 1)
        addm_h = np.where(forced_, 1e30, np.where(valid_, 0.0, -1e30)).astype(f32)
        vblk_h = valid_.astype(f32)
        padb_h = np.ascontiguousarray(np.broadcast_to(np.where(np.arange(32) < pad, NEGV, 0.0).astype(f32)[None, :], (128, 32)))
        stp_h = np.ascontiguousarray(np.asarray(state_pool, f32)[0, 4 * c:4 * c + 4].reshape(4, 15, 8, 128).transpose(3, 2, 0, 1))
        in_maps.append({
            "xTs": xTs, "xsT": xsT, "x_own": x_own, "w_in": w_in_t, "w_o": w_o_t, "w_g": w_g_t, "w_u": w_u_t,
            "w_d": w_d_t, "cosT": np.cos(ang).astype(f32), "sinT": np.sin(ang).astype(f32), "lnp": lnp,
            "cbc_d": cbc_h, "cba_d": cba_h, "padb_d": padb_h, "w1k_d": w1k_h, "w1v_d": w1v_h, "w2_d": w2_h,
            "peT_d": peT_h, "ccmp_d": ccmp_h, "scmp_d": scmp_h, "cval_d": np.ascontiguousarray(cval_h),
            "addm_d": np.ascontiguousarray(addm_h), "vblk_d": np.ascontiguousarray(vblk_h),
            "ckc_d": ckc_h, "cvc_d": cvc_h, "cks_d": cks_h, "cvs_d": cvs_h,
            "ptab_d": np.ascontiguousarray(ptab_all[4 * c:4 * c + 4]), "oh_d": oh_h, "ccs_d": ccs_h, "scs_d": scs_h,
            "addms_d": addms_h,
            "ident": ident, "rc_d": rc_h, "wp_d": wp_h, "psc_d": psc_h, "stp_d": stp_h,
            "st_kw": np.ascontiguousarray(np.asarray(state_k_win, f32)[0, 4 * c:4 * c + 4].reshape(4, 512, 512)),
            "st_vw": np.ascontiguousarray(np.asarray(state_v_win, f32)[0, 4 * c:4 * c + 4].reshape(4, 512, 512)),
            "st_pool": np.ascontiguousarray(np.asarray(state_pool, f32)[0, 4 * c:4 * c + 4]),
        })
    res = run_bass_kernel_spmd(nc, in_maps, core_ids=list(range(8)))
    R = res.results

    y_prompt = np.zeros((2, 4096, D), f32)
    y_sample = np.zeros((32, 4, D), f32)
    kvp = [np.zeros((1, 2, 4096, 4, 128), f32) for _ in range(6)]
    kvs = [np.zeros((1, 32, 4, 4, 128), f32) for _ in range(6)]
    kw_s = np.zeros((1, 32, 512, 4, 128), f32)
    vw_s = np.zeros((1, 32, 512, 4, 128), f32)
    pool_p = np.zeros((1, 2, 15, 1024), f32)
    pool_s = np.zeros((1, 32, 15, 1024), f32)
    for c in range(8):
        b, j = c // 4, c % 4
        r = R[c]
        yo = r["y_o"]
        for i in range(8):
            g = 4 * i + j
            y_prompt[b, g * 128:(g + 1) * 128] = yo[i * 128:(i + 1) * 128]
        y_sample[4 * c:4 * c + 4] = yo[1024:1040].reshape(4, 4, D)
        for t in range(6):
            kvs[t][0, 4 * c:4 * c + 4] = r["kvout"][t, 4096:4112].reshape(4, 4, 4, 128)
            if j == 3:
                kvp[t][0, b] = r["kvout"][t, 0:4096].reshape(4096, 4, 128)
        kw_s[0, 4 * c:4 * c + 4] = r["kws_o"].reshape(4, 512, 4, 128)
        vw_s[0, 4 * c:4 * c + 4] = r["vws_o"].reshape(4, 512, 4, 128)
        pool_s[0, 4 * c:4 * c + 4] = r["pools_o"]
        if j == 3:
            pool_p[0, b] = r["poolp_o"]
    kc_p, vc_p, ks_p, vs_p, kw_full, vw_full = kvp
    return (y_prompt, y_sample, kc_p, vc_p, ks_p, vs_p,
            np.ascontiguousarray(kw_full[:, :, 4096 - 512:]), np.ascontiguousarray(vw_full[:, :, 4096 - 512:]),
            pool_p, kvs[0], kvs[1], kvs[2], kvs[3], kw_s, vw_s, pool_s)
```
